# Optimizing a Trainium2 kernel written in Bass

```python
import jax, jax.numpy as jnp
from jax import lax
import numpy as np

D_MODEL = 1024
BATCH = 8
SEQ = 4096
DEPTH = 2

N_EVEN = (DEPTH + 1) // 2
N_ODD = DEPTH // 2

GM_WIDTH = 512
GM_HEADS = 8
GM_HEAD_DIM = GM_WIDTH // GM_HEADS
CHUNK = 128
POOL_WIDTH = 512
POOL_WINDOWS = (2, 4, 8, 16)
POOL_GROUPS = len(POOL_WINDOWS)
POOL_GROUP_DIM = POOL_WIDTH // POOL_GROUPS
EVEN_IN = 2 * GM_WIDTH + POOL_WIDTH
EVEN_MIX = GM_WIDTH + POOL_WIDTH
MLA_HEADS = 16
QK_NOPE = 64
QK_ROPE = 32
V_HEAD = 64
Q_LORA = 384
KV_LORA = 256
ODD_IN = Q_LORA + KV_LORA + QK_ROPE
ROPE_THETA = 10000.0
Q_BLOCK = 128
_FF_RAW = -(-8 * D_MODEL // 3)
D_FF = -(-_FF_RAW // 256) * 256
ALPHA = (2 * DEPTH) ** 0.25
BETA = (8 * DEPTH) ** -0.25
LN_EPS = 1e-5
RMS_EPS = 1e-6

kernel_name = "hybrid_gmlp_pool_mla_deepnorm"


def layer_norm(x, g, b):
    xf = x.astype(jnp.float32)
    mu = xf.mean(-1, keepdims=True)
    var = jnp.square(xf - mu).mean(-1, keepdims=True)
    return ((xf - mu) * lax.rsqrt(var + LN_EPS) * g + b).astype(x.dtype)


def rms_norm(x, g):
    xf = x.astype(jnp.float32)
    return (xf * lax.rsqrt(jnp.square(xf).mean(-1, keepdims=True) + RMS_EPS) * g).astype(x.dtype)


def chunked_spatial_gating(z, vnorm_g, vnorm_b, w_s, b_s):
    bsz, s, _ = z.shape
    u, v = z[..., :GM_WIDTH], z[..., GM_WIDTH:]
    v = layer_norm(v, vnorm_g, vnorm_b)
    v = v.reshape(bsz, s // CHUNK, CHUNK, GM_HEADS, GM_HEAD_DIM)
    causal = jnp.tril(jnp.ones((CHUNK, CHUNK), dtype=bool))
    w = jnp.where(causal[None], w_s, 0)
    mixed = jnp.einsum('hts,bcshd->bcthd', w, v) + b_s.T[None, None, :, :, None]
    return u * mixed.reshape(bsz, s, GM_WIDTH)


def multiscale_pool(xp, w_pool, scale):
    bsz, s, _ = xp.shape
    xf = xp.astype(jnp.float32).reshape(bsz, s, POOL_GROUPS, POOL_GROUP_DIM)
    csum = jnp.concatenate([jnp.zeros_like(xf[:, :1]), jnp.cumsum(xf, axis=1)], axis=1)
    t = jnp.arange(s)
    outs = []
    for g, w in enumerate(POOL_WINDOWS):
        c = csum[:, :, g]
        upper = c[:, 1:]
        lower = jnp.concatenate([jnp.zeros_like(c[:, :w - 1]), c[:, :s + 1 - w]], axis=1)
        count = jnp.minimum(t + 1, w).astype(jnp.float32)[None, :, None]
        outs.append((upper - lower) / count - xf[:, :, g])
    pooled = jnp.stack(outs, axis=2).astype(xp.dtype)
    mixed = jnp.einsum('bsgc,gcd->bsgd', pooled, w_pool).reshape(bsz, s, POOL_WIDTH)
    return mixed * scale


def even_mixer(x, w_in, vnorm_g, vnorm_b, w_s, b_s, w_pool, pool_scale, w_out):
    h = x @ w_in
    a = chunked_spatial_gating(jax.nn.gelu(h[..., :2 * GM_WIDTH], approximate=False),
                               vnorm_g, vnorm_b, w_s, b_s)
    bp = multiscale_pool(h[..., 2 * GM_WIDTH:], w_pool, pool_scale)
    return jnp.concatenate([a, bp], axis=-1) @ w_out


def apply_rope(x, cos, sin):
    xf = x.astype(jnp.float32)
    half = xf.shape[-1] // 2
    x1, x2 = xf[..., :half], xf[..., half:]
    return jnp.concatenate([x1 * cos - x2 * sin, x1 * sin + x2 * cos], axis=-1).astype(x.dtype)


def causal_attention(q, k, v):
    bsz, s, h, dqk = q.shape
    nb = s // Q_BLOCK
    qb = q.reshape(bsz, nb, Q_BLOCK, h, dqk).transpose(1, 0, 2, 3, 4)
    key_idx = jnp.arange(s)
    scale = dqk ** -0.5

    def block(args):
        q_blk, i = args
        sc = jnp.einsum('bqhd,bkhd->bhqk', q_blk, k,
                        preferred_element_type=jnp.float32) * scale
        q_idx = i * Q_BLOCK + jnp.arange(Q_BLOCK)
        sc = jnp.where(key_idx[None, :] <= q_idx[:, None], sc, -jnp.inf)
        p = jax.nn.softmax(sc, axis=-1).astype(v.dtype)
        return jnp.einsum('bhqk,bkhd->bqhd', p, v)

    out = lax.map(block, (qb, jnp.arange(nb)))
    return out.transpose(1, 0, 2, 3, 4).reshape(bsz, s, h, v.shape[-1])


def mla(x, positions, w_in, q_norm_g, w_q_up, kv_norm_g, w_kv_up, w_out):
    bsz, s, _ = x.shape
    h = x @ w_in
    c_q = h[..., :Q_LORA]
    c_kv = h[..., Q_LORA:Q_LORA + KV_LORA]
    k_rope = h[..., Q_LORA + KV_LORA:]
    q = (rms_norm(c_q, q_norm_g) @ w_q_up).reshape(bsz, s, MLA_HEADS, QK_NOPE + QK_ROPE)
    kv = (rms_norm(c_kv, kv_norm_g) @ w_kv_up).reshape(bsz, s, MLA_HEADS, QK_NOPE + V_HEAD)
    q_nope, q_rope = q[..., :QK_NOPE], q[..., QK_NOPE:]
    k_nope, v = kv[..., :QK_NOPE], kv[..., QK_NOPE:]
    freqs = ROPE_THETA ** (-jnp.arange(0, QK_ROPE, 2, dtype=jnp.float32) / QK_ROPE)
    ang = positions.astype(jnp.float32)[..., None] * freqs
    cos, sin = jnp.cos(ang), jnp.sin(ang)
    q_rope = apply_rope(q_rope, cos[:, :, None], sin[:, :, None])
    k_rope = apply_rope(k_rope, cos, sin)
    q = jnp.concatenate([q_nope, q_rope], axis=-1)
    k = jnp.concatenate([k_nope, jnp.broadcast_to(k_rope[:, :, None], (bsz, s, MLA_HEADS, QK_ROPE))], axis=-1)
    o = causal_attention(q, k, v)
    return o.reshape(bsz, s, MLA_HEADS * V_HEAD) @ w_out


def swiglu(x, w_gate_up, w_down):
    gu = x @ w_gate_up
    return (jax.nn.silu(gu[..., :D_FF]) * gu[..., D_FF:]) @ w_down


def setup_inputs(seed: int = 0) -> dict:
    key = jax.random.key(seed)
    ks = jax.random.split(key, 24)
    nrm = jax.random.normal
    f32 = jnp.float32
    x = nrm(ks[0], (BATCH, SEQ, D_MODEL), f32)
    offs = jax.random.randint(ks[1], (BATCH, 1), 0, 1024, dtype=jnp.int32)
    positions = jnp.arange(SEQ, dtype=jnp.int32)[None, :] + offs
    return {
        "x": x,
        "positions": positions,
        "even_w_in": nrm(ks[2], (N_EVEN, D_MODEL, EVEN_IN), f32) * D_MODEL ** -0.5,
        "even_vnorm_g": 1.0 + 0.1 * nrm(ks[3], (N_EVEN, GM_WIDTH), f32),
        "even_vnorm_b": 0.02 * nrm(ks[4], (N_EVEN, GM_WIDTH), f32),
        "even_spatial_w": nrm(ks[5], (N_EVEN, GM_HEADS, CHUNK, CHUNK), f32) * CHUNK ** -0.5,
        "even_spatial_b": 1.0 + 0.1 * nrm(ks[6], (N_EVEN, GM_HEADS, CHUNK), f32),
        "even_pool_w": nrm(ks[7], (N_EVEN, POOL_GROUPS, POOL_GROUP_DIM, POOL_GROUP_DIM), f32) * POOL_GROUP_DIM ** -0.5,
        "even_pool_scale": 1.0 + 0.1 * nrm(ks[8], (N_EVEN, POOL_WIDTH), f32),
        "even_w_out": nrm(ks[9], (N_EVEN, EVEN_MIX, D_MODEL), f32) * (EVEN_MIX ** -0.5 * BETA),
        "odd_w_in": nrm(ks[10], (N_ODD, D_MODEL, ODD_IN), f32) * D_MODEL ** -0.5,
        "odd_q_norm_g": 1.0 + 0.1 * nrm(ks[11], (N_ODD, Q_LORA), f32),
        "odd_w_q_up": nrm(ks[12], (N_ODD, Q_LORA, MLA_HEADS * (QK_NOPE + QK_ROPE)), f32) * Q_LORA ** -0.5,
        "odd_kv_norm_g": 1.0 + 0.1 * nrm(ks[13], (N_ODD, KV_LORA), f32),
        "odd_w_kv_up": nrm(ks[14], (N_ODD, KV_LORA, MLA_HEADS * (QK_NOPE + V_HEAD)), f32) * KV_LORA ** -0.5,
        "odd_w_out": nrm(ks[15], (N_ODD, MLA_HEADS * V_HEAD, D_MODEL), f32) * ((MLA_HEADS * V_HEAD) ** -0.5 * BETA),
        "mix_ln_g": 1.0 + 0.1 * nrm(ks[16], (DEPTH, D_MODEL), f32),
        "mix_ln_b": 0.02 * nrm(ks[17], (DEPTH, D_MODEL), f32),
        "ffn_w_gate_up": nrm(ks[18], (DEPTH, D_MODEL, 2 * D_FF), f32) * D_MODEL ** -0.5,
        "ffn_w_down": nrm(ks[19], (DEPTH, D_FF, D_MODEL), f32) * (D_FF ** -0.5 * BETA),
        "ffn_ln_g": 1.0 + 0.1 * nrm(ks[20], (DEPTH, D_MODEL), f32),
        "ffn_ln_b": 0.02 * nrm(ks[21], (DEPTH, D_MODEL), f32),
    }


def reference(x, positions, even_w_in, even_vnorm_g, even_vnorm_b, even_spatial_w,
              even_spatial_b, even_pool_w, even_pool_scale, even_w_out,
              odd_w_in, odd_q_norm_g, odd_w_q_up, odd_kv_norm_g, odd_w_kv_up, odd_w_out,
              mix_ln_g, mix_ln_b, ffn_w_gate_up, ffn_w_down, ffn_ln_g, ffn_ln_b):
    for layer in range(DEPTH):
        j = layer // 2
        if layer % 2 == 0:
            m = even_mixer(x, even_w_in[j], even_vnorm_g[j], even_vnorm_b[j], even_spatial_w[j],
                           even_spatial_b[j], even_pool_w[j], even_pool_scale[j], even_w_out[j])
        else:
            m = mla(x, positions, odd_w_in[j], odd_q_norm_g[j], odd_w_q_up[j],
                    odd_kv_norm_g[j], odd_w_kv_up[j], odd_w_out[j])
        x = layer_norm(ALPHA * x + m, mix_ln_g[layer], mix_ln_b[layer])
        x = layer_norm(ALPHA * x + swiglu(x, ffn_w_gate_up[layer], ffn_w_down[layer]),
                       ffn_ln_g[layer], ffn_ln_b[layer])
    return x
```

```python
import numpy as np
from contextlib import ExitStack
import concourse.bass as bass
import concourse.mybir as mybir
from concourse.bass_utils import run_bass_kernel_spmd

F32 = mybir.dt.float32
BF16 = mybir.dt.bfloat16
I32 = mybir.dt.int32
AF = mybir.ActivationFunctionType
ALU = mybir.AluOpType

S = 4096
D = 1024
NCH = S // 128
DFF = 2816
NFF = DFF // 128
ALPHA = float((2 * 2) ** 0.25)
LN_EPS = 1e-5
RMS_EPS = 1e-6
TWO_PI = float(2 * np.pi)
PI = float(np.pi)
NH = 16
SCALE = float(96 ** -0.5)


class Reg:
    __slots__ = ("name", "excl", "writers", "readers")

    def __init__(self, name, excl=False):
        self.name = name
        self.excl = excl
        self.writers = []
        self.readers = []


class Ins:
    __slots__ = ("eng", "fn", "dma", "deps", "signal", "count", "dsem", "dval", "seq", "emitted", "xw")

    def __init__(self, eng, fn, dma):
        self.eng = eng
        self.fn = fn
        self.dma = dma
        self.deps = []
        self.signal = False
        self.count = None
        self.dsem = None
        self.dval = None
        self.seq = None
        self.emitted = False
        self.xw = []


class Prog:
    def __init__(self, nc, es):
        self.nc = nc
        self.E = {"pe": nc.tensor, "act": nc.scalar, "dve": nc.vector, "pool": nc.gpsimd, "sp": nc.sync}
        self.sem = {e: es.enter_context(nc.semaphore("c_" + e)) for e in ("pe", "act", "dve", "pool")}
        self.cnt = {e: 0 for e in self.sem}
        self.dsems = {}
        for q, n in (("sp", 12), ("pool", 8), ("act", 4)):
            self.dsems[q] = [es.enter_context(nc.semaphore("d_%s%d" % (q, i))) for i in range(n)]
        self.dnext = {q: 0 for q in self.dsems}
        self.dval = {}
        self.dlast = {}
        self.pending = []
        self.seq = {e: 0 for e in self.E}
        self.sig_hist = {e: [] for e in self.sem}
        self.waited = {e: {x: 0 for x in self.sem} for e in self.E}
        self.dobs = {e: {} for e in self.E}
        self.extra = {e: [] for e in self.E}
        self.last = {e: None for e in self.E}

    def _add(self, eng, fn, reads, writes, dma, part):
        I = Ins(eng, fn, dma)
        I.seq = self.seq[eng]
        self.seq[eng] += 1
        deps = []
        for r in reads:
            if r.excl:
                deps += [(d, "raw") for d in r.writers] + [(d, "war") for d in r.readers]
            else:
                deps += [(d, "raw") for d in r.writers]
        for w in writes:
            if part and not w.excl:
                deps += [(d, "war") for d in w.readers]
            else:
                deps += [(d, "war") for d in w.readers] + [(d, "waw") for d in w.writers]
        for d in self.extra[eng]:
            deps.append((d, "raw"))
        self.extra[eng] = []
        if dma:
            q = eng
            sems = self.dsems[q]
            sem = sems[self.dnext[q] % len(sems)]
            self.dnext[q] += 1
            prev = self.dlast.get(id(sem))
            if prev is not None:
                deps.append((prev, "raw"))
            I.dsem = sem
            I.dval = self.dval.get(id(sem), 0) + 16
            self.dval[id(sem)] = I.dval
            self.dlast[id(sem)] = I
        seen = set()
        for d, kind in deps:
            if d is I or id(d) in seen:
                continue
            if d.dma or dma:
                pass
            elif d.eng == eng:
                if eng == "pe" or kind != "raw":
                    continue
            seen.add(id(d))
            I.deps.append(d)
            if not d.dma and not d.emitted:
                d.signal = True
        for r in reads:
            if not dma:
                r.readers = [x for x in r.readers if x.dma or x.eng != eng]
            r.readers.append(I)
        for w in writes:
            if part and not w.excl:
                if w.readers:
                    w.writers = [I]
                    w.readers = []
                else:
                    if not dma:
                        w.writers = [x for x in w.writers if x.dma or x.eng != eng]
                    w.writers.append(I)
            else:
                w.writers = [I]
                w.readers = []
        self.pending.append(I)
        self.last[eng] = I
        return I

    def op(self, eng, fn, reads=(), writes=(), part=False):
        return self._add(eng, fn, list(reads), list(writes), False, part)

    def dma(self, eng, fn, reads=(), writes=(), part=False):
        return self._add(eng, fn, list(reads), list(writes), True, part)

    def _count_of(self, d):
        if d.count is not None:
            return d.count
        for seq, c in self.sig_hist[d.eng]:
            if seq >= d.seq:
                return c
        raise RuntimeError("no signal after dep on %s" % d.eng)

    def flush(self):
        lastp = {}
        for I in self.pending:
            if not I.dma and I.eng in self.sem:
                lastp[I.eng] = I
        for I in lastp.values():
            I.signal = True
        for I in self.pending:
            e = I.eng
            eng = self.E[e]
            waits = []
            need_c = {}
            for d in I.deps:
                if d.dma:
                    k = id(d.dsem)
                    if self.dobs[e].get(k, 0) < d.dval:
                        self.dobs[e][k] = d.dval
                        waits.append((d.dsem, d.dval))
                else:
                    c = self._count_of(d)
                    if c > need_c.get(d.eng, 0):
                        need_c[d.eng] = c
            for x, c in need_c.items():
                if self.waited[e][x] < c:
                    self.waited[e][x] = c
                    waits.append((self.sem[x], c))
            for sem, val in I.xw:
                k = id(sem)
                if self.dobs[e].get(k, 0) < val:
                    self.dobs[e][k] = val
                    waits.append((sem, val))
            best = {}
            for sem, val in waits:
                k = id(sem)
                if k not in best or best[k][1] < val:
                    best[k] = (sem, val)
            waits = list(best.values())
            while len(waits) > 2:
                a = waits.pop()
                b = waits.pop()
                eng.wait_ge(a[0], a[1])
                eng.wait_ge(b[0], b[1])
                eng.nop()
            for sem, val in waits:
                eng.wait_ge(sem, val)
            bi = I.fn()
            if I.dma:
                bi.then_inc(I.dsem, 16)
            elif I.signal:
                self.cnt[e] += 1
                I.count = self.cnt[e]
                bi.then_inc(self.sem[e], 1)
                self.sig_hist[e].append((I.seq, I.count))
            I.emitted = True
            I.fn = None
        self.pending = []
        for e in self.sig_hist:
            if len(self.sig_hist[e]) > 4:
                self.sig_hist[e] = self.sig_hist[e][-4:]

    def all_dma_waits(self):
        out = []
        for q in self.dsems:
            for sem in self.dsems[q]:
                v = self.dval.get(id(sem), 0)
                if v:
                    out.append((sem, v))
        return out

    def barrier(self, marks):
        ms = []
        for e in ("act", "dve", "pool"):
            m = self.op(e, marks[e])
            m.xw = self.all_dma_waits()
            m.signal = True
            ms.append(m)
        for e in ("act", "dve", "pool", "sp"):
            self.extra[e] = list(ms)
        self.flush()

    def final_wait(self):
        eng = self.E["sp"]
        for sem, val in self.all_dma_waits():
            eng.wait_ge(sem, val)
            eng.nop()


CONST_SPECS = {
    "c_ident": ([128, 128], F32),
    "c_triu": ([128, 128], F32),
    "c_poolA": ([12, 128, 128], F32),
    "c_col": ([128, 4], F32),
    "c_sel": ([128, 64], F32),
}

W_SPECS = {
    "even_w_in": [1024, 1536], "even_vnorm_g": [1, 512], "even_vnorm_b": [1, 512],
    "even_spatial_w": [8, 128, 128], "even_spatial_b": [8, 128], "even_pool_w": [4, 128, 128],
    "even_pool_scale": [1, 512], "even_w_out": [1024, 1024],
    "odd_w_in": [1024, 672], "odd_q_norm_g": [1, 384], "odd_w_q_up": [384, 1536],
    "odd_kv_norm_g": [1, 256], "odd_w_kv_up": [256, 2048], "odd_w_out": [1024, 1024],
    "mix_ln_g": [2, 1024], "mix_ln_b": [2, 1024], "ffn_w_gate_up": [2, 1024, 5632],
    "ffn_w_down": [2, 2816, 1024], "ffn_ln_g": [2, 1024], "ffn_ln_b": [2, 1024],
}


def host_consts():
    ident = np.eye(128, dtype=np.float32)
    triu = np.triu(np.ones((128, 128), dtype=np.float32))
    A = np.zeros((12, 128, 128), dtype=np.float32)
    s = np.arange(128)[:, None]
    t = np.arange(128)[None, :]
    for g, w in enumerate((2, 4, 8, 16)):
        A[g] = ((s <= t) & (s > t - w)) / np.float32(w) - (s == t)
        A[4 + g] = (s >= 128 + t - w + 1) / np.float32(w)
        cnt = np.minimum(t + 1, w).astype(np.float32)
        A[8 + g] = ((s <= t) & (s > t - w)) / cnt - (s == t)
    col = np.zeros((128, 4), dtype=np.float32)
    freqs = (10000.0 ** (-np.arange(0, 32, 2, dtype=np.float32) / 32)).astype(np.float32)
    col[64:80, 0] = freqs
    col[80:96, 0] = freqs
    col[:, 1] = 1.0
    col[64:80, 1] = -1.0
    col[:, 2] = LN_EPS
    col[:, 3] = RMS_EPS
    sel = np.zeros((128, 64), dtype=np.float32)
    sel[64, :] = 1.0
    return {"c_ident": ident, "c_triu": triu, "c_poolA": A.astype(np.float32), "c_col": col, "c_sel": sel}


class Ctx:
    pass


def _dbg_heads():
    import os
    return int(os.environ.get("MK_DBG_HEADS", NH))


def build(phases, standalone):
    nc = bass.Bass("TRN2", target_bir_lowering=False)
    dr = {}
    for name, shp in W_SPECS.items():
        dr[name] = nc.dram_tensor(name, shp, F32, kind="ExternalInput").ap()
    for name, (shp, dt) in CONST_SPECS.items():
        dr[name] = nc.dram_tensor(name, shp, dt, kind="ExternalInput").ap()
    dr["positions"] = nc.dram_tensor("positions", [1, S], I32, kind="ExternalInput").ap()
    if standalone:
        src = nc.dram_tensor("src", [S, D], F32, kind="ExternalInput").ap()
        dst = nc.dram_tensor("dst", [S, D], F32, kind="ExternalOutput").ap()
        chain = {phases[0]: (src, dst)}
    else:
        x = nc.dram_tensor("x", [S, D], F32, kind="ExternalInput").ap()
        out = nc.dram_tensor("out", [S, D], F32, kind="ExternalOutput").ap()
        s1 = nc.dram_tensor("scr1", [S, D], F32).ap()
        s2 = nc.dram_tensor("scr2", [S, D], F32).ap()
        s3 = nc.dram_tensor("scr3", [S, D], F32).ap()
        chain = {1: (x, s1), 2: (s1, s2), 3: (s2, s3), 4: (s3, out)}
    dr["oTd"] = nc.dram_tensor("scr_oT", [D, S], BF16).ap()

    with ExitStack() as es:
        P = Prog(nc, es)
        C = Ctx()
        C.nc, C.P, C.dr = nc, P, dr
        C.ps = es.enter_context(nc.psum_tensor("ps", [128, 4096], F32))
        C.bank = [C.ps[:, 512 * i:512 * (i + 1)] for i in range(8)]
        C.bR = [Reg("bank%d" % i, excl=True) for i in range(8)]
        C.ident = es.enter_context(nc.sbuf_tensor("ident", [128, 128], F32))
        C.identR = Reg("ident")
        C.col = es.enter_context(nc.sbuf_tensor("colc", [128, 4], F32))
        C.colR = Reg("col")
        C.mk = es.enter_context(nc.sbuf_tensor("marks", [128, 8], F32))
        P.dma("sp", lambda: nc.sync.dma_start(out=C.ident[:], in_=dr["c_ident"][:, :]), writes=[C.identR])
        P.dma("sp", lambda: nc.sync.dma_start(out=C.col[:], in_=dr["c_col"][:, :]), writes=[C.colR])
        C.marks = {
            "act": lambda: nc.scalar.activation(out=C.mk[:, 0:1], in_=C.mk[:, 1:2], func=AF.Copy),
            "dve": lambda: nc.vector.memset(C.mk[:, 2:3], 0.0),
            "pool": lambda: nc.gpsimd.memset(C.mk[:, 4:5], 0.0),
        }
        P.op("dve", lambda: nc.vector.memset(C.mk[:], 0.0))
        for ph in phases:
            srcap, dstap = chain[ph]
            with ExitStack() as pes:
                if ph == 1:
                    phase_mixer0(C, pes, srcap, dstap)
                elif ph == 2:
                    phase_ffn(C, pes, 0, srcap, dstap)
                elif ph == 3:
                    phase_mla(C, pes, srcap, dstap)
                elif ph == 4:
                    phase_ffn(C, pes, 1, srcap, dstap)
                P.barrier(C.marks)
        P.flush()
        P.final_wait()
    return nc


def bcast_row(C, pes, name, row_ap, n):
    nc, P = C.nc, C.P
    t = pes.enter_context(nc.sbuf_tensor(name, [128, n], F32))
    R = Reg(name)
    P.dma("sp", lambda: nc.sync.dma_start(out=t[:], in_=row_ap.partition_broadcast(128)), writes=[R])
    return t, R


class LNState:
    pass


def ln_setup(C, pes, g_row, b_row, tag):
    nc = C.nc
    L = LNState()
    L.g, L.gR = bcast_row(C, pes, "lng_" + tag, g_row, D)
    L.b, L.bR = bcast_row(C, pes, "lnb_" + tag, b_row, D)
    L.s = [pes.enter_context(nc.sbuf_tensor("lns%d_%s" % (i, tag), [128, D], F32)) for i in range(2)]
    L.sR = [Reg("lns%d" % i) for i in range(2)]
    L.y = [pes.enter_context(nc.sbuf_tensor("lny%d_%s" % (i, tag), [128, D], F32)) for i in range(2)]
    L.yR = [Reg("lny%d" % i) for i in range(2)]
    L.st = [pes.enter_context(nc.sbuf_tensor("lnst%d_%s" % (i, tag), [128, 24], F32)) for i in range(2)]
    L.stR = [Reg("lnst%d" % i) for i in range(2)]
    L.n = 0
    return L


def ln_epilogue(C, L, bankA, bankB, x_ap, xR, dst_rows):
    nc, P = C.nc, C.P
    i = L.n % 2
    L.n += 1
    s, sR, y, yR, st, stR = L.s[i], L.sR[i], L.y[i], L.yR[i], L.st[i], L.stR[i]
    bA, bB = C.bank[bankA], C.bank[bankB]
    P.op("dve", lambda: nc.vector.scalar_tensor_tensor(out=s[:, 0:512], in0=x_ap[:, 0:512], scalar=ALPHA, in1=bA,
                                                      op0=ALU.mult, op1=ALU.add),
         reads=[xR, C.bR[bankA]], writes=[sR], part=True)
    P.op("dve", lambda: nc.vector.scalar_tensor_tensor(out=s[:, 512:1024], in0=x_ap[:, 512:1024], scalar=ALPHA, in1=bB,
                                                      op0=ALU.mult, op1=ALU.add),
         reads=[xR, C.bR[bankB]], writes=[sR], part=True)
    P.op("dve", lambda: nc.vector.bn_stats(out=st[:, 0:6], in_=s[:, 0:512]), reads=[sR], writes=[stR], part=True)
    P.op("dve", lambda: nc.vector.bn_stats(out=st[:, 6:12], in_=s[:, 512:1024]), reads=[sR], writes=[stR], part=True)
    R1, R2, R3 = Reg("mv"), Reg("sd"), Reg("rstd")
    P.op("dve", lambda: nc.vector.bn_aggr(out=st[:, 12:14], in_=st[:, 0:12]), reads=[stR], writes=[R1])
    P.op("act", lambda: nc.scalar.activation(out=st[:, 14:15], in_=st[:, 13:14], func=AF.Sqrt, bias=C.col[:, 2:3], scale=1.0),
         reads=[R1, C.colR], writes=[R2])
    P.op("dve", lambda: nc.vector.scalar_tensor_tensor(out=s[:], in0=s[:], scalar=st[:, 12:13], in1=L.g[:],
                                                      op0=ALU.subtract, op1=ALU.mult), reads=[sR, R1, L.gR], writes=[sR])
    P.op("dve", lambda: nc.vector.reciprocal(out=st[:, 15:16], in_=st[:, 14:15]), reads=[R2], writes=[R3])
    P.op("dve", lambda: nc.vector.scalar_tensor_tensor(out=y[:], in0=s[:], scalar=st[:, 15:16], in1=L.b[:],
                                                      op0=ALU.mult, op1=ALU.add), reads=[sR, R3, L.bR], writes=[yR])
    P.dma("sp", lambda: nc.sync.dma_start(out=dst_rows, in_=y[:]), reads=[yR])


def load_transpose_tile(C, src, t0, nchunk, xin, xinR, xT, xTR, tbanks, evac_engs=("act", "dve")):
    nc, P = C.nc, C.P
    rows = src[t0 * 128:(t0 + nchunk) * 128, :].rearrange("(c p) d -> p c d", p=128)
    P.dma("sp", lambda: nc.sync.dma_start(out=xin[:, 0:nchunk, :], in_=rows), writes=xinR)
    W = nchunk * 128
    for k in range(8):
        b = tbanks[k % len(tbanks)]
        for c in range(nchunk):
            P.op("pe", lambda c=c, k=k, b=b: nc.tensor.transpose(C.bank[b][:, c * 128:(c + 1) * 128],
                                                                xin[:, c, k * 128:(k + 1) * 128], C.ident[:]),
                 reads=[xinR[c], C.identR], writes=[C.bR[b]])
        e = evac_engs[k % len(evac_engs)]
        if e == "act":
            P.op("act", lambda k=k, b=b: nc.scalar.copy(out=xT[:, k, 0:W], in_=C.bank[b][:, 0:W]),
                 reads=[C.bR[b]], writes=[xTR], part=True)
        else:
            P.op("dve", lambda k=k, b=b: nc.vector.tensor_copy(out=xT[:, k, 0:W], in_=C.bank[b][:, 0:W]),
                 reads=[C.bR[b]], writes=[xTR], part=True)


def load_w_bf16(C, tile_ap, dram_ap, R, eng="pool"):
    nc, P = C.nc, C.P
    P.dma("pool", lambda: nc.gpsimd.dma_start(out=tile_ap, in_=dram_ap), writes=[R], part=True)


def phase_ffn(C, pes, layer, src, dst):
    nc, P, dr = C.nc, C.P, C.dr
    T = 2
    sfx = "_L%d" % layer
    W = T * 128
    NT = NCH // T
    wgu = pes.enter_context(nc.sbuf_tensor("wgu" + sfx, [128, 8, 2 * DFF], BF16))
    wd = pes.enter_context(nc.sbuf_tensor("wd" + sfx, [128, NFF, D], BF16))
    HJ = NFF // 2
    wguR = [[Reg("wgu%d_%d" % (k, ch)) for ch in range(4)] for k in range(8)]
    wdR = [Reg("wd%d" % j) for j in range(NFF)]
    gu = dr["ffn_w_gate_up"][layer]
    chunks = [(0, 0, HJ * 128), (1, DFF, DFF + HJ * 128), (2, HJ * 128, DFF), (3, DFF + HJ * 128, 2 * DFF)]
    for (ch, c0, c1) in chunks:
        for k in range(8):
            load_w_bf16(C, wgu[:, k, c0:c1], gu[k * 128:(k + 1) * 128, c0:c1], wguR[k][ch])
    dn = dr["ffn_w_down"][layer]
    for j in range(NFF):
        load_w_bf16(C, wd[:, j, :], dn[j * 128:(j + 1) * 128, :], wdR[j])
    L = ln_setup(C, pes, dr["ffn_ln_g"][layer:layer + 1, :], dr["ffn_ln_b"][layer:layer + 1, :], "f%d" % layer)
    xin = [pes.enter_context(nc.sbuf_tensor("fxin%d" % i + sfx, [128, T, D], F32)) for i in range(2)]
    xinR = [[Reg("fxin%d_%d" % (i, c)) for c in range(T)] for i in range(2)]
    xT = [pes.enter_context(nc.sbuf_tensor("fxT%d" % i + sfx, [128, 8, W], BF16)) for i in range(2)]
    xTR = [Reg("fxT%d" % i) for i in range(2)]
    hT = pes.enter_context(nc.sbuf_tensor("hT" + sfx, [128, NFF, W], BF16))
    hTR = [Reg("hT%d" % j) for j in range(NFF)]
    sg = [pes.enter_context(nc.sbuf_tensor("sg%d" % i + sfx, [128, W], F32)) for i in range(2)]
    sgR = [Reg("sg%d" % i) for i in range(2)]

    def transposes(t):
        load_transpose_tile(C, src, t * T, T, xin[t % 2], xinR[t % 2], xT[t % 2], xTR[t % 2], (0, 1))

    transposes(0)
    nsg = 0
    for t in range(NT):
        b = t % 2
        for j in range(NFF):
            bk = 2 + (j % 2)
            for which in range(2):
                col = which * DFF + j * 128
                for k in range(8):
                    P.op("pe", lambda k=k, col=col, bk=bk, which=which, b=b: nc.tensor.matmul(
                        C.bank[bk][:, which * W:(which + 1) * W], wgu[:, k, col:col + 128], xT[b][:, k, :],
                        start=(k == 0), stop=(k == 7)),
                        reads=[wguR[k][which + (2 if j >= HJ else 0)], xTR[b]], writes=[C.bR[bk]])
            si = nsg % 2
            nsg += 1
            P.op("act", lambda bk=bk, si=si: nc.scalar.activation(out=sg[si][:], in_=C.bank[bk][:, 0:W], func=AF.Silu),
                 reads=[C.bR[bk]], writes=[sgR[si]])
            P.op("dve", lambda bk=bk, si=si, j=j: nc.vector.tensor_tensor(out=hT[:, j, :], in0=sg[si][:], in1=C.bank[bk][:, W:2 * W],
                                                                      op=ALU.mult),
                 reads=[sgR[si], C.bR[bk]], writes=[hTR[j]])
        if t + 1 < NT:
            transposes(t + 1)
        for c in range(T):
            for half in range(2):
                bk = 4 + 2 * c + half
                for j in range(NFF):
                    P.op("pe", lambda j=j, c=c, half=half, bk=bk: nc.tensor.matmul(
                        C.bank[bk][:, 0:512], hT[:, j, c * 128:(c + 1) * 128], wd[:, j, half * 512:(half + 1) * 512],
                        start=(j == 0), stop=(j == NFF - 1)),
                        reads=[hTR[j], wdR[j]], writes=[C.bR[bk]])
            r0 = (t * T + c) * 128
            ln_epilogue(C, L, 4 + 2 * c, 5 + 2 * c, xin[b][:, c, :], xinR[b][c], dst[r0:r0 + 128, :])


def phase_mixer0(C, pes, src, dst):
    nc, P, dr = C.nc, C.P, C.dr
    T = 4
    W = 512
    NT = NCH // T
    sb = lambda name, shp, dt=F32: pes.enter_context(nc.sbuf_tensor(name, shp, dt))
    win = sb("m_win", [128, 8, 1536], BF16)
    winR = [Reg("win%d" % k) for k in range(8)]
    for k in range(8):
        load_w_bf16(C, win[:, k, :], dr["even_w_in"][k * 128:(k + 1) * 128, :], winR[k])
    wout = sb("m_wout", [128, 8, D], BF16)
    woutR = [Reg("wout%d" % k) for k in range(8)]
    for k in range(8):
        load_w_bf16(C, wout[:, k, :], dr["even_w_out"][k * 128:(k + 1) * 128, :], woutR[k])
    poolw = sb("m_poolw", [128, 4, 128], BF16)
    poolwR = Reg("poolw")
    load_w_bf16(C, poolw[:], dr["even_pool_w"].rearrange("g c d -> c g d"), poolwR)
    A = sb("m_A", [128, 12, 128])
    AR = Reg("A")
    P.dma("sp", lambda: nc.sync.dma_start(out=A[:], in_=dr["c_poolA"].rearrange("n s t -> s n t")), writes=[AR])
    triu = sb("m_triu", [128, 128])
    triuR = Reg("triu")
    P.dma("sp", lambda: nc.sync.dma_start(out=triu[:], in_=dr["c_triu"][:, :]), writes=[triuR])
    wsn = sb("m_wsn", [128, 8, 128])
    wsnR = Reg("wsn")
    P.dma("sp", lambda: nc.sync.dma_start(out=wsn[:], in_=dr["even_spatial_w"].rearrange("h t s -> t h s")), writes=[wsnR])
    wsT = sb("m_wsT", [128, 8, 128], BF16)
    wsTR = Reg("wsT")
    for h in range(8):
        P.op("pe", lambda h=h: nc.tensor.transpose(C.bank[0][:, 0:128], wsn[:, h, :], C.ident[:]),
             reads=[wsnR, C.identR], writes=[C.bR[0]])
        P.op("dve", lambda h=h: nc.vector.tensor_tensor(out=wsT[:, h, :], in0=C.bank[0][:, 0:128], in1=triu[:], op=ALU.mult),
             reads=[C.bR[0], triuR], writes=[wsTR], part=True)
    bs = sb("m_bs", [128, 8])
    bsR = Reg("bs")
    pscale = sb("m_pscale", [128, 4])
    pscaleR = Reg("pscale")
    with nc.allow_non_contiguous_dma(reason="tiny per-head bias / scale columns"):
        P.dma("sp", lambda: nc.sync.dma_start(out=bs[:], in_=dr["even_spatial_b"].rearrange("h t -> t h")), writes=[bsR])
        P.dma("sp", lambda: nc.sync.dma_start(out=pscale[:], in_=dr["even_pool_scale"].rearrange("o (g d) -> d (o g)", g=4)),
              writes=[pscaleR])
        P.flush()
    vg, vgR = bcast_row(C, pes, "m_vg", dr["even_vnorm_g"], 512)
    vb, vbR = bcast_row(C, pes, "m_vb", dr["even_vnorm_b"], 512)
    L = ln_setup(C, pes, dr["mix_ln_g"][0:1, :], dr["mix_ln_b"][0:1, :], "m")
    xin = [sb("m_xin%d" % i, [128, T, D]) for i in range(3)]
    xinR = [[Reg("mxin%d_%d" % (i, c)) for c in range(T)] for i in range(3)]
    xT = [sb("m_xT%d" % i, [128, 8, W], BF16) for i in range(2)]
    xTR = [Reg("mxT%d" % i) for i in range(2)]
    mixT = [sb("m_mixT%d" % i, [128, 8, 128], BF16) for i in range(2)]
    mixTR = [Reg("mixT%d" % i) for i in range(2)]
    u_sb = [sb("m_u%d" % i, [128, 512]) for i in range(3)]
    uR = [Reg("u%d" % i) for i in range(3)]
    v_sb = [sb("m_v%d" % i, [128, 512]) for i in range(2)]
    vR = [Reg("v%d" % i) for i in range(2)]
    vbf = [sb("m_vbf%d" % i, [128, 512], BF16) for i in range(2)]
    vbfR = [Reg("vbf%d" % i) for i in range(2)]
    a_sb = [sb("m_a%d" % i, [128, 512]) for i in range(2)]
    aR = [Reg("a%d" % i) for i in range(2)]
    xp = [sb("m_xp%d" % i, [128, 512]) for i in range(4)]
    xpR = [Reg("xp%d" % i) for i in range(4)]
    pooledT = [sb("m_pooledT%d" % i, [128, 4, 128], BF16) for i in range(2)]
    pooledTR = [Reg("pooledT%d" % i) for i in range(2)]
    vst = [sb("m_vst%d" % i, [128, 16]) for i in range(2)]
    BXA, BPM, BU, BV, BSP, BPF, BOA, BOB = 0, 1, 2, 3, 4, 5, 6, 7

    def geo(gc):
        t, c = divmod(gc, T)
        return t, c, t % 2, slice(c * 128, (c + 1) * 128)

    def A_pe(gc):
        t, c, b, cs = geo(gc)
        for which, bk in ((0, BU), (1, BV)):
            for k in range(8):
                P.op("pe", lambda k=k, which=which, bk=bk, cs=cs, b=b: nc.tensor.matmul(
                    C.bank[bk][:, 0:512], xT[b][:, k, cs], win[:, k, which * 512:(which + 1) * 512],
                    start=(k == 0), stop=(k == 7)), reads=[xTR[b], winR[k]], writes=[C.bR[bk]])
        for k in range(8):
            P.op("pe", lambda k=k, cs=cs, b=b: nc.tensor.matmul(
                C.bank[BXA][:, 0:512], xT[b][:, k, cs], win[:, k, 1024:1536],
                start=(k == 0), stop=(k == 7)), reads=[xTR[b], winR[k]], writes=[C.bR[BXA]])
        i4 = gc % 4
        P.op("act", lambda i4=i4: nc.scalar.copy(out=xp[i4][:], in_=C.bank[BXA][:, 0:512]),
             reads=[C.bR[BXA]], writes=[xpR[i4]])

    def A_gelu(gc):
        i2, i3 = gc % 2, gc % 3
        P.op("act", lambda i3=i3: nc.scalar.activation(out=u_sb[i3][:], in_=C.bank[BU][:, 0:512], func=AF.Gelu),
             reads=[C.bR[BU]], writes=[uR[i3]])
        P.op("act", lambda i2=i2: nc.scalar.activation(out=v_sb[i2][:], in_=C.bank[BV][:, 0:512], func=AF.Gelu),
             reads=[C.bR[BV]], writes=[vR[i2]])

    def A_vln(gc):
        i2, i3 = gc % 2, gc % 3
        st = vst[i2]
        R0, R1, R2, R3 = Reg("vs0"), Reg("vs1"), Reg("vs2"), Reg("vs3")
        P.op("dve", lambda st=st, i2=i2: nc.vector.bn_stats(out=st[:, 0:6], in_=v_sb[i2][:]), reads=[vR[i2]], writes=[R0])
        P.op("dve", lambda st=st: nc.vector.bn_aggr(out=st[:, 6:8], in_=st[:, 0:6]), reads=[R0], writes=[R1])
        P.op("act", lambda st=st: nc.scalar.activation(out=st[:, 8:9], in_=st[:, 7:8], func=AF.Sqrt, bias=C.col[:, 2:3], scale=1.0),
             reads=[R1, C.colR], writes=[R2])
        P.op("dve", lambda st=st, i2=i2: nc.vector.scalar_tensor_tensor(out=v_sb[i2][:], in0=v_sb[i2][:], scalar=st[:, 6:7], in1=vg[:],
                                                                     op0=ALU.subtract, op1=ALU.mult),
             reads=[vR[i2], R1, vgR], writes=[vR[i2]])
        P.op("dve", lambda st=st: nc.vector.reciprocal(out=st[:, 9:10], in_=st[:, 8:9]), reads=[R2], writes=[R3])
        P.op("dve", lambda st=st, i2=i2: nc.vector.scalar_tensor_tensor(out=vbf[i2][:], in0=v_sb[i2][:], scalar=st[:, 9:10], in1=vb[:],
                                                                     op0=ALU.mult, op1=ALU.add),
             reads=[vR[i2], R3, vbR], writes=[vbfR[i2]])

    def B_pe(gc):
        i2, i3, ip = gc % 2, gc % 4, (gc - 1) % 4
        for h in range(8):
            P.op("pe", lambda h=h, i2=i2: nc.tensor.matmul(C.bank[BSP][:, h * 64:(h + 1) * 64], wsT[:, h, :],
                                                          vbf[i2][:, h * 64:(h + 1) * 64], start=True, stop=True),
                 reads=[wsTR, vbfR[i2]], writes=[C.bR[BSP]])
        for g in range(4):
            gs = slice(g * 128, (g + 1) * 128)
            if gc == 0:
                P.op("pe", lambda g=g, gs=gs, i3=i3: nc.tensor.matmul(C.bank[BPF][:, gs], xp[i3][:, gs], A[:, 8 + g, :],
                                                                     start=True, stop=True),
                     reads=[xpR[i3], AR], writes=[C.bR[BPF]])
            else:
                P.op("pe", lambda g=g, gs=gs, i3=i3: nc.tensor.matmul(C.bank[BPF][:, gs], xp[i3][:, gs], A[:, g, :],
                                                                     start=True, stop=False),
                     reads=[xpR[i3], AR], writes=[C.bR[BPF]])
                P.op("pe", lambda g=g, gs=gs, ip=ip: nc.tensor.matmul(C.bank[BPF][:, gs], xp[ip][:, gs], A[:, 4 + g, :],
                                                                     start=False, stop=True),
                     reads=[xpR[ip], AR], writes=[C.bR[BPF]])

    def B_add(gc):
        i2, i3 = gc % 2, gc % 3
        P.op("dve", lambda i2=i2: nc.vector.tensor_tensor(
            out=a_sb[i2][:].rearrange("p (h d) -> p h d", h=8), in0=C.bank[BSP][:, 0:512].rearrange("p (h d) -> p h d", h=8),
            in1=bs[:, 0:8].unsqueeze(2).to_broadcast([128, 8, 64]), op=ALU.add),
            reads=[C.bR[BSP], bsR], writes=[aR[i2]])
        P.op("pool", lambda i2=i2, i3=i3: nc.gpsimd.tensor_tensor(out=a_sb[i2][:], in0=a_sb[i2][:], in1=u_sb[i3][:], op=ALU.mult),
             reads=[aR[i2], uR[i3]], writes=[aR[i2]])

    def B_copy(gc):
        i2 = gc % 2
        P.op("act", lambda i2=i2: nc.scalar.copy(out=pooledT[i2][:].rearrange("p g t -> p (g t)"), in_=C.bank[BPF][:, 0:512]),
             reads=[C.bR[BPF]], writes=[pooledTR[i2]])

    def C_pe(gc):
        i2 = gc % 2
        for kb in range(4):
            P.op("pe", lambda kb=kb, i2=i2: nc.tensor.transpose(C.bank[BXA][:, kb * 128:(kb + 1) * 128],
                                                               a_sb[i2][:, kb * 128:(kb + 1) * 128], C.ident[:]),
                 reads=[aR[i2], C.identR], writes=[C.bR[BXA]])
        for g in range(4):
            P.op("pe", lambda g=g, i2=i2: nc.tensor.matmul(C.bank[BPM][:, g * 128:(g + 1) * 128], poolw[:, g, :], pooledT[i2][:, g, :],
                                                          start=True, stop=True),
                 reads=[poolwR, pooledTR[i2]], writes=[C.bR[BPM]])

    def C_copy(gc):
        i2 = gc % 2
        P.op("act", lambda i2=i2: nc.scalar.copy(out=mixT[i2][:, 0:4, :], in_=C.bank[BXA][:, 0:512].rearrange("p (k t) -> p k t", k=4)),
             reads=[C.bR[BXA]], writes=[mixTR[i2]], part=True)

    def C_scale(gc):
        i2 = gc % 2
        P.op("dve", lambda i2=i2: nc.vector.tensor_tensor(
            out=mixT[i2][:, 4:8, :], in0=C.bank[BPM][:, 0:512].rearrange("p (g t) -> p g t", g=4),
            in1=pscale[:, 0:4].unsqueeze(2).to_broadcast([128, 4, 128]), op=ALU.mult),
            reads=[C.bR[BPM], pscaleR], writes=[mixTR[i2]], part=True)

    def D_pe(gc):
        i2 = gc % 2
        for half, bk in ((0, BOA), (1, BOB)):
            for k in range(8):
                P.op("pe", lambda k=k, half=half, bk=bk, i2=i2: nc.tensor.matmul(
                    C.bank[bk][:, 0:512], mixT[i2][:, k, :], wout[:, k, half * 512:(half + 1) * 512],
                    start=(k == 0), stop=(k == 7)), reads=[mixTR[i2], woutR[k]], writes=[C.bR[bk]])

    def D_post(gc):
        t, c, b, cs = geo(gc)
        xi = t % 3
        ln_epilogue(C, L, BOA, BOB, xin[xi][:, c, :], xinR[xi][c], dst[gc * 128:(gc + 1) * 128, :])

    def TX(t):
        load_transpose_tile(C, src, t * T, T, xin[t % 3], xinR[t % 3], xT[t % 2], xTR[t % 2], (BU, BV))

    TX(0)
    ok = lambda g: 0 <= g < NCH
    for s_ in range(NCH + 8):
        if ok(s_ - 1):
            A_gelu(s_ - 1)
        if ok(s_ - 3):
            B_copy(s_ - 3)
        if ok(s_ - 5):
            C_copy(s_ - 5)
            C_scale(s_ - 5)
        if ok(s_ - 3):
            B_add(s_ - 3)
        if ok(s_ - 1):
            A_vln(s_ - 1)
        if ok(s_ - 7):
            D_post(s_ - 7)
        if s_ % T == 2 and (s_ // T) + 1 < NT:
            TX(s_ // T + 1)
        if ok(s_):
            A_pe(s_)
        if ok(s_ - 2):
            B_pe(s_ - 2)
        if ok(s_ - 6):
            D_pe(s_ - 6)
        if ok(s_ - 4):
            C_pe(s_ - 4)
        if s_ % 4 == 3:
            P.flush()


def phase_mla(C, pes, src, dst):
    nc, P, dr = C.nc, C.P, C.dr
    with ExitStack() as aes:
        mla_latents_and_attention(C, aes, src)
    mla_outproj(C, pes, src, dst)


def mla_latents_and_attention(C, pes, src):
    nc, P, dr = C.nc, C.P, C.dr
    sb = lambda name, shp, dt=F32: pes.enter_context(nc.sbuf_tensor(name, shp, dt))
    T, W, NT = 4, 512, 8
    w_in = dr["odd_w_in"]
    wq_in = sb("a_wqin", [128, 8, 384], BF16)
    wkv_in = sb("a_wkvin", [128, 8, 256], BF16)
    wkr = sb("a_wkr", [128, 8, 96], BF16)
    wkrs = sb("a_wkrs", [128, 8, 96], BF16)
    winR = Reg("a_win")
    w3 = w_in.rearrange("(k p) n -> p k n", p=128)
    load_w_bf16(C, wq_in[:], w3[:, :, 0:384], winR)
    load_w_bf16(C, wkv_in[:], w3[:, :, 384:640], winR)
    load_w_bf16(C, wkr[:], w3[:, :, 576:672], winR)
    load_w_bf16(C, wkrs[:, :, 0:64], w3[:, :, 576:640], winR)
    load_w_bf16(C, wkrs[:, :, 64:80], w3[:, :, 656:672], winR)
    load_w_bf16(C, wkrs[:, :, 80:96], w3[:, :, 640:656], winR)
    wqu = sb("a_wqu", [128, 3, 1536], BF16)
    wqus = sb("a_wqus", [128, 3, 16, 96], BF16)
    wquR = Reg("a_wqu")
    q3 = dr["odd_w_q_up"].rearrange("(k p) n -> p k n", p=128)
    q4 = dr["odd_w_q_up"].rearrange("(k p) (h d) -> p k h d", p=128, h=16)
    load_w_bf16(C, wqu[:], q3, wquR)
    for k in range(3):
        for hs in (slice(0, 8), slice(8, 16)):
            load_w_bf16(C, wqus[:, k, hs, 0:64], q4[:, k, hs, 0:64], wquR)
            load_w_bf16(C, wqus[:, k, hs, 64:80], q4[:, k, hs, 80:96], wquR)
            load_w_bf16(C, wqus[:, k, hs, 80:96], q4[:, k, hs, 64:80], wquR)
    wkn = sb("a_wkn", [128, 2, 16, 64], BF16)
    wv = sb("a_wv", [128, 2, 16, 64], BF16)
    wkvR = Reg("a_wkv")
    kv4 = dr["odd_w_kv_up"].rearrange("(k p) (h d) -> p k h d", p=128, h=16)
    for k in range(2):
        for hs in (slice(0, 8), slice(8, 16)):
            load_w_bf16(C, wkn[:, k, hs, :], kv4[:, k, hs, 0:64], wkvR)
            load_w_bf16(C, wv[:, k, hs, :], kv4[:, k, hs, 64:128], wkvR)
    gq = sb("a_gq", [128, 3])
    gkv = sb("a_gkv", [128, 2])
    gR = Reg("a_g")
    with nc.allow_non_contiguous_dma(reason="tiny norm-gain columns"):
        P.dma("sp", lambda: nc.sync.dma_start(out=gq[:], in_=dr["odd_q_norm_g"].rearrange("o (k p) -> p (o k)", p=128)), writes=[gR], part=True)
        P.dma("sp", lambda: nc.sync.dma_start(out=gkv[:], in_=dr["odd_kv_norm_g"].rearrange("o (k p) -> p (o k)", p=128)), writes=[gR], part=True)
        P.flush()
    ones = sb("a_ones", [128, 128])
    onesR = Reg("a_ones")
    P.op("pool", lambda: nc.gpsimd.memset(ones[:], 1.0), writes=[onesR])
    sel = sb("a_sel", [128, 64])
    selR = Reg("a_sel")
    P.dma("sp", lambda: nc.sync.dma_start(out=sel[:], in_=dr["c_sel"][:, :]), writes=[selR])
    tri = sb("a_tri", [128, 128])
    trib = sb("a_trib", [128, 128], BF16)
    triR = Reg("a_tri")
    tribR = Reg("a_trib")
    P.dma("sp", lambda: nc.sync.dma_start(out=tri[:], in_=dr["c_triu"][:, :]), writes=[triR])
    P.op("pool", lambda: nc.gpsimd.tensor_copy(out=trib[:], in_=tri[:]), reads=[triR], writes=[tribR])
    negm = sb("a_negm", [128, 128], BF16)
    identb = sb("a_identb", [128, 128], BF16)
    mskR = Reg("a_msk")
    P.op("dve", lambda: nc.vector.tensor_scalar(out=negm[:], in0=tri[:], scalar1=-1.0, scalar2=30000.0, op0=ALU.add, op1=ALU.mult),
         reads=[triR], writes=[mskR], part=True)
    P.op("dve", lambda: nc.vector.tensor_copy(out=identb[:], in_=C.ident[:]), reads=[C.identR], writes=[mskR], part=True)

    cosT = sb("a_cos", [128, S])
    sinT = sb("a_sin", [128, S])
    csR = [Reg("a_cs%d" % t) for t in range(NT)]
    cqT = sb("a_cqT", [128, 3, S], BF16)
    ckvT = sb("a_ckvT", [128, 2, S], BF16)
    latR = [Reg("a_lat%d" % t) for t in range(NT)]
    kT = [sb("a_kT%d" % i, [96, S], BF16) for i in range(2)]
    kTropeR = [Reg("a_kTr%d" % t) for t in range(NT)]
    kTR = [Reg("a_kT%d" % i) for i in range(2)]
    t1 = sb("a_t1", [128, W])
    t2 = sb("a_t2", [128, W])
    t1R, t2R = Reg("a_t1"), Reg("a_t2")
    rp = slice(64, 96)
    shared_pes = pes
    pes = ExitStack()
    pes.__enter__()

    xin = [sb("a_xin%d" % i, [128, T, D]) for i in range(2)]
    xinR = [[Reg("axin%d_%d" % (i, c)) for c in range(T)] for i in range(2)]
    xT = [sb("a_xT%d" % i, [128, 8, W], BF16) for i in range(2)]
    xTR = [Reg("axT%d" % i) for i in range(2)]
    posi = sb("a_posi", [128, W], I32)
    ang = sb("a_ang", [128, W])
    tq = sb("a_tq", [128, W])
    ki = sb("a_ki", [128, W], I32)
    sq = [sb("a_sq%d" % i, [128, W]) for i in range(2)]
    sqR = [Reg("a_sq%d" % i) for i in range(2)]
    rstd = sb("a_rstd", [128, W])
    rstdR = Reg("a_rstd")
    posR, angR, tqR, kiR = Reg("a_pos"), Reg("a_ang"), Reg("a_tq"), Reg("a_ki")

    def rope_tables(t):
        ts_ = slice(t * W, (t + 1) * W)
        P.dma("sp", lambda ts_=ts_: nc.sync.dma_start(out=posi[rp, :], in_=dr["positions"][0:1, ts_].partition_broadcast(32)), writes=[posR])
        P.op("dve", lambda: nc.vector.tensor_copy(out=ang[rp, :], in_=posi[rp, :]), reads=[posR], writes=[angR])
        P.op("dve", lambda: nc.vector.tensor_scalar(out=ang[rp, :], in0=ang[rp, :], scalar1=C.col[rp, 0:1], scalar2=None, op0=ALU.mult),
             reads=[angR, C.colR], writes=[angR])
        for which in (0, 1):
            off = 0.0 if which == 0 else PI / 2
            P.op("dve", lambda off=off: nc.vector.tensor_scalar(out=tq[rp, :], in0=ang[rp, :], scalar1=off, scalar2=1.0 / TWO_PI,
                                                               op0=ALU.add, op1=ALU.mult), reads=[angR], writes=[tqR])
            P.op("dve", lambda: nc.vector.tensor_copy(out=ki[rp, :], in_=tq[rp, :]), reads=[tqR], writes=[kiR])
            P.op("dve", lambda: nc.vector.tensor_copy(out=tq[rp, :], in_=ki[rp, :]), reads=[kiR], writes=[tqR])
            P.op("dve", lambda: nc.vector.scalar_tensor_tensor(out=tq[rp, :], in0=tq[rp, :], scalar=-TWO_PI, in1=ang[rp, :],
                                                              op0=ALU.mult, op1=ALU.add), reads=[tqR, angR], writes=[tqR])
            P.op("dve", lambda off=off: nc.vector.tensor_scalar(out=tq[rp, :], in0=tq[rp, :], scalar1=off, scalar2=PI,
                                                               op0=ALU.add, op1=ALU.min), reads=[tqR], writes=[tqR])
            P.op("dve", lambda: nc.vector.tensor_scalar(out=tq[rp, :], in0=tq[rp, :], scalar1=-PI, scalar2=None, op0=ALU.max),
                 reads=[tqR], writes=[tqR])
            if which == 0:
                P.op("act", lambda ts_=ts_: nc.scalar.activation(out=sinT[rp, ts_], in_=tq[rp, :], func=AF.Sin, scale=C.col[rp, 1:2]),
                     reads=[tqR, C.colR], writes=[csR[t]], part=True)
            else:
                P.op("act", lambda ts_=ts_: nc.scalar.activation(out=cosT[rp, ts_], in_=tq[rp, :], func=AF.Sin),
                     reads=[tqR], writes=[csR[t]], part=True)

    def proj_mms(t, wt, nblk, bk0):
        b = t % 2
        for kb in range(nblk):
            bk = bk0 + kb
            for k in range(8):
                P.op("pe", lambda k=k, kb=kb, bk=bk, wt=wt, b=b: nc.tensor.matmul(
                    C.bank[bk][:, 0:W], wt[:, k, kb * 128:(kb + 1) * 128], xT[b][:, k, :], start=(k == 0), stop=(k == 7)),
                    reads=[winR, xTR[b]], writes=[C.bR[bk]])

    def rms_post(t, nblk, dstL, gcol, inv_n, bk0):
        ts_ = slice(t * W, (t + 1) * W)
        SSB = 5
        for kb in range(nblk):
            bk = bk0 + kb
            si = kb % 2
            P.op("act", lambda bk=bk, si=si: nc.scalar.activation(out=sq[si][:], in_=C.bank[bk][:, 0:W], func=AF.Square),
                 reads=[C.bR[bk]], writes=[sqR[si]])
            P.op("pe", lambda si=si, kb=kb, nblk=nblk: nc.tensor.matmul(C.bank[SSB][:, 0:W], ones[:], sq[si][:],
                                                                       start=(kb == 0), stop=(kb == nblk - 1)),
                 reads=[onesR, sqR[si]], writes=[C.bR[SSB]])
        P.op("act", lambda inv_n=inv_n: nc.scalar.activation(out=rstd[:], in_=C.bank[SSB][:, 0:W], func=AF.Sqrt,
                                                            bias=C.col[:, 3:4], scale=inv_n),
             reads=[C.bR[SSB], C.colR], writes=[rstdR])
        P.op("dve", lambda: nc.vector.reciprocal(out=rstd[:], in_=rstd[:]), reads=[rstdR], writes=[rstdR])
        for kb in range(nblk):
            bk = bk0 + kb
            P.op("dve", lambda kb=kb, bk=bk, dstL=dstL, gcol=gcol, ts_=ts_: nc.vector.scalar_tensor_tensor(
                out=dstL[:, kb, ts_], in0=C.bank[bk][:, 0:W], scalar=gcol[:, kb:kb + 1], in1=rstd[:], op0=ALU.mult, op1=ALU.mult),
                reads=[C.bR[bk], gR, rstdR], writes=[latR[t]], part=True)

    rope_tables(0)
    load_transpose_tile(C, src, 0, T, xin[0], xinR[0], xT[0], xTR[0], (0, 1))
    for t in range(NT):
        b = t % 2
        ts_ = slice(t * W, (t + 1) * W)
        for (wt, bk) in ((wkr, 0), (wkrs, 1)):
            for k in range(8):
                P.op("pe", lambda k=k, wt=wt, bk=bk, b=b: nc.tensor.matmul(C.bank[bk][0:96, 0:W], wt[:, k, :], xT[b][:, k, :],
                                                                          start=(k == 0), stop=(k == 7)),
                     reads=[winR, xTR[b]], writes=[C.bR[bk]])
        proj_mms(t, wq_in, 3, 2)
        P.op("dve", lambda ts_=ts_: nc.vector.tensor_tensor(out=t1[rp, :], in0=C.bank[0][rp, 0:W], in1=cosT[rp, ts_], op=ALU.mult),
             reads=[C.bR[0], csR[t]], writes=[t1R])
        P.op("dve", lambda ts_=ts_: nc.vector.tensor_tensor(out=t2[rp, :], in0=C.bank[1][rp, 0:W], in1=sinT[rp, ts_], op=ALU.mult),
             reads=[C.bR[1], csR[t]], writes=[t2R])
        for i in range(2):
            P.op("pool", lambda i=i, ts_=ts_: nc.gpsimd.tensor_tensor(out=kT[i][rp, ts_], in0=t1[rp, :], in1=t2[rp, :], op=ALU.add),
                 reads=[t1R, t2R], writes=[kTropeR[t]], part=True)
        proj_mms(t, wkv_in, 2, 6)
        rms_post(t, 3, cqT, gq, 1.0 / 384, 2)
        if t + 1 < NT:
            load_transpose_tile(C, src, (t + 1) * T, T, xin[1 - b], xinR[1 - b], xT[1 - b], xTR[1 - b], (0, 1))
        rms_post(t, 2, ckvT, gkv, 1.0 / 256, 6)
        if t + 1 < NT:
            rope_tables(t + 1)

    P.barrier(C.marks)
    pes.__exit__(None, None, None)
    pes = ExitStack()
    pes.__enter__()
    qT = [sb("a_qT%d" % i, [96, S], BF16) for i in range(2)]
    qTR = [Reg("a_qT%d" % i) for i in range(2)]
    Vg = sb("a_V", [128, NCH, 4, 65], BF16)
    VgR = Reg("a_V")
    VoneR = Reg("a_Vone")
    P.op("pool", lambda: nc.gpsimd.memset(Vg[:, :, :, 64:65], 1.0), writes=[VoneR])
    pT = [sb("a_pT%d" % i, [128, 1024], BF16) for i in range(3)]
    pTR = [Reg("a_pT%d" % i) for i in range(3)]
    oa = [sb("a_oa%d" % i, [128, W]) for i in range(2)]
    oaR = [Reg("a_oa%d" % i) for i in range(2)]
    oTh = [sb("a_oTh%d" % i, [64, S], BF16) for i in range(2)]
    oThR = [Reg("a_oTh%d" % i) for i in range(2)]
    allLat = latR + csR + kTropeR
    OB = (0, 1)
    SB3 = ((2, 3), (4, 5), (6, 7))
    npt = 0
    noa = 0
    def gen_v(hg):
        for c2 in range(NCH // 2):
            bk = 2 + (c2 % 2)
            for cc in range(2):
                c = 2 * c2 + cc
                for k in range(2):
                    P.op("pe", lambda k=k, c=c, cc=cc, bk=bk, hg=hg: nc.tensor.matmul(
                        C.bank[bk][:, cc * 256:(cc + 1) * 256], ckvT[:, k, c * 128:(c + 1) * 128],
                        wv[:, k, hg * 4:(hg + 1) * 4, :].rearrange("p h d -> p (h d)"), start=(k == 0), stop=(k == 1)),
                        reads=[wkvR] + latR, writes=[C.bR[bk]])
            P.op("dve", lambda c2=c2, bk=bk: nc.vector.tensor_copy(
                out=Vg[:, 2 * c2:2 * c2 + 2, :, 0:64], in_=C.bank[bk][:, 0:512].rearrange("p (c h d) -> p c h d", c=2, h=4)),
                reads=[C.bR[bk]], writes=[VgR], part=True)

    def gen_qk(h, t, pair=(4, 5)):
        b = h % 2
        ts_ = slice(t * W, (t + 1) * W)
        BQ, BS_, BK = pair[0], pair[1], pair[0]
        for (wt, bk) in ((None, BQ), (wqus, BS_)):
            for k in range(3):
                lhs = wqu[:, k, h * 96:(h + 1) * 96] if wt is None else wqus[:, k, h, :]
                P.op("pe", lambda k=k, lhs=lhs, bk=bk, ts_=ts_: nc.tensor.matmul(C.bank[bk][0:96, 0:W], lhs, cqT[:, k, ts_],
                                                                                 start=(k == 0), stop=(k == 2)),
                     reads=[wquR] + latR, writes=[C.bR[bk]])
        P.op("dve", lambda ts_=ts_, b=b: nc.vector.tensor_copy(out=qT[b][0:64, ts_], in_=C.bank[BQ][0:64, 0:W]),
             reads=[C.bR[BQ]], writes=[qTR[b]], part=True)
        P.op("dve", lambda ts_=ts_: nc.vector.tensor_tensor(out=t1[rp, :], in0=C.bank[BQ][rp, 0:W], in1=cosT[rp, ts_], op=ALU.mult),
             reads=[C.bR[BQ]] + csR, writes=[t1R])
        P.op("dve", lambda ts_=ts_: nc.vector.tensor_tensor(out=t2[rp, :], in0=C.bank[BS_][rp, 0:W], in1=sinT[rp, ts_], op=ALU.mult),
             reads=[C.bR[BS_]] + csR, writes=[t2R])
        P.op("pool", lambda ts_=ts_, b=b: nc.gpsimd.tensor_tensor(out=qT[b][rp, ts_], in0=t1[rp, :], in1=t2[rp, :], op=ALU.add),
             reads=[t1R, t2R], writes=[qTR[b]], part=True)
        for k in range(2):
            P.op("pe", lambda k=k, ts_=ts_: nc.tensor.matmul(C.bank[BK][0:64, 0:W], wkn[:, k, h, :], ckvT[:, k, ts_],
                                                             start=(k == 0), stop=(k == 1)),
                 reads=[wkvR] + latR, writes=[C.bR[BK]])
        P.op("dve", lambda ts_=ts_, b=b: nc.vector.tensor_copy(out=kT[b][0:64, ts_], in_=C.bank[BK][0:64, 0:W]),
             reads=[C.bR[BK]], writes=[kTR[b]], part=True)

    def emit_S(it):
        sb2 = it["sb2"]
        if it["kind"] in ("g1", "g2"):
            h2 = it["h2"]
            ts_ = slice(it["t"] * W, (it["t"] + 1) * W)
            if it["kind"] == "g1":
                for k in range(3):
                    P.op("pe", lambda k=k, ts_=ts_, h2=h2, sb2=sb2: nc.tensor.matmul(
                        C.bank[sb2[0]][0:96, 0:W], wqu[:, k, h2 * 96:(h2 + 1) * 96], cqT[:, k, ts_], start=(k == 0), stop=(k == 2)),
                        reads=[wquR] + latR, writes=[C.bR[sb2[0]]])
                for k in range(2):
                    P.op("pe", lambda k=k, ts_=ts_, h2=h2, sb2=sb2: nc.tensor.matmul(
                        C.bank[sb2[1]][0:64, 0:W], wkn[:, k, h2, :], ckvT[:, k, ts_], start=(k == 0), stop=(k == 1)),
                        reads=[wkvR] + latR, writes=[C.bR[sb2[1]]])
            else:
                for k in range(3):
                    P.op("pe", lambda k=k, ts_=ts_, h2=h2, sb2=sb2: nc.tensor.matmul(
                        C.bank[sb2[0]][0:96, 0:W], wqus[:, k, h2, :], cqT[:, k, ts_], start=(k == 0), stop=(k == 2)),
                        reads=[wquR] + latR, writes=[C.bR[sb2[0]]])
            return
        b, q0 = it["b"], it["q0"]
        if it["kind"] == "off":
            for u in range(2):
                kt = it["kt0"] + u
                P.op("pe", lambda kt=kt, u=u, sb2=sb2, b=b, q0=q0: nc.tensor.matmul(
                    C.bank[sb2[u]][:, 0:W], kT[b][:, kt * 128:(kt + 1) * 128], qT[b][:, q0:q0 + W], start=True, stop=True),
                    reads=[kTR[b], qTR[b]] + kTropeR, writes=[C.bR[sb2[u]]])
        else:
            kt, n0 = it["kt"], it["n0"]
            P.op("pe", lambda kt=kt, sb2=sb2, b=b, q0=q0, n0=n0: nc.tensor.matmul(
                C.bank[sb2[0]][:, n0:W], kT[b][:, kt * 128:(kt + 1) * 128], qT[b][:, q0 + n0:q0 + W], start=True, stop=False),
                reads=[kTR[b], qTR[b]] + kTropeR, writes=[C.bR[sb2[0]]])
            P.op("pe", lambda sb2=sb2, n0=n0: nc.tensor.matmul(
                C.bank[sb2[0]][:, n0:n0 + 128], identb[:], negm[:], start=False, stop=True),
                reads=[mskR], writes=[C.bR[sb2[0]]])

    def emit_EP(it):
        sb2 = it["sb2"]
        if it["kind"] in ("g1", "g2"):
            b2 = it["h2"] % 2
            ts_ = slice(it["t"] * W, (it["t"] + 1) * W)
            if it["kind"] == "g1":
                P.op("dve", lambda ts_=ts_, b2=b2, sb2=sb2: nc.vector.tensor_copy(out=qT[b2][0:64, ts_], in_=C.bank[sb2[0]][0:64, 0:W]),
                     reads=[C.bR[sb2[0]]], writes=[qTR[b2]], part=True)
                P.op("dve", lambda ts_=ts_, sb2=sb2: nc.vector.tensor_tensor(out=t1[rp, :], in0=C.bank[sb2[0]][rp, 0:W], in1=cosT[rp, ts_], op=ALU.mult),
                     reads=[C.bR[sb2[0]]] + csR, writes=[t1R])
                P.op("dve", lambda ts_=ts_, b2=b2, sb2=sb2: nc.vector.tensor_copy(out=kT[b2][0:64, ts_], in_=C.bank[sb2[1]][0:64, 0:W]),
                     reads=[C.bR[sb2[1]]], writes=[kTR[b2]], part=True)
            else:
                P.op("dve", lambda ts_=ts_, sb2=sb2: nc.vector.tensor_tensor(out=t2[rp, :], in0=C.bank[sb2[0]][rp, 0:W], in1=sinT[rp, ts_], op=ALU.mult),
                     reads=[C.bR[sb2[0]]] + csR, writes=[t2R])
                P.op("pool", lambda ts_=ts_, b2=b2: nc.gpsimd.tensor_tensor(out=qT[b2][rp, ts_], in0=t1[rp, :], in1=t2[rp, :], op=ALU.add),
                     reads=[t1R, t2R], writes=[qTR[b2]], part=True)
            return
        pi, ob, hh = it["pi"], it["ob"], it["hh"]
        if it["kind"] == "off":
            P.op("act", lambda sb2=sb2, pi=pi: nc.scalar.activation(out=pT[pi][:, 0:1024], in_=C.ps[:, sb2[0] * 512:sb2[0] * 512 + 1024],
                                                                   func=AF.Exp, scale=SCALE),
                 reads=[C.bR[sb2[0]], C.bR[sb2[1]]], writes=[pTR[pi]])
            for u in range(2):
                kt = it["kt0"] + u
                st = it["first"] and u == 0
                P.op("pe", lambda kt=kt, u=u, pi=pi, ob=ob, hh=hh, st=st: nc.tensor.matmul(
                    C.bank[ob][0:65, 0:W], Vg[:, kt, hh, :], pT[pi][:, u * W:(u + 1) * W], start=st, stop=False),
                    reads=[VgR, VoneR, pTR[pi]], writes=[C.bR[ob]])
        else:
            kt, n0 = it["kt"], it["n0"]
            P.op("act", lambda sb2=sb2, pi=pi, n0=n0: nc.scalar.activation(out=pT[pi][:, n0:W], in_=C.bank[sb2[0]][:, n0:W],
                                                                          func=AF.Exp, scale=SCALE),
                 reads=[C.bR[sb2[0]]], writes=[pTR[pi]])
            P.op("pe", lambda kt=kt, pi=pi, ob=ob, hh=hh, n0=n0, st=it["first"], sp_=it["last"]: nc.tensor.matmul(
                C.bank[ob][0:65, n0:W], Vg[:, kt, hh, :], pT[pi][:, n0:W], start=st, stop=sp_),
                reads=[VgR, VoneR, pTR[pi]], writes=[C.bR[ob]])

    def emit_norm_pre(h, j, ob):
        nonlocal noa
        oi = noa % 2
        noa += 1
        P.op("dve", lambda oi=oi, ob=ob: nc.vector.tensor_copy(out=oa[oi][0:65, :], in_=C.bank[ob][0:65, 0:W]),
             reads=[C.bR[ob]], writes=[oaR[oi]])
        return oi

    def emit_norm_recip(oi, q):
        qs = slice(q * 128, (q + 1) * 128)
        P.op("dve", lambda oi=oi, qs=qs: nc.vector.reciprocal(out=oa[oi][64:65, qs], in_=oa[oi][64:65, qs]),
             reads=[oaR[oi]], writes=[oaR[oi]])

    def emit_norm_post(h, j, ob, oi):
        ob_i = h % 2
        q0 = j * W
        P.op("pe", lambda oi=oi, ob=ob: nc.tensor.matmul(C.bank[ob][0:64, 0:W], sel[0:65, :], oa[oi][0:65, :], start=True, stop=True),
             reads=[selR, oaR[oi]], writes=[C.bR[ob]])
        P.op("dve", lambda oi=oi, ob=ob, ob_i=ob_i, q0=q0: nc.vector.tensor_tensor(
            out=oTh[ob_i][:, q0:q0 + W], in0=oa[oi][0:64, :], in1=C.bank[ob][0:64, 0:W], op=ALU.mult),
            reads=[oaR[oi], C.bR[ob]], writes=[oThR[ob_i]], part=True)
        if j == 7:
            P.dma("sp", lambda h=h, ob_i=ob_i: nc.sync.dma_start(out=C.dr["oTd"][h * 64:(h + 1) * 64, :], in_=oTh[ob_i][:]),
                  reads=[oThR[ob_i]])

    LA = 2
    gen_v(0)
    for t in range(NT):
        gen_qk(0, t)
    for h in range(_dbg_heads()):
        hg, hh = divmod(h, 4)
        b = h % 2
        items = []
        for j in range(8):
            ob = OB[j % 2]
            first = True
            for kt0 in range(0, 4 * j, 2):
                items.append(dict(kind="off", kt0=kt0, b=b, q0=j * W, ob=ob, hh=hh, first=first, last=False, j=j,
                                  sb2=SB3[npt % 3], pi=npt % 3))
                npt += 1
                first = False
            for r in range(4):
                items.append(dict(kind="diag", kt=4 * j + r, n0=128 * r, b=b, q0=j * W, ob=ob, hh=hh, first=first,
                                  last=(r == 3), j=j, sb2=SB3[npt % 3], pi=npt % 3))
                npt += 1
                first = False
            if h + 1 < NH and (h + 1) % 4 != 0:
                for kind in ("g1", "g2"):
                    items.append(dict(kind=kind, h2=h + 1, t=j, sb2=SB3[npt % 3], pi=npt % 3))
                    npt += 1
        n = len(items)
        pend = []
        for i in range(n + LA):
            if i < n:
                emit_S(items[i])
            if i - LA >= 0:
                it = items[i - LA]
                emit_EP(it)
                npend = []
                for (stg, args) in pend:
                    if stg < 4:
                        emit_norm_recip(args[3], stg)
                        npend.append((stg + 1, args))
                    else:
                        emit_norm_post(*args)
                pend = npend
                if it.get("last"):
                    oi = emit_norm_pre(h, it["j"], it["ob"])
                    pend.append((0, (h, it["j"], it["ob"], oi)))
        for (stg, args) in pend:
            for q in range(stg, 4):
                emit_norm_recip(args[3], q)
            emit_norm_post(*args)
        if h + 1 < NH and (h + 1) % 4 == 0:
            gen_v((h + 1) // 4)
            for t in range(NT):
                gen_qk(h + 1, t)
        P.flush()
    P.barrier(C.marks)
    pes.__exit__(None, None, None)


def mla_outproj(C, pes, src, dst):
    nc, P, dr = C.nc, C.P, C.dr
    sb = lambda name, shp, dt=F32: pes.enter_context(nc.sbuf_tensor(name, shp, dt))
    T, W, NT = 4, 512, 8
    wout = sb("o_wout", [128, 8, D], BF16)
    woutR = [Reg("o_wout%d" % k) for k in range(8)]
    for k in range(8):
        load_w_bf16(C, wout[:, k, :], dr["odd_w_out"][k * 128:(k + 1) * 128, :], woutR[k])
    L = ln_setup(C, pes, dr["mix_ln_g"][1:2, :], dr["mix_ln_b"][1:2, :], "o")
    xin = [sb("o_xin%d" % i, [128, T, D]) for i in range(2)]
    xinR = [[Reg("oxin%d_%d" % (i, c)) for c in range(T)] for i in range(2)]
    oTt = [sb("o_oT%d" % i, [128, 8, W], BF16) for i in range(2)]
    oTtR = [Reg("o_oT%d" % i) for i in range(2)]
    o3 = dr["oTd"].rearrange("(k p) t -> p k t", p=128)
    def loads(t):
        b = t % 2
        rows = src[t * W:(t + 1) * W, :].rearrange("(c p) d -> p c d", p=128)
        P.dma("sp", lambda rows=rows, b=b: nc.sync.dma_start(out=oTt[b][:], in_=o3[:, :, t * W:(t + 1) * W]), writes=[oTtR[b]])
        P.dma("sp", lambda rows=rows, b=b: nc.sync.dma_start(out=xin[b][:], in_=rows), writes=xinR[b])

    loads(0)
    for t in range(NT):
        b = t % 2
        if t + 1 < NT:
            loads(t + 1)
        for c in range(T):
            cs = slice(c * 128, (c + 1) * 128)
            pair = (0, 1) if c % 2 == 0 else (2, 3)
            for half in range(2):
                bk = pair[half]
                for k in range(8):
                    P.op("pe", lambda k=k, half=half, bk=bk, cs=cs, b=b: nc.tensor.matmul(
                        C.bank[bk][:, 0:512], oTt[b][:, k, cs], wout[:, k, half * 512:(half + 1) * 512],
                        start=(k == 0), stop=(k == 7)), reads=[oTtR[b], woutR[k]], writes=[C.bR[bk]])
            gc = t * T + c
            ln_epilogue(C, L, pair[0], pair[1], xin[b][:, c, :], xinR[b][c], dst[gc * 128:(gc + 1) * 128, :])


_NC_CACHE = {}


def _get_nc(phases, standalone):
    key = (tuple(phases), standalone)
    if key not in _NC_CACHE:
        _NC_CACHE[key] = build(list(phases), standalone)
    return _NC_CACHE[key]


def _weight_maps(inputs):
    m = {}
    for name, shp in W_SPECS.items():
        m[name] = np.ascontiguousarray(np.asarray(inputs[name], dtype=np.float32).reshape(shp))
    m.update(host_consts())
    return m


FUSED = True


def kernel(**inputs):
    x = np.asarray(inputs["x"], dtype=np.float32)
    pos = np.asarray(inputs["positions"], dtype=np.int32)
    wm = _weight_maps(inputs)
    n = 8
    if FUSED:
        nc = _get_nc((1, 2, 3, 4), False)
        in_maps = []
        for b in range(n):
            d = dict(wm)
            d["x"] = np.ascontiguousarray(x[b])
            d["positions"] = np.ascontiguousarray(pos[b:b + 1])
            in_maps.append(d)
        res = run_bass_kernel_spmd(nc, in_maps, core_ids=list(range(n)))
        return np.stack([res.results[b]["out"] for b in range(n)], axis=0)
    cur = [np.ascontiguousarray(x[b]) for b in range(n)]
    for ph in (1, 2, 3, 4):
        nc = _get_nc((ph,), True)
        in_maps = []
        for b in range(n):
            d = dict(wm)
            d["src"] = cur[b]
            d["positions"] = np.ascontiguousarray(pos[b:b + 1])
            in_maps.append(d)
        res = run_bass_kernel_spmd(nc, in_maps, core_ids=list(range(n)))
        cur = [np.ascontiguousarray(res.results[b]["dst"]) for b in range(n)]
    return np.stack(cur, axis=0)
```

```python
import numpy as np
from contextlib import ExitStack
import concourse.bass as bass
import concourse.mybir as mybir
from concourse.bass_utils import run_bass_kernel_spmd

F32 = mybir.dt.float32
BF16 = mybir.dt.bfloat16
I32 = mybir.dt.int32
AF = mybir.ActivationFunctionType
ALU = mybir.AluOpType

S = 4096
D = 1024
NCH = S // 128
DFF = 2816
NFF = DFF // 128
ALPHA = float((2 * 2) ** 0.25)
LN_EPS = 1e-5
RMS_EPS = 1e-6
TWO_PI = float(2 * np.pi)
PI = float(np.pi)
NH = 16
SCALE = float(96 ** -0.5)


class Reg:
    __slots__ = ("name", "excl", "writers", "readers")

    def __init__(self, name, excl=False):
        self.name = name
        self.excl = excl
        self.writers = []
        self.readers = []


class Ins:
    __slots__ = ("eng", "fn", "dma", "deps", "signal", "count", "dsem", "dval", "seq", "emitted", "xw")

    def __init__(self, eng, fn, dma):
        self.eng = eng
        self.fn = fn
        self.dma = dma
        self.deps = []
        self.signal = False
        self.count = None
        self.dsem = None
        self.dval = None
        self.seq = None
        self.emitted = False
        self.xw = []


class Prog:
    def __init__(self, nc, es):
        self.nc = nc
        self.E = {"pe": nc.tensor, "act": nc.scalar, "dve": nc.vector, "pool": nc.gpsimd, "sp": nc.sync}
        self.sem = {e: es.enter_context(nc.semaphore("c_" + e)) for e in ("pe", "act", "dve", "pool")}
        self.cnt = {e: 0 for e in self.sem}
        self.dsems = {}
        for q, n in (("sp", 12), ("pool", 8), ("act", 4)):
            self.dsems[q] = [es.enter_context(nc.semaphore("d_%s%d" % (q, i))) for i in range(n)]
        self.dnext = {q: 0 for q in self.dsems}
        self.dval = {}
        self.dlast = {}
        self.pending = []
        self.seq = {e: 0 for e in self.E}
        self.sig_hist = {e: [] for e in self.sem}
        self.waited = {e: {x: 0 for x in self.sem} for e in self.E}
        self.dobs = {e: {} for e in self.E}
        self.extra = {e: [] for e in self.E}
        self.last = {e: None for e in self.E}

    def _add(self, eng, fn, reads, writes, dma, part):
        I = Ins(eng, fn, dma)
        I.seq = self.seq[eng]
        self.seq[eng] += 1
        deps = []
        for r in reads:
            if r.excl:
                deps += [(d, "raw") for d in r.writers] + [(d, "war") for d in r.readers]
            else:
                deps += [(d, "raw") for d in r.writers]
        for w in writes:
            if part and not w.excl:
                deps += [(d, "war") for d in w.readers]
            else:
                deps += [(d, "war") for d in w.readers] + [(d, "waw") for d in w.writers]
        for d in self.extra[eng]:
            deps.append((d, "raw"))
        self.extra[eng] = []
        if dma:
            q = eng
            sems = self.dsems[q]
            sem = sems[self.dnext[q] % len(sems)]
            self.dnext[q] += 1
            prev = self.dlast.get(id(sem))
            if prev is not None:
                deps.append((prev, "raw"))
            I.dsem = sem
            I.dval = self.dval.get(id(sem), 0) + 16
            self.dval[id(sem)] = I.dval
            self.dlast[id(sem)] = I
        seen = set()
        for d, kind in deps:
            if d is I or id(d) in seen:
                continue
            if d.dma or dma:
                pass
            elif d.eng == eng:
                if eng == "pe" or kind != "raw":
                    continue
            seen.add(id(d))
            I.deps.append(d)
            if not d.dma and not d.emitted:
                d.signal = True
        for r in reads:
            if not dma:
                r.readers = [x for x in r.readers if x.dma or x.eng != eng]
            r.readers.append(I)
        for w in writes:
            if part and not w.excl:
                if w.readers:
                    w.writers = [I]
                    w.readers = []
                else:
                    if not dma:
                        w.writers = [x for x in w.writers if x.dma or x.eng != eng]
                    w.writers.append(I)
            else:
                w.writers = [I]
                w.readers = []
        self.pending.append(I)
        self.last[eng] = I
        return I

    def op(self, eng, fn, reads=(), writes=(), part=False):
        return self._add(eng, fn, list(reads), list(writes), False, part)

    def dma(self, eng, fn, reads=(), writes=(), part=False):
        return self._add(eng, fn, list(reads), list(writes), True, part)

    def _count_of(self, d):
        if d.count is not None:
            return d.count
        for seq, c in self.sig_hist[d.eng]:
            if seq >= d.seq:
                return c
        raise RuntimeError("no signal after dep on %s" % d.eng)

    def flush(self):
        lastp = {}
        for I in self.pending:
            if not I.dma and I.eng in self.sem:
                lastp[I.eng] = I
        for I in lastp.values():
            I.signal = True
        for I in self.pending:
            e = I.eng
            eng = self.E[e]
            waits = []
            need_c = {}
            for d in I.deps:
                if d.dma:
                    k = id(d.dsem)
                    if self.dobs[e].get(k, 0) < d.dval:
                        self.dobs[e][k] = d.dval
                        waits.append((d.dsem, d.dval))
                else:
                    c = self._count_of(d)
                    if c > need_c.get(d.eng, 0):
                        need_c[d.eng] = c
            for x, c in need_c.items():
                if self.waited[e][x] < c:
                    self.waited[e][x] = c
                    waits.append((self.sem[x], c))
            for sem, val in I.xw:
                k = id(sem)
                if self.dobs[e].get(k, 0) < val:
                    self.dobs[e][k] = val
                    waits.append((sem, val))
            best = {}
            for sem, val in waits:
                k = id(sem)
                if k not in best or best[k][1] < val:
                    best[k] = (sem, val)
            waits = list(best.values())
            while len(waits) > 2:
                a = waits.pop()
                b = waits.pop()
                eng.wait_ge(a[0], a[1])
                eng.wait_ge(b[0], b[1])
                eng.nop()
            for sem, val in waits:
                eng.wait_ge(sem, val)
            bi = I.fn()
            if I.dma:
                bi.then_inc(I.dsem, 16)
            elif I.signal:
                self.cnt[e] += 1
                I.count = self.cnt[e]
                bi.then_inc(self.sem[e], 1)
                self.sig_hist[e].append((I.seq, I.count))
            I.emitted = True
            I.fn = None
        self.pending = []
        for e in self.sig_hist:
            if len(self.sig_hist[e]) > 4:
                self.sig_hist[e] = self.sig_hist[e][-4:]

    def all_dma_waits(self):
        out = []
        for q in self.dsems:
            for sem in self.dsems[q]:
                v = self.dval.get(id(sem), 0)
                if v:
                    out.append((sem, v))
        return out

    def barrier(self, marks):
        ms = []
        for e in ("act", "dve", "pool"):
            m = self.op(e, marks[e])
            m.xw = self.all_dma_waits()
            m.signal = True
            ms.append(m)
        for e in ("act", "dve", "pool", "sp"):
            self.extra[e] = list(ms)
        self.flush()

    def final_wait(self):
        eng = self.E["sp"]
        for sem, val in self.all_dma_waits():
            eng.wait_ge(sem, val)
            eng.nop()


CONST_SPECS = {
    "c_ident": ([128, 128], F32),
    "c_triu": ([128, 128], F32),
    "c_poolA": ([12, 128, 128], F32),
    "c_col": ([128, 4], F32),
    "c_sel": ([128, 64], F32),
}

W_SPECS = {
    "even_w_in": [1024, 1536], "even_vnorm_g": [1, 512], "even_vnorm_b": [1, 512],
    "even_spatial_w": [8, 128, 128], "even_spatial_b": [8, 128], "even_pool_w": [4, 128, 128],
    "even_pool_scale": [1, 512], "even_w_out": [1024, 1024],
    "odd_w_in": [1024, 672], "odd_q_norm_g": [1, 384], "odd_w_q_up": [384, 1536],
    "odd_kv_norm_g": [1, 256], "odd_w_kv_up": [256, 2048], "odd_w_out": [1024, 1024],
    "mix_ln_g": [2, 1024], "mix_ln_b": [2, 1024], "ffn_w_gate_up": [2, 1024, 5632],
    "ffn_w_down": [2, 2816, 1024], "ffn_ln_g": [2, 1024], "ffn_ln_b": [2, 1024],
}


def host_consts():
    ident = np.eye(128, dtype=np.float32)
    triu = np.triu(np.ones((128, 128), dtype=np.float32))
    A = np.zeros((12, 128, 128), dtype=np.float32)
    s = np.arange(128)[:, None]
    t = np.arange(128)[None, :]
    for g, w in enumerate((2, 4, 8, 16)):
        A[g] = ((s <= t) & (s > t - w)) / np.float32(w) - (s == t)
        A[4 + g] = (s >= 128 + t - w + 1) / np.float32(w)
        cnt = np.minimum(t + 1, w).astype(np.float32)
        A[8 + g] = ((s <= t) & (s > t - w)) / cnt - (s == t)
    col = np.zeros((128, 4), dtype=np.float32)
    freqs = (10000.0 ** (-np.arange(0, 32, 2, dtype=np.float32) / 32)).astype(np.float32)
    col[64:80, 0] = freqs
    col[80:96, 0] = freqs
    col[:, 1] = 1.0
    col[64:80, 1] = -1.0
    col[:, 2] = LN_EPS
    col[:, 3] = RMS_EPS
    sel = np.zeros((128, 64), dtype=np.float32)
    sel[64, :] = 1.0
    return {"c_ident": ident, "c_triu": triu, "c_poolA": A.astype(np.float32), "c_col": col, "c_sel": sel}


class Ctx:
    pass


def _dbg_heads():
    import os
    return int(os.environ.get("MK_DBG_HEADS", NH))


def build(phases, standalone):
    nc = bass.Bass("TRN2", target_bir_lowering=False)
    dr = {}
    for name, shp in W_SPECS.items():
        dr[name] = nc.dram_tensor(name, shp, F32, kind="ExternalInput").ap()
    for name, (shp, dt) in CONST_SPECS.items():
        dr[name] = nc.dram_tensor(name, shp, dt, kind="ExternalInput").ap()
    dr["positions"] = nc.dram_tensor("positions", [1, S], I32, kind="ExternalInput").ap()
    if standalone:
        src = nc.dram_tensor("src", [S, D], F32, kind="ExternalInput").ap()
        dst = nc.dram_tensor("dst", [S, D], F32, kind="ExternalOutput").ap()
        chain = {phases[0]: (src, dst)}
    else:
        x = nc.dram_tensor("x", [S, D], F32, kind="ExternalInput").ap()
        out = nc.dram_tensor("out", [S, D], F32, kind="ExternalOutput").ap()
        s1 = nc.dram_tensor("scr1", [S, D], F32).ap()
        s2 = nc.dram_tensor("scr2", [S, D], F32).ap()
        s3 = nc.dram_tensor("scr3", [S, D], F32).ap()
        chain = {1: (x, s1), 2: (s1, s2), 3: (s2, s3), 4: (s3, out)}
    dr["oTd"] = nc.dram_tensor("scr_oT", [D, S], BF16).ap()

    with ExitStack() as es:
        P = Prog(nc, es)
        C = Ctx()
        C.nc, C.P, C.dr = nc, P, dr
        C.ps = es.enter_context(nc.psum_tensor("ps", [128, 4096], F32))
        C.bank = [C.ps[:, 512 * i:512 * (i + 1)] for i in range(8)]
        C.bR = [Reg("bank%d" % i, excl=True) for i in range(8)]
        C.ident = es.enter_context(nc.sbuf_tensor("ident", [128, 128], F32))
        C.identR = Reg("ident")
        C.col = es.enter_context(nc.sbuf_tensor("colc", [128, 4], F32))
        C.colR = Reg("col")
        C.mk = es.enter_context(nc.sbuf_tensor("marks", [128, 8], F32))
        P.dma("sp", lambda: nc.sync.dma_start(out=C.ident[:], in_=dr["c_ident"][:, :]), writes=[C.identR])
        P.dma("sp", lambda: nc.sync.dma_start(out=C.col[:], in_=dr["c_col"][:, :]), writes=[C.colR])
        C.marks = {
            "act": lambda: nc.scalar.activation(out=C.mk[:, 0:1], in_=C.mk[:, 1:2], func=AF.Copy),
            "dve": lambda: nc.vector.memset(C.mk[:, 2:3], 0.0),
            "pool": lambda: nc.gpsimd.memset(C.mk[:, 4:5], 0.0),
        }
        P.op("dve", lambda: nc.vector.memset(C.mk[:], 0.0))
        for ph in phases:
            srcap, dstap = chain[ph]
            with ExitStack() as pes:
                if ph == 1:
                    phase_mixer0(C, pes, srcap, dstap)
                elif ph == 2:
                    phase_ffn(C, pes, 0, srcap, dstap)
                elif ph == 3:
                    phase_mla(C, pes, srcap, dstap)
                elif ph == 4:
                    phase_ffn(C, pes, 1, srcap, dstap)
                P.barrier(C.marks)
        P.flush()
        P.final_wait()
    return nc


def bcast_row(C, pes, name, row_ap, n):
    nc, P = C.nc, C.P
    t = pes.enter_context(nc.sbuf_tensor(name, [128, n], F32))
    R = Reg(name)
    P.dma("sp", lambda: nc.sync.dma_start(out=t[:], in_=row_ap.partition_broadcast(128)), writes=[R])
    return t, R


class LNState:
    pass


def ln_setup(C, pes, g_row, b_row, tag):
    nc = C.nc
    L = LNState()
    L.g, L.gR = bcast_row(C, pes, "lng_" + tag, g_row, D)
    L.b, L.bR = bcast_row(C, pes, "lnb_" + tag, b_row, D)
    L.s = [pes.enter_context(nc.sbuf_tensor("lns%d_%s" % (i, tag), [128, D], F32)) for i in range(2)]
    L.sR = [Reg("lns%d" % i) for i in range(2)]
    L.y = [pes.enter_context(nc.sbuf_tensor("lny%d_%s" % (i, tag), [128, D], F32)) for i in range(2)]
    L.yR = [Reg("lny%d" % i) for i in range(2)]
    L.st = [pes.enter_context(nc.sbuf_tensor("lnst%d_%s" % (i, tag), [128, 24], F32)) for i in range(2)]
    L.stR = [Reg("lnst%d" % i) for i in range(2)]
    L.n = 0
    return L


def ln_epilogue(C, L, bankA, bankB, x_ap, xR, dst_rows):
    nc, P = C.nc, C.P
    i = L.n % 2
    L.n += 1
    s, sR, y, yR, st, stR = L.s[i], L.sR[i], L.y[i], L.yR[i], L.st[i], L.stR[i]
    bA, bB = C.bank[bankA], C.bank[bankB]
    P.op("dve", lambda: nc.vector.scalar_tensor_tensor(out=s[:, 0:512], in0=x_ap[:, 0:512], scalar=ALPHA, in1=bA,
                                                      op0=ALU.mult, op1=ALU.add),
         reads=[xR, C.bR[bankA]], writes=[sR], part=True)
    P.op("dve", lambda: nc.vector.scalar_tensor_tensor(out=s[:, 512:1024], in0=x_ap[:, 512:1024], scalar=ALPHA, in1=bB,
                                                      op0=ALU.mult, op1=ALU.add),
         reads=[xR, C.bR[bankB]], writes=[sR], part=True)
    P.op("dve", lambda: nc.vector.bn_stats(out=st[:, 0:6], in_=s[:, 0:512]), reads=[sR], writes=[stR], part=True)
    P.op("dve", lambda: nc.vector.bn_stats(out=st[:, 6:12], in_=s[:, 512:1024]), reads=[sR], writes=[stR], part=True)
    R1, R2, R3 = Reg("mv"), Reg("sd"), Reg("rstd")
    P.op("dve", lambda: nc.vector.bn_aggr(out=st[:, 12:14], in_=st[:, 0:12]), reads=[stR], writes=[R1])
    P.op("act", lambda: nc.scalar.activation(out=st[:, 14:15], in_=st[:, 13:14], func=AF.Sqrt, bias=C.col[:, 2:3], scale=1.0),
         reads=[R1, C.colR], writes=[R2])
    P.op("dve", lambda: nc.vector.scalar_tensor_tensor(out=s[:], in0=s[:], scalar=st[:, 12:13], in1=L.g[:],
                                                      op0=ALU.subtract, op1=ALU.mult), reads=[sR, R1, L.gR], writes=[sR])
    P.op("dve", lambda: nc.vector.reciprocal(out=st[:, 15:16], in_=st[:, 14:15]), reads=[R2], writes=[R3])
    P.op("dve", lambda: nc.vector.scalar_tensor_tensor(out=y[:], in0=s[:], scalar=st[:, 15:16], in1=L.b[:],
                                                      op0=ALU.mult, op1=ALU.add), reads=[sR, R3, L.bR], writes=[yR])
    P.dma("sp", lambda: nc.sync.dma_start(out=dst_rows, in_=y[:]), reads=[yR])


def load_transpose_tile(C, src, t0, nchunk, xin, xinR, xT, xTR, tbanks, evac_engs=("act", "dve")):
    nc, P = C.nc, C.P
    rows = src[t0 * 128:(t0 + nchunk) * 128, :].rearrange("(c p) d -> p c d", p=128)
    P.dma("sp", lambda: nc.sync.dma_start(out=xin[:, 0:nchunk, :], in_=rows), writes=xinR)
    W = nchunk * 128
    for k in range(8):
        b = tbanks[k % len(tbanks)]
        for c in range(nchunk):
            P.op("pe", lambda c=c, k=k, b=b: nc.tensor.transpose(C.bank[b][:, c * 128:(c + 1) * 128],
                                                                xin[:, c, k * 128:(k + 1) * 128], C.ident[:]),
                 reads=[xinR[c], C.identR], writes=[C.bR[b]])
        e = evac_engs[k % len(evac_engs)]
        if e == "act":
            P.op("act", lambda k=k, b=b: nc.scalar.copy(out=xT[:, k, 0:W], in_=C.bank[b][:, 0:W]),
                 reads=[C.bR[b]], writes=[xTR], part=True)
        else:
            P.op("dve", lambda k=k, b=b: nc.vector.tensor_copy(out=xT[:, k, 0:W], in_=C.bank[b][:, 0:W]),
                 reads=[C.bR[b]], writes=[xTR], part=True)


def load_w_bf16(C, tile_ap, dram_ap, R, eng="pool"):
    nc, P = C.nc, C.P
    P.dma("pool", lambda: nc.gpsimd.dma_start(out=tile_ap, in_=dram_ap), writes=[R], part=True)


def phase_ffn(C, pes, layer, src, dst):
    nc, P, dr = C.nc, C.P, C.dr
    T = 2
    sfx = "_L%d" % layer
    W = T * 128
    NT = NCH // T
    wgu = pes.enter_context(nc.sbuf_tensor("wgu" + sfx, [128, 8, 2 * DFF], BF16))
    wd = pes.enter_context(nc.sbuf_tensor("wd" + sfx, [128, NFF, D], BF16))
    HJ = NFF // 2
    wguR = [[Reg("wgu%d_%d" % (k, ch)) for ch in range(4)] for k in range(8)]
    wdR = [Reg("wd%d" % j) for j in range(NFF)]
    gu = dr["ffn_w_gate_up"][layer]
    chunks = [(0, 0, HJ * 128), (1, DFF, DFF + HJ * 128), (2, HJ * 128, DFF), (3, DFF + HJ * 128, 2 * DFF)]
    for (ch, c0, c1) in chunks:
        for k in range(8):
            load_w_bf16(C, wgu[:, k, c0:c1], gu[k * 128:(k + 1) * 128, c0:c1], wguR[k][ch])
    dn = dr["ffn_w_down"][layer]
    for j in range(NFF):
        load_w_bf16(C, wd[:, j, :], dn[j * 128:(j + 1) * 128, :], wdR[j])
    L = ln_setup(C, pes, dr["ffn_ln_g"][layer:layer + 1, :], dr["ffn_ln_b"][layer:layer + 1, :], "f%d" % layer)
    xin = [pes.enter_context(nc.sbuf_tensor("fxin%d" % i + sfx, [128, T, D], F32)) for i in range(2)]
    xinR = [[Reg("fxin%d_%d" % (i, c)) for c in range(T)] for i in range(2)]
    xT = [pes.enter_context(nc.sbuf_tensor("fxT%d" % i + sfx, [128, 8, W], BF16)) for i in range(2)]
    xTR = [Reg("fxT%d" % i) for i in range(2)]
    hT = pes.enter_context(nc.sbuf_tensor("hT" + sfx, [128, NFF, W], BF16))
    hTR = [Reg("hT%d" % j) for j in range(NFF)]
    sg = [pes.enter_context(nc.sbuf_tensor("sg%d" % i + sfx, [128, W], F32)) for i in range(2)]
    sgR = [Reg("sg%d" % i) for i in range(2)]

    def transposes(t):
        load_transpose_tile(C, src, t * T, T, xin[t % 2], xinR[t % 2], xT[t % 2], xTR[t % 2], (0, 1))

    transposes(0)
    nsg = 0
    for t in range(NT):
        b = t % 2
        for j in range(NFF):
            bk = 2 + (j % 2)
            for which in range(2):
                col = which * DFF + j * 128
                for k in range(8):
                    P.op("pe", lambda k=k, col=col, bk=bk, which=which, b=b: nc.tensor.matmul(
                        C.bank[bk][:, which * W:(which + 1) * W], wgu[:, k, col:col + 128], xT[b][:, k, :],
                        start=(k == 0), stop=(k == 7)),
                        reads=[wguR[k][which + (2 if j >= HJ else 0)], xTR[b]], writes=[C.bR[bk]])
            si = nsg % 2
            nsg += 1
            P.op("act", lambda bk=bk, si=si: nc.scalar.activation(out=sg[si][:], in_=C.bank[bk][:, 0:W], func=AF.Silu),
                 reads=[C.bR[bk]], writes=[sgR[si]])
            P.op("dve", lambda bk=bk, si=si, j=j: nc.vector.tensor_tensor(out=hT[:, j, :], in0=sg[si][:], in1=C.bank[bk][:, W:2 * W],
                                                                      op=ALU.mult),
                 reads=[sgR[si], C.bR[bk]], writes=[hTR[j]])
        if t + 1 < NT:
            transposes(t + 1)
        for c in range(T):
            for half in range(2):
                bk = 4 + 2 * c + half
                for j in range(NFF):
                    P.op("pe", lambda j=j, c=c, half=half, bk=bk: nc.tensor.matmul(
                        C.bank[bk][:, 0:512], hT[:, j, c * 128:(c + 1) * 128], wd[:, j, half * 512:(half + 1) * 512],
                        start=(j == 0), stop=(j == NFF - 1)),
                        reads=[hTR[j], wdR[j]], writes=[C.bR[bk]])
            r0 = (t * T + c) * 128
            ln_epilogue(C, L, 4 + 2 * c, 5 + 2 * c, xin[b][:, c, :], xinR[b][c], dst[r0:r0 + 128, :])


def phase_mixer0(C, pes, src, dst):
    nc, P, dr = C.nc, C.P, C.dr
    T = 4
    W = 512
    NT = NCH // T
    sb = lambda name, shp, dt=F32: pes.enter_context(nc.sbuf_tensor(name, shp, dt))
    win = sb("m_win", [128, 8, 1536], BF16)
    winR = [Reg("win%d" % k) for k in range(8)]
    for k in range(8):
        load_w_bf16(C, win[:, k, :], dr["even_w_in"][k * 128:(k + 1) * 128, :], winR[k])
    wout = sb("m_wout", [128, 8, D], BF16)
    woutR = [Reg("wout%d" % k) for k in range(8)]
    for k in range(8):
        load_w_bf16(C, wout[:, k, :], dr["even_w_out"][k * 128:(k + 1) * 128, :], woutR[k])
    poolw = sb("m_poolw", [128, 4, 128], BF16)
    poolwR = Reg("poolw")
    load_w_bf16(C, poolw[:], dr["even_pool_w"].rearrange("g c d -> c g d"), poolwR)
    A = sb("m_A", [128, 12, 128])
    AR = Reg("A")
    P.dma("sp", lambda: nc.sync.dma_start(out=A[:], in_=dr["c_poolA"].rearrange("n s t -> s n t")), writes=[AR])
    triu = sb("m_triu", [128, 128])
    triuR = Reg("triu")
    P.dma("sp", lambda: nc.sync.dma_start(out=triu[:], in_=dr["c_triu"][:, :]), writes=[triuR])
    wsn = sb("m_wsn", [128, 8, 128])
    wsnR = Reg("wsn")
    P.dma("sp", lambda: nc.sync.dma_start(out=wsn[:], in_=dr["even_spatial_w"].rearrange("h t s -> t h s")), writes=[wsnR])
    wsT = sb("m_wsT", [128, 8, 128], BF16)
    wsTR = Reg("wsT")
    for h in range(8):
        P.op("pe", lambda h=h: nc.tensor.transpose(C.bank[0][:, 0:128], wsn[:, h, :], C.ident[:]),
             reads=[wsnR, C.identR], writes=[C.bR[0]])
        P.op("dve", lambda h=h: nc.vector.tensor_tensor(out=wsT[:, h, :], in0=C.bank[0][:, 0:128], in1=triu[:], op=ALU.mult),
             reads=[C.bR[0], triuR], writes=[wsTR], part=True)
    bs = sb("m_bs", [128, 8])
    bsR = Reg("bs")
    pscale = sb("m_pscale", [128, 4])
    pscaleR = Reg("pscale")
    with nc.allow_non_contiguous_dma(reason="tiny per-head bias / scale columns"):
        P.dma("sp", lambda: nc.sync.dma_start(out=bs[:], in_=dr["even_spatial_b"].rearrange("h t -> t h")), writes=[bsR])
        P.dma("sp", lambda: nc.sync.dma_start(out=pscale[:], in_=dr["even_pool_scale"].rearrange("o (g d) -> d (o g)", g=4)),
              writes=[pscaleR])
        P.flush()
    vg, vgR = bcast_row(C, pes, "m_vg", dr["even_vnorm_g"], 512)
    vb, vbR = bcast_row(C, pes, "m_vb", dr["even_vnorm_b"], 512)
    L = ln_setup(C, pes, dr["mix_ln_g"][0:1, :], dr["mix_ln_b"][0:1, :], "m")
    xin = [sb("m_xin%d" % i, [128, T, D]) for i in range(3)]
    xinR = [[Reg("mxin%d_%d" % (i, c)) for c in range(T)] for i in range(3)]
    xT = [sb("m_xT%d" % i, [128, 8, W], BF16) for i in range(2)]
    xTR = [Reg("mxT%d" % i) for i in range(2)]
    mixT = [sb("m_mixT%d" % i, [128, 8, 128], BF16) for i in range(2)]
    mixTR = [Reg("mixT%d" % i) for i in range(2)]
    u_sb = [sb("m_u%d" % i, [128, 512]) for i in range(3)]
    uR = [Reg("u%d" % i) for i in range(3)]
    v_sb = [sb("m_v%d" % i, [128, 512]) for i in range(2)]
    vR = [Reg("v%d" % i) for i in range(2)]
    vbf = [sb("m_vbf%d" % i, [128, 512], BF16) for i in range(2)]
    vbfR = [Reg("vbf%d" % i) for i in range(2)]
    a_sb = [sb("m_a%d" % i, [128, 512]) for i in range(2)]
    aR = [Reg("a%d" % i) for i in range(2)]
    xp = [sb("m_xp%d" % i, [128, 512]) for i in range(4)]
    xpR = [Reg("xp%d" % i) for i in range(4)]
    pooledT = [sb("m_pooledT%d" % i, [128, 4, 128], BF16) for i in range(2)]
    pooledTR = [Reg("pooledT%d" % i) for i in range(2)]
    vst = [sb("m_vst%d" % i, [128, 16]) for i in range(2)]
    BXA, BPM, BU, BV, BSP, BPF, BOA, BOB = 0, 1, 2, 3, 4, 5, 6, 7

    def geo(gc):
        t, c = divmod(gc, T)
        return t, c, t % 2, slice(c * 128, (c + 1) * 128)

    def A_pe(gc):
        t, c, b, cs = geo(gc)
        for which, bk in ((0, BU), (1, BV)):
            for k in range(8):
                P.op("pe", lambda k=k, which=which, bk=bk, cs=cs, b=b: nc.tensor.matmul(
                    C.bank[bk][:, 0:512], xT[b][:, k, cs], win[:, k, which * 512:(which + 1) * 512],
                    start=(k == 0), stop=(k == 7)), reads=[xTR[b], winR[k]], writes=[C.bR[bk]])
        for k in range(8):
            P.op("pe", lambda k=k, cs=cs, b=b: nc.tensor.matmul(
                C.bank[BXA][:, 0:512], xT[b][:, k, cs], win[:, k, 1024:1536],
                start=(k == 0), stop=(k == 7)), reads=[xTR[b], winR[k]], writes=[C.bR[BXA]])
        i4 = gc % 4
        P.op("act", lambda i4=i4: nc.scalar.copy(out=xp[i4][:], in_=C.bank[BXA][:, 0:512]),
             reads=[C.bR[BXA]], writes=[xpR[i4]])

    def A_gelu(gc):
        i2, i3 = gc % 2, gc % 3
        P.op("act", lambda i3=i3: nc.scalar.activation(out=u_sb[i3][:], in_=C.bank[BU][:, 0:512], func=AF.Gelu),
             reads=[C.bR[BU]], writes=[uR[i3]])
        P.op("act", lambda i2=i2: nc.scalar.activation(out=v_sb[i2][:], in_=C.bank[BV][:, 0:512], func=AF.Gelu),
             reads=[C.bR[BV]], writes=[vR[i2]])

    def A_vln(gc):
        i2, i3 = gc % 2, gc % 3
        st = vst[i2]
        R0, R1, R2, R3 = Reg("vs0"), Reg("vs1"), Reg("vs2"), Reg("vs3")
        P.op("dve", lambda st=st, i2=i2: nc.vector.bn_stats(out=st[:, 0:6], in_=v_sb[i2][:]), reads=[vR[i2]], writes=[R0])
        P.op("dve", lambda st=st: nc.vector.bn_aggr(out=st[:, 6:8], in_=st[:, 0:6]), reads=[R0], writes=[R1])
        P.op("act", lambda st=st: nc.scalar.activation(out=st[:, 8:9], in_=st[:, 7:8], func=AF.Sqrt, bias=C.col[:, 2:3], scale=1.0),
             reads=[R1, C.colR], writes=[R2])
        P.op("dve", lambda st=st, i2=i2: nc.vector.scalar_tensor_tensor(out=v_sb[i2][:], in0=v_sb[i2][:], scalar=st[:, 6:7], in1=vg[:],
                                                                     op0=ALU.subtract, op1=ALU.mult),
             reads=[vR[i2], R1, vgR], writes=[vR[i2]])
        P.op("dve", lambda st=st: nc.vector.reciprocal(out=st[:, 9:10], in_=st[:, 8:9]), reads=[R2], writes=[R3])
        P.op("dve", lambda st=st, i2=i2: nc.vector.scalar_tensor_tensor(out=vbf[i2][:], in0=v_sb[i2][:], scalar=st[:, 9:10], in1=vb[:],
                                                                     op0=ALU.mult, op1=ALU.add),
             reads=[vR[i2], R3, vbR], writes=[vbfR[i2]])

    def B_pe(gc):
        i2, i3, ip = gc % 2, gc % 4, (gc - 1) % 4
        for h in range(8):
            P.op("pe", lambda h=h, i2=i2: nc.tensor.matmul(C.bank[BSP][:, h * 64:(h + 1) * 64], wsT[:, h, :],
                                                          vbf[i2][:, h * 64:(h + 1) * 64], start=True, stop=True),
                 reads=[wsTR, vbfR[i2]], writes=[C.bR[BSP]])
        for g in range(4):
            gs = slice(g * 128, (g + 1) * 128)
            if gc == 0:
                P.op("pe", lambda g=g, gs=gs, i3=i3: nc.tensor.matmul(C.bank[BPF][:, gs], xp[i3][:, gs], A[:, 8 + g, :],
                                                                     start=True, stop=True),
                     reads=[xpR[i3], AR], writes=[C.bR[BPF]])
            else:
                P.op("pe", lambda g=g, gs=gs, i3=i3: nc.tensor.matmul(C.bank[BPF][:, gs], xp[i3][:, gs], A[:, g, :],
                                                                     start=True, stop=False),
                     reads=[xpR[i3], AR], writes=[C.bR[BPF]])
                P.op("pe", lambda g=g, gs=gs, ip=ip: nc.tensor.matmul(C.bank[BPF][:, gs], xp[ip][:, gs], A[:, 4 + g, :],
                                                                     start=False, stop=True),
                     reads=[xpR[ip], AR], writes=[C.bR[BPF]])

    def B_add(gc):
        i2, i3 = gc % 2, gc % 3
        P.op("dve", lambda i2=i2: nc.vector.tensor_tensor(
            out=a_sb[i2][:].rearrange("p (h d) -> p h d", h=8), in0=C.bank[BSP][:, 0:512].rearrange("p (h d) -> p h d", h=8),
            in1=bs[:, 0:8].unsqueeze(2).to_broadcast([128, 8, 64]), op=ALU.add),
            reads=[C.bR[BSP], bsR], writes=[aR[i2]])
        P.op("pool", lambda i2=i2, i3=i3: nc.gpsimd.tensor_tensor(out=a_sb[i2][:], in0=a_sb[i2][:], in1=u_sb[i3][:], op=ALU.mult),
             reads=[aR[i2], uR[i3]], writes=[aR[i2]])

    def B_copy(gc):
        i2 = gc % 2
        P.op("act", lambda i2=i2: nc.scalar.copy(out=pooledT[i2][:].rearrange("p g t -> p (g t)"), in_=C.bank[BPF][:, 0:512]),
             reads=[C.bR[BPF]], writes=[pooledTR[i2]])

    def C_pe(gc):
        i2 = gc % 2
        for kb in range(4):
            P.op("pe", lambda kb=kb, i2=i2: nc.tensor.transpose(C.bank[BXA][:, kb * 128:(kb + 1) * 128],
                                                               a_sb[i2][:, kb * 128:(kb + 1) * 128], C.ident[:]),
                 reads=[aR[i2], C.identR], writes=[C.bR[BXA]])
        for g in range(4):
            P.op("pe", lambda g=g, i2=i2: nc.tensor.matmul(C.bank[BPM][:, g * 128:(g + 1) * 128], poolw[:, g, :], pooledT[i2][:, g, :],
                                                          start=True, stop=True),
                 reads=[poolwR, pooledTR[i2]], writes=[C.bR[BPM]])

    def C_copy(gc):
        i2 = gc % 2
        P.op("act", lambda i2=i2: nc.scalar.copy(out=mixT[i2][:, 0:4, :], in_=C.bank[BXA][:, 0:512].rearrange("p (k t) -> p k t", k=4)),
             reads=[C.bR[BXA]], writes=[mixTR[i2]], part=True)

    def C_scale(gc):
        i2 = gc % 2
        P.op("dve", lambda i2=i2: nc.vector.tensor_tensor(
            out=mixT[i2][:, 4:8, :], in0=C.bank[BPM][:, 0:512].rearrange("p (g t) -> p g t", g=4),
            in1=pscale[:, 0:4].unsqueeze(2).to_broadcast([128, 4, 128]), op=ALU.mult),
            reads=[C.bR[BPM], pscaleR], writes=[mixTR[i2]], part=True)

    def D_pe(gc):
        i2 = gc % 2
        for half, bk in ((0, BOA), (1, BOB)):
            for k in range(8):
                P.op("pe", lambda k=k, half=half, bk=bk, i2=i2: nc.tensor.matmul(
                    C.bank[bk][:, 0:512], mixT[i2][:, k, :], wout[:, k, half * 512:(half + 1) * 512],
                    start=(k == 0), stop=(k == 7)), reads=[mixTR[i2], woutR[k]], writes=[C.bR[bk]])

    def D_post(gc):
        t, c, b, cs = geo(gc)
        xi = t % 3
        ln_epilogue(C, L, BOA, BOB, xin[xi][:, c, :], xinR[xi][c], dst[gc * 128:(gc + 1) * 128, :])

    def TX(t):
        load_transpose_tile(C, src, t * T, T, xin[t % 3], xinR[t % 3], xT[t % 2], xTR[t % 2], (BU, BV))

    TX(0)
    ok = lambda g: 0 <= g < NCH
    for s_ in range(NCH + 8):
        if ok(s_ - 1):
            A_gelu(s_ - 1)
        if ok(s_ - 3):
            B_copy(s_ - 3)
        if ok(s_ - 5):
            C_copy(s_ - 5)
            C_scale(s_ - 5)
        if ok(s_ - 3):
            B_add(s_ - 3)
        if ok(s_ - 1):
            A_vln(s_ - 1)
        if ok(s_ - 7):
            D_post(s_ - 7)
        if s_ % T == 2 and (s_ // T) + 1 < NT:
            TX(s_ // T + 1)
        if ok(s_):
            A_pe(s_)
        if ok(s_ - 2):
            B_pe(s_ - 2)
        if ok(s_ - 6):
            D_pe(s_ - 6)
        if ok(s_ - 4):
            C_pe(s_ - 4)
        if s_ % 4 == 3:
            P.flush()


def phase_mla(C, pes, src, dst):
    nc, P, dr = C.nc, C.P, C.dr
    with ExitStack() as aes:
        mla_latents_and_attention(C, aes, src)
    mla_outproj(C, pes, src, dst)


def mla_latents_and_attention(C, pes, src):
    nc, P, dr = C.nc, C.P, C.dr
    sb = lambda name, shp, dt=F32: pes.enter_context(nc.sbuf_tensor(name, shp, dt))
    T, W, NT = 4, 512, 8
    w_in = dr["odd_w_in"]
    wq_in = sb("a_wqin", [128, 8, 384], BF16)
    wkv_in = sb("a_wkvin", [128, 8, 256], BF16)
    wkr = sb("a_wkr", [128, 8, 96], BF16)
    wkrs = sb("a_wkrs", [128, 8, 96], BF16)
    winR = Reg("a_win")
    w3 = w_in.rearrange("(k p) n -> p k n", p=128)
    load_w_bf16(C, wq_in[:], w3[:, :, 0:384], winR)
    load_w_bf16(C, wkv_in[:], w3[:, :, 384:640], winR)
    load_w_bf16(C, wkr[:], w3[:, :, 576:672], winR)
    load_w_bf16(C, wkrs[:, :, 0:64], w3[:, :, 576:640], winR)
    load_w_bf16(C, wkrs[:, :, 64:80], w3[:, :, 656:672], winR)
    load_w_bf16(C, wkrs[:, :, 80:96], w3[:, :, 640:656], winR)
    wqu = sb("a_wqu", [128, 3, 1536], BF16)
    wqus = sb("a_wqus", [128, 3, 16, 96], BF16)
    wquR = Reg("a_wqu")
    q3 = dr["odd_w_q_up"].rearrange("(k p) n -> p k n", p=128)
    q4 = dr["odd_w_q_up"].rearrange("(k p) (h d) -> p k h d", p=128, h=16)
    load_w_bf16(C, wqu[:], q3, wquR)
    for k in range(3):
        for hs in (slice(0, 8), slice(8, 16)):
            load_w_bf16(C, wqus[:, k, hs, 0:64], q4[:, k, hs, 0:64], wquR)
            load_w_bf16(C, wqus[:, k, hs, 64:80], q4[:, k, hs, 80:96], wquR)
            load_w_bf16(C, wqus[:, k, hs, 80:96], q4[:, k, hs, 64:80], wquR)
    wkn = sb("a_wkn", [128, 2, 16, 64], BF16)
    wv = sb("a_wv", [128, 2, 16, 64], BF16)
    wkvR = Reg("a_wkv")
    kv4 = dr["odd_w_kv_up"].rearrange("(k p) (h d) -> p k h d", p=128, h=16)
    for k in range(2):
        for hs in (slice(0, 8), slice(8, 16)):
            load_w_bf16(C, wkn[:, k, hs, :], kv4[:, k, hs, 0:64], wkvR)
            load_w_bf16(C, wv[:, k, hs, :], kv4[:, k, hs, 64:128], wkvR)
    gq = sb("a_gq", [128, 3])
    gkv = sb("a_gkv", [128, 2])
    gR = Reg("a_g")
    with nc.allow_non_contiguous_dma(reason="tiny norm-gain columns"):
        P.dma("sp", lambda: nc.sync.dma_start(out=gq[:], in_=dr["odd_q_norm_g"].rearrange("o (k p) -> p (o k)", p=128)), writes=[gR], part=True)
        P.dma("sp", lambda: nc.sync.dma_start(out=gkv[:], in_=dr["odd_kv_norm_g"].rearrange("o (k p) -> p (o k)", p=128)), writes=[gR], part=True)
        P.flush()
    ones = sb("a_ones", [128, 128])
    onesR = Reg("a_ones")
    P.op("pool", lambda: nc.gpsimd.memset(ones[:], 1.0), writes=[onesR])
    sel = sb("a_sel", [128, 64])
    selR = Reg("a_sel")
    P.dma("sp", lambda: nc.sync.dma_start(out=sel[:], in_=dr["c_sel"][:, :]), writes=[selR])
    tri = sb("a_tri", [128, 128])
    trib = sb("a_trib", [128, 128], BF16)
    triR = Reg("a_tri")
    tribR = Reg("a_trib")
    P.dma("sp", lambda: nc.sync.dma_start(out=tri[:], in_=dr["c_triu"][:, :]), writes=[triR])
    P.op("pool", lambda: nc.gpsimd.tensor_copy(out=trib[:], in_=tri[:]), reads=[triR], writes=[tribR])
    negm = sb("a_negm", [128, 128], BF16)
    identb = sb("a_identb", [128, 128], BF16)
    mskR = Reg("a_msk")
    P.op("dve", lambda: nc.vector.tensor_scalar(out=negm[:], in0=tri[:], scalar1=-1.0, scalar2=30000.0, op0=ALU.add, op1=ALU.mult),
         reads=[triR], writes=[mskR], part=True)
    P.op("dve", lambda: nc.vector.tensor_copy(out=identb[:], in_=C.ident[:]), reads=[C.identR], writes=[mskR], part=True)

    cosT = sb("a_cos", [128, S])
    sinT = sb("a_sin", [128, S])
    csR = [Reg("a_cs%d" % t) for t in range(NT)]
    cqT = sb("a_cqT", [128, 3, S], BF16)
    ckvT = sb("a_ckvT", [128, 2, S], BF16)
    latR = [Reg("a_lat%d" % t) for t in range(NT)]
    kT = [sb("a_kT%d" % i, [96, S], BF16) for i in range(2)]
    kTropeR = [Reg("a_kTr%d" % t) for t in range(NT)]
    kTR = [Reg("a_kT%d" % i) for i in range(2)]
    t1 = sb("a_t1", [128, W])
    t2 = sb("a_t2", [128, W])
    t1R, t2R = Reg("a_t1"), Reg("a_t2")
    rp = slice(64, 96)
    shared_pes = pes
    pes = ExitStack()
    pes.__enter__()

    xin = [sb("a_xin%d" % i, [128, T, D]) for i in range(2)]
    xinR = [[Reg("axin%d_%d" % (i, c)) for c in range(T)] for i in range(2)]
    xT = [sb("a_xT%d" % i, [128, 8, W], BF16) for i in range(2)]
    xTR = [Reg("axT%d" % i) for i in range(2)]
    posi = sb("a_posi", [128, W], I32)
    ang = sb("a_ang", [128, W])
    tq = sb("a_tq", [128, W])
    ki = sb("a_ki", [128, W], I32)
    sq = [sb("a_sq%d" % i, [128, W]) for i in range(2)]
    sqR = [Reg("a_sq%d" % i) for i in range(2)]
    rstd = sb("a_rstd", [128, W])
    rstdR = Reg("a_rstd")
    posR, angR, tqR, kiR = Reg("a_pos"), Reg("a_ang"), Reg("a_tq"), Reg("a_ki")

    def rope_tables(t):
        ts_ = slice(t * W, (t + 1) * W)
        P.dma("sp", lambda ts_=ts_: nc.sync.dma_start(out=posi[rp, :], in_=dr["positions"][0:1, ts_].partition_broadcast(32)), writes=[posR])
        P.op("dve", lambda: nc.vector.tensor_copy(out=ang[rp, :], in_=posi[rp, :]), reads=[posR], writes=[angR])
        P.op("dve", lambda: nc.vector.tensor_scalar(out=ang[rp, :], in0=ang[rp, :], scalar1=C.col[rp, 0:1], scalar2=None, op0=ALU.mult),
             reads=[angR, C.colR], writes=[angR])
        for which in (0, 1):
            off = 0.0 if which == 0 else PI / 2
            P.op("dve", lambda off=off: nc.vector.tensor_scalar(out=tq[rp, :], in0=ang[rp, :], scalar1=off, scalar2=1.0 / TWO_PI,
                                                               op0=ALU.add, op1=ALU.mult), reads=[angR], writes=[tqR])
            P.op("dve", lambda: nc.vector.tensor_copy(out=ki[rp, :], in_=tq[rp, :]), reads=[tqR], writes=[kiR])
            P.op("dve", lambda: nc.vector.tensor_copy(out=tq[rp, :], in_=ki[rp, :]), reads=[kiR], writes=[tqR])
            P.op("dve", lambda: nc.vector.scalar_tensor_tensor(out=tq[rp, :], in0=tq[rp, :], scalar=-TWO_PI, in1=ang[rp, :],
                                                              op0=ALU.mult, op1=ALU.add), reads=[tqR, angR], writes=[tqR])
            P.op("dve", lambda off=off: nc.vector.tensor_scalar(out=tq[rp, :], in0=tq[rp, :], scalar1=off, scalar2=PI,
                                                               op0=ALU.add, op1=ALU.min), reads=[tqR], writes=[tqR])
            P.op("dve", lambda: nc.vector.tensor_scalar(out=tq[rp, :], in0=tq[rp, :], scalar1=-PI, scalar2=None, op0=ALU.max),
                 reads=[tqR], writes=[tqR])
            if which == 0:
                P.op("act", lambda ts_=ts_: nc.scalar.activation(out=sinT[rp, ts_], in_=tq[rp, :], func=AF.Sin, scale=C.col[rp, 1:2]),
                     reads=[tqR, C.colR], writes=[csR[t]], part=True)
            else:
                P.op("act", lambda ts_=ts_: nc.scalar.activation(out=cosT[rp, ts_], in_=tq[rp, :], func=AF.Sin),
                     reads=[tqR], writes=[csR[t]], part=True)

    def proj_mms(t, wt, nblk, bk0):
        b = t % 2
        for kb in range(nblk):
            bk = bk0 + kb
            for k in range(8):
                P.op("pe", lambda k=k, kb=kb, bk=bk, wt=wt, b=b: nc.tensor.matmul(
                    C.bank[bk][:, 0:W], wt[:, k, kb * 128:(kb + 1) * 128], xT[b][:, k, :], start=(k == 0), stop=(k == 7)),
                    reads=[winR, xTR[b]], writes=[C.bR[bk]])

    def rms_post(t, nblk, dstL, gcol, inv_n, bk0):
        ts_ = slice(t * W, (t + 1) * W)
        SSB = 5
        for kb in range(nblk):
            bk = bk0 + kb
            si = kb % 2
            P.op("act", lambda bk=bk, si=si: nc.scalar.activation(out=sq[si][:], in_=C.bank[bk][:, 0:W], func=AF.Square),
                 reads=[C.bR[bk]], writes=[sqR[si]])
            P.op("pe", lambda si=si, kb=kb, nblk=nblk: nc.tensor.matmul(C.bank[SSB][:, 0:W], ones[:], sq[si][:],
                                                                       start=(kb == 0), stop=(kb == nblk - 1)),
                 reads=[onesR, sqR[si]], writes=[C.bR[SSB]])
        P.op("act", lambda inv_n=inv_n: nc.scalar.activation(out=rstd[:], in_=C.bank[SSB][:, 0:W], func=AF.Sqrt,
                                                            bias=C.col[:, 3:4], scale=inv_n),
             reads=[C.bR[SSB], C.colR], writes=[rstdR])
        P.op("dve", lambda: nc.vector.reciprocal(out=rstd[:], in_=rstd[:]), reads=[rstdR], writes=[rstdR])
        for kb in range(nblk):
            bk = bk0 + kb
            P.op("dve", lambda kb=kb, bk=bk, dstL=dstL, gcol=gcol, ts_=ts_: nc.vector.scalar_tensor_tensor(
                out=dstL[:, kb, ts_], in0=C.bank[bk][:, 0:W], scalar=gcol[:, kb:kb + 1], in1=rstd[:], op0=ALU.mult, op1=ALU.mult),
                reads=[C.bR[bk], gR, rstdR], writes=[latR[t]], part=True)

    rope_tables(0)
    load_transpose_tile(C, src, 0, T, xin[0], xinR[0], xT[0], xTR[0], (0, 1))
    for t in range(NT):
        b = t % 2
        ts_ = slice(t * W, (t + 1) * W)
        for (wt, bk) in ((wkr, 0), (wkrs, 1)):
            for k in range(8):
                P.op("pe", lambda k=k, wt=wt, bk=bk, b=b: nc.tensor.matmul(C.bank[bk][0:96, 0:W], wt[:, k, :], xT[b][:, k, :],
                                                                          start=(k == 0), stop=(k == 7)),
                     reads=[winR, xTR[b]], writes=[C.bR[bk]])
        proj_mms(t, wq_in, 3, 2)
        P.op("dve", lambda ts_=ts_: nc.vector.tensor_tensor(out=t1[rp, :], in0=C.bank[0][rp, 0:W], in1=cosT[rp, ts_], op=ALU.mult),
             reads=[C.bR[0], csR[t]], writes=[t1R])
        P.op("dve", lambda ts_=ts_: nc.vector.tensor_tensor(out=t2[rp, :], in0=C.bank[1][rp, 0:W], in1=sinT[rp, ts_], op=ALU.mult),
             reads=[C.bR[1], csR[t]], writes=[t2R])
        for i in range(2):
            P.op("pool", lambda i=i, ts_=ts_: nc.gpsimd.tensor_tensor(out=kT[i][rp, ts_], in0=t1[rp, :], in1=t2[rp, :], op=ALU.add),
                 reads=[t1R, t2R], writes=[kTropeR[t]], part=True)
        proj_mms(t, wkv_in, 2, 6)
        rms_post(t, 3, cqT, gq, 1.0 / 384, 2)
        if t + 1 < NT:
            load_transpose_tile(C, src, (t + 1) * T, T, xin[1 - b], xinR[1 - b], xT[1 - b], xTR[1 - b], (0, 1))
        rms_post(t, 2, ckvT, gkv, 1.0 / 256, 6)
        if t + 1 < NT:
            rope_tables(t + 1)

    P.barrier(C.marks)
    pes.__exit__(None, None, None)
    pes = ExitStack()
    pes.__enter__()
    qT = [sb("a_qT%d" % i, [96, S], BF16) for i in range(2)]
    qTR = [Reg("a_qT%d" % i) for i in range(2)]
    Vg = sb("a_V", [128, NCH, 4, 65], BF16)
    VgR = Reg("a_V")
    VoneR = Reg("a_Vone")
    P.op("pool", lambda: nc.gpsimd.memset(Vg[:, :, :, 64:65], 1.0), writes=[VoneR])
    pT = [sb("a_pT%d" % i, [128, 1024], BF16) for i in range(3)]
    pTR = [Reg("a_pT%d" % i) for i in range(3)]
    oa = [sb("a_oa%d" % i, [128, W]) for i in range(2)]
    oaR = [Reg("a_oa%d" % i) for i in range(2)]
    oTh = [sb("a_oTh%d" % i, [64, S], BF16) for i in range(2)]
    oThR = [Reg("a_oTh%d" % i) for i in range(2)]
    allLat = latR + csR + kTropeR
    OB = (0, 1)
    SB3 = ((2, 3), (4, 5), (6, 7))
    npt = 0
    noa = 0
    def gen_v(hg):
        for c2 in range(NCH // 2):
            bk = 2 + (c2 % 2)
            for cc in range(2):
                c = 2 * c2 + cc
                for k in range(2):
                    P.op("pe", lambda k=k, c=c, cc=cc, bk=bk, hg=hg: nc.tensor.matmul(
                        C.bank[bk][:, cc * 256:(cc + 1) * 256], ckvT[:, k, c * 128:(c + 1) * 128],
                        wv[:, k, hg * 4:(hg + 1) * 4, :].rearrange("p h d -> p (h d)"), start=(k == 0), stop=(k == 1)),
                        reads=[wkvR] + latR, writes=[C.bR[bk]])
            P.op("dve", lambda c2=c2, bk=bk: nc.vector.tensor_copy(
                out=Vg[:, 2 * c2:2 * c2 + 2, :, 0:64], in_=C.bank[bk][:, 0:512].rearrange("p (c h d) -> p c h d", c=2, h=4)),
                reads=[C.bR[bk]], writes=[VgR], part=True)

    def gen_qk(h, t, pair=(4, 5)):
        b = h % 2
        ts_ = slice(t * W, (t + 1) * W)
        BQ, BS_, BK = pair[0], pair[1], pair[0]
        for (wt, bk) in ((None, BQ), (wqus, BS_)):
            for k in range(3):
                lhs = wqu[:, k, h * 96:(h + 1) * 96] if wt is None else wqus[:, k, h, :]
                P.op("pe", lambda k=k, lhs=lhs, bk=bk, ts_=ts_: nc.tensor.matmul(C.bank[bk][0:96, 0:W], lhs, cqT[:, k, ts_],
                                                                                 start=(k == 0), stop=(k == 2)),
                     reads=[wquR] + latR, writes=[C.bR[bk]])
        P.op("dve", lambda ts_=ts_, b=b: nc.vector.tensor_copy(out=qT[b][0:64, ts_], in_=C.bank[BQ][0:64, 0:W]),
             reads=[C.bR[BQ]], writes=[qTR[b]], part=True)
        P.op("dve", lambda ts_=ts_: nc.vector.tensor_tensor(out=t1[rp, :], in0=C.bank[BQ][rp, 0:W], in1=cosT[rp, ts_], op=ALU.mult),
             reads=[C.bR[BQ]] + csR, writes=[t1R])
        P.op("dve", lambda ts_=ts_: nc.vector.tensor_tensor(out=t2[rp, :], in0=C.bank[BS_][rp, 0:W], in1=sinT[rp, ts_], op=ALU.mult),
             reads=[C.bR[BS_]] + csR, writes=[t2R])
        P.op("pool", lambda ts_=ts_, b=b: nc.gpsimd.tensor_tensor(out=qT[b][rp, ts_], in0=t1[rp, :], in1=t2[rp, :], op=ALU.add),
             reads=[t1R, t2R], writes=[qTR[b]], part=True)
        for k in range(2):
            P.op("pe", lambda k=k, ts_=ts_: nc.tensor.matmul(C.bank[BK][0:64, 0:W], wkn[:, k, h, :], ckvT[:, k, ts_],
                                                             start=(k == 0), stop=(k == 1)),
                 reads=[wkvR] + latR, writes=[C.bR[BK]])
        P.op("dve", lambda ts_=ts_, b=b: nc.vector.tensor_copy(out=kT[b][0:64, ts_], in_=C.bank[BK][0:64, 0:W]),
             reads=[C.bR[BK]], writes=[kTR[b]], part=True)

    def emit_S(it):
        sb2 = it["sb2"]
        if it["kind"] in ("g1", "g2"):
            h2 = it["h2"]
            ts_ = slice(it["t"] * W, (it["t"] + 1) * W)
            if it["kind"] == "g1":
                for k in range(3):
                    P.op("pe", lambda k=k, ts_=ts_, h2=h2, sb2=sb2: nc.tensor.matmul(
                        C.bank[sb2[0]][0:96, 0:W], wqu[:, k, h2 * 96:(h2 + 1) * 96], cqT[:, k, ts_], start=(k == 0), stop=(k == 2)),
                        reads=[wquR] + latR, writes=[C.bR[sb2[0]]])
                for k in range(2):
                    P.op("pe", lambda k=k, ts_=ts_, h2=h2, sb2=sb2: nc.tensor.matmul(
                        C.bank[sb2[1]][0:64, 0:W], wkn[:, k, h2, :], ckvT[:, k, ts_], start=(k == 0), stop=(k == 1)),
                        reads=[wkvR] + latR, writes=[C.bR[sb2[1]]])
            else:
                for k in range(3):
                    P.op("pe", lambda k=k, ts_=ts_, h2=h2, sb2=sb2: nc.tensor.matmul(
                        C.bank[sb2[0]][0:96, 0:W], wqus[:, k, h2, :], cqT[:, k, ts_], start=(k == 0), stop=(k == 2)),
                        reads=[wquR] + latR, writes=[C.bR[sb2[0]]])
            return
        b, q0 = it["b"], it["q0"]
        if it["kind"] == "off":
            for u in range(2):
                kt = it["kt0"] + u
                P.op("pe", lambda kt=kt, u=u, sb2=sb2, b=b, q0=q0: nc.tensor.matmul(
                    C.bank[sb2[u]][:, 0:W], kT[b][:, kt * 128:(kt + 1) * 128], qT[b][:, q0:q0 + W], start=True, stop=True),
                    reads=[kTR[b], qTR[b]] + kTropeR, writes=[C.bR[sb2[u]]])
        else:
            kt, n0 = it["kt"], it["n0"]
            P.op("pe", lambda kt=kt, sb2=sb2, b=b, q0=q0, n0=n0: nc.tensor.matmul(
                C.bank[sb2[0]][:, n0:W], kT[b][:, kt * 128:(kt + 1) * 128], qT[b][:, q0 + n0:q0 + W], start=True, stop=False),
                reads=[kTR[b], qTR[b]] + kTropeR, writes=[C.bR[sb2[0]]])
            P.op("pe", lambda sb2=sb2, n0=n0: nc.tensor.matmul(
                C.bank[sb2[0]][:, n0:n0 + 128], identb[:], negm[:], start=False, stop=True),
                reads=[mskR], writes=[C.bR[sb2[0]]])

    def emit_EP(it):
        sb2 = it["sb2"]
        if it["kind"] in ("g1", "g2"):
            b2 = it["h2"] % 2
            ts_ = slice(it["t"] * W, (it["t"] + 1) * W)
            if it["kind"] == "g1":
                P.op("dve", lambda ts_=ts_, b2=b2, sb2=sb2: nc.vector.tensor_copy(out=qT[b2][0:64, ts_], in_=C.bank[sb2[0]][0:64, 0:W]),
                     reads=[C.bR[sb2[0]]], writes=[qTR[b2]], part=True)
                P.op("dve", lambda ts_=ts_, sb2=sb2: nc.vector.tensor_tensor(out=t1[rp, :], in0=C.bank[sb2[0]][rp, 0:W], in1=cosT[rp, ts_], op=ALU.mult),
                     reads=[C.bR[sb2[0]]] + csR, writes=[t1R])
                P.op("dve", lambda ts_=ts_, b2=b2, sb2=sb2: nc.vector.tensor_copy(out=kT[b2][0:64, ts_], in_=C.bank[sb2[1]][0:64, 0:W]),
                     reads=[C.bR[sb2[1]]], writes=[kTR[b2]], part=True)
            else:
                P.op("dve", lambda ts_=ts_, sb2=sb2: nc.vector.tensor_tensor(out=t2[rp, :], in0=C.bank[sb2[0]][rp, 0:W], in1=sinT[rp, ts_], op=ALU.mult),
                     reads=[C.bR[sb2[0]]] + csR, writes=[t2R])
                P.op("pool", lambda ts_=ts_, b2=b2: nc.gpsimd.tensor_tensor(out=qT[b2][rp, ts_], in0=t1[rp, :], in1=t2[rp, :], op=ALU.add),
                     reads=[t1R, t2R], writes=[qTR[b2]], part=True)
            return
        pi, ob, hh = it["pi"], it["ob"], it["hh"]
        if it["kind"] == "off":
            P.op("act", lambda sb2=sb2, pi=pi: nc.scalar.activation(out=pT[pi][:, 0:1024], in_=C.ps[:, sb2[0] * 512:sb2[0] * 512 + 1024],
                                                                   func=AF.Exp, scale=SCALE),
                 reads=[C.bR[sb2[0]], C.bR[sb2[1]]], writes=[pTR[pi]])
            for u in range(2):
                kt = it["kt0"] + u
                st = it["first"] and u == 0
                P.op("pe", lambda kt=kt, u=u, pi=pi, ob=ob, hh=hh, st=st: nc.tensor.matmul(
                    C.bank[ob][0:65, 0:W], Vg[:, kt, hh, :], pT[pi][:, u * W:(u + 1) * W], start=st, stop=False),
                    reads=[VgR, VoneR, pTR[pi]], writes=[C.bR[ob]])
        else:
            kt, n0 = it["kt"], it["n0"]
            P.op("act", lambda sb2=sb2, pi=pi, n0=n0: nc.scalar.activation(out=pT[pi][:, n0:W], in_=C.bank[sb2[0]][:, n0:W],
                                                                          func=AF.Exp, scale=SCALE),
                 reads=[C.bR[sb2[0]]], writes=[pTR[pi]])
            P.op("pe", lambda kt=kt, pi=pi, ob=ob, hh=hh, n0=n0, st=it["first"], sp_=it["last"]: nc.tensor.matmul(
                C.bank[ob][0:65, n0:W], Vg[:, kt, hh, :], pT[pi][:, n0:W], start=st, stop=sp_),
                reads=[VgR, VoneR, pTR[pi]], writes=[C.bR[ob]])

    def emit_norm_pre(h, j, ob):
        nonlocal noa
        oi = noa % 2
        noa += 1
        P.op("dve", lambda oi=oi, ob=ob: nc.vector.tensor_copy(out=oa[oi][0:65, :], in_=C.bank[ob][0:65, 0:W]),
             reads=[C.bR[ob]], writes=[oaR[oi]])
        return oi

    def emit_norm_recip(oi, q):
        qs = slice(q * 128, (q + 1) * 128)
        P.op("dve", lambda oi=oi, qs=qs: nc.vector.reciprocal(out=oa[oi][64:65, qs], in_=oa[oi][64:65, qs]),
             reads=[oaR[oi]], writes=[oaR[oi]])

    def emit_norm_post(h, j, ob, oi):
        ob_i = h % 2
        q0 = j * W
        P.op("pe", lambda oi=oi, ob=ob: nc.tensor.matmul(C.bank[ob][0:64, 0:W], sel[0:65, :], oa[oi][0:65, :], start=True, stop=True),
             reads=[selR, oaR[oi]], writes=[C.bR[ob]])
        P.op("dve", lambda oi=oi, ob=ob, ob_i=ob_i, q0=q0: nc.vector.tensor_tensor(
            out=oTh[ob_i][:, q0:q0 + W], in0=oa[oi][0:64, :], in1=C.bank[ob][0:64, 0:W], op=ALU.mult),
            reads=[oaR[oi], C.bR[ob]], writes=[oThR[ob_i]], part=True)
        if j == 7:
            P.dma("sp", lambda h=h, ob_i=ob_i: nc.sync.dma_start(out=C.dr["oTd"][h * 64:(h + 1) * 64, :], in_=oTh[ob_i][:]),
                  reads=[oThR[ob_i]])

    LA = 2
    gen_v(0)
    for t in range(NT):
        gen_qk(0, t)
    for h in range(_dbg_heads()):
        hg, hh = divmod(h, 4)
        b = h % 2
        items = []
        for j in range(8):
            ob = OB[j % 2]
            first = True
            for kt0 in range(0, 4 * j, 2):
                items.append(dict(kind="off", kt0=kt0, b=b, q0=j * W, ob=ob, hh=hh, first=first, last=False, j=j,
                                  sb2=SB3[npt % 3], pi=npt % 3))
                npt += 1
                first = False
            for r in range(4):
                items.append(dict(kind="diag", kt=4 * j + r, n0=128 * r, b=b, q0=j * W, ob=ob, hh=hh, first=first,
                                  last=(r == 3), j=j, sb2=SB3[npt % 3], pi=npt % 3))
                npt += 1
                first = False
            if h + 1 < NH and (h + 1) % 4 != 0:
                for kind in ("g1", "g2"):
                    items.append(dict(kind=kind, h2=h + 1, t=j, sb2=SB3[npt % 3], pi=npt % 3))
                    npt += 1
        n = len(items)
        pend = []
        for i in range(n + LA):
            if i < n:
                emit_S(items[i])
            if i - LA >= 0:
                it = items[i - LA]
                emit_EP(it)
                npend = []
                for (stg, args) in pend:
                    if stg < 0:
                        npend.append((stg + 1, args))
                    elif stg == 0:
                        args[3] = emit_norm_pre(args[0], args[1], args[2])
                        npend.append((1, args))
                    elif stg <= 4:
                        emit_norm_recip(args[3], stg - 1)
                        npend.append((stg + 1, args))
                    else:
                        emit_norm_post(*args)
                pend = npend
                if it.get("last"):
                    has_gen = (h + 1 < NH and (h + 1) % 4 != 0)
                    if has_gen:
                        pend.append((-1, [h, it["j"], it["ob"], None]))
                    else:
                        oi = emit_norm_pre(h, it["j"], it["ob"])
                        pend.append((1, [h, it["j"], it["ob"], oi]))
        for (stg, args) in pend:
            if stg <= 0:
                args[3] = emit_norm_pre(args[0], args[1], args[2])
                stg = 1
            for q in range(stg - 1, 4):
                emit_norm_recip(args[3], q)
            emit_norm_post(*args)
        if h + 1 < NH and (h + 1) % 4 == 0:
            gen_v((h + 1) // 4)
            for t in range(NT):
                gen_qk(h + 1, t)
        P.flush()
    P.barrier(C.marks)
    pes.__exit__(None, None, None)


def mla_outproj(C, pes, src, dst):
    nc, P, dr = C.nc, C.P, C.dr
    sb = lambda name, shp, dt=F32: pes.enter_context(nc.sbuf_tensor(name, shp, dt))
    T, W, NT = 4, 512, 8
    wout = sb("o_wout", [128, 8, D], BF16)
    woutR = [Reg("o_wout%d" % k) for k in range(8)]
    for k in range(8):
        load_w_bf16(C, wout[:, k, :], dr["odd_w_out"][k * 128:(k + 1) * 128, :], woutR[k])
    L = ln_setup(C, pes, dr["mix_ln_g"][1:2, :], dr["mix_ln_b"][1:2, :], "o")
    xin = [sb("o_xin%d" % i, [128, T, D]) for i in range(2)]
    xinR = [[Reg("oxin%d_%d" % (i, c)) for c in range(T)] for i in range(2)]
    oTt = [sb("o_oT%d" % i, [128, 8, W], BF16) for i in range(2)]
    oTtR = [Reg("o_oT%d" % i) for i in range(2)]
    o3 = dr["oTd"].rearrange("(k p) t -> p k t", p=128)
    def loads(t):
        b = t % 2
        rows = src[t * W:(t + 1) * W, :].rearrange("(c p) d -> p c d", p=128)
        P.dma("sp", lambda rows=rows, b=b: nc.sync.dma_start(out=oTt[b][:], in_=o3[:, :, t * W:(t + 1) * W]), writes=[oTtR[b]])
        P.dma("sp", lambda rows=rows, b=b: nc.sync.dma_start(out=xin[b][:], in_=rows), writes=xinR[b])

    loads(0)
    for t in range(NT):
        b = t % 2
        if t + 1 < NT:
            loads(t + 1)
        for c in range(T):
            cs = slice(c * 128, (c + 1) * 128)
            pair = (0, 1) if c % 2 == 0 else (2, 3)
            for half in range(2):
                bk = pair[half]
                for k in range(8):
                    P.op("pe", lambda k=k, half=half, bk=bk, cs=cs, b=b: nc.tensor.matmul(
                        C.bank[bk][:, 0:512], oTt[b][:, k, cs], wout[:, k, half * 512:(half + 1) * 512],
                        start=(k == 0), stop=(k == 7)), reads=[oTtR[b], woutR[k]], writes=[C.bR[bk]])
            gc = t * T + c
            ln_epilogue(C, L, pair[0], pair[1], xin[b][:, c, :], xinR[b][c], dst[gc * 128:(gc + 1) * 128, :])


_NC_CACHE = {}


def _get_nc(phases, standalone):
    key = (tuple(phases), standalone)
    if key not in _NC_CACHE:
        _NC_CACHE[key] = build(list(phases), standalone)
    return _NC_CACHE[key]


def _weight_maps(inputs):
    m = {}
    for name, shp in W_SPECS.items():
        m[name] = np.ascontiguousarray(np.asarray(inputs[name], dtype=np.float32).reshape(shp))
    m.update(host_consts())
    return m


FUSED = True


def kernel(**inputs):
    x = np.asarray(inputs["x"], dtype=np.float32)
    pos = np.asarray(inputs["positions"], dtype=np.int32)
    wm = _weight_maps(inputs)
    n = 8
    if FUSED:
        nc = _get_nc((1, 2, 3, 4), False)
        in_maps = []
        for b in range(n):
            d = dict(wm)
            d["x"] = np.ascontiguousarray(x[b])
            d["positions"] = np.ascontiguousarray(pos[b:b + 1])
            in_maps.append(d)
        res = run_bass_kernel_spmd(nc, in_maps, core_ids=list(range(n)))
        return np.stack([res.results[b]["out"] for b in range(n)], axis=0)
    cur = [np.ascontiguousarray(x[b]) for b in range(n)]
    for ph in (1, 2, 3, 4):
        nc = _get_nc((ph,), True)
        in_maps = []
        for b in range(n):
            d = dict(wm)
            d["src"] = cur[b]
            d["positions"] = np.ascontiguousarray(pos[b:b + 1])
            in_maps.append(d)
        res = run_bass_kernel_spmd(nc, in_maps, core_ids=list(range(n)))
        cur = [np.ascontiguousarray(res.results[b]["dst"]) for b in range(n)]
    return np.stack(cur, axis=0)
```

```python
import numpy as np
from contextlib import ExitStack
import concourse.bass as bass
import concourse.mybir as mybir
from concourse.bass_utils import run_bass_kernel_spmd

F32 = mybir.dt.float32
BF16 = mybir.dt.bfloat16
I32 = mybir.dt.int32
AF = mybir.ActivationFunctionType
ALU = mybir.AluOpType

S = 4096
D = 1024
NCH = S // 128
DFF = 2816
NFF = DFF // 128
ALPHA = float((2 * 2) ** 0.25)
LN_EPS = 1e-5
RMS_EPS = 1e-6
TWO_PI = float(2 * np.pi)
PI = float(np.pi)
NH = 16
SCALE = float(96 ** -0.5)


class Reg:
    __slots__ = ("name", "excl", "writers", "readers")

    def __init__(self, name, excl=False):
        self.name = name
        self.excl = excl
        self.writers = []
        self.readers = []


class Ins:
    __slots__ = ("eng", "fn", "dma", "deps", "signal", "count", "dsem", "dval", "seq", "emitted", "xw")

    def __init__(self, eng, fn, dma):
        self.eng = eng
        self.fn = fn
        self.dma = dma
        self.deps = []
        self.signal = False
        self.count = None
        self.dsem = None
        self.dval = None
        self.seq = None
        self.emitted = False
        self.xw = []


class Prog:
    def __init__(self, nc, es):
        self.nc = nc
        self.E = {"pe": nc.tensor, "act": nc.scalar, "dve": nc.vector, "pool": nc.gpsimd, "sp": nc.sync}
        self.sem = {e: es.enter_context(nc.semaphore("c_" + e)) for e in ("pe", "act", "dve", "pool")}
        self.cnt = {e: 0 for e in self.sem}
        self.dsems = {}
        for q, n in (("sp", 12), ("pool", 8), ("act", 4)):
            self.dsems[q] = [es.enter_context(nc.semaphore("d_%s%d" % (q, i))) for i in range(n)]
        self.dnext = {q: 0 for q in self.dsems}
        self.dval = {}
        self.dlast = {}
        self.pending = []
        self.seq = {e: 0 for e in self.E}
        self.sig_hist = {e: [] for e in self.sem}
        self.waited = {e: {x: 0 for x in self.sem} for e in self.E}
        self.dobs = {e: {} for e in self.E}
        self.extra = {e: [] for e in self.E}
        self.last = {e: None for e in self.E}

    def _add(self, eng, fn, reads, writes, dma, part):
        I = Ins(eng, fn, dma)
        I.seq = self.seq[eng]
        self.seq[eng] += 1
        deps = []
        for r in reads:
            if r.excl:
                deps += [(d, "raw") for d in r.writers] + [(d, "war") for d in r.readers]
            else:
                deps += [(d, "raw") for d in r.writers]
        for w in writes:
            if part and not w.excl:
                deps += [(d, "war") for d in w.readers]
            else:
                deps += [(d, "war") for d in w.readers] + [(d, "waw") for d in w.writers]
        for d in self.extra[eng]:
            deps.append((d, "raw"))
        self.extra[eng] = []
        if dma:
            q = eng
            sems = self.dsems[q]
            sem = sems[self.dnext[q] % len(sems)]
            self.dnext[q] += 1
            prev = self.dlast.get(id(sem))
            if prev is not None:
                deps.append((prev, "raw"))
            I.dsem = sem
            I.dval = self.dval.get(id(sem), 0) + 16
            self.dval[id(sem)] = I.dval
            self.dlast[id(sem)] = I
        seen = set()
        for d, kind in deps:
            if d is I or id(d) in seen:
                continue
            if d.dma or dma:
                pass
            elif d.eng == eng:
                if eng == "pe" or kind != "raw":
                    continue
            seen.add(id(d))
            I.deps.append(d)
            if not d.dma and not d.emitted:
                d.signal = True
        for r in reads:
            if not dma:
                r.readers = [x for x in r.readers if x.dma or x.eng != eng]
            r.readers.append(I)
        for w in writes:
            if part and not w.excl:
                if w.readers:
                    w.writers = [I]
                    w.readers = []
                else:
                    if not dma:
                        w.writers = [x for x in w.writers if x.dma or x.eng != eng]
                    w.writers.append(I)
            else:
                w.writers = [I]
                w.readers = []
        self.pending.append(I)
        self.last[eng] = I
        return I

    def op(self, eng, fn, reads=(), writes=(), part=False):
        return self._add(eng, fn, list(reads), list(writes), False, part)

    def dma(self, eng, fn, reads=(), writes=(), part=False):
        return self._add(eng, fn, list(reads), list(writes), True, part)

    def _count_of(self, d):
        if d.count is not None:
            return d.count
        for seq, c in self.sig_hist[d.eng]:
            if seq >= d.seq:
                return c
        raise RuntimeError("no signal after dep on %s" % d.eng)

    def flush(self):
        lastp = {}
        for I in self.pending:
            if not I.dma and I.eng in self.sem:
                lastp[I.eng] = I
        for I in lastp.values():
            I.signal = True
        for I in self.pending:
            e = I.eng
            eng = self.E[e]
            waits = []
            need_c = {}
            for d in I.deps:
                if d.dma:
                    k = id(d.dsem)
                    if self.dobs[e].get(k, 0) < d.dval:
                        self.dobs[e][k] = d.dval
                        waits.append((d.dsem, d.dval))
                else:
                    c = self._count_of(d)
                    if c > need_c.get(d.eng, 0):
                        need_c[d.eng] = c
            for x, c in need_c.items():
                if self.waited[e][x] < c:
                    self.waited[e][x] = c
                    waits.append((self.sem[x], c))
            for sem, val in I.xw:
                k = id(sem)
                if self.dobs[e].get(k, 0) < val:
                    self.dobs[e][k] = val
                    waits.append((sem, val))
            best = {}
            for sem, val in waits:
                k = id(sem)
                if k not in best or best[k][1] < val:
                    best[k] = (sem, val)
            waits = list(best.values())
            while len(waits) > 2:
                a = waits.pop()
                b = waits.pop()
                eng.wait_ge(a[0], a[1])
                eng.wait_ge(b[0], b[1])
                eng.nop()
            for sem, val in waits:
                eng.wait_ge(sem, val)
            bi = I.fn()
            if I.dma:
                bi.then_inc(I.dsem, 16)
            elif I.signal:
                self.cnt[e] += 1
                I.count = self.cnt[e]
                bi.then_inc(self.sem[e], 1)
                self.sig_hist[e].append((I.seq, I.count))
            I.emitted = True
            I.fn = None
        self.pending = []
        for e in self.sig_hist:
            if len(self.sig_hist[e]) > 4:
                self.sig_hist[e] = self.sig_hist[e][-4:]

    def all_dma_waits(self):
        out = []
        for q in self.dsems:
            for sem in self.dsems[q]:
                v = self.dval.get(id(sem), 0)
                if v:
                    out.append((sem, v))
        return out

    def barrier(self, marks):
        ms = []
        for e in ("act", "dve", "pool"):
            m = self.op(e, marks[e])
            m.xw = self.all_dma_waits()
            m.signal = True
            ms.append(m)
        for e in ("act", "dve", "pool", "sp"):
            self.extra[e] = list(ms)
        self.flush()

    def final_wait(self):
        eng = self.E["sp"]
        for sem, val in self.all_dma_waits():
            eng.wait_ge(sem, val)
            eng.nop()


CONST_SPECS = {
    "c_ident": ([128, 128], F32),
    "c_triu": ([128, 128], F32),
    "c_poolA": ([12, 128, 128], F32),
    "c_col": ([128, 4], F32),
    "c_sel": ([128, 64], F32),
}

W_SPECS = {
    "even_w_in": [1024, 1536], "even_vnorm_g": [1, 512], "even_vnorm_b": [1, 512],
    "even_spatial_w": [8, 128, 128], "even_spatial_b": [8, 128], "even_pool_w": [4, 128, 128],
    "even_pool_scale": [1, 512], "even_w_out": [1024, 1024],
    "odd_w_in": [1024, 672], "odd_q_norm_g": [1, 384], "odd_w_q_up": [384, 1536],
    "odd_kv_norm_g": [1, 256], "odd_w_kv_up": [256, 2048], "odd_w_out": [1024, 1024],
    "mix_ln_g": [2, 1024], "mix_ln_b": [2, 1024], "ffn_w_gate_up": [2, 1024, 5632],
    "ffn_w_down": [2, 2816, 1024], "ffn_ln_g": [2, 1024], "ffn_ln_b": [2, 1024],
}


def host_consts():
    ident = np.eye(128, dtype=np.float32)
    triu = np.triu(np.ones((128, 128), dtype=np.float32))
    A = np.zeros((12, 128, 128), dtype=np.float32)
    s = np.arange(128)[:, None]
    t = np.arange(128)[None, :]
    for g, w in enumerate((2, 4, 8, 16)):
        A[g] = ((s <= t) & (s > t - w)) / np.float32(w) - (s == t)
        A[4 + g] = (s >= 128 + t - w + 1) / np.float32(w)
        cnt = np.minimum(t + 1, w).astype(np.float32)
        A[8 + g] = ((s <= t) & (s > t - w)) / cnt - (s == t)
    col = np.zeros((128, 4), dtype=np.float32)
    freqs = (10000.0 ** (-np.arange(0, 32, 2, dtype=np.float32) / 32)).astype(np.float32)
    col[64:80, 0] = freqs
    col[80:96, 0] = freqs
    col[:, 1] = 1.0
    col[64:80, 1] = -1.0
    col[:, 2] = LN_EPS
    col[:, 3] = RMS_EPS
    sel = np.zeros((128, 64), dtype=np.float32)
    sel[64, :] = 1.0
    return {"c_ident": ident, "c_triu": triu, "c_poolA": A.astype(np.float32), "c_col": col, "c_sel": sel}


class Ctx:
    pass


def _dbg_heads():
    import os
    return int(os.environ.get("MK_DBG_HEADS", NH))


def build(phases, standalone):
    nc = bass.Bass("TRN2", target_bir_lowering=False)
    dr = {}
    for name, shp in W_SPECS.items():
        dr[name] = nc.dram_tensor(name, shp, F32, kind="ExternalInput").ap()
    for name, (shp, dt) in CONST_SPECS.items():
        dr[name] = nc.dram_tensor(name, shp, dt, kind="ExternalInput").ap()
    dr["positions"] = nc.dram_tensor("positions", [1, S], I32, kind="ExternalInput").ap()
    if standalone:
        src = nc.dram_tensor("src", [S, D], F32, kind="ExternalInput").ap()
        dst = nc.dram_tensor("dst", [S, D], F32, kind="ExternalOutput").ap()
        chain = {phases[0]: (src, dst)}
    else:
        x = nc.dram_tensor("x", [S, D], F32, kind="ExternalInput").ap()
        out = nc.dram_tensor("out", [S, D], F32, kind="ExternalOutput").ap()
        s1 = nc.dram_tensor("scr1", [S, D], F32).ap()
        s2 = nc.dram_tensor("scr2", [S, D], F32).ap()
        s3 = nc.dram_tensor("scr3", [S, D], F32).ap()
        chain = {1: (x, s1), 2: (s1, s2), 3: (s2, s3), 4: (s3, out)}
    dr["oTd"] = nc.dram_tensor("scr_oT", [D, S], BF16).ap()

    with ExitStack() as es:
        P = Prog(nc, es)
        C = Ctx()
        C.nc, C.P, C.dr = nc, P, dr
        C.ps = es.enter_context(nc.psum_tensor("ps", [128, 4096], F32))
        C.bank = [C.ps[:, 512 * i:512 * (i + 1)] for i in range(8)]
        C.bR = [Reg("bank%d" % i, excl=True) for i in range(8)]
        C.ident = es.enter_context(nc.sbuf_tensor("ident", [128, 128], F32))
        C.identR = Reg("ident")
        C.col = es.enter_context(nc.sbuf_tensor("colc", [128, 4], F32))
        C.colR = Reg("col")
        C.mk = es.enter_context(nc.sbuf_tensor("marks", [128, 8], F32))
        P.dma("sp", lambda: nc.sync.dma_start(out=C.ident[:], in_=dr["c_ident"][:, :]), writes=[C.identR])
        P.dma("sp", lambda: nc.sync.dma_start(out=C.col[:], in_=dr["c_col"][:, :]), writes=[C.colR])
        C.marks = {
            "act": lambda: nc.scalar.activation(out=C.mk[:, 0:1], in_=C.mk[:, 1:2], func=AF.Copy),
            "dve": lambda: nc.vector.memset(C.mk[:, 2:3], 0.0),
            "pool": lambda: nc.gpsimd.memset(C.mk[:, 4:5], 0.0),
        }
        P.op("dve", lambda: nc.vector.memset(C.mk[:], 0.0))
        for ph in phases:
            srcap, dstap = chain[ph]
            with ExitStack() as pes:
                if ph == 1:
                    phase_mixer0(C, pes, srcap, dstap)
                elif ph == 2:
                    phase_ffn(C, pes, 0, srcap, dstap)
                elif ph == 3:
                    phase_mla(C, pes, srcap, dstap)
                elif ph == 4:
                    phase_ffn(C, pes, 1, srcap, dstap)
                P.barrier(C.marks)
        P.flush()
        P.final_wait()
    return nc


def bcast_row(C, pes, name, row_ap, n):
    nc, P = C.nc, C.P
    t = pes.enter_context(nc.sbuf_tensor(name, [128, n], F32))
    R = Reg(name)
    P.dma("sp", lambda: nc.sync.dma_start(out=t[:], in_=row_ap.partition_broadcast(128)), writes=[R])
    return t, R


class LNState:
    pass


def ln_setup(C, pes, g_row, b_row, tag):
    nc = C.nc
    L = LNState()
    L.g, L.gR = bcast_row(C, pes, "lng_" + tag, g_row, D)
    L.b, L.bR = bcast_row(C, pes, "lnb_" + tag, b_row, D)
    L.s = [pes.enter_context(nc.sbuf_tensor("lns%d_%s" % (i, tag), [128, D], F32)) for i in range(2)]
    L.sR = [Reg("lns%d" % i) for i in range(2)]
    L.y = [pes.enter_context(nc.sbuf_tensor("lny%d_%s" % (i, tag), [128, D], F32)) for i in range(2)]
    L.yR = [Reg("lny%d" % i) for i in range(2)]
    L.st = [pes.enter_context(nc.sbuf_tensor("lnst%d_%s" % (i, tag), [128, 24], F32)) for i in range(2)]
    L.stR = [Reg("lnst%d" % i) for i in range(2)]
    L.n = 0
    return L


def ln_epilogue(C, L, bankA, bankB, x_ap, xR, dst_rows):
    nc, P = C.nc, C.P
    i = L.n % 2
    L.n += 1
    s, sR, y, yR, st, stR = L.s[i], L.sR[i], L.y[i], L.yR[i], L.st[i], L.stR[i]
    bA, bB = C.bank[bankA], C.bank[bankB]
    P.op("dve", lambda: nc.vector.scalar_tensor_tensor(out=s[:, 0:512], in0=x_ap[:, 0:512], scalar=ALPHA, in1=bA,
                                                      op0=ALU.mult, op1=ALU.add),
         reads=[xR, C.bR[bankA]], writes=[sR], part=True)
    P.op("dve", lambda: nc.vector.scalar_tensor_tensor(out=s[:, 512:1024], in0=x_ap[:, 512:1024], scalar=ALPHA, in1=bB,
                                                      op0=ALU.mult, op1=ALU.add),
         reads=[xR, C.bR[bankB]], writes=[sR], part=True)
    P.op("dve", lambda: nc.vector.bn_stats(out=st[:, 0:6], in_=s[:, 0:512]), reads=[sR], writes=[stR], part=True)
    P.op("dve", lambda: nc.vector.bn_stats(out=st[:, 6:12], in_=s[:, 512:1024]), reads=[sR], writes=[stR], part=True)
    R1, R2, R3 = Reg("mv"), Reg("sd"), Reg("rstd")
    P.op("dve", lambda: nc.vector.bn_aggr(out=st[:, 12:14], in_=st[:, 0:12]), reads=[stR], writes=[R1])
    P.op("act", lambda: nc.scalar.activation(out=st[:, 14:15], in_=st[:, 13:14], func=AF.Sqrt, bias=C.col[:, 2:3], scale=1.0),
         reads=[R1, C.colR], writes=[R2])
    P.op("dve", lambda: nc.vector.scalar_tensor_tensor(out=s[:], in0=s[:], scalar=st[:, 12:13], in1=L.g[:],
                                                      op0=ALU.subtract, op1=ALU.mult), reads=[sR, R1, L.gR], writes=[sR])
    P.op("dve", lambda: nc.vector.reciprocal(out=st[:, 15:16], in_=st[:, 14:15]), reads=[R2], writes=[R3])
    P.op("dve", lambda: nc.vector.scalar_tensor_tensor(out=y[:], in0=s[:], scalar=st[:, 15:16], in1=L.b[:],
                                                      op0=ALU.mult, op1=ALU.add), reads=[sR, R3, L.bR], writes=[yR])
    P.dma("sp", lambda: nc.sync.dma_start(out=dst_rows, in_=y[:]), reads=[yR])


def load_transpose_tile(C, src, t0, nchunk, xin, xinR, xT, xTR, tbanks, evac_engs=("act", "dve")):
    nc, P = C.nc, C.P
    rows = src[t0 * 128:(t0 + nchunk) * 128, :].rearrange("(c p) d -> p c d", p=128)
    P.dma("sp", lambda: nc.sync.dma_start(out=xin[:, 0:nchunk, :], in_=rows), writes=xinR)
    W = nchunk * 128
    for k in range(8):
        b = tbanks[k % len(tbanks)]
        for c in range(nchunk):
            P.op("pe", lambda c=c, k=k, b=b: nc.tensor.transpose(C.bank[b][:, c * 128:(c + 1) * 128],
                                                                xin[:, c, k * 128:(k + 1) * 128], C.ident[:]),
                 reads=[xinR[c], C.identR], writes=[C.bR[b]])
        e = evac_engs[k % len(evac_engs)]
        if e == "act":
            P.op("act", lambda k=k, b=b: nc.scalar.copy(out=xT[:, k, 0:W], in_=C.bank[b][:, 0:W]),
                 reads=[C.bR[b]], writes=[xTR], part=True)
        else:
            P.op("dve", lambda k=k, b=b: nc.vector.tensor_copy(out=xT[:, k, 0:W], in_=C.bank[b][:, 0:W]),
                 reads=[C.bR[b]], writes=[xTR], part=True)


def load_w_bf16(C, tile_ap, dram_ap, R, eng="pool"):
    nc, P = C.nc, C.P
    P.dma("pool", lambda: nc.gpsimd.dma_start(out=tile_ap, in_=dram_ap), writes=[R], part=True)


def phase_ffn(C, pes, layer, src, dst):
    nc, P, dr = C.nc, C.P, C.dr
    T = 2
    sfx = "_L%d" % layer
    W = T * 128
    NT = NCH // T
    wgu = pes.enter_context(nc.sbuf_tensor("wgu" + sfx, [128, 8, 2 * DFF], BF16))
    wd = pes.enter_context(nc.sbuf_tensor("wd" + sfx, [128, NFF, D], BF16))
    HJ = NFF // 2
    wguR = [[Reg("wgu%d_%d" % (k, ch)) for ch in range(4)] for k in range(8)]
    wdR = [Reg("wd%d" % j) for j in range(NFF)]
    gu = dr["ffn_w_gate_up"][layer]
    chunks = [(0, 0, HJ * 128), (1, DFF, DFF + HJ * 128), (2, HJ * 128, DFF), (3, DFF + HJ * 128, 2 * DFF)]
    for (ch, c0, c1) in chunks:
        for k in range(8):
            load_w_bf16(C, wgu[:, k, c0:c1], gu[k * 128:(k + 1) * 128, c0:c1], wguR[k][ch])
    dn = dr["ffn_w_down"][layer]
    for j in range(NFF):
        load_w_bf16(C, wd[:, j, :], dn[j * 128:(j + 1) * 128, :], wdR[j])
    L = ln_setup(C, pes, dr["ffn_ln_g"][layer:layer + 1, :], dr["ffn_ln_b"][layer:layer + 1, :], "f%d" % layer)
    xin = [pes.enter_context(nc.sbuf_tensor("fxin%d" % i + sfx, [128, T, D], F32)) for i in range(2)]
    xinR = [[Reg("fxin%d_%d" % (i, c)) for c in range(T)] for i in range(2)]
    xT = [pes.enter_context(nc.sbuf_tensor("fxT%d" % i + sfx, [128, 8, W], BF16)) for i in range(2)]
    xTR = [Reg("fxT%d" % i) for i in range(2)]
    hT = pes.enter_context(nc.sbuf_tensor("hT" + sfx, [128, NFF, W], BF16))
    hTR = [Reg("hT%d" % j) for j in range(NFF)]
    sg = [pes.enter_context(nc.sbuf_tensor("sg%d" % i + sfx, [128, W], F32)) for i in range(2)]
    sgR = [Reg("sg%d" % i) for i in range(2)]

    def transposes(t):
        load_transpose_tile(C, src, t * T, T, xin[t % 2], xinR[t % 2], xT[t % 2], xTR[t % 2], (0, 1))

    transposes(0)
    nsg = 0
    for t in range(NT):
        b = t % 2
        for j in range(NFF):
            bk = 2 + (j % 2)
            for which in range(2):
                col = which * DFF + j * 128
                for k in range(8):
                    P.op("pe", lambda k=k, col=col, bk=bk, which=which, b=b: nc.tensor.matmul(
                        C.bank[bk][:, which * W:(which + 1) * W], wgu[:, k, col:col + 128], xT[b][:, k, :],
                        start=(k == 0), stop=(k == 7)),
                        reads=[wguR[k][which + (2 if j >= HJ else 0)], xTR[b]], writes=[C.bR[bk]])
            si = nsg % 2
            nsg += 1
            P.op("act", lambda bk=bk, si=si: nc.scalar.activation(out=sg[si][:], in_=C.bank[bk][:, 0:W], func=AF.Silu),
                 reads=[C.bR[bk]], writes=[sgR[si]])
            P.op("dve", lambda bk=bk, si=si, j=j: nc.vector.tensor_tensor(out=hT[:, j, :], in0=sg[si][:], in1=C.bank[bk][:, W:2 * W],
                                                                      op=ALU.mult),
                 reads=[sgR[si], C.bR[bk]], writes=[hTR[j]])
        if t + 1 < NT:
            transposes(t + 1)
        for c in range(T):
            for half in range(2):
                bk = 4 + 2 * c + half
                for j in range(NFF):
                    P.op("pe", lambda j=j, c=c, half=half, bk=bk: nc.tensor.matmul(
                        C.bank[bk][:, 0:512], hT[:, j, c * 128:(c + 1) * 128], wd[:, j, half * 512:(half + 1) * 512],
                        start=(j == 0), stop=(j == NFF - 1)),
                        reads=[hTR[j], wdR[j]], writes=[C.bR[bk]])
            r0 = (t * T + c) * 128
            ln_epilogue(C, L, 4 + 2 * c, 5 + 2 * c, xin[b][:, c, :], xinR[b][c], dst[r0:r0 + 128, :])


def phase_mixer0(C, pes, src, dst):
    nc, P, dr = C.nc, C.P, C.dr
    T = 4
    W = 512
    NT = NCH // T
    sb = lambda name, shp, dt=F32: pes.enter_context(nc.sbuf_tensor(name, shp, dt))
    win = sb("m_win", [128, 8, 1536], BF16)
    winR = [Reg("win%d" % k) for k in range(8)]
    for k in range(8):
        load_w_bf16(C, win[:, k, :], dr["even_w_in"][k * 128:(k + 1) * 128, :], winR[k])
    wout = sb("m_wout", [128, 8, D], BF16)
    woutR = [Reg("wout%d" % k) for k in range(8)]
    for k in range(8):
        load_w_bf16(C, wout[:, k, :], dr["even_w_out"][k * 128:(k + 1) * 128, :], woutR[k])
    poolw = sb("m_poolw", [128, 4, 128], BF16)
    poolwR = Reg("poolw")
    load_w_bf16(C, poolw[:], dr["even_pool_w"].rearrange("g c d -> c g d"), poolwR)
    A = sb("m_A", [128, 12, 128])
    AR = Reg("A")
    P.dma("sp", lambda: nc.sync.dma_start(out=A[:], in_=dr["c_poolA"].rearrange("n s t -> s n t")), writes=[AR])
    triu = sb("m_triu", [128, 128])
    triuR = Reg("triu")
    P.dma("sp", lambda: nc.sync.dma_start(out=triu[:], in_=dr["c_triu"][:, :]), writes=[triuR])
    wsn = sb("m_wsn", [128, 8, 128])
    wsnR = Reg("wsn")
    P.dma("sp", lambda: nc.sync.dma_start(out=wsn[:], in_=dr["even_spatial_w"].rearrange("h t s -> t h s")), writes=[wsnR])
    wsT = sb("m_wsT", [128, 8, 128], BF16)
    wsTR = Reg("wsT")
    for h in range(8):
        P.op("pe", lambda h=h: nc.tensor.transpose(C.bank[0][:, 0:128], wsn[:, h, :], C.ident[:]),
             reads=[wsnR, C.identR], writes=[C.bR[0]])
        P.op("dve", lambda h=h: nc.vector.tensor_tensor(out=wsT[:, h, :], in0=C.bank[0][:, 0:128], in1=triu[:], op=ALU.mult),
             reads=[C.bR[0], triuR], writes=[wsTR], part=True)
    bs = sb("m_bs", [128, 8])
    bsR = Reg("bs")
    pscale = sb("m_pscale", [128, 4])
    pscaleR = Reg("pscale")
    with nc.allow_non_contiguous_dma(reason="tiny per-head bias / scale columns"):
        P.dma("sp", lambda: nc.sync.dma_start(out=bs[:], in_=dr["even_spatial_b"].rearrange("h t -> t h")), writes=[bsR])
        P.dma("sp", lambda: nc.sync.dma_start(out=pscale[:], in_=dr["even_pool_scale"].rearrange("o (g d) -> d (o g)", g=4)),
              writes=[pscaleR])
        P.flush()
    vg, vgR = bcast_row(C, pes, "m_vg", dr["even_vnorm_g"], 512)
    vb, vbR = bcast_row(C, pes, "m_vb", dr["even_vnorm_b"], 512)
    L = ln_setup(C, pes, dr["mix_ln_g"][0:1, :], dr["mix_ln_b"][0:1, :], "m")
    xin = [sb("m_xin%d" % i, [128, T, D]) for i in range(3)]
    xinR = [[Reg("mxin%d_%d" % (i, c)) for c in range(T)] for i in range(3)]
    xT = [sb("m_xT%d" % i, [128, 8, W], BF16) for i in range(2)]
    xTR = [Reg("mxT%d" % i) for i in range(2)]
    mixT = [sb("m_mixT%d" % i, [128, 8, 128], BF16) for i in range(2)]
    mixTR = [Reg("mixT%d" % i) for i in range(2)]
    u_sb = [sb("m_u%d" % i, [128, 512]) for i in range(3)]
    uR = [Reg("u%d" % i) for i in range(3)]
    v_sb = [sb("m_v%d" % i, [128, 512]) for i in range(2)]
    vR = [Reg("v%d" % i) for i in range(2)]
    vbf = [sb("m_vbf%d" % i, [128, 512], BF16) for i in range(2)]
    vbfR = [Reg("vbf%d" % i) for i in range(2)]
    a_sb = [sb("m_a%d" % i, [128, 512]) for i in range(2)]
    aR = [Reg("a%d" % i) for i in range(2)]
    xp = [sb("m_xp%d" % i, [128, 512]) for i in range(4)]
    xpR = [Reg("xp%d" % i) for i in range(4)]
    pooledT = [sb("m_pooledT%d" % i, [128, 4, 128], BF16) for i in range(2)]
    pooledTR = [Reg("pooledT%d" % i) for i in range(2)]
    vst = [sb("m_vst%d" % i, [128, 16]) for i in range(2)]
    BXA, BPM, BU, BV, BSP, BPF, BOA, BOB = 0, 1, 2, 3, 4, 5, 6, 7

    def geo(gc):
        t, c = divmod(gc, T)
        return t, c, t % 2, slice(c * 128, (c + 1) * 128)

    def A_pe(gc):
        t, c, b, cs = geo(gc)
        for which, bk in ((0, BU), (1, BV)):
            for k in range(8):
                P.op("pe", lambda k=k, which=which, bk=bk, cs=cs, b=b: nc.tensor.matmul(
                    C.bank[bk][:, 0:512], xT[b][:, k, cs], win[:, k, which * 512:(which + 1) * 512],
                    start=(k == 0), stop=(k == 7)), reads=[xTR[b], winR[k]], writes=[C.bR[bk]])
        for k in range(8):
            P.op("pe", lambda k=k, cs=cs, b=b: nc.tensor.matmul(
                C.bank[BXA][:, 0:512], xT[b][:, k, cs], win[:, k, 1024:1536],
                start=(k == 0), stop=(k == 7)), reads=[xTR[b], winR[k]], writes=[C.bR[BXA]])
        i4 = gc % 4
        P.op("act", lambda i4=i4: nc.scalar.copy(out=xp[i4][:], in_=C.bank[BXA][:, 0:512]),
             reads=[C.bR[BXA]], writes=[xpR[i4]])

    def A_gelu(gc):
        i2, i3 = gc % 2, gc % 3
        P.op("act", lambda i3=i3: nc.scalar.activation(out=u_sb[i3][:], in_=C.bank[BU][:, 0:512], func=AF.Gelu),
             reads=[C.bR[BU]], writes=[uR[i3]])
        P.op("act", lambda i2=i2: nc.scalar.activation(out=v_sb[i2][:], in_=C.bank[BV][:, 0:512], func=AF.Gelu),
             reads=[C.bR[BV]], writes=[vR[i2]])

    def A_vln(gc):
        i2, i3 = gc % 2, gc % 3
        st = vst[i2]
        R0, R1, R2, R3 = Reg("vs0"), Reg("vs1"), Reg("vs2"), Reg("vs3")
        P.op("dve", lambda st=st, i2=i2: nc.vector.bn_stats(out=st[:, 0:6], in_=v_sb[i2][:]), reads=[vR[i2]], writes=[R0])
        P.op("dve", lambda st=st: nc.vector.bn_aggr(out=st[:, 6:8], in_=st[:, 0:6]), reads=[R0], writes=[R1])
        P.op("act", lambda st=st: nc.scalar.activation(out=st[:, 8:9], in_=st[:, 7:8], func=AF.Sqrt, bias=C.col[:, 2:3], scale=1.0),
             reads=[R1, C.colR], writes=[R2])
        P.op("dve", lambda st=st, i2=i2: nc.vector.scalar_tensor_tensor(out=v_sb[i2][:], in0=v_sb[i2][:], scalar=st[:, 6:7], in1=vg[:],
                                                                     op0=ALU.subtract, op1=ALU.mult),
             reads=[vR[i2], R1, vgR], writes=[vR[i2]])
        P.op("dve", lambda st=st: nc.vector.reciprocal(out=st[:, 9:10], in_=st[:, 8:9]), reads=[R2], writes=[R3])
        P.op("dve", lambda st=st, i2=i2: nc.vector.scalar_tensor_tensor(out=vbf[i2][:], in0=v_sb[i2][:], scalar=st[:, 9:10], in1=vb[:],
                                                                     op0=ALU.mult, op1=ALU.add),
             reads=[vR[i2], R3, vbR], writes=[vbfR[i2]])

    def B_pe(gc):
        i2, i3, ip = gc % 2, gc % 4, (gc - 1) % 4
        for h in range(8):
            P.op("pe", lambda h=h, i2=i2: nc.tensor.matmul(C.bank[BSP][:, h * 64:(h + 1) * 64], wsT[:, h, :],
                                                          vbf[i2][:, h * 64:(h + 1) * 64], start=True, stop=True),
                 reads=[wsTR, vbfR[i2]], writes=[C.bR[BSP]])
        for g in range(4):
            gs = slice(g * 128, (g + 1) * 128)
            if gc == 0:
                P.op("pe", lambda g=g, gs=gs, i3=i3: nc.tensor.matmul(C.bank[BPF][:, gs], xp[i3][:, gs], A[:, 8 + g, :],
                                                                     start=True, stop=True),
                     reads=[xpR[i3], AR], writes=[C.bR[BPF]])
            else:
                P.op("pe", lambda g=g, gs=gs, i3=i3: nc.tensor.matmul(C.bank[BPF][:, gs], xp[i3][:, gs], A[:, g, :],
                                                                     start=True, stop=False),
                     reads=[xpR[i3], AR], writes=[C.bR[BPF]])
                P.op("pe", lambda g=g, gs=gs, ip=ip: nc.tensor.matmul(C.bank[BPF][:, gs], xp[ip][:, gs], A[:, 4 + g, :],
                                                                     start=False, stop=True),
                     reads=[xpR[ip], AR], writes=[C.bR[BPF]])

    def B_add(gc):
        i2, i3 = gc % 2, gc % 3
        P.op("dve", lambda i2=i2: nc.vector.tensor_tensor(
            out=a_sb[i2][:].rearrange("p (h d) -> p h d", h=8), in0=C.bank[BSP][:, 0:512].rearrange("p (h d) -> p h d", h=8),
            in1=bs[:, 0:8].unsqueeze(2).to_broadcast([128, 8, 64]), op=ALU.add),
            reads=[C.bR[BSP], bsR], writes=[aR[i2]])
        P.op("pool", lambda i2=i2, i3=i3: nc.gpsimd.tensor_tensor(out=a_sb[i2][:], in0=a_sb[i2][:], in1=u_sb[i3][:], op=ALU.mult),
             reads=[aR[i2], uR[i3]], writes=[aR[i2]])

    def B_copy(gc):
        i2 = gc % 2
        P.op("act", lambda i2=i2: nc.scalar.copy(out=pooledT[i2][:].rearrange("p g t -> p (g t)"), in_=C.bank[BPF][:, 0:512]),
             reads=[C.bR[BPF]], writes=[pooledTR[i2]])

    def C_pe(gc):
        i2 = gc % 2
        for kb in range(4):
            P.op("pe", lambda kb=kb, i2=i2: nc.tensor.transpose(C.bank[BXA][:, kb * 128:(kb + 1) * 128],
                                                               a_sb[i2][:, kb * 128:(kb + 1) * 128], C.ident[:]),
                 reads=[aR[i2], C.identR], writes=[C.bR[BXA]])
        for g in range(4):
            P.op("pe", lambda g=g, i2=i2: nc.tensor.matmul(C.bank[BPM][:, g * 128:(g + 1) * 128], poolw[:, g, :], pooledT[i2][:, g, :],
                                                          start=True, stop=True),
                 reads=[poolwR, pooledTR[i2]], writes=[C.bR[BPM]])

    def C_copy(gc):
        i2 = gc % 2
        P.op("act", lambda i2=i2: nc.scalar.copy(out=mixT[i2][:, 0:4, :], in_=C.bank[BXA][:, 0:512].rearrange("p (k t) -> p k t", k=4)),
             reads=[C.bR[BXA]], writes=[mixTR[i2]], part=True)

    def C_scale(gc):
        i2 = gc % 2
        P.op("dve", lambda i2=i2: nc.vector.tensor_tensor(
            out=mixT[i2][:, 4:8, :], in0=C.bank[BPM][:, 0:512].rearrange("p (g t) -> p g t", g=4),
            in1=pscale[:, 0:4].unsqueeze(2).to_broadcast([128, 4, 128]), op=ALU.mult),
            reads=[C.bR[BPM], pscaleR], writes=[mixTR[i2]], part=True)

    def D_pe(gc):
        i2 = gc % 2
        for half, bk in ((0, BOA), (1, BOB)):
            for k in range(8):
                P.op("pe", lambda k=k, half=half, bk=bk, i2=i2: nc.tensor.matmul(
                    C.bank[bk][:, 0:512], mixT[i2][:, k, :], wout[:, k, half * 512:(half + 1) * 512],
                    start=(k == 0), stop=(k == 7)), reads=[mixTR[i2], woutR[k]], writes=[C.bR[bk]])

    def D_post(gc):
        t, c, b, cs = geo(gc)
        xi = t % 3
        ln_epilogue(C, L, BOA, BOB, xin[xi][:, c, :], xinR[xi][c], dst[gc * 128:(gc + 1) * 128, :])

    def TX(t):
        load_transpose_tile(C, src, t * T, T, xin[t % 3], xinR[t % 3], xT[t % 2], xTR[t % 2], (BU, BV))

    TX(0)
    ok = lambda g: 0 <= g < NCH
    for s_ in range(NCH + 8):
        if ok(s_ - 1):
            A_gelu(s_ - 1)
        if ok(s_ - 3):
            B_copy(s_ - 3)
        if ok(s_ - 5):
            C_copy(s_ - 5)
            C_scale(s_ - 5)
        if ok(s_ - 3):
            B_add(s_ - 3)
        if ok(s_ - 1):
            A_vln(s_ - 1)
        if ok(s_ - 7):
            D_post(s_ - 7)
        if s_ % T == 2 and (s_ // T) + 1 < NT:
            TX(s_ // T + 1)
        if ok(s_):
            A_pe(s_)
        if ok(s_ - 2):
            B_pe(s_ - 2)
        if ok(s_ - 6):
            D_pe(s_ - 6)
        if ok(s_ - 4):
            C_pe(s_ - 4)
        if s_ % 4 == 3:
            P.flush()


def phase_mla(C, pes, src, dst):
    nc, P, dr = C.nc, C.P, C.dr
    with ExitStack() as aes:
        mla_latents_and_attention(C, aes, src)
    mla_outproj(C, pes, src, dst)


def mla_latents_and_attention(C, pes, src):
    nc, P, dr = C.nc, C.P, C.dr
    sb = lambda name, shp, dt=F32: pes.enter_context(nc.sbuf_tensor(name, shp, dt))
    T, W, NT = 4, 512, 8
    w_in = dr["odd_w_in"]
    wq_in = sb("a_wqin", [128, 8, 384], BF16)
    wkv_in = sb("a_wkvin", [128, 8, 256], BF16)
    wkr = sb("a_wkr", [128, 8, 96], BF16)
    wkrs = sb("a_wkrs", [128, 8, 96], BF16)
    winR = Reg("a_win")
    w3 = w_in.rearrange("(k p) n -> p k n", p=128)
    load_w_bf16(C, wq_in[:], w3[:, :, 0:384], winR)
    load_w_bf16(C, wkv_in[:], w3[:, :, 384:640], winR)
    load_w_bf16(C, wkr[:], w3[:, :, 576:672], winR)
    load_w_bf16(C, wkrs[:, :, 0:64], w3[:, :, 576:640], winR)
    load_w_bf16(C, wkrs[:, :, 64:80], w3[:, :, 656:672], winR)
    load_w_bf16(C, wkrs[:, :, 80:96], w3[:, :, 640:656], winR)
    wqu = sb("a_wqu", [128, 3, 1536], BF16)
    wqus = sb("a_wqus", [128, 3, 16, 96], BF16)
    wquR = Reg("a_wqu")
    q3 = dr["odd_w_q_up"].rearrange("(k p) n -> p k n", p=128)
    q4 = dr["odd_w_q_up"].rearrange("(k p) (h d) -> p k h d", p=128, h=16)
    load_w_bf16(C, wqu[:], q3, wquR)
    for k in range(3):
        for hs in (slice(0, 8), slice(8, 16)):
            load_w_bf16(C, wqus[:, k, hs, 0:64], q4[:, k, hs, 0:64], wquR)
            load_w_bf16(C, wqus[:, k, hs, 64:80], q4[:, k, hs, 80:96], wquR)
            load_w_bf16(C, wqus[:, k, hs, 80:96], q4[:, k, hs, 64:80], wquR)
    wkn = sb("a_wkn", [128, 2, 16, 64], BF16)
    wv = sb("a_wv", [128, 2, 16, 64], BF16)
    wkvR = Reg("a_wkv")
    kv4 = dr["odd_w_kv_up"].rearrange("(k p) (h d) -> p k h d", p=128, h=16)
    for k in range(2):
        for hs in (slice(0, 8), slice(8, 16)):
            load_w_bf16(C, wkn[:, k, hs, :], kv4[:, k, hs, 0:64], wkvR)
            load_w_bf16(C, wv[:, k, hs, :], kv4[:, k, hs, 64:128], wkvR)
    gq = sb("a_gq", [128, 3])
    gkv = sb("a_gkv", [128, 2])
    gR = Reg("a_g")
    with nc.allow_non_contiguous_dma(reason="tiny norm-gain columns"):
        P.dma("sp", lambda: nc.sync.dma_start(out=gq[:], in_=dr["odd_q_norm_g"].rearrange("o (k p) -> p (o k)", p=128)), writes=[gR], part=True)
        P.dma("sp", lambda: nc.sync.dma_start(out=gkv[:], in_=dr["odd_kv_norm_g"].rearrange("o (k p) -> p (o k)", p=128)), writes=[gR], part=True)
        P.flush()
    ones = sb("a_ones", [128, 128])
    onesR = Reg("a_ones")
    P.op("pool", lambda: nc.gpsimd.memset(ones[:], 1.0), writes=[onesR])
    sel = sb("a_sel", [128, 64])
    selR = Reg("a_sel")
    P.dma("sp", lambda: nc.sync.dma_start(out=sel[:], in_=dr["c_sel"][:, :]), writes=[selR])
    tri = sb("a_tri", [128, 128])
    trib = sb("a_trib", [128, 128], BF16)
    triR = Reg("a_tri")
    tribR = Reg("a_trib")
    P.dma("sp", lambda: nc.sync.dma_start(out=tri[:], in_=dr["c_triu"][:, :]), writes=[triR])
    P.op("pool", lambda: nc.gpsimd.tensor_copy(out=trib[:], in_=tri[:]), reads=[triR], writes=[tribR])
    negm = sb("a_negm", [128, 128], BF16)
    identb = sb("a_identb", [128, 128], BF16)
    mskR = Reg("a_msk")
    P.op("dve", lambda: nc.vector.tensor_scalar(out=negm[:], in0=tri[:], scalar1=-1.0, scalar2=30000.0, op0=ALU.add, op1=ALU.mult),
         reads=[triR], writes=[mskR], part=True)
    P.op("dve", lambda: nc.vector.tensor_copy(out=identb[:], in_=C.ident[:]), reads=[C.identR], writes=[mskR], part=True)

    cosT = sb("a_cos", [128, S])
    sinT = sb("a_sin", [128, S])
    csR = [Reg("a_cs%d" % t) for t in range(NT)]
    cqT = sb("a_cqT", [128, 3, S], BF16)
    ckvT = sb("a_ckvT", [128, 2, S], BF16)
    latR = [Reg("a_lat%d" % t) for t in range(NT)]
    kT = [sb("a_kT%d" % i, [96, S], BF16) for i in range(2)]
    kTropeR = [Reg("a_kTr%d" % t) for t in range(NT)]
    kTR = [Reg("a_kT%d" % i) for i in range(2)]
    t1 = sb("a_t1", [128, W])
    t2 = sb("a_t2", [128, W])
    t1R, t2R = Reg("a_t1"), Reg("a_t2")
    rp = slice(64, 96)
    shared_pes = pes
    pes = ExitStack()
    pes.__enter__()

    xin = [sb("a_xin%d" % i, [128, T, D]) for i in range(2)]
    xinR = [[Reg("axin%d_%d" % (i, c)) for c in range(T)] for i in range(2)]
    xT = [sb("a_xT%d" % i, [128, 8, W], BF16) for i in range(2)]
    xTR = [Reg("axT%d" % i) for i in range(2)]
    posi = sb("a_posi", [128, W], I32)
    ang = sb("a_ang", [128, W])
    tq = sb("a_tq", [128, W])
    ki = sb("a_ki", [128, W], I32)
    sq = [sb("a_sq%d" % i, [128, W]) for i in range(2)]
    sqR = [Reg("a_sq%d" % i) for i in range(2)]
    rstd = sb("a_rstd", [128, W])
    rstdR = Reg("a_rstd")
    posR, angR, tqR, kiR = Reg("a_pos"), Reg("a_ang"), Reg("a_tq"), Reg("a_ki")

    def rope_tables(t):
        ts_ = slice(t * W, (t + 1) * W)
        P.dma("sp", lambda ts_=ts_: nc.sync.dma_start(out=posi[rp, :], in_=dr["positions"][0:1, ts_].partition_broadcast(32)), writes=[posR])
        P.op("dve", lambda: nc.vector.tensor_copy(out=ang[rp, :], in_=posi[rp, :]), reads=[posR], writes=[angR])
        P.op("dve", lambda: nc.vector.tensor_scalar(out=ang[rp, :], in0=ang[rp, :], scalar1=C.col[rp, 0:1], scalar2=None, op0=ALU.mult),
             reads=[angR, C.colR], writes=[angR])
        for which in (0, 1):
            off = 0.0 if which == 0 else PI / 2
            P.op("dve", lambda off=off: nc.vector.tensor_scalar(out=tq[rp, :], in0=ang[rp, :], scalar1=off, scalar2=1.0 / TWO_PI,
                                                               op0=ALU.add, op1=ALU.mult), reads=[angR], writes=[tqR])
            P.op("dve", lambda: nc.vector.tensor_copy(out=ki[rp, :], in_=tq[rp, :]), reads=[tqR], writes=[kiR])
            P.op("dve", lambda: nc.vector.tensor_copy(out=tq[rp, :], in_=ki[rp, :]), reads=[kiR], writes=[tqR])
            P.op("dve", lambda: nc.vector.scalar_tensor_tensor(out=tq[rp, :], in0=tq[rp, :], scalar=-TWO_PI, in1=ang[rp, :],
                                                              op0=ALU.mult, op1=ALU.add), reads=[tqR, angR], writes=[tqR])
            P.op("dve", lambda off=off: nc.vector.tensor_scalar(out=tq[rp, :], in0=tq[rp, :], scalar1=off, scalar2=PI,
                                                               op0=ALU.add, op1=ALU.min), reads=[tqR], writes=[tqR])
            P.op("dve", lambda: nc.vector.tensor_scalar(out=tq[rp, :], in0=tq[rp, :], scalar1=-PI, scalar2=None, op0=ALU.max),
                 reads=[tqR], writes=[tqR])
            if which == 0:
                P.op("act", lambda ts_=ts_: nc.scalar.activation(out=sinT[rp, ts_], in_=tq[rp, :], func=AF.Sin, scale=C.col[rp, 1:2]),
                     reads=[tqR, C.colR], writes=[csR[t]], part=True)
            else:
                P.op("act", lambda ts_=ts_: nc.scalar.activation(out=cosT[rp, ts_], in_=tq[rp, :], func=AF.Sin),
                     reads=[tqR], writes=[csR[t]], part=True)

    def proj_mms(t, wt, nblk, bk0):
        b = t % 2
        for kb in range(nblk):
            bk = bk0 + kb
            for k in range(8):
                P.op("pe", lambda k=k, kb=kb, bk=bk, wt=wt, b=b: nc.tensor.matmul(
                    C.bank[bk][:, 0:W], wt[:, k, kb * 128:(kb + 1) * 128], xT[b][:, k, :], start=(k == 0), stop=(k == 7)),
                    reads=[winR, xTR[b]], writes=[C.bR[bk]])

    def rms_post(t, nblk, dstL, gcol, inv_n, bk0):
        ts_ = slice(t * W, (t + 1) * W)
        SSB = 5
        for kb in range(nblk):
            bk = bk0 + kb
            si = kb % 2
            P.op("act", lambda bk=bk, si=si: nc.scalar.activation(out=sq[si][:], in_=C.bank[bk][:, 0:W], func=AF.Square),
                 reads=[C.bR[bk]], writes=[sqR[si]])
            P.op("pe", lambda si=si, kb=kb, nblk=nblk: nc.tensor.matmul(C.bank[SSB][:, 0:W], ones[:], sq[si][:],
                                                                       start=(kb == 0), stop=(kb == nblk - 1)),
                 reads=[onesR, sqR[si]], writes=[C.bR[SSB]])
        P.op("act", lambda inv_n=inv_n: nc.scalar.activation(out=rstd[:], in_=C.bank[SSB][:, 0:W], func=AF.Sqrt,
                                                            bias=C.col[:, 3:4], scale=inv_n),
             reads=[C.bR[SSB], C.colR], writes=[rstdR])
        P.op("dve", lambda: nc.vector.reciprocal(out=rstd[:], in_=rstd[:]), reads=[rstdR], writes=[rstdR])
        for kb in range(nblk):
            bk = bk0 + kb
            P.op("dve", lambda kb=kb, bk=bk, dstL=dstL, gcol=gcol, ts_=ts_: nc.vector.scalar_tensor_tensor(
                out=dstL[:, kb, ts_], in0=C.bank[bk][:, 0:W], scalar=gcol[:, kb:kb + 1], in1=rstd[:], op0=ALU.mult, op1=ALU.mult),
                reads=[C.bR[bk], gR, rstdR], writes=[latR[t]], part=True)

    rope_tables(0)
    load_transpose_tile(C, src, 0, T, xin[0], xinR[0], xT[0], xTR[0], (0, 1))
    for t in range(NT):
        b = t % 2
        ts_ = slice(t * W, (t + 1) * W)
        for (wt, bk) in ((wkr, 0), (wkrs, 1)):
            for k in range(8):
                P.op("pe", lambda k=k, wt=wt, bk=bk, b=b: nc.tensor.matmul(C.bank[bk][0:96, 0:W], wt[:, k, :], xT[b][:, k, :],
                                                                          start=(k == 0), stop=(k == 7)),
                     reads=[winR, xTR[b]], writes=[C.bR[bk]])
        proj_mms(t, wq_in, 3, 2)
        P.op("dve", lambda ts_=ts_: nc.vector.tensor_tensor(out=t1[rp, :], in0=C.bank[0][rp, 0:W], in1=cosT[rp, ts_], op=ALU.mult),
             reads=[C.bR[0], csR[t]], writes=[t1R])
        P.op("dve", lambda ts_=ts_: nc.vector.tensor_tensor(out=t2[rp, :], in0=C.bank[1][rp, 0:W], in1=sinT[rp, ts_], op=ALU.mult),
             reads=[C.bR[1], csR[t]], writes=[t2R])
        for i in range(2):
            P.op("pool", lambda i=i, ts_=ts_: nc.gpsimd.tensor_tensor(out=kT[i][rp, ts_], in0=t1[rp, :], in1=t2[rp, :], op=ALU.add),
                 reads=[t1R, t2R], writes=[kTropeR[t]], part=True)
        proj_mms(t, wkv_in, 2, 6)
        rms_post(t, 3, cqT, gq, 1.0 / 384, 2)
        if t + 1 < NT:
            load_transpose_tile(C, src, (t + 1) * T, T, xin[1 - b], xinR[1 - b], xT[1 - b], xTR[1 - b], (0, 1))
        rms_post(t, 2, ckvT, gkv, 1.0 / 256, 6)
        if t + 1 < NT:
            rope_tables(t + 1)

    P.barrier(C.marks)
    pes.__exit__(None, None, None)
    pes = ExitStack()
    pes.__enter__()
    qT = [sb("a_qT%d" % i, [96, S], BF16) for i in range(2)]
    qTR = [Reg("a_qT%d" % i) for i in range(2)]
    Vg = sb("a_V", [128, NCH, 4, 65], BF16)
    VgR = Reg("a_V")
    VoneR = Reg("a_Vone")
    P.op("pool", lambda: nc.gpsimd.memset(Vg[:, :, :, 64:65], 1.0), writes=[VoneR])
    pT = [sb("a_pT%d" % i, [128, 1024], BF16) for i in range(3)]
    pTR = [Reg("a_pT%d" % i) for i in range(3)]
    oa = [sb("a_oa%d" % i, [128, W]) for i in range(2)]
    oaR = [Reg("a_oa%d" % i) for i in range(2)]
    oTh = [sb("a_oTh%d" % i, [64, S], BF16) for i in range(2)]
    oThR = [Reg("a_oTh%d" % i) for i in range(2)]
    allLat = latR + csR + kTropeR
    OB = (0, 1)
    SB3 = ((2, 3), (4, 5), (6, 7))
    npt = 0
    noa = 0
    def gen_v(hg):
        for c2 in range(NCH // 2):
            bk = 2 + (c2 % 2)
            for cc in range(2):
                c = 2 * c2 + cc
                for k in range(2):
                    P.op("pe", lambda k=k, c=c, cc=cc, bk=bk, hg=hg: nc.tensor.matmul(
                        C.bank[bk][:, cc * 256:(cc + 1) * 256], ckvT[:, k, c * 128:(c + 1) * 128],
                        wv[:, k, hg * 4:(hg + 1) * 4, :].rearrange("p h d -> p (h d)"), start=(k == 0), stop=(k == 1)),
                        reads=[wkvR] + latR, writes=[C.bR[bk]])
            P.op("dve", lambda c2=c2, bk=bk: nc.vector.tensor_copy(
                out=Vg[:, 2 * c2:2 * c2 + 2, :, 0:64], in_=C.bank[bk][:, 0:512].rearrange("p (c h d) -> p c h d", c=2, h=4)),
                reads=[C.bR[bk]], writes=[VgR], part=True)

    def gen_qk(h, t, pair=(4, 5)):
        b = h % 2
        ts_ = slice(t * W, (t + 1) * W)
        BQ, BS_, BK = pair[0], pair[1], pair[0]
        for (wt, bk) in ((None, BQ), (wqus, BS_)):
            for k in range(3):
                lhs = wqu[:, k, h * 96:(h + 1) * 96] if wt is None else wqus[:, k, h, :]
                P.op("pe", lambda k=k, lhs=lhs, bk=bk, ts_=ts_: nc.tensor.matmul(C.bank[bk][0:96, 0:W], lhs, cqT[:, k, ts_],
                                                                                 start=(k == 0), stop=(k == 2)),
                     reads=[wquR] + latR, writes=[C.bR[bk]])
        P.op("dve", lambda ts_=ts_, b=b: nc.vector.tensor_copy(out=qT[b][0:64, ts_], in_=C.bank[BQ][0:64, 0:W]),
             reads=[C.bR[BQ]], writes=[qTR[b]], part=True)
        P.op("dve", lambda ts_=ts_: nc.vector.tensor_tensor(out=t1[rp, :], in0=C.bank[BQ][rp, 0:W], in1=cosT[rp, ts_], op=ALU.mult),
             reads=[C.bR[BQ]] + csR, writes=[t1R])
        P.op("dve", lambda ts_=ts_: nc.vector.tensor_tensor(out=t2[rp, :], in0=C.bank[BS_][rp, 0:W], in1=sinT[rp, ts_], op=ALU.mult),
             reads=[C.bR[BS_]] + csR, writes=[t2R])
        P.op("pool", lambda ts_=ts_, b=b: nc.gpsimd.tensor_tensor(out=qT[b][rp, ts_], in0=t1[rp, :], in1=t2[rp, :], op=ALU.add),
             reads=[t1R, t2R], writes=[qTR[b]], part=True)
        for k in range(2):
            P.op("pe", lambda k=k, ts_=ts_: nc.tensor.matmul(C.bank[BK][0:64, 0:W], wkn[:, k, h, :], ckvT[:, k, ts_],
                                                             start=(k == 0), stop=(k == 1)),
                 reads=[wkvR] + latR, writes=[C.bR[BK]])
        P.op("dve", lambda ts_=ts_, b=b: nc.vector.tensor_copy(out=kT[b][0:64, ts_], in_=C.bank[BK][0:64, 0:W]),
             reads=[C.bR[BK]], writes=[kTR[b]], part=True)

    def emit_S(it):
        sb2 = it["sb2"]
        if it["kind"] in ("g1", "g2"):
            h2 = it["h2"]
            ts_ = slice(it["t"] * W, (it["t"] + 1) * W)
            if it["kind"] == "g1":
                for k in range(3):
                    P.op("pe", lambda k=k, ts_=ts_, h2=h2, sb2=sb2: nc.tensor.matmul(
                        C.bank[sb2[0]][0:96, 0:W], wqu[:, k, h2 * 96:(h2 + 1) * 96], cqT[:, k, ts_], start=(k == 0), stop=(k == 2)),
                        reads=[wquR] + latR, writes=[C.bR[sb2[0]]])
                for k in range(2):
                    P.op("pe", lambda k=k, ts_=ts_, h2=h2, sb2=sb2: nc.tensor.matmul(
                        C.bank[sb2[1]][0:64, 0:W], wkn[:, k, h2, :], ckvT[:, k, ts_], start=(k == 0), stop=(k == 1)),
                        reads=[wkvR] + latR, writes=[C.bR[sb2[1]]])
            else:
                for k in range(3):
                    P.op("pe", lambda k=k, ts_=ts_, h2=h2, sb2=sb2: nc.tensor.matmul(
                        C.bank[sb2[0]][0:96, 0:W], wqus[:, k, h2, :], cqT[:, k, ts_], start=(k == 0), stop=(k == 2)),
                        reads=[wquR] + latR, writes=[C.bR[sb2[0]]])
            return
        b, q0 = it["b"], it["q0"]
        if it["kind"] == "off":
            for u in range(2):
                kt = it["kt0"] + u
                P.op("pe", lambda kt=kt, u=u, sb2=sb2, b=b, q0=q0: nc.tensor.matmul(
                    C.bank[sb2[u]][:, 0:W], kT[b][:, kt * 128:(kt + 1) * 128], qT[b][:, q0:q0 + W], start=True, stop=True),
                    reads=[kTR[b], qTR[b]] + kTropeR, writes=[C.bR[sb2[u]]])
        else:
            kt, n0 = it["kt"], it["n0"]
            P.op("pe", lambda kt=kt, sb2=sb2, b=b, q0=q0, n0=n0: nc.tensor.matmul(
                C.bank[sb2[0]][:, n0:W], kT[b][:, kt * 128:(kt + 1) * 128], qT[b][:, q0 + n0:q0 + W], start=True, stop=False),
                reads=[kTR[b], qTR[b]] + kTropeR, writes=[C.bR[sb2[0]]])
            P.op("pe", lambda sb2=sb2, n0=n0: nc.tensor.matmul(
                C.bank[sb2[0]][:, n0:n0 + 128], identb[:], negm[:], start=False, stop=True),
                reads=[mskR], writes=[C.bR[sb2[0]]])

    def emit_EP(it):
        sb2 = it["sb2"]
        if it["kind"] in ("g1", "g2"):
            b2 = it["h2"] % 2
            ts_ = slice(it["t"] * W, (it["t"] + 1) * W)
            if it["kind"] == "g1":
                P.op("act", lambda ts_=ts_, b2=b2, sb2=sb2: nc.scalar.copy(out=qT[b2][0:64, ts_], in_=C.bank[sb2[0]][0:64, 0:W]),
                     reads=[C.bR[sb2[0]]], writes=[qTR[b2]], part=True)
                P.op("dve", lambda ts_=ts_, sb2=sb2: nc.vector.tensor_tensor(out=t1[rp, :], in0=C.bank[sb2[0]][rp, 0:W], in1=cosT[rp, ts_], op=ALU.mult),
                     reads=[C.bR[sb2[0]]] + csR, writes=[t1R])
                P.op("act", lambda ts_=ts_, b2=b2, sb2=sb2: nc.scalar.copy(out=kT[b2][0:64, ts_], in_=C.bank[sb2[1]][0:64, 0:W]),
                     reads=[C.bR[sb2[1]]], writes=[kTR[b2]], part=True)
            else:
                P.op("dve", lambda ts_=ts_, sb2=sb2: nc.vector.tensor_tensor(out=t2[rp, :], in0=C.bank[sb2[0]][rp, 0:W], in1=sinT[rp, ts_], op=ALU.mult),
                     reads=[C.bR[sb2[0]]] + csR, writes=[t2R])
                P.op("pool", lambda ts_=ts_, b2=b2: nc.gpsimd.tensor_tensor(out=qT[b2][rp, ts_], in0=t1[rp, :], in1=t2[rp, :], op=ALU.add),
                     reads=[t1R, t2R], writes=[qTR[b2]], part=True)
            return
        pi, ob, hh = it["pi"], it["ob"], it["hh"]
        if it["kind"] == "off":
            P.op("act", lambda sb2=sb2, pi=pi: nc.scalar.activation(out=pT[pi][:, 0:1024], in_=C.ps[:, sb2[0] * 512:sb2[0] * 512 + 1024],
                                                                   func=AF.Exp, scale=SCALE),
                 reads=[C.bR[sb2[0]], C.bR[sb2[1]]], writes=[pTR[pi]])
            for u in range(2):
                kt = it["kt0"] + u
                st = it["first"] and u == 0
                P.op("pe", lambda kt=kt, u=u, pi=pi, ob=ob, hh=hh, st=st: nc.tensor.matmul(
                    C.bank[ob][0:65, 0:W], Vg[:, kt, hh, :], pT[pi][:, u * W:(u + 1) * W], start=st, stop=False),
                    reads=[VgR, VoneR, pTR[pi]], writes=[C.bR[ob]])
        else:
            kt, n0 = it["kt"], it["n0"]
            P.op("act", lambda sb2=sb2, pi=pi, n0=n0: nc.scalar.activation(out=pT[pi][:, n0:W], in_=C.bank[sb2[0]][:, n0:W],
                                                                          func=AF.Exp, scale=SCALE),
                 reads=[C.bR[sb2[0]]], writes=[pTR[pi]])
            P.op("pe", lambda kt=kt, pi=pi, ob=ob, hh=hh, n0=n0, st=it["first"], sp_=it["last"]: nc.tensor.matmul(
                C.bank[ob][0:65, n0:W], Vg[:, kt, hh, :], pT[pi][:, n0:W], start=st, stop=sp_),
                reads=[VgR, VoneR, pTR[pi]], writes=[C.bR[ob]])

    def emit_norm_pre(h, j, ob):
        nonlocal noa
        oi = noa % 2
        noa += 1
        P.op("dve", lambda oi=oi, ob=ob: nc.vector.tensor_copy(out=oa[oi][0:65, :], in_=C.bank[ob][0:65, 0:W]),
             reads=[C.bR[ob]], writes=[oaR[oi]])
        return oi

    def emit_norm_recip(oi, q):
        qs = slice(q * 128, (q + 1) * 128)
        P.op("dve", lambda oi=oi, qs=qs: nc.vector.reciprocal(out=oa[oi][64:65, qs], in_=oa[oi][64:65, qs]),
             reads=[oaR[oi]], writes=[oaR[oi]])

    def emit_norm_post(h, j, ob, oi):
        ob_i = h % 2
        q0 = j * W
        P.op("pe", lambda oi=oi, ob=ob: nc.tensor.matmul(C.bank[ob][0:64, 0:W], sel[0:65, :], oa[oi][0:65, :], start=True, stop=True),
             reads=[selR, oaR[oi]], writes=[C.bR[ob]])
        P.op("dve", lambda oi=oi, ob=ob, ob_i=ob_i, q0=q0: nc.vector.tensor_tensor(
            out=oTh[ob_i][:, q0:q0 + W], in0=oa[oi][0:64, :], in1=C.bank[ob][0:64, 0:W], op=ALU.mult),
            reads=[oaR[oi], C.bR[ob]], writes=[oThR[ob_i]], part=True)
        if j == 7:
            P.dma("sp", lambda h=h, ob_i=ob_i: nc.sync.dma_start(out=C.dr["oTd"][h * 64:(h + 1) * 64, :], in_=oTh[ob_i][:]),
                  reads=[oThR[ob_i]])

    LA = 2
    pend = []
    gen_v(0)
    for t in range(NT):
        gen_qk(0, t)
    for h in range(_dbg_heads()):
        hg, hh = divmod(h, 4)
        b = h % 2
        items = []
        for j in range(8):
            ob = OB[j % 2]
            first = True
            for kt0 in range(0, 4 * j, 2):
                items.append(dict(kind="off", kt0=kt0, b=b, q0=j * W, ob=ob, hh=hh, first=first, last=False, j=j,
                                  sb2=SB3[npt % 3], pi=npt % 3))
                npt += 1
                first = False
            for r in range(4):
                items.append(dict(kind="diag", kt=4 * j + r, n0=128 * r, b=b, q0=j * W, ob=ob, hh=hh, first=first,
                                  last=(r == 3), j=j, sb2=SB3[npt % 3], pi=npt % 3))
                npt += 1
                first = False
            if h + 1 < NH:
                for kind in ("g1", "g2"):
                    items.append(dict(kind=kind, h2=h + 1, t=j, sb2=SB3[npt % 3], pi=npt % 3))
                    npt += 1
        n = len(items)
        for i in range(n + LA):
            if i < n:
                emit_S(items[i])
            if i - LA >= 0:
                it = items[i - LA]
                emit_EP(it)
                npend = []
                for (stg, args) in pend:
                    if stg < 0:
                        npend.append((stg + 1, args))
                    elif stg == 0:
                        args[3] = emit_norm_pre(args[0], args[1], args[2])
                        npend.append((1, args))
                    elif stg <= 4:
                        emit_norm_recip(args[3], stg - 1)
                        npend.append((stg + 1, args))
                    else:
                        emit_norm_post(*args)
                pend = npend
                if it.get("last"):
                    has_gen = (h + 1 < NH)
                    if has_gen:
                        pend.append((-1, [h, it["j"], it["ob"], None]))
                    else:
                        oi = emit_norm_pre(h, it["j"], it["ob"])
                        pend.append((1, [h, it["j"], it["ob"], oi]))
        if h + 2 < NH:
            pend = [(max(stg, 0), args) for (stg, args) in pend]
        else:
            for (stg, args) in pend:
                if stg <= 0:
                    args[3] = emit_norm_pre(args[0], args[1], args[2])
                    stg = 1
                for q in range(stg - 1, 4):
                    emit_norm_recip(args[3], q)
                emit_norm_post(*args)
            pend = []
        if h + 1 < NH and (h + 1) % 4 == 0:
            gen_v((h + 1) // 4)
        P.flush()
    P.barrier(C.marks)
    pes.__exit__(None, None, None)


def mla_outproj(C, pes, src, dst):
    nc, P, dr = C.nc, C.P, C.dr
    sb = lambda name, shp, dt=F32: pes.enter_context(nc.sbuf_tensor(name, shp, dt))
    T, W, NT = 4, 512, 8
    wout = sb("o_wout", [128, 8, D], BF16)
    woutR = [Reg("o_wout%d" % k) for k in range(8)]
    for k in range(8):
        load_w_bf16(C, wout[:, k, :], dr["odd_w_out"][k * 128:(k + 1) * 128, :], woutR[k])
    L = ln_setup(C, pes, dr["mix_ln_g"][1:2, :], dr["mix_ln_b"][1:2, :], "o")
    xin = [sb("o_xin%d" % i, [128, T, D]) for i in range(2)]
    xinR = [[Reg("oxin%d_%d" % (i, c)) for c in range(T)] for i in range(2)]
    oTt = [sb("o_oT%d" % i, [128, 8, W], BF16) for i in range(2)]
    oTtR = [Reg("o_oT%d" % i) for i in range(2)]
    o3 = dr["oTd"].rearrange("(k p) t -> p k t", p=128)
    def loads(t):
        b = t % 2
        rows = src[t * W:(t + 1) * W, :].rearrange("(c p) d -> p c d", p=128)
        P.dma("sp", lambda rows=rows, b=b: nc.sync.dma_start(out=oTt[b][:], in_=o3[:, :, t * W:(t + 1) * W]), writes=[oTtR[b]])
        P.dma("sp", lambda rows=rows, b=b: nc.sync.dma_start(out=xin[b][:], in_=rows), writes=xinR[b])

    loads(0)
    for t in range(NT):
        b = t % 2
        if t + 1 < NT:
            loads(t + 1)
        for c in range(T):
            cs = slice(c * 128, (c + 1) * 128)
            pair = (0, 1) if c % 2 == 0 else (2, 3)
            for half in range(2):
                bk = pair[half]
                for k in range(8):
                    P.op("pe", lambda k=k, half=half, bk=bk, cs=cs, b=b: nc.tensor.matmul(
                        C.bank[bk][:, 0:512], oTt[b][:, k, cs], wout[:, k, half * 512:(half + 1) * 512],
                        start=(k == 0), stop=(k == 7)), reads=[oTtR[b], woutR[k]], writes=[C.bR[bk]])
            gc = t * T + c
            ln_epilogue(C, L, pair[0], pair[1], xin[b][:, c, :], xinR[b][c], dst[gc * 128:(gc + 1) * 128, :])


_NC_CACHE = {}


def _get_nc(phases, standalone):
    key = (tuple(phases), standalone)
    if key not in _NC_CACHE:
        _NC_CACHE[key] = build(list(phases), standalone)
    return _NC_CACHE[key]


def _weight_maps(inputs):
    m = {}
    for name, shp in W_SPECS.items():
        m[name] = np.ascontiguousarray(np.asarray(inputs[name], dtype=np.float32).reshape(shp))
    m.update(host_consts())
    return m


FUSED = True


def kernel(**inputs):
    x = np.asarray(inputs["x"], dtype=np.float32)
    pos = np.asarray(inputs["positions"], dtype=np.int32)
    wm = _weight_maps(inputs)
    n = 8
    if FUSED:
        nc = _get_nc((1, 2, 3, 4), False)
        in_maps = []
        for b in range(n):
            d = dict(wm)
            d["x"] = np.ascontiguousarray(x[b])
            d["positions"] = np.ascontiguousarray(pos[b:b + 1])
            in_maps.append(d)
        res = run_bass_kernel_spmd(nc, in_maps, core_ids=list(range(n)))
        return np.stack([res.results[b]["out"] for b in range(n)], axis=0)
    cur = [np.ascontiguousarray(x[b]) for b in range(n)]
    for ph in (1, 2, 3, 4):
        nc = _get_nc((ph,), True)
        in_maps = []
        for b in range(n):
            d = dict(wm)
            d["src"] = cur[b]
            d["positions"] = np.ascontiguousarray(pos[b:b + 1])
            in_maps.append(d)
        res = run_bass_kernel_spmd(nc, in_maps, core_ids=list(range(n)))
        cur = [np.ascontiguousarray(res.results[b]["dst"]) for b in range(n)]
    return np.stack(cur, axis=0)
```

```python
import numpy as np
from contextlib import ExitStack
import concourse.bass as bass
import concourse.mybir as mybir
from concourse.bass_utils import run_bass_kernel_spmd

F32 = mybir.dt.float32
BF16 = mybir.dt.bfloat16
I32 = mybir.dt.int32
AF = mybir.ActivationFunctionType
ALU = mybir.AluOpType

S = 4096
D = 1024
NCH = S // 128
DFF = 2816
NFF = DFF // 128
ALPHA = float((2 * 2) ** 0.25)
LN_EPS = 1e-5
RMS_EPS = 1e-6
TWO_PI = float(2 * np.pi)
PI = float(np.pi)
NH = 16
SCALE = float(96 ** -0.5)


class Reg:
    __slots__ = ("name", "excl", "writers", "readers")

    def __init__(self, name, excl=False):
        self.name = name
        self.excl = excl
        self.writers = []
        self.readers = []


class Ins:
    __slots__ = ("eng", "fn", "dma", "deps", "signal", "count", "dsem", "dval", "seq", "emitted", "xw")

    def __init__(self, eng, fn, dma):
        self.eng = eng
        self.fn = fn
        self.dma = dma
        self.deps = []
        self.signal = False
        self.count = None
        self.dsem = None
        self.dval = None
        self.seq = None
        self.emitted = False
        self.xw = []


class Prog:
    def __init__(self, nc, es):
        self.nc = nc
        self.E = {"pe": nc.tensor, "act": nc.scalar, "dve": nc.vector, "pool": nc.gpsimd, "sp": nc.sync}
        self.sem = {e: es.enter_context(nc.semaphore("c_" + e)) for e in ("pe", "act", "dve", "pool")}
        self.cnt = {e: 0 for e in self.sem}
        self.dsems = {}
        for q, n in (("sp", 12), ("pool", 8), ("act", 4)):
            self.dsems[q] = [es.enter_context(nc.semaphore("d_%s%d" % (q, i))) for i in range(n)]
        self.dnext = {q: 0 for q in self.dsems}
        self.dval = {}
        self.dlast = {}
        self.pending = []
        self.seq = {e: 0 for e in self.E}
        self.sig_hist = {e: [] for e in self.sem}
        self.waited = {e: {x: 0 for x in self.sem} for e in self.E}
        self.dobs = {e: {} for e in self.E}
        self.extra = {e: [] for e in self.E}
        self.last = {e: None for e in self.E}

    def _add(self, eng, fn, reads, writes, dma, part):
        I = Ins(eng, fn, dma)
        I.seq = self.seq[eng]
        self.seq[eng] += 1
        deps = []
        for r in reads:
            if r.excl:
                deps += [(d, "raw") for d in r.writers] + [(d, "war") for d in r.readers]
            else:
                deps += [(d, "raw") for d in r.writers]
        for w in writes:
            if part and not w.excl:
                deps += [(d, "war") for d in w.readers]
            else:
                deps += [(d, "war") for d in w.readers] + [(d, "waw") for d in w.writers]
        for d in self.extra[eng]:
            deps.append((d, "raw"))
        self.extra[eng] = []
        if dma:
            q = eng
            sems = self.dsems[q]
            sem = sems[self.dnext[q] % len(sems)]
            self.dnext[q] += 1
            prev = self.dlast.get(id(sem))
            if prev is not None:
                deps.append((prev, "raw"))
            I.dsem = sem
            I.dval = self.dval.get(id(sem), 0) + 16
            self.dval[id(sem)] = I.dval
            self.dlast[id(sem)] = I
        seen = set()
        for d, kind in deps:
            if d is I or id(d) in seen:
                continue
            if d.dma or dma:
                pass
            elif d.eng == eng:
                if eng == "pe" or kind != "raw":
                    continue
            seen.add(id(d))
            I.deps.append(d)
            if not d.dma and not d.emitted:
                d.signal = True
        for r in reads:
            if not dma:
                r.readers = [x for x in r.readers if x.dma or x.eng != eng]
            r.readers.append(I)
        for w in writes:
            if part and not w.excl:
                if w.readers:
                    w.writers = [I]
                    w.readers = []
                else:
                    if not dma:
                        w.writers = [x for x in w.writers if x.dma or x.eng != eng]
                    w.writers.append(I)
            else:
                w.writers = [I]
                w.readers = []
        self.pending.append(I)
        self.last[eng] = I
        return I

    def op(self, eng, fn, reads=(), writes=(), part=False):
        return self._add(eng, fn, list(reads), list(writes), False, part)

    def dma(self, eng, fn, reads=(), writes=(), part=False):
        return self._add(eng, fn, list(reads), list(writes), True, part)

    def _count_of(self, d):
        if d.count is not None:
            return d.count
        for seq, c in self.sig_hist[d.eng]:
            if seq >= d.seq:
                return c
        raise RuntimeError("no signal after dep on %s" % d.eng)

    def flush(self):
        lastp = {}
        for I in self.pending:
            if not I.dma and I.eng in self.sem:
                lastp[I.eng] = I
        for I in lastp.values():
            I.signal = True
        for I in self.pending:
            e = I.eng
            eng = self.E[e]
            waits = []
            need_c = {}
            for d in I.deps:
                if d.dma:
                    k = id(d.dsem)
                    if self.dobs[e].get(k, 0) < d.dval:
                        self.dobs[e][k] = d.dval
                        waits.append((d.dsem, d.dval))
                else:
                    c = self._count_of(d)
                    if c > need_c.get(d.eng, 0):
                        need_c[d.eng] = c
            for x, c in need_c.items():
                if self.waited[e][x] < c:
                    self.waited[e][x] = c
                    waits.append((self.sem[x], c))
            for sem, val in I.xw:
                k = id(sem)
                if self.dobs[e].get(k, 0) < val:
                    self.dobs[e][k] = val
                    waits.append((sem, val))
            best = {}
            for sem, val in waits:
                k = id(sem)
                if k not in best or best[k][1] < val:
                    best[k] = (sem, val)
            waits = list(best.values())
            while len(waits) > 2:
                a = waits.pop()
                b = waits.pop()
                eng.wait_ge(a[0], a[1])
                eng.wait_ge(b[0], b[1])
                eng.nop()
            for sem, val in waits:
                eng.wait_ge(sem, val)
            bi = I.fn()
            if I.dma:
                bi.then_inc(I.dsem, 16)
            elif I.signal:
                self.cnt[e] += 1
                I.count = self.cnt[e]
                bi.then_inc(self.sem[e], 1)
                self.sig_hist[e].append((I.seq, I.count))
            I.emitted = True
            I.fn = None
        self.pending = []
        for e in self.sig_hist:
            if len(self.sig_hist[e]) > 4:
                self.sig_hist[e] = self.sig_hist[e][-4:]

    def all_dma_waits(self):
        out = []
        for q in self.dsems:
            for sem in self.dsems[q]:
                v = self.dval.get(id(sem), 0)
                if v:
                    out.append((sem, v))
        return out

    def barrier(self, marks):
        ms = []
        for e in ("act", "dve", "pool"):
            m = self.op(e, marks[e])
            m.xw = self.all_dma_waits()
            m.signal = True
            ms.append(m)
        for e in ("act", "dve", "pool", "sp"):
            self.extra[e] = list(ms)
        self.flush()

    def final_wait(self):
        eng = self.E["sp"]
        for sem, val in self.all_dma_waits():
            eng.wait_ge(sem, val)
            eng.nop()


CONST_SPECS = {
    "c_ident": ([128, 128], F32),
    "c_triu": ([128, 128], F32),
    "c_poolA": ([12, 128, 128], F32),
    "c_col": ([128, 4], F32),
    "c_sel": ([128, 64], F32),
}

W_SPECS = {
    "even_w_in": [1024, 1536], "even_vnorm_g": [1, 512], "even_vnorm_b": [1, 512],
    "even_spatial_w": [8, 128, 128], "even_spatial_b": [8, 128], "even_pool_w": [4, 128, 128],
    "even_pool_scale": [1, 512], "even_w_out": [1024, 1024],
    "odd_w_in": [1024, 672], "odd_q_norm_g": [1, 384], "odd_w_q_up": [384, 1536],
    "odd_kv_norm_g": [1, 256], "odd_w_kv_up": [256, 2048], "odd_w_out": [1024, 1024],
    "mix_ln_g": [2, 1024], "mix_ln_b": [2, 1024], "ffn_w_gate_up": [2, 1024, 5632],
    "ffn_w_down": [2, 2816, 1024], "ffn_ln_g": [2, 1024], "ffn_ln_b": [2, 1024],
}


def host_consts():
    ident = np.eye(128, dtype=np.float32)
    triu = np.triu(np.ones((128, 128), dtype=np.float32))
    A = np.zeros((12, 128, 128), dtype=np.float32)
    s = np.arange(128)[:, None]
    t = np.arange(128)[None, :]
    for g, w in enumerate((2, 4, 8, 16)):
        A[g] = ((s <= t) & (s > t - w)) / np.float32(w) - (s == t)
        A[4 + g] = (s >= 128 + t - w + 1) / np.float32(w)
        cnt = np.minimum(t + 1, w).astype(np.float32)
        A[8 + g] = ((s <= t) & (s > t - w)) / cnt - (s == t)
    col = np.zeros((128, 4), dtype=np.float32)
    freqs = (10000.0 ** (-np.arange(0, 32, 2, dtype=np.float32) / 32)).astype(np.float32)
    col[64:80, 0] = freqs
    col[80:96, 0] = freqs
    col[:, 1] = 1.0
    col[64:80, 1] = -1.0
    col[:, 2] = LN_EPS
    col[:, 3] = RMS_EPS
    sel = np.zeros((128, 64), dtype=np.float32)
    sel[64, :] = 1.0
    return {"c_ident": ident, "c_triu": triu, "c_poolA": A.astype(np.float32), "c_col": col, "c_sel": sel}


class Ctx:
    pass


def _dbg_heads():
    import os
    return int(os.environ.get("MK_DBG_HEADS", NH))


def build(phases, standalone):
    nc = bass.Bass("TRN2", target_bir_lowering=False)
    dr = {}
    for name, shp in W_SPECS.items():
        dr[name] = nc.dram_tensor(name, shp, F32, kind="ExternalInput").ap()
    for name, (shp, dt) in CONST_SPECS.items():
        dr[name] = nc.dram_tensor(name, shp, dt, kind="ExternalInput").ap()
    dr["positions"] = nc.dram_tensor("positions", [1, S], I32, kind="ExternalInput").ap()
    if standalone:
        src = nc.dram_tensor("src", [S, D], F32, kind="ExternalInput").ap()
        dst = nc.dram_tensor("dst", [S, D], F32, kind="ExternalOutput").ap()
        chain = {phases[0]: (src, dst)}
    else:
        x = nc.dram_tensor("x", [S, D], F32, kind="ExternalInput").ap()
        out = nc.dram_tensor("out", [S, D], F32, kind="ExternalOutput").ap()
        s1 = nc.dram_tensor("scr1", [S, D], F32).ap()
        s2 = nc.dram_tensor("scr2", [S, D], F32).ap()
        s3 = nc.dram_tensor("scr3", [S, D], F32).ap()
        chain = {1: (x, s1), 2: (s1, s2), 3: (s2, s3), 4: (s3, out)}
    dr["oTd"] = nc.dram_tensor("scr_oT", [D, S], BF16).ap()

    with ExitStack() as es:
        P = Prog(nc, es)
        C = Ctx()
        C.nc, C.P, C.dr = nc, P, dr
        C.ps = es.enter_context(nc.psum_tensor("ps", [128, 4096], F32))
        C.bank = [C.ps[:, 512 * i:512 * (i + 1)] for i in range(8)]
        C.bR = [Reg("bank%d" % i, excl=True) for i in range(8)]
        C.ident = es.enter_context(nc.sbuf_tensor("ident", [128, 128], F32))
        C.identR = Reg("ident")
        C.col = es.enter_context(nc.sbuf_tensor("colc", [128, 4], F32))
        C.colR = Reg("col")
        C.mk = es.enter_context(nc.sbuf_tensor("marks", [128, 8], F32))
        P.dma("sp", lambda: nc.sync.dma_start(out=C.ident[:], in_=dr["c_ident"][:, :]), writes=[C.identR])
        P.dma("sp", lambda: nc.sync.dma_start(out=C.col[:], in_=dr["c_col"][:, :]), writes=[C.colR])
        C.marks = {
            "act": lambda: nc.scalar.activation(out=C.mk[:, 0:1], in_=C.mk[:, 1:2], func=AF.Copy),
            "dve": lambda: nc.vector.memset(C.mk[:, 2:3], 0.0),
            "pool": lambda: nc.gpsimd.memset(C.mk[:, 4:5], 0.0),
        }
        P.op("dve", lambda: nc.vector.memset(C.mk[:], 0.0))
        for ph in phases:
            srcap, dstap = chain[ph]
            with ExitStack() as pes:
                if ph == 1:
                    phase_mixer0(C, pes, srcap, dstap)
                elif ph == 2:
                    phase_ffn(C, pes, 0, srcap, dstap)
                elif ph == 3:
                    phase_mla(C, pes, srcap, dstap)
                elif ph == 4:
                    phase_ffn(C, pes, 1, srcap, dstap)
                P.barrier(C.marks)
        P.flush()
        P.final_wait()
    return nc


def bcast_row(C, pes, name, row_ap, n):
    nc, P = C.nc, C.P
    t = pes.enter_context(nc.sbuf_tensor(name, [128, n], F32))
    R = Reg(name)
    P.dma("sp", lambda: nc.sync.dma_start(out=t[:], in_=row_ap.partition_broadcast(128)), writes=[R])
    return t, R


class LNState:
    pass


def ln_setup(C, pes, g_row, b_row, tag, lean=False):
    nc = C.nc
    L = LNState()
    L.g, L.gR = bcast_row(C, pes, "lng_" + tag, g_row, D)
    L.b, L.bR = bcast_row(C, pes, "lnb_" + tag, b_row, D)
    if lean:
        s0 = pes.enter_context(nc.sbuf_tensor("lns0_%s" % tag, [128, D], F32))
        r0 = Reg("lns0")
        L.s, L.sR, L.y, L.yR = [s0, s0], [r0, r0], None, None
    else:
        L.s = [pes.enter_context(nc.sbuf_tensor("lns%d_%s" % (i, tag), [128, D], F32)) for i in range(2)]
        L.sR = [Reg("lns%d" % i) for i in range(2)]
        L.y = [pes.enter_context(nc.sbuf_tensor("lny%d_%s" % (i, tag), [128, D], F32)) for i in range(2)]
        L.yR = [Reg("lny%d" % i) for i in range(2)]
    L.st = [pes.enter_context(nc.sbuf_tensor("lnst%d_%s" % (i, tag), [128, 24], F32)) for i in range(2)]
    L.stR = [Reg("lnst%d" % i) for i in range(2)]
    L.n = 0
    return L


def ln_epilogue(C, L, bankA, bankB, x_ap, xR, dst_rows):
    nc, P = C.nc, C.P
    i = L.n % 2
    L.n += 1
    s, sR, st, stR = L.s[i], L.sR[i], L.st[i], L.stR[i]
    if L.y is None:
        y_ap, yR = x_ap, xR
    else:
        y_ap, yR = L.y[i][:], L.yR[i]
    bA, bB = C.bank[bankA], C.bank[bankB]
    P.op("dve", lambda: nc.vector.scalar_tensor_tensor(out=s[:, 0:512], in0=x_ap[:, 0:512], scalar=ALPHA, in1=bA,
                                                      op0=ALU.mult, op1=ALU.add),
         reads=[xR, C.bR[bankA]], writes=[sR], part=True)
    P.op("dve", lambda: nc.vector.scalar_tensor_tensor(out=s[:, 512:1024], in0=x_ap[:, 512:1024], scalar=ALPHA, in1=bB,
                                                      op0=ALU.mult, op1=ALU.add),
         reads=[xR, C.bR[bankB]], writes=[sR], part=True)
    P.op("dve", lambda: nc.vector.bn_stats(out=st[:, 0:6], in_=s[:, 0:512]), reads=[sR], writes=[stR], part=True)
    P.op("dve", lambda: nc.vector.bn_stats(out=st[:, 6:12], in_=s[:, 512:1024]), reads=[sR], writes=[stR], part=True)
    R1, R2, R3 = Reg("mv"), Reg("sd"), Reg("rstd")
    P.op("dve", lambda: nc.vector.bn_aggr(out=st[:, 12:14], in_=st[:, 0:12]), reads=[stR], writes=[R1])
    P.op("act", lambda: nc.scalar.activation(out=st[:, 14:15], in_=st[:, 13:14], func=AF.Sqrt, bias=C.col[:, 2:3], scale=1.0),
         reads=[R1, C.colR], writes=[R2])
    P.op("dve", lambda: nc.vector.scalar_tensor_tensor(out=s[:], in0=s[:], scalar=st[:, 12:13], in1=L.g[:],
                                                      op0=ALU.subtract, op1=ALU.mult), reads=[sR, R1, L.gR], writes=[sR])
    P.op("dve", lambda: nc.vector.reciprocal(out=st[:, 15:16], in_=st[:, 14:15]), reads=[R2], writes=[R3])
    P.op("dve", lambda: nc.vector.scalar_tensor_tensor(out=y_ap, in0=s[:], scalar=st[:, 15:16], in1=L.b[:],
                                                      op0=ALU.mult, op1=ALU.add), reads=[sR, R3, L.bR], writes=[yR])
    P.dma("sp", lambda: nc.sync.dma_start(out=dst_rows, in_=y_ap), reads=[yR])


def load_transpose_tile(C, src, t0, nchunk, xin, xinR, xT, xTR, tbanks, evac_engs=("act", "dve")):
    nc, P = C.nc, C.P
    rows = src[t0 * 128:(t0 + nchunk) * 128, :].rearrange("(c p) d -> p c d", p=128)
    P.dma("sp", lambda: nc.sync.dma_start(out=xin[:, 0:nchunk, :], in_=rows), writes=xinR)
    W = nchunk * 128
    for k in range(8):
        b = tbanks[k % len(tbanks)]
        for c in range(nchunk):
            P.op("pe", lambda c=c, k=k, b=b: nc.tensor.transpose(C.bank[b][:, c * 128:(c + 1) * 128],
                                                                xin[:, c, k * 128:(k + 1) * 128], C.ident[:]),
                 reads=[xinR[c], C.identR], writes=[C.bR[b]])
        e = evac_engs[k % len(evac_engs)]
        if e == "act":
            P.op("act", lambda k=k, b=b: nc.scalar.copy(out=xT[:, k, 0:W], in_=C.bank[b][:, 0:W]),
                 reads=[C.bR[b]], writes=[xTR], part=True)
        else:
            P.op("dve", lambda k=k, b=b: nc.vector.tensor_copy(out=xT[:, k, 0:W], in_=C.bank[b][:, 0:W]),
                 reads=[C.bR[b]], writes=[xTR], part=True)


def load_w_bf16(C, tile_ap, dram_ap, R, eng="pool"):
    nc, P = C.nc, C.P
    P.dma("pool", lambda: nc.gpsimd.dma_start(out=tile_ap, in_=dram_ap), writes=[R], part=True)


def phase_ffn(C, pes, layer, src, dst):
    nc, P, dr = C.nc, C.P, C.dr
    sfx = "_L%d" % layer
    wgu = pes.enter_context(nc.sbuf_tensor("wgu" + sfx, [128, 8, 2 * DFF], BF16))
    wd = pes.enter_context(nc.sbuf_tensor("wd" + sfx, [128, NFF, D], BF16))
    HJ = NFF // 2
    wguR = [[Reg("wgu%d_%d" % (k, ch)) for ch in range(4)] for k in range(8)]
    wdR = [Reg("wd%d" % j) for j in range(NFF)]
    gu = dr["ffn_w_gate_up"][layer]
    chunks = [(0, 0, HJ * 128), (1, DFF, DFF + HJ * 128), (2, HJ * 128, DFF), (3, DFF + HJ * 128, 2 * DFF)]
    for (ch, c0, c1) in chunks:
        for k in range(8):
            load_w_bf16(C, wgu[:, k, c0:c1], gu[k * 128:(k + 1) * 128, c0:c1], wguR[k][ch])
    dn = dr["ffn_w_down"][layer]
    for j in range(NFF):
        load_w_bf16(C, wd[:, j, :], dn[j * 128:(j + 1) * 128, :], wdR[j])
    L = ln_setup(C, pes, dr["ffn_ln_g"][layer:layer + 1, :], dr["ffn_ln_b"][layer:layer + 1, :], "f%d" % layer, lean=True)
    NP = NCH // 4
    xin = [pes.enter_context(nc.sbuf_tensor("fxin%d" % i + sfx, [128, 2, D], F32)) for i in range(2)]
    xinR = [[Reg("fxin%d_%d" % (i, c)) for c in range(2)] for i in range(2)]
    xT = pes.enter_context(nc.sbuf_tensor("fxT" + sfx, [128, 8, 512], BF16))
    xTR = [Reg("fxT%d" % i) for i in range(2)]
    hT = pes.enter_context(nc.sbuf_tensor("hT" + sfx, [128, NFF, 512], BF16))
    hTR = [Reg("hT%d" % j) for j in range(NFF)]
    sg = [pes.enter_context(nc.sbuf_tensor("sg%d" % i + sfx, [128, 512], F32)) for i in range(2)]
    sgR = [Reg("sg%d" % i) for i in range(2)]
    xr = [pes.enter_context(nc.sbuf_tensor("fxr%d" % i + sfx, [128, D], F32)) for i in range(2)]
    xrR = [Reg("fxr%d" % i) for i in range(2)]

    def transposes(p):
        for h2 in range(2):
            t = 2 * p + h2
            load_transpose_tile(C, src, t * 2, 2, xin[h2], xinR[h2], xT[:, :, h2 * 256:(h2 + 1) * 256], xTR[h2], (0, 1))

    def load_xr(gc):
        i = gc % 2
        P.dma("sp", lambda gc=gc, i=i: nc.sync.dma_start(out=xr[i][:], in_=src[gc * 128:(gc + 1) * 128, :]), writes=[xrR[i]])

    transposes(0)
    nsg = 0
    for p in range(NP):
        for j in range(NFF):
            bkg, bku = (2, 3) if j % 2 == 0 else (4, 5)
            for which, bk in ((0, bkg), (1, bku)):
                col = which * DFF + j * 128
                for k in range(8):
                    P.op("pe", lambda k=k, col=col, bk=bk: nc.tensor.matmul(
                        C.bank[bk][:, 0:512], wgu[:, k, col:col + 128], xT[:, k, :], start=(k == 0), stop=(k == 7)),
                        reads=[wguR[k][which + (2 if j >= HJ else 0)], xTR[0], xTR[1]], writes=[C.bR[bk]])
            si = nsg % 2
            nsg += 1
            P.op("act", lambda bkg=bkg, si=si: nc.scalar.activation(out=sg[si][:], in_=C.bank[bkg][:, 0:512], func=AF.Silu),
                 reads=[C.bR[bkg]], writes=[sgR[si]])
            P.op("dve", lambda bku=bku, si=si, j=j: nc.vector.tensor_tensor(out=hT[:, j, :], in0=sg[si][:], in1=C.bank[bku][:, 0:512],
                                                                        op=ALU.mult),
                 reads=[sgR[si], C.bR[bku]], writes=[hTR[j]])
        if p + 1 < NP:
            transposes(p + 1)
        load_xr(4 * p)
        load_xr(4 * p + 1)
        for c in range(4):
            gc = 4 * p + c
            pair = (6, 7) if c % 2 == 0 else (0, 1)
            for half in range(2):
                bk = pair[half]
                for j in range(NFF):
                    P.op("pe", lambda j=j, c=c, half=half, bk=bk: nc.tensor.matmul(
                        C.bank[bk][:, 0:512], hT[:, j, c * 128:(c + 1) * 128], wd[:, j, half * 512:(half + 1) * 512],
                        start=(j == 0), stop=(j == NFF - 1)),
                        reads=[hTR[j], wdR[j]], writes=[C.bR[bk]])
            ln_epilogue(C, L, pair[0], pair[1], xr[gc % 2][:], xrR[gc % 2], dst[gc * 128:(gc + 1) * 128, :])
            if c + 2 < 4:
                load_xr(gc + 2)


def phase_mixer0(C, pes, src, dst):
    nc, P, dr = C.nc, C.P, C.dr
    T = 4
    W = 512
    NT = NCH // T
    sb = lambda name, shp, dt=F32: pes.enter_context(nc.sbuf_tensor(name, shp, dt))
    win = sb("m_win", [128, 8, 1536], BF16)
    winR = [[Reg("win%d_%d" % (k, g)) for g in range(3)] for k in range(8)]
    for g in range(3):
        for k in range(8):
            load_w_bf16(C, win[:, k, g * 512:(g + 1) * 512], dr["even_w_in"][k * 128:(k + 1) * 128, g * 512:(g + 1) * 512], winR[k][g])
    wout = sb("m_wout", [128, 8, D], BF16)
    woutR = [Reg("wout%d" % k) for k in range(8)]
    for k in range(8):
        load_w_bf16(C, wout[:, k, :], dr["even_w_out"][k * 128:(k + 1) * 128, :], woutR[k])
    poolw = sb("m_poolw", [128, 4, 128], BF16)
    poolwR = Reg("poolw")
    load_w_bf16(C, poolw[:], dr["even_pool_w"].rearrange("g c d -> c g d"), poolwR)
    A = sb("m_A", [128, 12, 128])
    AR = Reg("A")
    P.dma("sp", lambda: nc.sync.dma_start(out=A[:], in_=dr["c_poolA"].rearrange("n s t -> s n t")), writes=[AR])
    triu = sb("m_triu", [128, 128])
    triuR = Reg("triu")
    P.dma("sp", lambda: nc.sync.dma_start(out=triu[:], in_=dr["c_triu"][:, :]), writes=[triuR])
    wsn = sb("m_wsn", [128, 8, 128])
    wsnR = Reg("wsn")
    P.dma("sp", lambda: nc.sync.dma_start(out=wsn[:], in_=dr["even_spatial_w"].rearrange("h t s -> t h s")), writes=[wsnR])
    wsT = sb("m_wsT", [128, 8, 128], BF16)
    wsTR = Reg("wsT")
    for h in range(8):
        P.op("pe", lambda h=h: nc.tensor.transpose(C.bank[0][:, 0:128], wsn[:, h, :], C.ident[:]),
             reads=[wsnR, C.identR], writes=[C.bR[0]])
        P.op("dve", lambda h=h: nc.vector.tensor_tensor(out=wsT[:, h, :], in0=C.bank[0][:, 0:128], in1=triu[:], op=ALU.mult),
             reads=[C.bR[0], triuR], writes=[wsTR], part=True)
    bs = sb("m_bs", [128, 8])
    bsR = Reg("bs")
    pscale = sb("m_pscale", [128, 4])
    pscaleR = Reg("pscale")
    with nc.allow_non_contiguous_dma(reason="tiny per-head bias / scale columns"):
        P.dma("sp", lambda: nc.sync.dma_start(out=bs[:], in_=dr["even_spatial_b"].rearrange("h t -> t h")), writes=[bsR])
        P.dma("sp", lambda: nc.sync.dma_start(out=pscale[:], in_=dr["even_pool_scale"].rearrange("o (g d) -> d (o g)", g=4)),
              writes=[pscaleR])
        P.flush()
    vg, vgR = bcast_row(C, pes, "m_vg", dr["even_vnorm_g"], 512)
    vb, vbR = bcast_row(C, pes, "m_vb", dr["even_vnorm_b"], 512)
    L = ln_setup(C, pes, dr["mix_ln_g"][0:1, :], dr["mix_ln_b"][0:1, :], "m")
    xin = [sb("m_xin%d" % i, [128, T, D]) for i in range(3)]
    xinR = [[Reg("mxin%d_%d" % (i, c)) for c in range(T)] for i in range(3)]
    xT = [sb("m_xT%d" % i, [128, 8, W], BF16) for i in range(2)]
    xTR = [Reg("mxT%d" % i) for i in range(2)]
    mixT = [sb("m_mixT%d" % i, [128, 8, 128], BF16) for i in range(2)]
    mixTR = [Reg("mixT%d" % i) for i in range(2)]
    u_sb = [sb("m_u%d" % i, [128, 512]) for i in range(3)]
    uR = [Reg("u%d" % i) for i in range(3)]
    v_sb = [sb("m_v%d" % i, [128, 512]) for i in range(2)]
    vR = [Reg("v%d" % i) for i in range(2)]
    vbf = [sb("m_vbf%d" % i, [128, 512], BF16) for i in range(2)]
    vbfR = [Reg("vbf%d" % i) for i in range(2)]
    a_sb = [sb("m_a%d" % i, [128, 512]) for i in range(2)]
    aR = [Reg("a%d" % i) for i in range(2)]
    xp = [sb("m_xp%d" % i, [128, 512]) for i in range(4)]
    xpR = [Reg("xp%d" % i) for i in range(4)]
    pooledT = [sb("m_pooledT%d" % i, [128, 4, 128], BF16) for i in range(2)]
    pooledTR = [Reg("pooledT%d" % i) for i in range(2)]
    vst = [sb("m_vst%d" % i, [128, 16]) for i in range(2)]
    BXA, BPM, BU, BV, BSP, BPF, BOA, BOB = 0, 1, 2, 3, 4, 5, 6, 7

    def geo(gc):
        t, c = divmod(gc, T)
        return t, c, t % 2, slice(c * 128, (c + 1) * 128)

    def A_pe(gc):
        t, c, b, cs = geo(gc)
        for which, bk in ((0, BU), (1, BV)):
            for k in range(8):
                P.op("pe", lambda k=k, which=which, bk=bk, cs=cs, b=b: nc.tensor.matmul(
                    C.bank[bk][:, 0:512], xT[b][:, k, cs], win[:, k, which * 512:(which + 1) * 512],
                    start=(k == 0), stop=(k == 7)), reads=[xTR[b], winR[k][which]], writes=[C.bR[bk]])
        for k in range(8):
            P.op("pe", lambda k=k, cs=cs, b=b: nc.tensor.matmul(
                C.bank[BXA][:, 0:512], xT[b][:, k, cs], win[:, k, 1024:1536],
                start=(k == 0), stop=(k == 7)), reads=[xTR[b], winR[k][2]], writes=[C.bR[BXA]])
        i4 = gc % 4
        P.op("act", lambda i4=i4: nc.scalar.copy(out=xp[i4][:], in_=C.bank[BXA][:, 0:512]),
             reads=[C.bR[BXA]], writes=[xpR[i4]])

    def A_gelu(gc):
        i2, i3 = gc % 2, gc % 3
        P.op("act", lambda i3=i3: nc.scalar.activation(out=u_sb[i3][:], in_=C.bank[BU][:, 0:512], func=AF.Gelu),
             reads=[C.bR[BU]], writes=[uR[i3]])
        P.op("act", lambda i2=i2: nc.scalar.activation(out=v_sb[i2][:], in_=C.bank[BV][:, 0:512], func=AF.Gelu),
             reads=[C.bR[BV]], writes=[vR[i2]])

    def A_vln(gc):
        i2, i3 = gc % 2, gc % 3
        st = vst[i2]
        R0, R1, R2, R3 = Reg("vs0"), Reg("vs1"), Reg("vs2"), Reg("vs3")
        P.op("dve", lambda st=st, i2=i2: nc.vector.bn_stats(out=st[:, 0:6], in_=v_sb[i2][:]), reads=[vR[i2]], writes=[R0])
        P.op("dve", lambda st=st: nc.vector.bn_aggr(out=st[:, 6:8], in_=st[:, 0:6]), reads=[R0], writes=[R1])
        P.op("act", lambda st=st: nc.scalar.activation(out=st[:, 8:9], in_=st[:, 7:8], func=AF.Sqrt, bias=C.col[:, 2:3], scale=1.0),
             reads=[R1, C.colR], writes=[R2])
        P.op("dve", lambda st=st, i2=i2: nc.vector.scalar_tensor_tensor(out=v_sb[i2][:], in0=v_sb[i2][:], scalar=st[:, 6:7], in1=vg[:],
                                                                     op0=ALU.subtract, op1=ALU.mult),
             reads=[vR[i2], R1, vgR], writes=[vR[i2]])
        P.op("dve", lambda st=st: nc.vector.reciprocal(out=st[:, 9:10], in_=st[:, 8:9]), reads=[R2], writes=[R3])
        P.op("dve", lambda st=st, i2=i2: nc.vector.scalar_tensor_tensor(out=vbf[i2][:], in0=v_sb[i2][:], scalar=st[:, 9:10], in1=vb[:],
                                                                     op0=ALU.mult, op1=ALU.add),
             reads=[vR[i2], R3, vbR], writes=[vbfR[i2]])

    def B_pe(gc):
        i2, i3, ip = gc % 2, gc % 4, (gc - 1) % 4
        for h in range(8):
            P.op("pe", lambda h=h, i2=i2: nc.tensor.matmul(C.bank[BSP][:, h * 64:(h + 1) * 64], wsT[:, h, :],
                                                          vbf[i2][:, h * 64:(h + 1) * 64], start=True, stop=True),
                 reads=[wsTR, vbfR[i2]], writes=[C.bR[BSP]])
        for g in range(4):
            gs = slice(g * 128, (g + 1) * 128)
            if gc == 0:
                P.op("pe", lambda g=g, gs=gs, i3=i3: nc.tensor.matmul(C.bank[BPF][:, gs], xp[i3][:, gs], A[:, 8 + g, :],
                                                                     start=True, stop=True),
                     reads=[xpR[i3], AR], writes=[C.bR[BPF]])
            else:
                P.op("pe", lambda g=g, gs=gs, i3=i3: nc.tensor.matmul(C.bank[BPF][:, gs], xp[i3][:, gs], A[:, g, :],
                                                                     start=True, stop=False),
                     reads=[xpR[i3], AR], writes=[C.bR[BPF]])
                P.op("pe", lambda g=g, gs=gs, ip=ip: nc.tensor.matmul(C.bank[BPF][:, gs], xp[ip][:, gs], A[:, 4 + g, :],
                                                                     start=False, stop=True),
                     reads=[xpR[ip], AR], writes=[C.bR[BPF]])

    def B_add(gc):
        i2, i3 = gc % 2, gc % 3
        P.op("dve", lambda i2=i2: nc.vector.tensor_tensor(
            out=a_sb[i2][:].rearrange("p (h d) -> p h d", h=8), in0=C.bank[BSP][:, 0:512].rearrange("p (h d) -> p h d", h=8),
            in1=bs[:, 0:8].unsqueeze(2).to_broadcast([128, 8, 64]), op=ALU.add),
            reads=[C.bR[BSP], bsR], writes=[aR[i2]])
        P.op("pool", lambda i2=i2, i3=i3: nc.gpsimd.tensor_tensor(out=a_sb[i2][:], in0=a_sb[i2][:], in1=u_sb[i3][:], op=ALU.mult),
             reads=[aR[i2], uR[i3]], writes=[aR[i2]])

    def B_copy(gc):
        i2 = gc % 2
        P.op("act", lambda i2=i2: nc.scalar.copy(out=pooledT[i2][:].rearrange("p g t -> p (g t)"), in_=C.bank[BPF][:, 0:512]),
             reads=[C.bR[BPF]], writes=[pooledTR[i2]])

    def C_pe(gc):
        i2 = gc % 2
        for kb in range(4):
            P.op("pe", lambda kb=kb, i2=i2: nc.tensor.transpose(C.bank[BXA][:, kb * 128:(kb + 1) * 128],
                                                               a_sb[i2][:, kb * 128:(kb + 1) * 128], C.ident[:]),
                 reads=[aR[i2], C.identR], writes=[C.bR[BXA]])
        for g in range(4):
            P.op("pe", lambda g=g, i2=i2: nc.tensor.matmul(C.bank[BPM][:, g * 128:(g + 1) * 128], poolw[:, g, :], pooledT[i2][:, g, :],
                                                          start=True, stop=True),
                 reads=[poolwR, pooledTR[i2]], writes=[C.bR[BPM]])

    def C_copy(gc):
        i2 = gc % 2
        P.op("act", lambda i2=i2: nc.scalar.copy(out=mixT[i2][:, 0:4, :], in_=C.bank[BXA][:, 0:512].rearrange("p (k t) -> p k t", k=4)),
             reads=[C.bR[BXA]], writes=[mixTR[i2]], part=True)

    def C_scale(gc):
        i2 = gc % 2
        P.op("dve", lambda i2=i2: nc.vector.tensor_tensor(
            out=mixT[i2][:, 4:8, :], in0=C.bank[BPM][:, 0:512].rearrange("p (g t) -> p g t", g=4),
            in1=pscale[:, 0:4].unsqueeze(2).to_broadcast([128, 4, 128]), op=ALU.mult),
            reads=[C.bR[BPM], pscaleR], writes=[mixTR[i2]], part=True)

    def D_pe(gc):
        i2 = gc % 2
        for half, bk in ((0, BOA), (1, BOB)):
            for k in range(8):
                P.op("pe", lambda k=k, half=half, bk=bk, i2=i2: nc.tensor.matmul(
                    C.bank[bk][:, 0:512], mixT[i2][:, k, :], wout[:, k, half * 512:(half + 1) * 512],
                    start=(k == 0), stop=(k == 7)), reads=[mixTR[i2], woutR[k]], writes=[C.bR[bk]])

    def D_post(gc):
        t, c, b, cs = geo(gc)
        xi = t % 3
        ln_epilogue(C, L, BOA, BOB, xin[xi][:, c, :], xinR[xi][c], dst[gc * 128:(gc + 1) * 128, :])

    def TX(t):
        load_transpose_tile(C, src, t * T, T, xin[t % 3], xinR[t % 3], xT[t % 2], xTR[t % 2], (BU, BV))

    TX(0)
    ok = lambda g: 0 <= g < NCH
    for s_ in range(NCH + 8):
        if ok(s_ - 1):
            A_gelu(s_ - 1)
        if ok(s_ - 3):
            B_copy(s_ - 3)
        if ok(s_ - 5):
            C_copy(s_ - 5)
            C_scale(s_ - 5)
        if ok(s_ - 3):
            B_add(s_ - 3)
        if ok(s_ - 1):
            A_vln(s_ - 1)
        if ok(s_ - 7):
            D_post(s_ - 7)
        if s_ % T == 2 and (s_ // T) + 1 < NT:
            TX(s_ // T + 1)
        if ok(s_):
            A_pe(s_)
        if ok(s_ - 2):
            B_pe(s_ - 2)
        if ok(s_ - 6):
            D_pe(s_ - 6)
        if ok(s_ - 4):
            C_pe(s_ - 4)
        if s_ % 4 == 3:
            P.flush()


def phase_mla(C, pes, src, dst):
    nc, P, dr = C.nc, C.P, C.dr
    with ExitStack() as aes:
        mla_latents_and_attention(C, aes, src)
    mla_outproj(C, pes, src, dst)


def mla_latents_and_attention(C, pes, src):
    nc, P, dr = C.nc, C.P, C.dr
    sb = lambda name, shp, dt=F32: pes.enter_context(nc.sbuf_tensor(name, shp, dt))
    T, W, NT = 4, 512, 8
    w_in = dr["odd_w_in"]
    winR = Reg("a_win")
    w3 = w_in.rearrange("(k p) n -> p k n", p=128)
    wqu = sb("a_wqu", [128, 3, 1536], BF16)
    wqus = sb("a_wqus", [128, 3, 16, 96], BF16)
    wquR = Reg("a_wqu")
    q3 = dr["odd_w_q_up"].rearrange("(k p) n -> p k n", p=128)
    q4 = dr["odd_w_q_up"].rearrange("(k p) (h d) -> p k h d", p=128, h=16)
    wkn = sb("a_wkn", [128, 2, 16, 64], BF16)
    wv = sb("a_wv", [128, 2, 16, 64], BF16)
    wkvR = Reg("a_wkv")
    kv4 = dr["odd_w_kv_up"].rearrange("(k p) (h d) -> p k h d", p=128, h=16)

    def load_attn_weights():
        load_w_bf16(C, wqu[:], q3, wquR)
        for k in range(3):
            for hs in (slice(0, 8), slice(8, 16)):
                load_w_bf16(C, wqus[:, k, hs, 0:64], q4[:, k, hs, 0:64], wquR)
                load_w_bf16(C, wqus[:, k, hs, 64:80], q4[:, k, hs, 80:96], wquR)
                load_w_bf16(C, wqus[:, k, hs, 80:96], q4[:, k, hs, 64:80], wquR)
        for k in range(2):
            for hs in (slice(0, 8), slice(8, 16)):
                load_w_bf16(C, wkn[:, k, hs, :], kv4[:, k, hs, 0:64], wkvR)
                load_w_bf16(C, wv[:, k, hs, :], kv4[:, k, hs, 64:128], wkvR)
    gq = sb("a_gq", [128, 3])
    gkv = sb("a_gkv", [128, 2])
    gR = Reg("a_g")
    with nc.allow_non_contiguous_dma(reason="tiny norm-gain columns"):
        P.dma("sp", lambda: nc.sync.dma_start(out=gq[:], in_=dr["odd_q_norm_g"].rearrange("o (k p) -> p (o k)", p=128)), writes=[gR], part=True)
        P.dma("sp", lambda: nc.sync.dma_start(out=gkv[:], in_=dr["odd_kv_norm_g"].rearrange("o (k p) -> p (o k)", p=128)), writes=[gR], part=True)
        P.flush()
    ones = sb("a_ones", [128, 128], BF16)
    onesR = Reg("a_ones")
    P.op("pool", lambda: nc.gpsimd.memset(ones[:], 1.0), writes=[onesR])
    sel = sb("a_sel", [128, 64])
    selR = Reg("a_sel")
    P.dma("sp", lambda: nc.sync.dma_start(out=sel[:], in_=dr["c_sel"][:, :]), writes=[selR])
    tri = sb("a_tri", [128, 128])
    trib = sb("a_trib", [128, 128], BF16)
    triR = Reg("a_tri")
    tribR = Reg("a_trib")
    P.dma("sp", lambda: nc.sync.dma_start(out=tri[:], in_=dr["c_triu"][:, :]), writes=[triR])
    P.op("pool", lambda: nc.gpsimd.tensor_copy(out=trib[:], in_=tri[:]), reads=[triR], writes=[tribR])
    negm = sb("a_negm", [128, 128], BF16)
    identb = sb("a_identb", [128, 128], BF16)
    mskR = Reg("a_msk")
    P.op("dve", lambda: nc.vector.tensor_scalar(out=negm[:], in0=tri[:], scalar1=-1.0, scalar2=30000.0, op0=ALU.add, op1=ALU.mult),
         reads=[triR], writes=[mskR], part=True)
    P.op("dve", lambda: nc.vector.tensor_copy(out=identb[:], in_=C.ident[:]), reads=[C.identR], writes=[mskR], part=True)

    cosT = sb("a_cos", [128, S])
    sinT = sb("a_sin", [128, S])
    csR = [Reg("a_cs%d" % t) for t in range(NT)]
    cqT = sb("a_cqT", [128, 3, S], BF16)
    ckvT = sb("a_ckvT", [128, 2, S], BF16)
    latR = [Reg("a_lat%d" % t) for t in range(NT)]
    kT = [sb("a_kT%d" % i, [96, S], BF16) for i in range(2)]
    kTropeR = [Reg("a_kTr%d" % t) for t in range(NT)]
    kTR = [Reg("a_kT%d" % i) for i in range(2)]
    t1 = sb("a_t1", [128, W])
    t2 = sb("a_t2", [128, W])
    t1R, t2R = Reg("a_t1"), Reg("a_t2")
    rp = slice(64, 96)
    shared_pes = pes
    pes = ExitStack()
    pes.__enter__()

    wq_in = sb("a_wqin", [128, 8, 384], BF16)
    wkv_in = sb("a_wkvin", [128, 8, 256], BF16)
    wkr = sb("a_wkr", [128, 8, 96], BF16)
    wkrs = sb("a_wkrs", [128, 8, 96], BF16)
    load_w_bf16(C, wq_in[:], w3[:, :, 0:384], winR)
    load_w_bf16(C, wkv_in[:], w3[:, :, 384:640], winR)
    load_w_bf16(C, wkr[:], w3[:, :, 576:672], winR)
    load_w_bf16(C, wkrs[:, :, 0:64], w3[:, :, 576:640], winR)
    load_w_bf16(C, wkrs[:, :, 64:80], w3[:, :, 656:672], winR)
    load_w_bf16(C, wkrs[:, :, 80:96], w3[:, :, 640:656], winR)
    load_attn_weights()
    xin = [sb("a_xin%d" % i, [128, T, D]) for i in range(2)]
    xinR = [[Reg("axin%d_%d" % (i, c)) for c in range(T)] for i in range(2)]
    xT = [sb("a_xT%d" % i, [128, 8, W], BF16) for i in range(2)]
    xTR = [Reg("axT%d" % i) for i in range(2)]
    posi = sb("a_posi", [128, W], I32)
    ang = sb("a_ang", [128, W])
    tq = sb("a_tq", [128, W])
    ki = sb("a_ki", [128, W], I32)
    sq = [sb("a_sq%d" % i, [128, W], BF16) for i in range(2)]
    sqR = [Reg("a_sq%d" % i) for i in range(2)]
    rstd = sb("a_rstd", [128, W])
    rstdR = Reg("a_rstd")
    posR, angR, tqR, kiR = Reg("a_pos"), Reg("a_ang"), Reg("a_tq"), Reg("a_ki")

    def rope_tables(t):
        ts_ = slice(t * W, (t + 1) * W)
        P.dma("sp", lambda ts_=ts_: nc.sync.dma_start(out=posi[rp, :], in_=dr["positions"][0:1, ts_].partition_broadcast(32)), writes=[posR])
        P.op("dve", lambda: nc.vector.tensor_copy(out=ang[rp, :], in_=posi[rp, :]), reads=[posR], writes=[angR])
        P.op("dve", lambda: nc.vector.tensor_scalar(out=ang[rp, :], in0=ang[rp, :], scalar1=C.col[rp, 0:1], scalar2=None, op0=ALU.mult),
             reads=[angR, C.colR], writes=[angR])
        for which in (0, 1):
            off = 0.0 if which == 0 else PI / 2
            P.op("dve", lambda off=off: nc.vector.tensor_scalar(out=tq[rp, :], in0=ang[rp, :], scalar1=off, scalar2=1.0 / TWO_PI,
                                                               op0=ALU.add, op1=ALU.mult), reads=[angR], writes=[tqR])
            P.op("dve", lambda: nc.vector.tensor_copy(out=ki[rp, :], in_=tq[rp, :]), reads=[tqR], writes=[kiR])
            P.op("dve", lambda: nc.vector.tensor_copy(out=tq[rp, :], in_=ki[rp, :]), reads=[kiR], writes=[tqR])
            P.op("dve", lambda: nc.vector.scalar_tensor_tensor(out=tq[rp, :], in0=tq[rp, :], scalar=-TWO_PI, in1=ang[rp, :],
                                                              op0=ALU.mult, op1=ALU.add), reads=[tqR, angR], writes=[tqR])
            P.op("dve", lambda off=off: nc.vector.tensor_scalar(out=tq[rp, :], in0=tq[rp, :], scalar1=off, scalar2=PI,
                                                               op0=ALU.add, op1=ALU.min), reads=[tqR], writes=[tqR])
            P.op("dve", lambda: nc.vector.tensor_scalar(out=tq[rp, :], in0=tq[rp, :], scalar1=-PI, scalar2=None, op0=ALU.max),
                 reads=[tqR], writes=[tqR])
            if which == 0:
                P.op("act", lambda ts_=ts_: nc.scalar.activation(out=sinT[rp, ts_], in_=tq[rp, :], func=AF.Sin, scale=C.col[rp, 1:2]),
                     reads=[tqR, C.colR], writes=[csR[t]], part=True)
            else:
                P.op("act", lambda ts_=ts_: nc.scalar.activation(out=cosT[rp, ts_], in_=tq[rp, :], func=AF.Sin),
                     reads=[tqR], writes=[csR[t]], part=True)

    def proj_mms(t, wt, nblk, bk0):
        b = t % 2
        for kb in range(nblk):
            bk = bk0 + kb
            for k in range(8):
                P.op("pe", lambda k=k, kb=kb, bk=bk, wt=wt, b=b: nc.tensor.matmul(
                    C.bank[bk][:, 0:W], wt[:, k, kb * 128:(kb + 1) * 128], xT[b][:, k, :], start=(k == 0), stop=(k == 7)),
                    reads=[winR, xTR[b]], writes=[C.bR[bk]])

    def rms_post(t, nblk, dstL, gcol, inv_n, bk0):
        ts_ = slice(t * W, (t + 1) * W)
        SSB = 5
        for kb in range(nblk):
            bk = bk0 + kb
            si = kb % 2
            P.op("act", lambda bk=bk, si=si: nc.scalar.activation(out=sq[si][:], in_=C.bank[bk][:, 0:W], func=AF.Square),
                 reads=[C.bR[bk]], writes=[sqR[si]])
            P.op("pe", lambda si=si, kb=kb, nblk=nblk: nc.tensor.matmul(C.bank[SSB][:, 0:W], ones[:], sq[si][:],
                                                                       start=(kb == 0), stop=(kb == nblk - 1)),
                 reads=[onesR, sqR[si]], writes=[C.bR[SSB]])
        P.op("act", lambda inv_n=inv_n: nc.scalar.activation(out=rstd[:], in_=C.bank[SSB][:, 0:W], func=AF.Sqrt,
                                                            bias=C.col[:, 3:4], scale=inv_n),
             reads=[C.bR[SSB], C.colR], writes=[rstdR])
        P.op("dve", lambda: nc.vector.reciprocal(out=rstd[:], in_=rstd[:]), reads=[rstdR], writes=[rstdR])
        for kb in range(nblk):
            bk = bk0 + kb
            P.op("dve", lambda kb=kb, bk=bk, dstL=dstL, gcol=gcol, ts_=ts_: nc.vector.scalar_tensor_tensor(
                out=dstL[:, kb, ts_], in0=C.bank[bk][:, 0:W], scalar=gcol[:, kb:kb + 1], in1=rstd[:], op0=ALU.mult, op1=ALU.mult),
                reads=[C.bR[bk], gR, rstdR], writes=[latR[t]], part=True)

    rope_tables(0)
    load_transpose_tile(C, src, 0, T, xin[0], xinR[0], xT[0], xTR[0], (0, 1))
    for t in range(NT):
        b = t % 2
        ts_ = slice(t * W, (t + 1) * W)
        for (wt, bk) in ((wkr, 0), (wkrs, 1)):
            for k in range(8):
                P.op("pe", lambda k=k, wt=wt, bk=bk, b=b: nc.tensor.matmul(C.bank[bk][0:96, 0:W], wt[:, k, :], xT[b][:, k, :],
                                                                          start=(k == 0), stop=(k == 7)),
                     reads=[winR, xTR[b]], writes=[C.bR[bk]])
        proj_mms(t, wq_in, 3, 2)
        P.op("dve", lambda ts_=ts_: nc.vector.tensor_tensor(out=t1[rp, :], in0=C.bank[0][rp, 0:W], in1=cosT[rp, ts_], op=ALU.mult),
             reads=[C.bR[0], csR[t]], writes=[t1R])
        P.op("dve", lambda ts_=ts_: nc.vector.tensor_tensor(out=t2[rp, :], in0=C.bank[1][rp, 0:W], in1=sinT[rp, ts_], op=ALU.mult),
             reads=[C.bR[1], csR[t]], writes=[t2R])
        for i in range(2):
            P.op("pool", lambda i=i, ts_=ts_: nc.gpsimd.tensor_tensor(out=kT[i][rp, ts_], in0=t1[rp, :], in1=t2[rp, :], op=ALU.add),
                 reads=[t1R, t2R], writes=[kTropeR[t]], part=True)
        proj_mms(t, wkv_in, 2, 6)
        rms_post(t, 3, cqT, gq, 1.0 / 384, 2)
        if t + 1 < NT:
            load_transpose_tile(C, src, (t + 1) * T, T, xin[1 - b], xinR[1 - b], xT[1 - b], xTR[1 - b], (0, 1))
        rms_post(t, 2, ckvT, gkv, 1.0 / 256, 6)
        if t + 1 < NT:
            rope_tables(t + 1)

    P.barrier(C.marks)
    pes.__exit__(None, None, None)
    pes = ExitStack()
    pes.__enter__()
    qT = [sb("a_qT%d" % i, [96, S], BF16) for i in range(2)]
    qTR = [Reg("a_qT%d" % i) for i in range(2)]
    Vgs = [sb("a_V%d" % i, [128, NCH, 4, 65], BF16) for i in range(2)]
    VgRs = [Reg("a_V%d" % i) for i in range(2)]
    VoneR = Reg("a_Vone")
    for i in range(2):
        P.op("pool", lambda i=i: nc.gpsimd.memset(Vgs[i][:, :, :, 64:65], 1.0), writes=[VoneR], part=True)
    pT = [sb("a_pT%d" % i, [128, 1024], BF16) for i in range(3)]
    pTR = [Reg("a_pT%d" % i) for i in range(3)]
    oa = [sb("a_oa%d" % i, [128, W]) for i in range(2)]
    oaR = [Reg("a_oa%d" % i) for i in range(2)]
    oTh = [sb("a_oTh%d" % i, [64, S], BF16) for i in range(2)]
    oThR = [Reg("a_oTh%d" % i) for i in range(2)]
    allLat = latR + csR + kTropeR
    OB = (0, 1)
    SB3 = ((2, 3), (4, 5), (6, 7))
    npt = 0
    noa = 0
    def gen_v(hg):
        Vg, VgR = Vgs[hg % 2], VgRs[hg % 2]
        for c2 in range(NCH // 2):
            bk = 2 + (c2 % 2)
            for cc in range(2):
                c = 2 * c2 + cc
                for k in range(2):
                    P.op("pe", lambda k=k, c=c, cc=cc, bk=bk, hg=hg: nc.tensor.matmul(
                        C.bank[bk][:, cc * 256:(cc + 1) * 256], ckvT[:, k, c * 128:(c + 1) * 128],
                        wv[:, k, hg * 4:(hg + 1) * 4, :].rearrange("p h d -> p (h d)"), start=(k == 0), stop=(k == 1)),
                        reads=[wkvR] + latR, writes=[C.bR[bk]])
            P.op("dve", lambda c2=c2, bk=bk: nc.vector.tensor_copy(
                out=Vg[:, 2 * c2:2 * c2 + 2, :, 0:64], in_=C.bank[bk][:, 0:512].rearrange("p (c h d) -> p c h d", c=2, h=4)),
                reads=[C.bR[bk]], writes=[VgR], part=True)

    def gen_qk(h, t, pair=(4, 5)):
        b = h % 2
        ts_ = slice(t * W, (t + 1) * W)
        BQ, BS_, BK = pair[0], pair[1], pair[0]
        for (wt, bk) in ((None, BQ), (wqus, BS_)):
            for k in range(3):
                lhs = wqu[:, k, h * 96:(h + 1) * 96] if wt is None else wqus[:, k, h, :]
                P.op("pe", lambda k=k, lhs=lhs, bk=bk, ts_=ts_: nc.tensor.matmul(C.bank[bk][0:96, 0:W], lhs, cqT[:, k, ts_],
                                                                                 start=(k == 0), stop=(k == 2)),
                     reads=[wquR] + latR, writes=[C.bR[bk]])
        P.op("dve", lambda ts_=ts_, b=b: nc.vector.tensor_copy(out=qT[b][0:64, ts_], in_=C.bank[BQ][0:64, 0:W]),
             reads=[C.bR[BQ]], writes=[qTR[b]], part=True)
        P.op("dve", lambda ts_=ts_: nc.vector.tensor_tensor(out=t1[rp, :], in0=C.bank[BQ][rp, 0:W], in1=cosT[rp, ts_], op=ALU.mult),
             reads=[C.bR[BQ]] + csR, writes=[t1R])
        P.op("dve", lambda ts_=ts_: nc.vector.tensor_tensor(out=t2[rp, :], in0=C.bank[BS_][rp, 0:W], in1=sinT[rp, ts_], op=ALU.mult),
             reads=[C.bR[BS_]] + csR, writes=[t2R])
        P.op("pool", lambda ts_=ts_, b=b: nc.gpsimd.tensor_tensor(out=qT[b][rp, ts_], in0=t1[rp, :], in1=t2[rp, :], op=ALU.add),
             reads=[t1R, t2R], writes=[qTR[b]], part=True)
        for k in range(2):
            P.op("pe", lambda k=k, ts_=ts_: nc.tensor.matmul(C.bank[BK][0:64, 0:W], wkn[:, k, h, :], ckvT[:, k, ts_],
                                                             start=(k == 0), stop=(k == 1)),
                 reads=[wkvR] + latR, writes=[C.bR[BK]])
        P.op("dve", lambda ts_=ts_, b=b: nc.vector.tensor_copy(out=kT[b][0:64, ts_], in_=C.bank[BK][0:64, 0:W]),
             reads=[C.bR[BK]], writes=[kTR[b]], part=True)

    def emit_S(it):
        sb2 = it["sb2"]
        if it["kind"] == "gv":
            hg2, c2 = it["hg2"], it["c2"]
            for cc in range(2):
                c = 2 * c2 + cc
                for k in range(2):
                    P.op("pe", lambda k=k, c=c, cc=cc, sb2=sb2, hg2=hg2: nc.tensor.matmul(
                        C.bank[sb2[0]][:, cc * 256:(cc + 1) * 256], ckvT[:, k, c * 128:(c + 1) * 128],
                        wv[:, k, hg2 * 4:(hg2 + 1) * 4, :].rearrange("p h d -> p (h d)"), start=(k == 0), stop=(k == 1)),
                        reads=[wkvR] + latR, writes=[C.bR[sb2[0]]])
            return
        if it["kind"] in ("g1", "g2"):
            h2 = it["h2"]
            ts_ = slice(it["t"] * W, (it["t"] + 1) * W)
            if it["kind"] == "g1":
                for k in range(3):
                    P.op("pe", lambda k=k, ts_=ts_, h2=h2, sb2=sb2: nc.tensor.matmul(
                        C.bank[sb2[0]][0:96, 0:W], wqu[:, k, h2 * 96:(h2 + 1) * 96], cqT[:, k, ts_], start=(k == 0), stop=(k == 2)),
                        reads=[wquR] + latR, writes=[C.bR[sb2[0]]])
                for k in range(2):
                    P.op("pe", lambda k=k, ts_=ts_, h2=h2, sb2=sb2: nc.tensor.matmul(
                        C.bank[sb2[1]][0:64, 0:W], wkn[:, k, h2, :], ckvT[:, k, ts_], start=(k == 0), stop=(k == 1)),
                        reads=[wkvR] + latR, writes=[C.bR[sb2[1]]])
            else:
                for k in range(3):
                    P.op("pe", lambda k=k, ts_=ts_, h2=h2, sb2=sb2: nc.tensor.matmul(
                        C.bank[sb2[0]][0:96, 0:W], wqus[:, k, h2, :], cqT[:, k, ts_], start=(k == 0), stop=(k == 2)),
                        reads=[wquR] + latR, writes=[C.bR[sb2[0]]])
            return
        b, q0 = it["b"], it["q0"]
        if it["kind"] == "off":
            for u in range(2):
                kt = it["kt0"] + u
                P.op("pe", lambda kt=kt, u=u, sb2=sb2, b=b, q0=q0: nc.tensor.matmul(
                    C.bank[sb2[u]][:, 0:W], kT[b][:, kt * 128:(kt + 1) * 128], qT[b][:, q0:q0 + W], start=True, stop=True),
                    reads=[kTR[b], qTR[b]] + kTropeR, writes=[C.bR[sb2[u]]])
        else:
            kt, n0 = it["kt"], it["n0"]
            P.op("pe", lambda kt=kt, sb2=sb2, b=b, q0=q0, n0=n0: nc.tensor.matmul(
                C.bank[sb2[0]][:, n0:W], kT[b][:, kt * 128:(kt + 1) * 128], qT[b][:, q0 + n0:q0 + W], start=True, stop=False),
                reads=[kTR[b], qTR[b]] + kTropeR, writes=[C.bR[sb2[0]]])
            P.op("pe", lambda sb2=sb2, n0=n0: nc.tensor.matmul(
                C.bank[sb2[0]][:, n0:n0 + 128], identb[:], negm[:], start=False, stop=True),
                reads=[mskR], writes=[C.bR[sb2[0]]])

    def emit_EP(it):
        sb2 = it["sb2"]
        if it["kind"] == "gv":
            hg2, c2 = it["hg2"], it["c2"]
            P.op("dve", lambda c2=c2, sb2=sb2, hg2=hg2: nc.vector.tensor_copy(
                out=Vgs[hg2 % 2][:, 2 * c2:2 * c2 + 2, :, 0:64],
                in_=C.bank[sb2[0]][:, 0:512].rearrange("p (c h d) -> p c h d", c=2, h=4)),
                reads=[C.bR[sb2[0]]], writes=[VgRs[hg2 % 2]], part=True)
            return
        if it["kind"] in ("g1", "g2"):
            b2 = it["h2"] % 2
            ts_ = slice(it["t"] * W, (it["t"] + 1) * W)
            if it["kind"] == "g1":
                P.op("act", lambda ts_=ts_, b2=b2, sb2=sb2: nc.scalar.copy(out=qT[b2][0:64, ts_], in_=C.bank[sb2[0]][0:64, 0:W]),
                     reads=[C.bR[sb2[0]]], writes=[qTR[b2]], part=True)
                P.op("dve", lambda ts_=ts_, sb2=sb2: nc.vector.tensor_tensor(out=t1[rp, :], in0=C.bank[sb2[0]][rp, 0:W], in1=cosT[rp, ts_], op=ALU.mult),
                     reads=[C.bR[sb2[0]]] + csR, writes=[t1R])
                P.op("act", lambda ts_=ts_, b2=b2, sb2=sb2: nc.scalar.copy(out=kT[b2][0:64, ts_], in_=C.bank[sb2[1]][0:64, 0:W]),
                     reads=[C.bR[sb2[1]]], writes=[kTR[b2]], part=True)
            else:
                P.op("dve", lambda ts_=ts_, sb2=sb2: nc.vector.tensor_tensor(out=t2[rp, :], in0=C.bank[sb2[0]][rp, 0:W], in1=sinT[rp, ts_], op=ALU.mult),
                     reads=[C.bR[sb2[0]]] + csR, writes=[t2R])
                P.op("pool", lambda ts_=ts_, b2=b2: nc.gpsimd.tensor_tensor(out=qT[b2][rp, ts_], in0=t1[rp, :], in1=t2[rp, :], op=ALU.add),
                     reads=[t1R, t2R], writes=[qTR[b2]], part=True)
            return
        pi, ob, hh = it["pi"], it["ob"], it["hh"]
        Vg, VgR = Vgs[it["hg"] % 2], VgRs[it["hg"] % 2]
        if it["kind"] == "off":
            P.op("act", lambda sb2=sb2, pi=pi: nc.scalar.activation(out=pT[pi][:, 0:1024], in_=C.ps[:, sb2[0] * 512:sb2[0] * 512 + 1024],
                                                                   func=AF.Exp, scale=SCALE),
                 reads=[C.bR[sb2[0]], C.bR[sb2[1]]], writes=[pTR[pi]])
            for u in range(2):
                kt = it["kt0"] + u
                st = it["first"] and u == 0
                P.op("pe", lambda kt=kt, u=u, pi=pi, ob=ob, hh=hh, st=st: nc.tensor.matmul(
                    C.bank[ob][0:65, 0:W], Vg[:, kt, hh, :], pT[pi][:, u * W:(u + 1) * W], start=st, stop=False),
                    reads=[VgR, VoneR, pTR[pi]], writes=[C.bR[ob]])
        else:
            kt, n0 = it["kt"], it["n0"]
            P.op("act", lambda sb2=sb2, pi=pi, n0=n0: nc.scalar.activation(out=pT[pi][:, n0:W], in_=C.bank[sb2[0]][:, n0:W],
                                                                          func=AF.Exp, scale=SCALE),
                 reads=[C.bR[sb2[0]]], writes=[pTR[pi]])
            P.op("pe", lambda kt=kt, pi=pi, ob=ob, hh=hh, n0=n0, st=it["first"], sp_=it["last"]: nc.tensor.matmul(
                C.bank[ob][0:65, n0:W], Vg[:, kt, hh, :], pT[pi][:, n0:W], start=st, stop=sp_),
                reads=[VgR, VoneR, pTR[pi]], writes=[C.bR[ob]])

    def emit_norm_pre(h, j, ob):
        nonlocal noa
        oi = noa % 2
        noa += 1
        P.op("dve", lambda oi=oi, ob=ob: nc.vector.tensor_copy(out=oa[oi][0:65, :], in_=C.bank[ob][0:65, 0:W]),
             reads=[C.bR[ob]], writes=[oaR[oi]])
        return oi

    def emit_norm_recip(oi, q):
        qs = slice(q * 128, (q + 1) * 128)
        P.op("dve", lambda oi=oi, qs=qs: nc.vector.reciprocal(out=oa[oi][64:65, qs], in_=oa[oi][64:65, qs]),
             reads=[oaR[oi]], writes=[oaR[oi]])

    def emit_norm_post(h, j, ob, oi):
        ob_i = h % 2
        q0 = j * W
        P.op("pe", lambda oi=oi, ob=ob: nc.tensor.matmul(C.bank[ob][0:64, 0:W], sel[0:65, :], oa[oi][0:65, :], start=True, stop=True),
             reads=[selR, oaR[oi]], writes=[C.bR[ob]])
        P.op("dve", lambda oi=oi, ob=ob, ob_i=ob_i, q0=q0: nc.vector.tensor_tensor(
            out=oTh[ob_i][:, q0:q0 + W], in0=oa[oi][0:64, :], in1=C.bank[ob][0:64, 0:W], op=ALU.mult),
            reads=[oaR[oi], C.bR[ob]], writes=[oThR[ob_i]], part=True)
        if j == 7:
            P.dma("sp", lambda h=h, ob_i=ob_i: nc.sync.dma_start(out=C.dr["oTd"][h * 64:(h + 1) * 64, :], in_=oTh[ob_i][:]),
                  reads=[oThR[ob_i]])

    LA = 2
    pend = []
    gen_v(0)
    for t in range(NT):
        gen_qk(0, t)
    for h in range(_dbg_heads()):
        hg, hh = divmod(h, 4)
        b = h % 2
        items = []
        for j in range(8):
            ob = OB[j % 2]
            first = True
            for kt0 in range(0, 4 * j, 2):
                items.append(dict(kind="off", kt0=kt0, b=b, q0=j * W, ob=ob, hh=hh, hg=hg, first=first, last=False, j=j,
                                  sb2=SB3[npt % 3], pi=npt % 3))
                npt += 1
                first = False
            for r in range(4):
                items.append(dict(kind="diag", kt=4 * j + r, n0=128 * r, b=b, q0=j * W, ob=ob, hh=hh, hg=hg, first=first,
                                  last=(r == 3), j=j, sb2=SB3[npt % 3], pi=npt % 3))
                npt += 1
                first = False
            if h + 1 < NH and hh == 3:
                for c2 in (2 * j, 2 * j + 1):
                    items.append(dict(kind="gv", hg2=hg + 1, c2=c2, sb2=SB3[npt % 3], pi=npt % 3))
                    npt += 1
            if h + 1 < NH:
                for kind in ("g1", "g2"):
                    items.append(dict(kind=kind, h2=h + 1, t=j, sb2=SB3[npt % 3], pi=npt % 3))
                    npt += 1
        n = len(items)
        for i in range(n + LA):
            if i < n:
                emit_S(items[i])
            if i - LA >= 0:
                it = items[i - LA]
                emit_EP(it)
                npend = []
                for (stg, args) in pend:
                    if stg < 0:
                        npend.append((stg + 1, args))
                    elif stg == 0:
                        args[3] = emit_norm_pre(args[0], args[1], args[2])
                        npend.append((1, args))
                    elif stg <= 4:
                        emit_norm_recip(args[3], stg - 1)
                        npend.append((stg + 1, args))
                    else:
                        emit_norm_post(*args)
                pend = npend
                if it.get("last"):
                    has_gen = (h + 1 < NH)
                    if has_gen:
                        pend.append((-1, [h, it["j"], it["ob"], None]))
                    else:
                        oi = emit_norm_pre(h, it["j"], it["ob"])
                        pend.append((1, [h, it["j"], it["ob"], oi]))
        if h + 2 < NH:
            pend = [(max(stg, 0), args) for (stg, args) in pend]
        else:
            for (stg, args) in pend:
                if stg <= 0:
                    args[3] = emit_norm_pre(args[0], args[1], args[2])
                    stg = 1
                for q in range(stg - 1, 4):
                    emit_norm_recip(args[3], q)
                emit_norm_post(*args)
            pend = []
        P.flush()
    P.barrier(C.marks)
    pes.__exit__(None, None, None)


def mla_outproj(C, pes, src, dst):
    nc, P, dr = C.nc, C.P, C.dr
    sb = lambda name, shp, dt=F32: pes.enter_context(nc.sbuf_tensor(name, shp, dt))
    T, W, NT = 4, 512, 8
    wout = sb("o_wout", [128, 8, D], BF16)
    woutR = [Reg("o_wout%d" % k) for k in range(8)]
    for k in range(8):
        load_w_bf16(C, wout[:, k, :], dr["odd_w_out"][k * 128:(k + 1) * 128, :], woutR[k])
    L = ln_setup(C, pes, dr["mix_ln_g"][1:2, :], dr["mix_ln_b"][1:2, :], "o")
    xin = [sb("o_xin%d" % i, [128, T, D]) for i in range(2)]
    xinR = [[Reg("oxin%d_%d" % (i, c)) for c in range(T)] for i in range(2)]
    oTt = [sb("o_oT%d" % i, [128, 8, W], BF16) for i in range(2)]
    oTtR = [Reg("o_oT%d" % i) for i in range(2)]
    o3 = dr["oTd"].rearrange("(k p) t -> p k t", p=128)
    def loads(t):
        b = t % 2
        rows = src[t * W:(t + 1) * W, :].rearrange("(c p) d -> p c d", p=128)
        P.dma("sp", lambda rows=rows, b=b: nc.sync.dma_start(out=oTt[b][:], in_=o3[:, :, t * W:(t + 1) * W]), writes=[oTtR[b]])
        P.dma("sp", lambda rows=rows, b=b: nc.sync.dma_start(out=xin[b][:], in_=rows), writes=xinR[b])

    loads(0)
    for t in range(NT):
        b = t % 2
        if t + 1 < NT:
            loads(t + 1)
        for c in range(T):
            cs = slice(c * 128, (c + 1) * 128)
            pair = (0, 1) if c % 2 == 0 else (2, 3)
            for half in range(2):
                bk = pair[half]
                for k in range(8):
                    P.op("pe", lambda k=k, half=half, bk=bk, cs=cs, b=b: nc.tensor.matmul(
                        C.bank[bk][:, 0:512], oTt[b][:, k, cs], wout[:, k, half * 512:(half + 1) * 512],
                        start=(k == 0), stop=(k == 7)), reads=[oTtR[b], woutR[k]], writes=[C.bR[bk]])
            gc = t * T + c
            ln_epilogue(C, L, pair[0], pair[1], xin[b][:, c, :], xinR[b][c], dst[gc * 128:(gc + 1) * 128, :])


_NC_CACHE = {}


def _get_nc(phases, standalone):
    key = (tuple(phases), standalone)
    if key not in _NC_CACHE:
        _NC_CACHE[key] = build(list(phases), standalone)
    return _NC_CACHE[key]


def _weight_maps(inputs):
    m = {}
    for name, shp in W_SPECS.items():
        m[name] = np.ascontiguousarray(np.asarray(inputs[name], dtype=np.float32).reshape(shp))
    m.update(host_consts())
    return m


FUSED = True


def kernel(**inputs):
    x = np.asarray(inputs["x"], dtype=np.float32)
    pos = np.asarray(inputs["positions"], dtype=np.int32)
    wm = _weight_maps(inputs)
    n = 8
    if FUSED:
        nc = _get_nc((1, 2, 3, 4), False)
        in_maps = []
        for b in range(n):
            d = dict(wm)
            d["x"] = np.ascontiguousarray(x[b])
            d["positions"] = np.ascontiguousarray(pos[b:b + 1])
            in_maps.append(d)
        res = run_bass_kernel_spmd(nc, in_maps, core_ids=list(range(n)))
        return np.stack([res.results[b]["out"] for b in range(n)], axis=0)
    cur = [np.ascontiguousarray(x[b]) for b in range(n)]
    for ph in (1, 2, 3, 4):
        nc = _get_nc((ph,), True)
        in_maps = []
        for b in range(n):
            d = dict(wm)
            d["src"] = cur[b]
            d["positions"] = np.ascontiguousarray(pos[b:b + 1])
            in_maps.append(d)
        res = run_bass_kernel_spmd(nc, in_maps, core_ids=list(range(n)))
        cur = [np.ascontiguousarray(res.results[b]["dst"]) for b in range(n)]
    return np.stack(cur, axis=0)
```

```python
import numpy as np
from contextlib import ExitStack
import concourse.bass as bass
import concourse.mybir as mybir
from concourse.bass_utils import run_bass_kernel_spmd

F32 = mybir.dt.float32
BF16 = mybir.dt.bfloat16
I32 = mybir.dt.int32
AF = mybir.ActivationFunctionType
ALU = mybir.AluOpType

S = 4096
D = 1024
NCH = S // 128
DFF = 2816
NFF = DFF // 128
ALPHA = float((2 * 2) ** 0.25)
LN_EPS = 1e-5
RMS_EPS = 1e-6
TWO_PI = float(2 * np.pi)
PI = float(np.pi)
NH = 16
SCALE = float(96 ** -0.5)


class Reg:
    __slots__ = ("name", "excl", "writers", "readers")

    def __init__(self, name, excl=False):
        self.name = name
        self.excl = excl
        self.writers = []
        self.readers = []


class Ins:
    __slots__ = ("eng", "fn", "dma", "deps", "signal", "count", "dsem", "dval", "seq", "emitted", "xw")

    def __init__(self, eng, fn, dma):
        self.eng = eng
        self.fn = fn
        self.dma = dma
        self.deps = []
        self.signal = False
        self.count = None
        self.dsem = None
        self.dval = None
        self.seq = None
        self.emitted = False
        self.xw = []


class Prog:
    def __init__(self, nc, es):
        self.nc = nc
        self.E = {"pe": nc.tensor, "act": nc.scalar, "dve": nc.vector, "pool": nc.gpsimd, "sp": nc.sync}
        self.sem = {e: es.enter_context(nc.semaphore("c_" + e)) for e in ("pe", "act", "dve", "pool")}
        self.cnt = {e: 0 for e in self.sem}
        self.dsems = {}
        for q, n in (("sp", 12), ("pool", 8), ("act", 4)):
            self.dsems[q] = [es.enter_context(nc.semaphore("d_%s%d" % (q, i))) for i in range(n)]
        self.dnext = {q: 0 for q in self.dsems}
        self.dval = {}
        self.dlast = {}
        self.pending = []
        self.seq = {e: 0 for e in self.E}
        self.sig_hist = {e: [] for e in self.sem}
        self.waited = {e: {x: 0 for x in self.sem} for e in self.E}
        self.dobs = {e: {} for e in self.E}
        self.extra = {e: [] for e in self.E}
        self.last = {e: None for e in self.E}

    def _add(self, eng, fn, reads, writes, dma, part, strict=False):
        I = Ins(eng, fn, dma)
        I.seq = self.seq[eng]
        self.seq[eng] += 1
        deps = []
        for r in reads:
            if r.excl:
                deps += [(d, "raw") for d in r.writers] + [(d, "war") for d in r.readers]
            else:
                deps += [(d, "raw") for d in r.writers]
        for w in writes:
            if part and not w.excl:
                deps += [(d, "war") for d in w.readers]
            else:
                deps += [(d, "war") for d in w.readers] + [(d, "waw") for d in w.writers]
        for d in self.extra[eng]:
            deps.append((d, "raw"))
        self.extra[eng] = []
        if dma:
            q = eng
            sems = self.dsems[q]
            sem = sems[self.dnext[q] % len(sems)]
            self.dnext[q] += 1
            prev = self.dlast.get(id(sem))
            if prev is not None:
                deps.append((prev, "raw"))
            I.dsem = sem
            I.dval = self.dval.get(id(sem), 0) + 16
            self.dval[id(sem)] = I.dval
            self.dlast[id(sem)] = I
        seen = set()
        for d, kind in deps:
            if d is I or id(d) in seen:
                continue
            if d.dma or dma:
                pass
            elif d.eng == eng:
                if eng == "pe" or (kind != "raw" and not strict):
                    continue
            seen.add(id(d))
            I.deps.append(d)
            if not d.dma and not d.emitted:
                d.signal = True
        for r in reads:
            if not dma:
                r.readers = [x for x in r.readers if x.dma or x.eng != eng]
            r.readers.append(I)
        for w in writes:
            if part and not w.excl:
                if w.readers:
                    w.writers = [I]
                    w.readers = []
                else:
                    if not dma:
                        w.writers = [x for x in w.writers if x.dma or x.eng != eng]
                    w.writers.append(I)
            else:
                w.writers = [I]
                w.readers = []
        self.pending.append(I)
        self.last[eng] = I
        return I

    def op(self, eng, fn, reads=(), writes=(), part=False, strict=False):
        return self._add(eng, fn, list(reads), list(writes), False, part, strict)

    def dma(self, eng, fn, reads=(), writes=(), part=False):
        return self._add(eng, fn, list(reads), list(writes), True, part)

    def _count_of(self, d):
        if d.count is not None:
            return d.count
        for seq, c in self.sig_hist[d.eng]:
            if seq >= d.seq:
                return c
        raise RuntimeError("no signal after dep on %s" % d.eng)

    def flush(self):
        lastp = {}
        for I in self.pending:
            if not I.dma and I.eng in self.sem:
                lastp[I.eng] = I
        for I in lastp.values():
            I.signal = True
        for I in self.pending:
            e = I.eng
            eng = self.E[e]
            waits = []
            need_c = {}
            for d in I.deps:
                if d.dma:
                    k = id(d.dsem)
                    if self.dobs[e].get(k, 0) < d.dval:
                        self.dobs[e][k] = d.dval
                        waits.append((d.dsem, d.dval))
                else:
                    c = self._count_of(d)
                    if c > need_c.get(d.eng, 0):
                        need_c[d.eng] = c
            for x, c in need_c.items():
                if self.waited[e][x] < c:
                    self.waited[e][x] = c
                    waits.append((self.sem[x], c))
            for sem, val in I.xw:
                k = id(sem)
                if self.dobs[e].get(k, 0) < val:
                    self.dobs[e][k] = val
                    waits.append((sem, val))
            best = {}
            for sem, val in waits:
                k = id(sem)
                if k not in best or best[k][1] < val:
                    best[k] = (sem, val)
            waits = list(best.values())
            while len(waits) > 2:
                a = waits.pop()
                b = waits.pop()
                eng.wait_ge(a[0], a[1])
                eng.wait_ge(b[0], b[1])
                eng.nop()
            for sem, val in waits:
                eng.wait_ge(sem, val)
            bi = I.fn()
            if I.dma:
                bi.then_inc(I.dsem, 16)
            elif I.signal:
                self.cnt[e] += 1
                I.count = self.cnt[e]
                bi.then_inc(self.sem[e], 1)
                self.sig_hist[e].append((I.seq, I.count))
            I.emitted = True
            I.fn = None
        self.pending = []
        for e in self.sig_hist:
            if len(self.sig_hist[e]) > 4:
                self.sig_hist[e] = self.sig_hist[e][-4:]

    def all_dma_waits(self):
        out = []
        for q in self.dsems:
            for sem in self.dsems[q]:
                v = self.dval.get(id(sem), 0)
                if v:
                    out.append((sem, v))
        return out

    def barrier(self, marks):
        ms = []
        for e in ("act", "dve", "pool"):
            m = self.op(e, marks[e])
            m.xw = self.all_dma_waits()
            m.signal = True
            ms.append(m)
        for e in ("act", "dve", "pool", "sp"):
            self.extra[e] = list(ms)
        self.flush()

    def final_wait(self):
        eng = self.E["sp"]
        for sem, val in self.all_dma_waits():
            eng.wait_ge(sem, val)
            eng.nop()


CONST_SPECS = {
    "c_ident": ([128, 128], F32),
    "c_triu": ([128, 128], F32),
    "c_poolA": ([12, 128, 128], F32),
    "c_col": ([128, 4], F32),
    "c_sel": ([128, 64], F32),
}

W_SPECS = {
    "even_w_in": [1024, 1536], "even_vnorm_g": [1, 512], "even_vnorm_b": [1, 512],
    "even_spatial_w": [8, 128, 128], "even_spatial_b": [8, 128], "even_pool_w": [4, 128, 128],
    "even_pool_scale": [1, 512], "even_w_out": [1024, 1024],
    "odd_w_in": [1024, 672], "odd_q_norm_g": [1, 384], "odd_w_q_up": [384, 1536],
    "odd_kv_norm_g": [1, 256], "odd_w_kv_up": [256, 2048], "odd_w_out": [1024, 1024],
    "mix_ln_g": [2, 1024], "mix_ln_b": [2, 1024], "ffn_w_gate_up": [2, 1024, 5632],
    "ffn_w_down": [2, 2816, 1024], "ffn_ln_g": [2, 1024], "ffn_ln_b": [2, 1024],
}


def host_consts():
    ident = np.eye(128, dtype=np.float32)
    triu = np.triu(np.ones((128, 128), dtype=np.float32))
    A = np.zeros((12, 128, 128), dtype=np.float32)
    s = np.arange(128)[:, None]
    t = np.arange(128)[None, :]
    for g, w in enumerate((2, 4, 8, 16)):
        A[g] = ((s <= t) & (s > t - w)) / np.float32(w) - (s == t)
        A[4 + g] = (s >= 128 + t - w + 1) / np.float32(w)
        cnt = np.minimum(t + 1, w).astype(np.float32)
        A[8 + g] = ((s <= t) & (s > t - w)) / cnt - (s == t)
    col = np.zeros((128, 4), dtype=np.float32)
    freqs = (10000.0 ** (-np.arange(0, 32, 2, dtype=np.float32) / 32)).astype(np.float32)
    col[64:80, 0] = freqs
    col[80:96, 0] = freqs
    col[:, 1] = 1.0
    col[64:80, 1] = -1.0
    col[:, 2] = LN_EPS
    col[:, 3] = RMS_EPS
    sel = np.zeros((128, 64), dtype=np.float32)
    sel[64, :] = 1.0
    return {"c_ident": ident, "c_triu": triu, "c_poolA": A.astype(np.float32), "c_col": col, "c_sel": sel}


class Ctx:
    pass


def _dbg_heads():
    import os
    return int(os.environ.get("MK_DBG_HEADS", NH))


def build(phases, standalone):
    nc = bass.Bass("TRN2", target_bir_lowering=False)
    dr = {}
    for name, shp in W_SPECS.items():
        dr[name] = nc.dram_tensor(name, shp, F32, kind="ExternalInput").ap()
    for name, (shp, dt) in CONST_SPECS.items():
        dr[name] = nc.dram_tensor(name, shp, dt, kind="ExternalInput").ap()
    dr["positions"] = nc.dram_tensor("positions", [1, S], I32, kind="ExternalInput").ap()
    if standalone:
        src = nc.dram_tensor("src", [S, D], F32, kind="ExternalInput").ap()
        dst = nc.dram_tensor("dst", [S, D], F32, kind="ExternalOutput").ap()
        chain = {phases[0]: (src, dst)}
    else:
        x = nc.dram_tensor("x", [S, D], F32, kind="ExternalInput").ap()
        out = nc.dram_tensor("out", [S, D], F32, kind="ExternalOutput").ap()
        s1 = nc.dram_tensor("scr1", [S, D], F32).ap()
        s2 = nc.dram_tensor("scr2", [S, D], F32).ap()
        s3 = nc.dram_tensor("scr3", [S, D], F32).ap()
        chain = {1: (x, s1), 2: (s1, s2), 3: (s2, s3), 4: (s3, out)}
    dr["oTd"] = nc.dram_tensor("scr_oT", [D, S], BF16).ap()

    with ExitStack() as es:
        P = Prog(nc, es)
        C = Ctx()
        C.nc, C.P, C.dr = nc, P, dr
        C.ps = es.enter_context(nc.psum_tensor("ps", [128, 4096], F32))
        C.bank = [C.ps[:, 512 * i:512 * (i + 1)] for i in range(8)]
        C.bR = [Reg("bank%d" % i, excl=True) for i in range(8)]
        C.ident = es.enter_context(nc.sbuf_tensor("ident", [128, 128], F32))
        C.identR = Reg("ident")
        C.col = es.enter_context(nc.sbuf_tensor("colc", [128, 4], F32))
        C.colR = Reg("col")
        C.mk = es.enter_context(nc.sbuf_tensor("marks", [128, 8], F32))
        P.dma("sp", lambda: nc.sync.dma_start(out=C.ident[:], in_=dr["c_ident"][:, :]), writes=[C.identR])
        P.dma("sp", lambda: nc.sync.dma_start(out=C.col[:], in_=dr["c_col"][:, :]), writes=[C.colR])
        C.marks = {
            "act": lambda: nc.scalar.activation(out=C.mk[:, 0:1], in_=C.mk[:, 1:2], func=AF.Copy),
            "dve": lambda: nc.vector.memset(C.mk[:, 2:3], 0.0),
            "pool": lambda: nc.gpsimd.memset(C.mk[:, 4:5], 0.0),
        }
        P.op("dve", lambda: nc.vector.memset(C.mk[:], 0.0))
        for ph in phases:
            srcap, dstap = chain[ph]
            with ExitStack() as pes:
                if ph == 1:
                    phase_mixer0(C, pes, srcap, dstap)
                elif ph == 2:
                    phase_ffn(C, pes, 0, srcap, dstap)
                elif ph == 3:
                    phase_mla(C, pes, srcap, dstap)
                elif ph == 4:
                    phase_ffn(C, pes, 1, srcap, dstap)
                P.barrier(C.marks)
        P.flush()
        P.final_wait()
    return nc


def bcast_row(C, pes, name, row_ap, n):
    nc, P = C.nc, C.P
    t = pes.enter_context(nc.sbuf_tensor(name, [128, n], F32))
    R = Reg(name)
    P.dma("sp", lambda: nc.sync.dma_start(out=t[:], in_=row_ap.partition_broadcast(128)), writes=[R])
    return t, R


class LNState:
    pass


def ln_setup(C, pes, g_row, b_row, tag, lean=False):
    nc = C.nc
    L = LNState()
    L.g, L.gR = bcast_row(C, pes, "lng_" + tag, g_row, D)
    L.b, L.bR = bcast_row(C, pes, "lnb_" + tag, b_row, D)
    if lean:
        s0 = pes.enter_context(nc.sbuf_tensor("lns0_%s" % tag, [128, D], F32))
        r0 = Reg("lns0")
        L.s, L.sR, L.y, L.yR = [s0, s0], [r0, r0], None, None
    else:
        L.s = [pes.enter_context(nc.sbuf_tensor("lns%d_%s" % (i, tag), [128, D], F32)) for i in range(2)]
        L.sR = [Reg("lns%d" % i) for i in range(2)]
        L.y = [pes.enter_context(nc.sbuf_tensor("lny%d_%s" % (i, tag), [128, D], F32)) for i in range(2)]
        L.yR = [Reg("lny%d" % i) for i in range(2)]
    L.st = [pes.enter_context(nc.sbuf_tensor("lnst%d_%s" % (i, tag), [128, 24], F32)) for i in range(2)]
    L.stR = [Reg("lnst%d" % i) for i in range(2)]
    L.n = 0
    return L


def ln_epilogue(C, L, bankA, bankB, x_ap, xR, dst_rows):
    nc, P = C.nc, C.P
    i = L.n % 2
    L.n += 1
    s, sR, st, stR = L.s[i], L.sR[i], L.st[i], L.stR[i]
    if L.y is None:
        y_ap, yR = x_ap, xR
    else:
        y_ap, yR = L.y[i][:], L.yR[i]
    bA, bB = C.bank[bankA], C.bank[bankB]
    P.op("dve", lambda: nc.vector.scalar_tensor_tensor(out=s[:, 0:512], in0=x_ap[:, 0:512], scalar=ALPHA, in1=bA,
                                                      op0=ALU.mult, op1=ALU.add),
         reads=[xR, C.bR[bankA]], writes=[sR], part=True, strict=(L.y is None))
    P.op("dve", lambda: nc.vector.scalar_tensor_tensor(out=s[:, 512:1024], in0=x_ap[:, 512:1024], scalar=ALPHA, in1=bB,
                                                      op0=ALU.mult, op1=ALU.add),
         reads=[xR, C.bR[bankB]], writes=[sR], part=True, strict=(L.y is None))
    P.op("dve", lambda: nc.vector.bn_stats(out=st[:, 0:6], in_=s[:, 0:512]), reads=[sR], writes=[stR], part=True)
    P.op("dve", lambda: nc.vector.bn_stats(out=st[:, 6:12], in_=s[:, 512:1024]), reads=[sR], writes=[stR], part=True)
    R1, R2, R3 = Reg("mv"), Reg("sd"), Reg("rstd")
    P.op("dve", lambda: nc.vector.bn_aggr(out=st[:, 12:14], in_=st[:, 0:12]), reads=[stR], writes=[R1])
    P.op("act", lambda: nc.scalar.activation(out=st[:, 14:15], in_=st[:, 13:14], func=AF.Sqrt, bias=C.col[:, 2:3], scale=1.0),
         reads=[R1, C.colR], writes=[R2])
    P.op("dve", lambda: nc.vector.scalar_tensor_tensor(out=s[:], in0=s[:], scalar=st[:, 12:13], in1=L.g[:],
                                                      op0=ALU.subtract, op1=ALU.mult), reads=[sR, R1, L.gR], writes=[sR])
    P.op("dve", lambda: nc.vector.reciprocal(out=st[:, 15:16], in_=st[:, 14:15]), reads=[R2], writes=[R3])
    P.op("dve", lambda: nc.vector.scalar_tensor_tensor(out=y_ap, in0=s[:], scalar=st[:, 15:16], in1=L.b[:],
                                                      op0=ALU.mult, op1=ALU.add), reads=[sR, R3, L.bR], writes=[yR])
    P.dma("sp", lambda: nc.sync.dma_start(out=dst_rows, in_=y_ap), reads=[yR])


def load_transpose_tile(C, src, t0, nchunk, xin, xinR, xT, xTR, tbanks, evac_engs=("act", "dve")):
    nc, P = C.nc, C.P
    rows = src[t0 * 128:(t0 + nchunk) * 128, :].rearrange("(c p) d -> p c d", p=128)
    P.dma("sp", lambda: nc.sync.dma_start(out=xin[:, 0:nchunk, :], in_=rows), writes=xinR)
    W = nchunk * 128
    for k in range(8):
        b = tbanks[k % len(tbanks)]
        for c in range(nchunk):
            P.op("pe", lambda c=c, k=k, b=b: nc.tensor.transpose(C.bank[b][:, c * 128:(c + 1) * 128],
                                                                xin[:, c, k * 128:(k + 1) * 128], C.ident[:]),
                 reads=[xinR[c], C.identR], writes=[C.bR[b]])
        e = evac_engs[k % len(evac_engs)]
        if e == "act":
            P.op("act", lambda k=k, b=b: nc.scalar.copy(out=xT[:, k, 0:W], in_=C.bank[b][:, 0:W]),
                 reads=[C.bR[b]], writes=[xTR], part=True)
        else:
            P.op("dve", lambda k=k, b=b: nc.vector.tensor_copy(out=xT[:, k, 0:W], in_=C.bank[b][:, 0:W]),
                 reads=[C.bR[b]], writes=[xTR], part=True)


def load_w_bf16(C, tile_ap, dram_ap, R, eng="pool"):
    nc, P = C.nc, C.P
    P.dma("pool", lambda: nc.gpsimd.dma_start(out=tile_ap, in_=dram_ap), writes=[R], part=True)


def phase_ffn(C, pes, layer, src, dst):
    nc, P, dr = C.nc, C.P, C.dr
    sfx = "_L%d" % layer
    wgu = pes.enter_context(nc.sbuf_tensor("wgu" + sfx, [128, 8, 2 * DFF], BF16))
    wd = pes.enter_context(nc.sbuf_tensor("wd" + sfx, [128, NFF, D], BF16))
    HJ = NFF // 2
    wguR = [[Reg("wgu%d_%d" % (k, ch)) for ch in range(4)] for k in range(8)]
    wdR = [Reg("wd%d" % j) for j in range(NFF)]
    gu = dr["ffn_w_gate_up"][layer]
    chunks = [(0, 0, HJ * 128), (1, DFF, DFF + HJ * 128), (2, HJ * 128, DFF), (3, DFF + HJ * 128, 2 * DFF)]
    for (ch, c0, c1) in chunks:
        for k in range(8):
            load_w_bf16(C, wgu[:, k, c0:c1], gu[k * 128:(k + 1) * 128, c0:c1], wguR[k][ch])
    dn = dr["ffn_w_down"][layer]
    for j in range(NFF):
        load_w_bf16(C, wd[:, j, :], dn[j * 128:(j + 1) * 128, :], wdR[j])
    L = ln_setup(C, pes, dr["ffn_ln_g"][layer:layer + 1, :], dr["ffn_ln_b"][layer:layer + 1, :], "f%d" % layer, lean=True)
    NP = NCH // 4
    xin = [pes.enter_context(nc.sbuf_tensor("fxin%d" % i + sfx, [128, 2, D], F32)) for i in range(2)]
    xinR = [[Reg("fxin%d_%d" % (i, c)) for c in range(2)] for i in range(2)]
    xT = pes.enter_context(nc.sbuf_tensor("fxT" + sfx, [128, 8, 512], BF16))
    xTR = [Reg("fxT%d" % i) for i in range(2)]
    hT = pes.enter_context(nc.sbuf_tensor("hT" + sfx, [128, NFF, 512], BF16))
    hTR = [Reg("hT%d" % j) for j in range(NFF)]
    sg = [pes.enter_context(nc.sbuf_tensor("sg%d" % i + sfx, [128, 512], F32)) for i in range(2)]
    sgR = [Reg("sg%d" % i) for i in range(2)]
    xr = [pes.enter_context(nc.sbuf_tensor("fxr%d" % i + sfx, [128, D], F32)) for i in range(2)]
    xrR = [Reg("fxr%d" % i) for i in range(2)]

    def transposes(p):
        for h2 in range(2):
            t = 2 * p + h2
            load_transpose_tile(C, src, t * 2, 2, xin[h2], xinR[h2], xT[:, :, h2 * 256:(h2 + 1) * 256], xTR[h2], (0, 1))

    def load_xr(gc):
        i = gc % 2
        P.dma("sp", lambda gc=gc, i=i: nc.sync.dma_start(out=xr[i][:], in_=src[gc * 128:(gc + 1) * 128, :]), writes=[xrR[i]])

    transposes(0)
    nsg = 0
    for p in range(NP):
        for j in range(NFF):
            bkg, bku = (2, 3) if j % 2 == 0 else (4, 5)
            for which, bk in ((0, bkg), (1, bku)):
                col = which * DFF + j * 128
                for k in range(8):
                    P.op("pe", lambda k=k, col=col, bk=bk: nc.tensor.matmul(
                        C.bank[bk][:, 0:512], wgu[:, k, col:col + 128], xT[:, k, :], start=(k == 0), stop=(k == 7)),
                        reads=[wguR[k][which + (2 if j >= HJ else 0)], xTR[0], xTR[1]], writes=[C.bR[bk]])
            si = nsg % 2
            nsg += 1
            P.op("act", lambda bkg=bkg, si=si: nc.scalar.activation(out=sg[si][:], in_=C.bank[bkg][:, 0:512], func=AF.Silu),
                 reads=[C.bR[bkg]], writes=[sgR[si]])
            P.op("dve", lambda bku=bku, si=si, j=j: nc.vector.tensor_tensor(out=hT[:, j, :], in0=sg[si][:], in1=C.bank[bku][:, 0:512],
                                                                        op=ALU.mult),
                 reads=[sgR[si], C.bR[bku]], writes=[hTR[j]])
        if p + 1 < NP:
            transposes(p + 1)
        load_xr(4 * p)
        load_xr(4 * p + 1)
        for c in range(4):
            gc = 4 * p + c
            pair = (6, 7) if c % 2 == 0 else (0, 1)
            for half in range(2):
                bk = pair[half]
                for j in range(NFF):
                    P.op("pe", lambda j=j, c=c, half=half, bk=bk: nc.tensor.matmul(
                        C.bank[bk][:, 0:512], hT[:, j, c * 128:(c + 1) * 128], wd[:, j, half * 512:(half + 1) * 512],
                        start=(j == 0), stop=(j == NFF - 1)),
                        reads=[hTR[j], wdR[j]], writes=[C.bR[bk]])
            ln_epilogue(C, L, pair[0], pair[1], xr[gc % 2][:], xrR[gc % 2], dst[gc * 128:(gc + 1) * 128, :])
            if c + 2 < 4:
                load_xr(gc + 2)


def phase_mixer0(C, pes, src, dst):
    nc, P, dr = C.nc, C.P, C.dr
    T = 4
    W = 512
    NT = NCH // T
    sb = lambda name, shp, dt=F32: pes.enter_context(nc.sbuf_tensor(name, shp, dt))
    win = sb("m_win", [128, 8, 1536], BF16)
    winR = [[Reg("win%d_%d" % (k, g)) for g in range(3)] for k in range(8)]
    for g in range(3):
        for k in range(8):
            load_w_bf16(C, win[:, k, g * 512:(g + 1) * 512], dr["even_w_in"][k * 128:(k + 1) * 128, g * 512:(g + 1) * 512], winR[k][g])
    wout = sb("m_wout", [128, 8, D], BF16)
    woutR = [Reg("wout%d" % k) for k in range(8)]
    for k in range(8):
        load_w_bf16(C, wout[:, k, :], dr["even_w_out"][k * 128:(k + 1) * 128, :], woutR[k])
    poolw = sb("m_poolw", [128, 4, 128], BF16)
    poolwR = Reg("poolw")
    load_w_bf16(C, poolw[:], dr["even_pool_w"].rearrange("g c d -> c g d"), poolwR)
    A = sb("m_A", [128, 12, 128])
    AR = Reg("A")
    P.dma("sp", lambda: nc.sync.dma_start(out=A[:], in_=dr["c_poolA"].rearrange("n s t -> s n t")), writes=[AR])
    triu = sb("m_triu", [128, 128])
    triuR = Reg("triu")
    P.dma("sp", lambda: nc.sync.dma_start(out=triu[:], in_=dr["c_triu"][:, :]), writes=[triuR])
    wsn = sb("m_wsn", [128, 8, 128])
    wsnR = Reg("wsn")
    P.dma("sp", lambda: nc.sync.dma_start(out=wsn[:], in_=dr["even_spatial_w"].rearrange("h t s -> t h s")), writes=[wsnR])
    wsT = sb("m_wsT", [128, 8, 128], BF16)
    wsTR = Reg("wsT")
    for h in range(8):
        P.op("pe", lambda h=h: nc.tensor.transpose(C.bank[0][:, 0:128], wsn[:, h, :], C.ident[:]),
             reads=[wsnR, C.identR], writes=[C.bR[0]])
        P.op("dve", lambda h=h: nc.vector.tensor_tensor(out=wsT[:, h, :], in0=C.bank[0][:, 0:128], in1=triu[:], op=ALU.mult),
             reads=[C.bR[0], triuR], writes=[wsTR], part=True)
    bs = sb("m_bs", [128, 8])
    bsR = Reg("bs")
    pscale = sb("m_pscale", [128, 4])
    pscaleR = Reg("pscale")
    with nc.allow_non_contiguous_dma(reason="tiny per-head bias / scale columns"):
        P.dma("sp", lambda: nc.sync.dma_start(out=bs[:], in_=dr["even_spatial_b"].rearrange("h t -> t h")), writes=[bsR])
        P.dma("sp", lambda: nc.sync.dma_start(out=pscale[:], in_=dr["even_pool_scale"].rearrange("o (g d) -> d (o g)", g=4)),
              writes=[pscaleR])
        P.flush()
    vg, vgR = bcast_row(C, pes, "m_vg", dr["even_vnorm_g"], 512)
    vb, vbR = bcast_row(C, pes, "m_vb", dr["even_vnorm_b"], 512)
    L = ln_setup(C, pes, dr["mix_ln_g"][0:1, :], dr["mix_ln_b"][0:1, :], "m")
    xin = [sb("m_xin%d" % i, [128, T, D]) for i in range(3)]
    xinR = [[Reg("mxin%d_%d" % (i, c)) for c in range(T)] for i in range(3)]
    xT = [sb("m_xT%d" % i, [128, 8, W], BF16) for i in range(2)]
    xTR = [Reg("mxT%d" % i) for i in range(2)]
    mixT = [sb("m_mixT%d" % i, [128, 8, 128], BF16) for i in range(2)]
    mixTR = [Reg("mixT%d" % i) for i in range(2)]
    u_sb = [sb("m_u%d" % i, [128, 512]) for i in range(3)]
    uR = [Reg("u%d" % i) for i in range(3)]
    v_sb = [sb("m_v%d" % i, [128, 512]) for i in range(2)]
    vR = [Reg("v%d" % i) for i in range(2)]
    vbf = [sb("m_vbf%d" % i, [128, 512], BF16) for i in range(2)]
    vbfR = [Reg("vbf%d" % i) for i in range(2)]
    a_sb = [sb("m_a%d" % i, [128, 512]) for i in range(2)]
    aR = [Reg("a%d" % i) for i in range(2)]
    xp = [sb("m_xp%d" % i, [128, 512]) for i in range(4)]
    xpR = [Reg("xp%d" % i) for i in range(4)]
    pooledT = [sb("m_pooledT%d" % i, [128, 4, 128], BF16) for i in range(2)]
    pooledTR = [Reg("pooledT%d" % i) for i in range(2)]
    vst = [sb("m_vst%d" % i, [128, 16]) for i in range(2)]
    BXA, BPM, BU, BV, BSP, BPF, BOA, BOB = 0, 1, 2, 3, 4, 5, 6, 7

    def geo(gc):
        t, c = divmod(gc, T)
        return t, c, t % 2, slice(c * 128, (c + 1) * 128)

    def A_pe(gc):
        t, c, b, cs = geo(gc)
        for which, bk in ((0, BU), (1, BV)):
            for k in range(8):
                P.op("pe", lambda k=k, which=which, bk=bk, cs=cs, b=b: nc.tensor.matmul(
                    C.bank[bk][:, 0:512], xT[b][:, k, cs], win[:, k, which * 512:(which + 1) * 512],
                    start=(k == 0), stop=(k == 7)), reads=[xTR[b], winR[k][which]], writes=[C.bR[bk]])
        for k in range(8):
            P.op("pe", lambda k=k, cs=cs, b=b: nc.tensor.matmul(
                C.bank[BXA][:, 0:512], xT[b][:, k, cs], win[:, k, 1024:1536],
                start=(k == 0), stop=(k == 7)), reads=[xTR[b], winR[k][2]], writes=[C.bR[BXA]])
        i4 = gc % 4
        P.op("act", lambda i4=i4: nc.scalar.copy(out=xp[i4][:], in_=C.bank[BXA][:, 0:512]),
             reads=[C.bR[BXA]], writes=[xpR[i4]])

    def A_gelu(gc):
        i2, i3 = gc % 2, gc % 3
        P.op("act", lambda i3=i3: nc.scalar.activation(out=u_sb[i3][:], in_=C.bank[BU][:, 0:512], func=AF.Gelu),
             reads=[C.bR[BU]], writes=[uR[i3]])
        P.op("act", lambda i2=i2: nc.scalar.activation(out=v_sb[i2][:], in_=C.bank[BV][:, 0:512], func=AF.Gelu),
             reads=[C.bR[BV]], writes=[vR[i2]])

    def A_vln(gc):
        i2, i3 = gc % 2, gc % 3
        st = vst[i2]
        R0, R1, R2, R3 = Reg("vs0"), Reg("vs1"), Reg("vs2"), Reg("vs3")
        P.op("dve", lambda st=st, i2=i2: nc.vector.bn_stats(out=st[:, 0:6], in_=v_sb[i2][:]), reads=[vR[i2]], writes=[R0])
        P.op("dve", lambda st=st: nc.vector.bn_aggr(out=st[:, 6:8], in_=st[:, 0:6]), reads=[R0], writes=[R1])
        P.op("act", lambda st=st: nc.scalar.activation(out=st[:, 8:9], in_=st[:, 7:8], func=AF.Sqrt, bias=C.col[:, 2:3], scale=1.0),
             reads=[R1, C.colR], writes=[R2])
        P.op("dve", lambda st=st, i2=i2: nc.vector.scalar_tensor_tensor(out=v_sb[i2][:], in0=v_sb[i2][:], scalar=st[:, 6:7], in1=vg[:],
                                                                     op0=ALU.subtract, op1=ALU.mult),
             reads=[vR[i2], R1, vgR], writes=[vR[i2]])
        P.op("dve", lambda st=st: nc.vector.reciprocal(out=st[:, 9:10], in_=st[:, 8:9]), reads=[R2], writes=[R3])
        P.op("dve", lambda st=st, i2=i2: nc.vector.scalar_tensor_tensor(out=vbf[i2][:], in0=v_sb[i2][:], scalar=st[:, 9:10], in1=vb[:],
                                                                     op0=ALU.mult, op1=ALU.add),
             reads=[vR[i2], R3, vbR], writes=[vbfR[i2]])

    def B_pe(gc):
        i2, i3, ip = gc % 2, gc % 4, (gc - 1) % 4
        for h in range(8):
            P.op("pe", lambda h=h, i2=i2: nc.tensor.matmul(C.bank[BSP][:, h * 64:(h + 1) * 64], wsT[:, h, :],
                                                          vbf[i2][:, h * 64:(h + 1) * 64], start=True, stop=True),
                 reads=[wsTR, vbfR[i2]], writes=[C.bR[BSP]])
        for g in range(4):
            gs = slice(g * 128, (g + 1) * 128)
            if gc == 0:
                P.op("pe", lambda g=g, gs=gs, i3=i3: nc.tensor.matmul(C.bank[BPF][:, gs], xp[i3][:, gs], A[:, 8 + g, :],
                                                                     start=True, stop=True),
                     reads=[xpR[i3], AR], writes=[C.bR[BPF]])
            else:
                P.op("pe", lambda g=g, gs=gs, i3=i3: nc.tensor.matmul(C.bank[BPF][:, gs], xp[i3][:, gs], A[:, g, :],
                                                                     start=True, stop=False),
                     reads=[xpR[i3], AR], writes=[C.bR[BPF]])
                P.op("pe", lambda g=g, gs=gs, ip=ip: nc.tensor.matmul(C.bank[BPF][:, gs], xp[ip][:, gs], A[:, 4 + g, :],
                                                                     start=False, stop=True),
                     reads=[xpR[ip], AR], writes=[C.bR[BPF]])

    def B_add(gc):
        i2, i3 = gc % 2, gc % 3
        P.op("dve", lambda i2=i2: nc.vector.tensor_tensor(
            out=a_sb[i2][:].rearrange("p (h d) -> p h d", h=8), in0=C.bank[BSP][:, 0:512].rearrange("p (h d) -> p h d", h=8),
            in1=bs[:, 0:8].unsqueeze(2).to_broadcast([128, 8, 64]), op=ALU.add),
            reads=[C.bR[BSP], bsR], writes=[aR[i2]])
        P.op("pool", lambda i2=i2, i3=i3: nc.gpsimd.tensor_tensor(out=a_sb[i2][:], in0=a_sb[i2][:], in1=u_sb[i3][:], op=ALU.mult),
             reads=[aR[i2], uR[i3]], writes=[aR[i2]])

    def B_copy(gc):
        i2 = gc % 2
        P.op("act", lambda i2=i2: nc.scalar.copy(out=pooledT[i2][:].rearrange("p g t -> p (g t)"), in_=C.bank[BPF][:, 0:512]),
             reads=[C.bR[BPF]], writes=[pooledTR[i2]])

    def C_pe(gc):
        i2 = gc % 2
        for kb in range(4):
            P.op("pe", lambda kb=kb, i2=i2: nc.tensor.transpose(C.bank[BXA][:, kb * 128:(kb + 1) * 128],
                                                               a_sb[i2][:, kb * 128:(kb + 1) * 128], C.ident[:]),
                 reads=[aR[i2], C.identR], writes=[C.bR[BXA]])
        for g in range(4):
            P.op("pe", lambda g=g, i2=i2: nc.tensor.matmul(C.bank[BPM][:, g * 128:(g + 1) * 128], poolw[:, g, :], pooledT[i2][:, g, :],
                                                          start=True, stop=True),
                 reads=[poolwR, pooledTR[i2]], writes=[C.bR[BPM]])

    def C_copy(gc):
        i2 = gc % 2
        P.op("act", lambda i2=i2: nc.scalar.copy(out=mixT[i2][:, 0:4, :], in_=C.bank[BXA][:, 0:512].rearrange("p (k t) -> p k t", k=4)),
             reads=[C.bR[BXA]], writes=[mixTR[i2]], part=True)

    def C_scale(gc):
        i2 = gc % 2
        P.op("dve", lambda i2=i2: nc.vector.tensor_tensor(
            out=mixT[i2][:, 4:8, :], in0=C.bank[BPM][:, 0:512].rearrange("p (g t) -> p g t", g=4),
            in1=pscale[:, 0:4].unsqueeze(2).to_broadcast([128, 4, 128]), op=ALU.mult),
            reads=[C.bR[BPM], pscaleR], writes=[mixTR[i2]], part=True)

    def D_pe(gc):
        i2 = gc % 2
        for half, bk in ((0, BOA), (1, BOB)):
            for k in range(8):
                P.op("pe", lambda k=k, half=half, bk=bk, i2=i2: nc.tensor.matmul(
                    C.bank[bk][:, 0:512], mixT[i2][:, k, :], wout[:, k, half * 512:(half + 1) * 512],
                    start=(k == 0), stop=(k == 7)), reads=[mixTR[i2], woutR[k]], writes=[C.bR[bk]])

    def D_post(gc):
        t, c, b, cs = geo(gc)
        xi = t % 3
        ln_epilogue(C, L, BOA, BOB, xin[xi][:, c, :], xinR[xi][c], dst[gc * 128:(gc + 1) * 128, :])

    def TX(t):
        load_transpose_tile(C, src, t * T, T, xin[t % 3], xinR[t % 3], xT[t % 2], xTR[t % 2], (BU, BV))

    TX(0)
    ok = lambda g: 0 <= g < NCH
    for s_ in range(NCH + 8):
        if ok(s_ - 1):
            A_gelu(s_ - 1)
        if ok(s_ - 3):
            B_copy(s_ - 3)
        if ok(s_ - 5):
            C_copy(s_ - 5)
            C_scale(s_ - 5)
        if ok(s_ - 3):
            B_add(s_ - 3)
        if ok(s_ - 1):
            A_vln(s_ - 1)
        if ok(s_ - 7):
            D_post(s_ - 7)
        if s_ % T == 2 and (s_ // T) + 1 < NT:
            TX(s_ // T + 1)
        if ok(s_):
            A_pe(s_)
        if ok(s_ - 2):
            B_pe(s_ - 2)
        if ok(s_ - 6):
            D_pe(s_ - 6)
        if ok(s_ - 4):
            C_pe(s_ - 4)
        if s_ % 4 == 3:
            P.flush()


def phase_mla(C, pes, src, dst):
    nc, P, dr = C.nc, C.P, C.dr
    with ExitStack() as aes:
        mla_latents_and_attention(C, aes, src)
    mla_outproj(C, pes, src, dst)


def mla_latents_and_attention(C, pes, src):
    nc, P, dr = C.nc, C.P, C.dr
    sb = lambda name, shp, dt=F32: pes.enter_context(nc.sbuf_tensor(name, shp, dt))
    T, W, NT = 4, 512, 8
    w_in = dr["odd_w_in"]
    winR = Reg("a_win")
    w3 = w_in.rearrange("(k p) n -> p k n", p=128)
    wqu = sb("a_wqu", [128, 3, 1536], BF16)
    wqus = sb("a_wqus", [128, 3, 16, 96], BF16)
    wquR = Reg("a_wqu")
    q3 = dr["odd_w_q_up"].rearrange("(k p) n -> p k n", p=128)
    q4 = dr["odd_w_q_up"].rearrange("(k p) (h d) -> p k h d", p=128, h=16)
    wkn = sb("a_wkn", [128, 2, 16, 64], BF16)
    wv = sb("a_wv", [128, 2, 16, 64], BF16)
    wkvR = Reg("a_wkv")
    kv4 = dr["odd_w_kv_up"].rearrange("(k p) (h d) -> p k h d", p=128, h=16)

    def load_attn_weights():
        load_w_bf16(C, wqu[:], q3, wquR)
        for k in range(3):
            for hs in (slice(0, 8), slice(8, 16)):
                load_w_bf16(C, wqus[:, k, hs, 0:64], q4[:, k, hs, 0:64], wquR)
                load_w_bf16(C, wqus[:, k, hs, 64:80], q4[:, k, hs, 80:96], wquR)
                load_w_bf16(C, wqus[:, k, hs, 80:96], q4[:, k, hs, 64:80], wquR)
        for k in range(2):
            for hs in (slice(0, 8), slice(8, 16)):
                load_w_bf16(C, wkn[:, k, hs, :], kv4[:, k, hs, 0:64], wkvR)
                load_w_bf16(C, wv[:, k, hs, :], kv4[:, k, hs, 64:128], wkvR)
    gq = sb("a_gq", [128, 3])
    gkv = sb("a_gkv", [128, 2])
    gR = Reg("a_g")
    with nc.allow_non_contiguous_dma(reason="tiny norm-gain columns"):
        P.dma("sp", lambda: nc.sync.dma_start(out=gq[:], in_=dr["odd_q_norm_g"].rearrange("o (k p) -> p (o k)", p=128)), writes=[gR], part=True)
        P.dma("sp", lambda: nc.sync.dma_start(out=gkv[:], in_=dr["odd_kv_norm_g"].rearrange("o (k p) -> p (o k)", p=128)), writes=[gR], part=True)
        P.flush()
    ones = sb("a_ones", [128, 128], BF16)
    onesR = Reg("a_ones")
    P.op("pool", lambda: nc.gpsimd.memset(ones[:], 1.0), writes=[onesR])
    sel = sb("a_sel", [128, 64])
    selR = Reg("a_sel")
    P.dma("sp", lambda: nc.sync.dma_start(out=sel[:], in_=dr["c_sel"][:, :]), writes=[selR])
    tri = sb("a_tri", [128, 128])
    trib = sb("a_trib", [128, 128], BF16)
    triR = Reg("a_tri")
    tribR = Reg("a_trib")
    P.dma("sp", lambda: nc.sync.dma_start(out=tri[:], in_=dr["c_triu"][:, :]), writes=[triR])
    P.op("pool", lambda: nc.gpsimd.tensor_copy(out=trib[:], in_=tri[:]), reads=[triR], writes=[tribR])
    negm = sb("a_negm", [128, 128], BF16)
    identb = sb("a_identb", [128, 128], BF16)
    mskR = Reg("a_msk")
    P.op("dve", lambda: nc.vector.tensor_scalar(out=negm[:], in0=tri[:], scalar1=-1.0, scalar2=30000.0, op0=ALU.add, op1=ALU.mult),
         reads=[triR], writes=[mskR], part=True)
    P.op("dve", lambda: nc.vector.tensor_copy(out=identb[:], in_=C.ident[:]), reads=[C.identR], writes=[mskR], part=True)

    cosT = sb("a_cos", [128, S])
    sinT = sb("a_sin", [128, S])
    csR = [Reg("a_cs%d" % t) for t in range(NT)]
    cqT = sb("a_cqT", [128, 3, S], BF16)
    ckvT = sb("a_ckvT", [128, 2, S], BF16)
    latR = [Reg("a_lat%d" % t) for t in range(NT)]
    kT = [sb("a_kT%d" % i, [96, S], BF16) for i in range(2)]
    kTropeR = [Reg("a_kTr%d" % t) for t in range(NT)]
    kTR = [Reg("a_kT%d" % i) for i in range(2)]
    t1 = sb("a_t1", [128, W])
    t2 = sb("a_t2", [128, W])
    t1R, t2R = Reg("a_t1"), Reg("a_t2")
    rp = slice(64, 96)
    shared_pes = pes
    pes = ExitStack()
    pes.__enter__()

    wq_in = sb("a_wqin", [128, 8, 384], BF16)
    wkv_in = sb("a_wkvin", [128, 8, 256], BF16)
    wkr = sb("a_wkr", [128, 8, 96], BF16)
    wkrs = sb("a_wkrs", [128, 8, 96], BF16)
    load_w_bf16(C, wq_in[:], w3[:, :, 0:384], winR)
    load_w_bf16(C, wkv_in[:], w3[:, :, 384:640], winR)
    load_w_bf16(C, wkr[:], w3[:, :, 576:672], winR)
    load_w_bf16(C, wkrs[:, :, 0:64], w3[:, :, 576:640], winR)
    load_w_bf16(C, wkrs[:, :, 64:80], w3[:, :, 656:672], winR)
    load_w_bf16(C, wkrs[:, :, 80:96], w3[:, :, 640:656], winR)
    load_attn_weights()
    xin = [sb("a_xin%d" % i, [128, T, D]) for i in range(2)]
    xinR = [[Reg("axin%d_%d" % (i, c)) for c in range(T)] for i in range(2)]
    xT = [sb("a_xT%d" % i, [128, 8, W], BF16) for i in range(2)]
    xTR = [Reg("axT%d" % i) for i in range(2)]
    posi = sb("a_posi", [128, W], I32)
    ang = sb("a_ang", [128, W])
    tq = sb("a_tq", [128, W])
    ki = sb("a_ki", [128, W], I32)
    sq = [sb("a_sq%d" % i, [128, W], BF16) for i in range(2)]
    sqR = [Reg("a_sq%d" % i) for i in range(2)]
    rstd = sb("a_rstd", [128, W])
    rstdR = Reg("a_rstd")
    posR, angR, tqR, kiR = Reg("a_pos"), Reg("a_ang"), Reg("a_tq"), Reg("a_ki")

    def rope_tables(t):
        ts_ = slice(t * W, (t + 1) * W)
        P.dma("sp", lambda ts_=ts_: nc.sync.dma_start(out=posi[rp, :], in_=dr["positions"][0:1, ts_].partition_broadcast(32)), writes=[posR])
        P.op("dve", lambda: nc.vector.tensor_copy(out=ang[rp, :], in_=posi[rp, :]), reads=[posR], writes=[angR])
        P.op("dve", lambda: nc.vector.tensor_scalar(out=ang[rp, :], in0=ang[rp, :], scalar1=C.col[rp, 0:1], scalar2=None, op0=ALU.mult),
             reads=[angR, C.colR], writes=[angR])
        for which in (0, 1):
            off = 0.0 if which == 0 else PI / 2
            P.op("dve", lambda off=off: nc.vector.tensor_scalar(out=tq[rp, :], in0=ang[rp, :], scalar1=off, scalar2=1.0 / TWO_PI,
                                                               op0=ALU.add, op1=ALU.mult), reads=[angR], writes=[tqR])
            P.op("dve", lambda: nc.vector.tensor_copy(out=ki[rp, :], in_=tq[rp, :]), reads=[tqR], writes=[kiR])
            P.op("dve", lambda: nc.vector.tensor_copy(out=tq[rp, :], in_=ki[rp, :]), reads=[kiR], writes=[tqR])
            P.op("dve", lambda: nc.vector.scalar_tensor_tensor(out=tq[rp, :], in0=tq[rp, :], scalar=-TWO_PI, in1=ang[rp, :],
                                                              op0=ALU.mult, op1=ALU.add), reads=[tqR, angR], writes=[tqR])
            P.op("dve", lambda off=off: nc.vector.tensor_scalar(out=tq[rp, :], in0=tq[rp, :], scalar1=off, scalar2=PI,
                                                               op0=ALU.add, op1=ALU.min), reads=[tqR], writes=[tqR])
            P.op("dve", lambda: nc.vector.tensor_scalar(out=tq[rp, :], in0=tq[rp, :], scalar1=-PI, scalar2=None, op0=ALU.max),
                 reads=[tqR], writes=[tqR])
            if which == 0:
                P.op("act", lambda ts_=ts_: nc.scalar.activation(out=sinT[rp, ts_], in_=tq[rp, :], func=AF.Sin, scale=C.col[rp, 1:2]),
                     reads=[tqR, C.colR], writes=[csR[t]], part=True)
            else:
                P.op("act", lambda ts_=ts_: nc.scalar.activation(out=cosT[rp, ts_], in_=tq[rp, :], func=AF.Sin),
                     reads=[tqR], writes=[csR[t]], part=True)

    def proj_mms(t, wt, nblk, bk0):
        b = t % 2
        for kb in range(nblk):
            bk = bk0 + kb
            for k in range(8):
                P.op("pe", lambda k=k, kb=kb, bk=bk, wt=wt, b=b: nc.tensor.matmul(
                    C.bank[bk][:, 0:W], wt[:, k, kb * 128:(kb + 1) * 128], xT[b][:, k, :], start=(k == 0), stop=(k == 7)),
                    reads=[winR, xTR[b]], writes=[C.bR[bk]])

    def rms_post(t, nblk, dstL, gcol, inv_n, bk0):
        ts_ = slice(t * W, (t + 1) * W)
        SSB = 5
        for kb in range(nblk):
            bk = bk0 + kb
            si = kb % 2
            P.op("act", lambda bk=bk, si=si: nc.scalar.activation(out=sq[si][:], in_=C.bank[bk][:, 0:W], func=AF.Square),
                 reads=[C.bR[bk]], writes=[sqR[si]])
            P.op("pe", lambda si=si, kb=kb, nblk=nblk: nc.tensor.matmul(C.bank[SSB][:, 0:W], ones[:], sq[si][:],
                                                                       start=(kb == 0), stop=(kb == nblk - 1)),
                 reads=[onesR, sqR[si]], writes=[C.bR[SSB]])
        P.op("act", lambda inv_n=inv_n: nc.scalar.activation(out=rstd[:], in_=C.bank[SSB][:, 0:W], func=AF.Sqrt,
                                                            bias=C.col[:, 3:4], scale=inv_n),
             reads=[C.bR[SSB], C.colR], writes=[rstdR])
        P.op("dve", lambda: nc.vector.reciprocal(out=rstd[:], in_=rstd[:]), reads=[rstdR], writes=[rstdR])
        for kb in range(nblk):
            bk = bk0 + kb
            P.op("dve", lambda kb=kb, bk=bk, dstL=dstL, gcol=gcol, ts_=ts_: nc.vector.scalar_tensor_tensor(
                out=dstL[:, kb, ts_], in0=C.bank[bk][:, 0:W], scalar=gcol[:, kb:kb + 1], in1=rstd[:], op0=ALU.mult, op1=ALU.mult),
                reads=[C.bR[bk], gR, rstdR], writes=[latR[t]], part=True)

    rope_tables(0)
    load_transpose_tile(C, src, 0, T, xin[0], xinR[0], xT[0], xTR[0], (0, 1))
    for t in range(NT):
        b = t % 2
        ts_ = slice(t * W, (t + 1) * W)
        for (wt, bk) in ((wkr, 0), (wkrs, 1)):
            for k in range(8):
                P.op("pe", lambda k=k, wt=wt, bk=bk, b=b: nc.tensor.matmul(C.bank[bk][0:96, 0:W], wt[:, k, :], xT[b][:, k, :],
                                                                          start=(k == 0), stop=(k == 7)),
                     reads=[winR, xTR[b]], writes=[C.bR[bk]])
        proj_mms(t, wq_in, 3, 2)
        P.op("dve", lambda ts_=ts_: nc.vector.tensor_tensor(out=t1[rp, :], in0=C.bank[0][rp, 0:W], in1=cosT[rp, ts_], op=ALU.mult),
             reads=[C.bR[0], csR[t]], writes=[t1R])
        P.op("dve", lambda ts_=ts_: nc.vector.tensor_tensor(out=t2[rp, :], in0=C.bank[1][rp, 0:W], in1=sinT[rp, ts_], op=ALU.mult),
             reads=[C.bR[1], csR[t]], writes=[t2R])
        for i in range(2):
            P.op("pool", lambda i=i, ts_=ts_: nc.gpsimd.tensor_tensor(out=kT[i][rp, ts_], in0=t1[rp, :], in1=t2[rp, :], op=ALU.add),
                 reads=[t1R, t2R], writes=[kTropeR[t]], part=True)
        proj_mms(t, wkv_in, 2, 6)
        rms_post(t, 3, cqT, gq, 1.0 / 384, 2)
        if t + 1 < NT:
            load_transpose_tile(C, src, (t + 1) * T, T, xin[1 - b], xinR[1 - b], xT[1 - b], xTR[1 - b], (0, 1))
        rms_post(t, 2, ckvT, gkv, 1.0 / 256, 6)
        if t + 1 < NT:
            rope_tables(t + 1)

    P.barrier(C.marks)
    pes.__exit__(None, None, None)
    pes = ExitStack()
    pes.__enter__()
    qT = [sb("a_qT%d" % i, [96, S], BF16) for i in range(2)]
    qTR = [Reg("a_qT%d" % i) for i in range(2)]
    Vgs = [sb("a_V%d" % i, [128, NCH, 4, 65], BF16) for i in range(2)]
    VgRs = [Reg("a_V%d" % i) for i in range(2)]
    VoneR = Reg("a_Vone")
    for i in range(2):
        P.op("pool", lambda i=i: nc.gpsimd.memset(Vgs[i][:, :, :, 64:65], 1.0), writes=[VoneR], part=True)
    pT = [sb("a_pT%d" % i, [128, 1024], BF16) for i in range(3)]
    pTR = [Reg("a_pT%d" % i) for i in range(3)]
    oa = [sb("a_oa%d" % i, [128, W]) for i in range(2)]
    oaR = [Reg("a_oa%d" % i) for i in range(2)]
    oTh = [sb("a_oTh%d" % i, [64, S], BF16) for i in range(2)]
    oThR = [Reg("a_oTh%d" % i) for i in range(2)]
    allLat = latR + csR + kTropeR
    OB = (0, 1)
    SB3 = ((2, 3), (4, 5), (6, 7))
    npt = 0
    noa = 0
    def gen_v(hg):
        Vg, VgR = Vgs[hg % 2], VgRs[hg % 2]
        for c2 in range(NCH // 2):
            bk = 2 + (c2 % 2)
            for cc in range(2):
                c = 2 * c2 + cc
                for k in range(2):
                    P.op("pe", lambda k=k, c=c, cc=cc, bk=bk, hg=hg: nc.tensor.matmul(
                        C.bank[bk][:, cc * 256:(cc + 1) * 256], ckvT[:, k, c * 128:(c + 1) * 128],
                        wv[:, k, hg * 4:(hg + 1) * 4, :].rearrange("p h d -> p (h d)"), start=(k == 0), stop=(k == 1)),
                        reads=[wkvR] + latR, writes=[C.bR[bk]])
            P.op("dve", lambda c2=c2, bk=bk: nc.vector.tensor_copy(
                out=Vg[:, 2 * c2:2 * c2 + 2, :, 0:64], in_=C.bank[bk][:, 0:512].rearrange("p (c h d) -> p c h d", c=2, h=4)),
                reads=[C.bR[bk]], writes=[VgR], part=True)

    def gen_qk(h, t, pair=(4, 5)):
        b = h % 2
        ts_ = slice(t * W, (t + 1) * W)
        BQ, BS_, BK = pair[0], pair[1], pair[0]
        for (wt, bk) in ((None, BQ), (wqus, BS_)):
            for k in range(3):
                lhs = wqu[:, k, h * 96:(h + 1) * 96] if wt is None else wqus[:, k, h, :]
                P.op("pe", lambda k=k, lhs=lhs, bk=bk, ts_=ts_: nc.tensor.matmul(C.bank[bk][0:96, 0:W], lhs, cqT[:, k, ts_],
                                                                                 start=(k == 0), stop=(k == 2)),
                     reads=[wquR] + latR, writes=[C.bR[bk]])
        P.op("dve", lambda ts_=ts_, b=b: nc.vector.tensor_copy(out=qT[b][0:64, ts_], in_=C.bank[BQ][0:64, 0:W]),
             reads=[C.bR[BQ]], writes=[qTR[b]], part=True)
        P.op("dve", lambda ts_=ts_: nc.vector.tensor_tensor(out=t1[rp, :], in0=C.bank[BQ][rp, 0:W], in1=cosT[rp, ts_], op=ALU.mult),
             reads=[C.bR[BQ]] + csR, writes=[t1R])
        P.op("dve", lambda ts_=ts_: nc.vector.tensor_tensor(out=t2[rp, :], in0=C.bank[BS_][rp, 0:W], in1=sinT[rp, ts_], op=ALU.mult),
             reads=[C.bR[BS_]] + csR, writes=[t2R])
        P.op("pool", lambda ts_=ts_, b=b: nc.gpsimd.tensor_tensor(out=qT[b][rp, ts_], in0=t1[rp, :], in1=t2[rp, :], op=ALU.add),
             reads=[t1R, t2R], writes=[qTR[b]], part=True)
        for k in range(2):
            P.op("pe", lambda k=k, ts_=ts_: nc.tensor.matmul(C.bank[BK][0:64, 0:W], wkn[:, k, h, :], ckvT[:, k, ts_],
                                                             start=(k == 0), stop=(k == 1)),
                 reads=[wkvR] + latR, writes=[C.bR[BK]])
        P.op("dve", lambda ts_=ts_, b=b: nc.vector.tensor_copy(out=kT[b][0:64, ts_], in_=C.bank[BK][0:64, 0:W]),
             reads=[C.bR[BK]], writes=[kTR[b]], part=True)

    def emit_S(it):
        sb2 = it["sb2"]
        if it["kind"] == "gv":
            hg2, c2 = it["hg2"], it["c2"]
            for cc in range(2):
                c = 2 * c2 + cc
                for k in range(2):
                    P.op("pe", lambda k=k, c=c, cc=cc, sb2=sb2, hg2=hg2: nc.tensor.matmul(
                        C.bank[sb2[0]][:, cc * 256:(cc + 1) * 256], ckvT[:, k, c * 128:(c + 1) * 128],
                        wv[:, k, hg2 * 4:(hg2 + 1) * 4, :].rearrange("p h d -> p (h d)"), start=(k == 0), stop=(k == 1)),
                        reads=[wkvR] + latR, writes=[C.bR[sb2[0]]])
            return
        if it["kind"] in ("g1", "g2"):
            h2 = it["h2"]
            ts_ = slice(it["t"] * W, (it["t"] + 1) * W)
            if it["kind"] == "g1":
                for k in range(3):
                    P.op("pe", lambda k=k, ts_=ts_, h2=h2, sb2=sb2: nc.tensor.matmul(
                        C.bank[sb2[0]][0:96, 0:W], wqu[:, k, h2 * 96:(h2 + 1) * 96], cqT[:, k, ts_], start=(k == 0), stop=(k == 2)),
                        reads=[wquR] + latR, writes=[C.bR[sb2[0]]])
                for k in range(2):
                    P.op("pe", lambda k=k, ts_=ts_, h2=h2, sb2=sb2: nc.tensor.matmul(
                        C.bank[sb2[1]][0:64, 0:W], wkn[:, k, h2, :], ckvT[:, k, ts_], start=(k == 0), stop=(k == 1)),
                        reads=[wkvR] + latR, writes=[C.bR[sb2[1]]])
            else:
                for k in range(3):
                    P.op("pe", lambda k=k, ts_=ts_, h2=h2, sb2=sb2: nc.tensor.matmul(
                        C.bank[sb2[0]][0:96, 0:W], wqus[:, k, h2, :], cqT[:, k, ts_], start=(k == 0), stop=(k == 2)),
                        reads=[wquR] + latR, writes=[C.bR[sb2[0]]])
            return
        b, q0 = it["b"], it["q0"]
        if it["kind"] == "off":
            for u in range(2):
                kt = it["kt0"] + u
                P.op("pe", lambda kt=kt, u=u, sb2=sb2, b=b, q0=q0: nc.tensor.matmul(
                    C.bank[sb2[u]][:, 0:W], kT[b][:, kt * 128:(kt + 1) * 128], qT[b][:, q0:q0 + W], start=True, stop=True),
                    reads=[kTR[b], qTR[b]] + kTropeR, writes=[C.bR[sb2[u]]])
        else:
            kt, n0 = it["kt"], it["n0"]
            P.op("pe", lambda kt=kt, sb2=sb2, b=b, q0=q0, n0=n0: nc.tensor.matmul(
                C.bank[sb2[0]][:, n0:W], kT[b][:, kt * 128:(kt + 1) * 128], qT[b][:, q0 + n0:q0 + W], start=True, stop=False),
                reads=[kTR[b], qTR[b]] + kTropeR, writes=[C.bR[sb2[0]]])
            P.op("pe", lambda sb2=sb2, n0=n0: nc.tensor.matmul(
                C.bank[sb2[0]][:, n0:n0 + 128], identb[:], negm[:], start=False, stop=True),
                reads=[mskR], writes=[C.bR[sb2[0]]])

    def emit_EP(it):
        sb2 = it["sb2"]
        if it["kind"] == "gv":
            hg2, c2 = it["hg2"], it["c2"]
            P.op("dve", lambda c2=c2, sb2=sb2, hg2=hg2: nc.vector.tensor_copy(
                out=Vgs[hg2 % 2][:, 2 * c2:2 * c2 + 2, :, 0:64],
                in_=C.bank[sb2[0]][:, 0:512].rearrange("p (c h d) -> p c h d", c=2, h=4)),
                reads=[C.bR[sb2[0]]], writes=[VgRs[hg2 % 2]], part=True)
            return
        if it["kind"] in ("g1", "g2"):
            b2 = it["h2"] % 2
            ts_ = slice(it["t"] * W, (it["t"] + 1) * W)
            if it["kind"] == "g1":
                P.op("act", lambda ts_=ts_, b2=b2, sb2=sb2: nc.scalar.copy(out=qT[b2][0:64, ts_], in_=C.bank[sb2[0]][0:64, 0:W]),
                     reads=[C.bR[sb2[0]]], writes=[qTR[b2]], part=True)
                P.op("dve", lambda ts_=ts_, sb2=sb2: nc.vector.tensor_tensor(out=t1[rp, :], in0=C.bank[sb2[0]][rp, 0:W], in1=cosT[rp, ts_], op=ALU.mult),
                     reads=[C.bR[sb2[0]]] + csR, writes=[t1R])
                P.op("act", lambda ts_=ts_, b2=b2, sb2=sb2: nc.scalar.copy(out=kT[b2][0:64, ts_], in_=C.bank[sb2[1]][0:64, 0:W]),
                     reads=[C.bR[sb2[1]]], writes=[kTR[b2]], part=True)
            else:
                P.op("dve", lambda ts_=ts_, sb2=sb2: nc.vector.tensor_tensor(out=t2[rp, :], in0=C.bank[sb2[0]][rp, 0:W], in1=sinT[rp, ts_], op=ALU.mult),
                     reads=[C.bR[sb2[0]]] + csR, writes=[t2R])
                P.op("pool", lambda ts_=ts_, b2=b2: nc.gpsimd.tensor_tensor(out=qT[b2][rp, ts_], in0=t1[rp, :], in1=t2[rp, :], op=ALU.add),
                     reads=[t1R, t2R], writes=[qTR[b2]], part=True)
            return
        pi, ob, hh = it["pi"], it["ob"], it["hh"]
        Vg, VgR = Vgs[it["hg"] % 2], VgRs[it["hg"] % 2]
        if it["kind"] == "off":
            P.op("act", lambda sb2=sb2, pi=pi: nc.scalar.activation(out=pT[pi][:, 0:1024], in_=C.ps[:, sb2[0] * 512:sb2[0] * 512 + 1024],
                                                                   func=AF.Exp, scale=SCALE),
                 reads=[C.bR[sb2[0]], C.bR[sb2[1]]], writes=[pTR[pi]])
            for u in range(2):
                kt = it["kt0"] + u
                st = it["first"] and u == 0
                P.op("pe", lambda kt=kt, u=u, pi=pi, ob=ob, hh=hh, st=st: nc.tensor.matmul(
                    C.bank[ob][0:65, 0:W], Vg[:, kt, hh, :], pT[pi][:, u * W:(u + 1) * W], start=st, stop=False),
                    reads=[VgR, VoneR, pTR[pi]], writes=[C.bR[ob]])
        else:
            kt, n0 = it["kt"], it["n0"]
            P.op("act", lambda sb2=sb2, pi=pi, n0=n0: nc.scalar.activation(out=pT[pi][:, n0:W], in_=C.bank[sb2[0]][:, n0:W],
                                                                          func=AF.Exp, scale=SCALE),
                 reads=[C.bR[sb2[0]]], writes=[pTR[pi]])
            P.op("pe", lambda kt=kt, pi=pi, ob=ob, hh=hh, n0=n0, st=it["first"], sp_=it["last"]: nc.tensor.matmul(
                C.bank[ob][0:65, n0:W], Vg[:, kt, hh, :], pT[pi][:, n0:W], start=st, stop=sp_),
                reads=[VgR, VoneR, pTR[pi]], writes=[C.bR[ob]])

    def emit_norm_pre(h, j, ob):
        nonlocal noa
        oi = noa % 2
        noa += 1
        P.op("dve", lambda oi=oi, ob=ob: nc.vector.tensor_copy(out=oa[oi][0:65, :], in_=C.bank[ob][0:65, 0:W]),
             reads=[C.bR[ob]], writes=[oaR[oi]])
        return oi

    def emit_norm_recip(oi, q):
        qs = slice(q * 128, (q + 1) * 128)
        P.op("dve", lambda oi=oi, qs=qs: nc.vector.reciprocal(out=oa[oi][64:65, qs], in_=oa[oi][64:65, qs]),
             reads=[oaR[oi]], writes=[oaR[oi]])

    def emit_norm_post(h, j, ob, oi):
        ob_i = h % 2
        q0 = j * W
        P.op("pe", lambda oi=oi, ob=ob: nc.tensor.matmul(C.bank[ob][0:64, 0:W], sel[0:65, :], oa[oi][0:65, :], start=True, stop=True),
             reads=[selR, oaR[oi]], writes=[C.bR[ob]])
        P.op("dve", lambda oi=oi, ob=ob, ob_i=ob_i, q0=q0: nc.vector.tensor_tensor(
            out=oTh[ob_i][:, q0:q0 + W], in0=oa[oi][0:64, :], in1=C.bank[ob][0:64, 0:W], op=ALU.mult),
            reads=[oaR[oi], C.bR[ob]], writes=[oThR[ob_i]], part=True)
        if j == 7:
            P.dma("sp", lambda h=h, ob_i=ob_i: nc.sync.dma_start(out=C.dr["oTd"][h * 64:(h + 1) * 64, :], in_=oTh[ob_i][:]),
                  reads=[oThR[ob_i]])

    LA = 2
    pend = []
    gen_v(0)
    for t in range(NT):
        gen_qk(0, t)
    for h in range(_dbg_heads()):
        hg, hh = divmod(h, 4)
        b = h % 2
        items = []
        for j in range(8):
            ob = OB[j % 2]
            first = True
            for kt0 in range(0, 4 * j, 2):
                items.append(dict(kind="off", kt0=kt0, b=b, q0=j * W, ob=ob, hh=hh, hg=hg, first=first, last=False, j=j,
                                  sb2=SB3[npt % 3], pi=npt % 3))
                npt += 1
                first = False
            for r in range(4):
                items.append(dict(kind="diag", kt=4 * j + r, n0=128 * r, b=b, q0=j * W, ob=ob, hh=hh, hg=hg, first=first,
                                  last=(r == 3), j=j, sb2=SB3[npt % 3], pi=npt % 3))
                npt += 1
                first = False
            if h + 1 < NH and hh == 3:
                for c2 in (2 * j, 2 * j + 1):
                    items.append(dict(kind="gv", hg2=hg + 1, c2=c2, sb2=SB3[npt % 3], pi=npt % 3))
                    npt += 1
            if h + 1 < NH:
                for kind in ("g1", "g2"):
                    items.append(dict(kind=kind, h2=h + 1, t=j, sb2=SB3[npt % 3], pi=npt % 3))
                    npt += 1
        n = len(items)
        for i in range(n + LA):
            if i < n:
                emit_S(items[i])
            if i - LA >= 0:
                it = items[i - LA]
                emit_EP(it)
                npend = []
                for (stg, args) in pend:
                    if stg < 0:
                        npend.append((stg + 1, args))
                    elif stg == 0:
                        args[3] = emit_norm_pre(args[0], args[1], args[2])
                        npend.append((1, args))
                    elif stg <= 4:
                        emit_norm_recip(args[3], stg - 1)
                        npend.append((stg + 1, args))
                    else:
                        emit_norm_post(*args)
                pend = npend
                if it.get("last"):
                    has_gen = (h + 1 < NH)
                    if has_gen:
                        pend.append((-1, [h, it["j"], it["ob"], None]))
                    else:
                        oi = emit_norm_pre(h, it["j"], it["ob"])
                        pend.append((1, [h, it["j"], it["ob"], oi]))
        if h + 2 < NH:
            pend = [(max(stg, 0), args) for (stg, args) in pend]
        else:
            for (stg, args) in pend:
                if stg <= 0:
                    args[3] = emit_norm_pre(args[0], args[1], args[2])
                    stg = 1
                for q in range(stg - 1, 4):
                    emit_norm_recip(args[3], q)
                emit_norm_post(*args)
            pend = []
        P.flush()
    P.barrier(C.marks)
    pes.__exit__(None, None, None)


def mla_outproj(C, pes, src, dst):
    nc, P, dr = C.nc, C.P, C.dr
    sb = lambda name, shp, dt=F32: pes.enter_context(nc.sbuf_tensor(name, shp, dt))
    T, W, NT = 4, 512, 8
    wout = sb("o_wout", [128, 8, D], BF16)
    woutR = [Reg("o_wout%d" % k) for k in range(8)]
    for k in range(8):
        load_w_bf16(C, wout[:, k, :], dr["odd_w_out"][k * 128:(k + 1) * 128, :], woutR[k])
    L = ln_setup(C, pes, dr["mix_ln_g"][1:2, :], dr["mix_ln_b"][1:2, :], "o")
    xin = [sb("o_xin%d" % i, [128, T, D]) for i in range(2)]
    xinR = [[Reg("oxin%d_%d" % (i, c)) for c in range(T)] for i in range(2)]
    oTt = [sb("o_oT%d" % i, [128, 8, W], BF16) for i in range(2)]
    oTtR = [Reg("o_oT%d" % i) for i in range(2)]
    o3 = dr["oTd"].rearrange("(k p) t -> p k t", p=128)
    def loads(t):
        b = t % 2
        rows = src[t * W:(t + 1) * W, :].rearrange("(c p) d -> p c d", p=128)
        P.dma("sp", lambda rows=rows, b=b: nc.sync.dma_start(out=oTt[b][:], in_=o3[:, :, t * W:(t + 1) * W]), writes=[oTtR[b]])
        P.dma("sp", lambda rows=rows, b=b: nc.sync.dma_start(out=xin[b][:], in_=rows), writes=xinR[b])

    loads(0)
    for t in range(NT):
        b = t % 2
        if t + 1 < NT:
            loads(t + 1)
        for c in range(T):
            cs = slice(c * 128, (c + 1) * 128)
            pair = (0, 1) if c % 2 == 0 else (2, 3)
            for half in range(2):
                bk = pair[half]
                for k in range(8):
                    P.op("pe", lambda k=k, half=half, bk=bk, cs=cs, b=b: nc.tensor.matmul(
                        C.bank[bk][:, 0:512], oTt[b][:, k, cs], wout[:, k, half * 512:(half + 1) * 512],
                        start=(k == 0), stop=(k == 7)), reads=[oTtR[b], woutR[k]], writes=[C.bR[bk]])
            gc = t * T + c
            ln_epilogue(C, L, pair[0], pair[1], xin[b][:, c, :], xinR[b][c], dst[gc * 128:(gc + 1) * 128, :])


_NC_CACHE = {}


def _get_nc(phases, standalone):
    key = (tuple(phases), standalone)
    if key not in _NC_CACHE:
        _NC_CACHE[key] = build(list(phases), standalone)
    return _NC_CACHE[key]


def _weight_maps(inputs):
    m = {}
    for name, shp in W_SPECS.items():
        m[name] = np.ascontiguousarray(np.asarray(inputs[name], dtype=np.float32).reshape(shp))
    m.update(host_consts())
    return m


FUSED = True


def kernel(**inputs):
    x = np.asarray(inputs["x"], dtype=np.float32)
    pos = np.asarray(inputs["positions"], dtype=np.int32)
    wm = _weight_maps(inputs)
    n = 8
    if FUSED:
        nc = _get_nc((1, 2, 3, 4), False)
        in_maps = []
        for b in range(n):
            d = dict(wm)
            d["x"] = np.ascontiguousarray(x[b])
            d["positions"] = np.ascontiguousarray(pos[b:b + 1])
            in_maps.append(d)
        res = run_bass_kernel_spmd(nc, in_maps, core_ids=list(range(n)))
        return np.stack([res.results[b]["out"] for b in range(n)], axis=0)
    cur = [np.ascontiguousarray(x[b]) for b in range(n)]
    for ph in (1, 2, 3, 4):
        nc = _get_nc((ph,), True)
        in_maps = []
        for b in range(n):
            d = dict(wm)
            d["src"] = cur[b]
            d["positions"] = np.ascontiguousarray(pos[b:b + 1])
            in_maps.append(d)
        res = run_bass_kernel_spmd(nc, in_maps, core_ids=list(range(n)))
        cur = [np.ascontiguousarray(res.results[b]["dst"]) for b in range(n)]
    return np.stack(cur, axis=0)
```

```python
import numpy as np
from contextlib import ExitStack
import concourse.bass as bass
import concourse.mybir as mybir
from concourse.bass_utils import run_bass_kernel_spmd

F32 = mybir.dt.float32
BF16 = mybir.dt.bfloat16
I32 = mybir.dt.int32
AF = mybir.ActivationFunctionType
ALU = mybir.AluOpType

S = 4096
D = 1024
NCH = S // 128
DFF = 2816
NFF = DFF // 128
ALPHA = float((2 * 2) ** 0.25)
LN_EPS = 1e-5
RMS_EPS = 1e-6
TWO_PI = float(2 * np.pi)
PI = float(np.pi)
NH = 16
SCALE = float(96 ** -0.5)


class Reg:
    __slots__ = ("name", "excl", "writers", "readers")

    def __init__(self, name, excl=False):
        self.name = name
        self.excl = excl
        self.writers = []
        self.readers = []


class Ins:
    __slots__ = ("eng", "fn", "dma", "deps", "signal", "count", "dsem", "dval", "seq", "emitted", "xw")

    def __init__(self, eng, fn, dma):
        self.eng = eng
        self.fn = fn
        self.dma = dma
        self.deps = []
        self.signal = False
        self.count = None
        self.dsem = None
        self.dval = None
        self.seq = None
        self.emitted = False
        self.xw = []


class Prog:
    def __init__(self, nc, es):
        self.nc = nc
        self.E = {"pe": nc.tensor, "act": nc.scalar, "dve": nc.vector, "pool": nc.gpsimd, "sp": nc.sync}
        self.sem = {e: es.enter_context(nc.semaphore("c_" + e)) for e in ("pe", "act", "dve", "pool")}
        self.cnt = {e: 0 for e in self.sem}
        self.dsems = {}
        for q, n in (("sp", 12), ("pool", 8), ("act", 4)):
            self.dsems[q] = [es.enter_context(nc.semaphore("d_%s%d" % (q, i))) for i in range(n)]
        self.dnext = {q: 0 for q in self.dsems}
        self.dval = {}
        self.dlast = {}
        self.pending = []
        self.seq = {e: 0 for e in self.E}
        self.sig_hist = {e: [] for e in self.sem}
        self.waited = {e: {x: 0 for x in self.sem} for e in self.E}
        self.dobs = {e: {} for e in self.E}
        self.extra = {e: [] for e in self.E}
        self.last = {e: None for e in self.E}

    def _add(self, eng, fn, reads, writes, dma, part, strict=False):
        I = Ins(eng, fn, dma)
        I.seq = self.seq[eng]
        self.seq[eng] += 1
        deps = []
        for r in reads:
            if r.excl:
                deps += [(d, "raw") for d in r.writers] + [(d, "war") for d in r.readers]
            else:
                deps += [(d, "raw") for d in r.writers]
        for w in writes:
            if part and not w.excl:
                deps += [(d, "war") for d in w.readers]
            else:
                deps += [(d, "war") for d in w.readers] + [(d, "waw") for d in w.writers]
        for d in self.extra[eng]:
            deps.append((d, "raw"))
        self.extra[eng] = []
        if dma:
            q = eng
            sems = self.dsems[q]
            sem = sems[self.dnext[q] % len(sems)]
            self.dnext[q] += 1
            prev = self.dlast.get(id(sem))
            if prev is not None:
                deps.append((prev, "raw"))
            I.dsem = sem
            I.dval = self.dval.get(id(sem), 0) + 16
            self.dval[id(sem)] = I.dval
            self.dlast[id(sem)] = I
        seen = set()
        for d, kind in deps:
            if d is I or id(d) in seen:
                continue
            if d.dma or dma:
                pass
            elif d.eng == eng:
                if eng == "pe" or (kind != "raw" and not strict):
                    continue
            seen.add(id(d))
            I.deps.append(d)
            if not d.dma and not d.emitted:
                d.signal = True
        for r in reads:
            if not dma:
                r.readers = [x for x in r.readers if x.dma or x.eng != eng]
            r.readers.append(I)
        for w in writes:
            if part and not w.excl:
                if w.readers:
                    w.writers = [I]
                    w.readers = []
                else:
                    if not dma:
                        w.writers = [x for x in w.writers if x.dma or x.eng != eng]
                    w.writers.append(I)
            else:
                w.writers = [I]
                w.readers = []
        self.pending.append(I)
        self.last[eng] = I
        return I

    def op(self, eng, fn, reads=(), writes=(), part=False, strict=False):
        return self._add(eng, fn, list(reads), list(writes), False, part, strict)

    def dma(self, eng, fn, reads=(), writes=(), part=False):
        return self._add(eng, fn, list(reads), list(writes), True, part)

    def _count_of(self, d):
        if d.count is not None:
            return d.count
        for seq, c in self.sig_hist[d.eng]:
            if seq >= d.seq:
                return c
        raise RuntimeError("no signal after dep on %s" % d.eng)

    def flush(self):
        lastp = {}
        for I in self.pending:
            if not I.dma and I.eng in self.sem:
                lastp[I.eng] = I
        for I in lastp.values():
            I.signal = True
        for I in self.pending:
            e = I.eng
            eng = self.E[e]
            waits = []
            need_c = {}
            for d in I.deps:
                if d.dma:
                    k = id(d.dsem)
                    if self.dobs[e].get(k, 0) < d.dval:
                        self.dobs[e][k] = d.dval
                        waits.append((d.dsem, d.dval))
                else:
                    c = self._count_of(d)
                    if c > need_c.get(d.eng, 0):
                        need_c[d.eng] = c
            for x, c in need_c.items():
                if self.waited[e][x] < c:
                    self.waited[e][x] = c
                    waits.append((self.sem[x], c))
            for sem, val in I.xw:
                k = id(sem)
                if self.dobs[e].get(k, 0) < val:
                    self.dobs[e][k] = val
                    waits.append((sem, val))
            best = {}
            for sem, val in waits:
                k = id(sem)
                if k not in best or best[k][1] < val:
                    best[k] = (sem, val)
            waits = list(best.values())
            while len(waits) > 2:
                a = waits.pop()
                b = waits.pop()
                eng.wait_ge(a[0], a[1])
                eng.wait_ge(b[0], b[1])
                eng.nop()
            for sem, val in waits:
                eng.wait_ge(sem, val)
            bi = I.fn()
            if I.dma:
                bi.then_inc(I.dsem, 16)
            elif I.signal:
                self.cnt[e] += 1
                I.count = self.cnt[e]
                bi.then_inc(self.sem[e], 1)
                self.sig_hist[e].append((I.seq, I.count))
            I.emitted = True
            I.fn = None
        self.pending = []
        for e in self.sig_hist:
            if len(self.sig_hist[e]) > 4:
                self.sig_hist[e] = self.sig_hist[e][-4:]

    def all_dma_waits(self):
        out = []
        for q in self.dsems:
            for sem in self.dsems[q]:
                v = self.dval.get(id(sem), 0)
                if v:
                    out.append((sem, v))
        return out

    def barrier(self, marks):
        ms = []
        for e in ("act", "dve", "pool"):
            m = self.op(e, marks[e])
            m.xw = self.all_dma_waits()
            m.signal = True
            ms.append(m)
        for e in ("act", "dve", "pool", "sp"):
            self.extra[e] = list(ms)
        self.flush()

    def final_wait(self):
        eng = self.E["sp"]
        for sem, val in self.all_dma_waits():
            eng.wait_ge(sem, val)
            eng.nop()


CONST_SPECS = {
    "c_ident": ([128, 128], F32),
    "c_triu": ([128, 128], F32),
    "c_poolA": ([12, 128, 128], F32),
    "c_col": ([128, 4], F32),
    "c_sel": ([128, 64], F32),
}

W_SPECS = {
    "even_w_in": [1024, 1536], "even_vnorm_g": [1, 512], "even_vnorm_b": [1, 512],
    "even_spatial_w": [8, 128, 128], "even_spatial_b": [8, 128], "even_pool_w": [4, 128, 128],
    "even_pool_scale": [1, 512], "even_w_out": [1024, 1024],
    "odd_w_in": [1024, 672], "odd_q_norm_g": [1, 384], "odd_w_q_up": [384, 1536],
    "odd_kv_norm_g": [1, 256], "odd_w_kv_up": [256, 2048], "odd_w_out": [1024, 1024],
    "mix_ln_g": [2, 1024], "mix_ln_b": [2, 1024], "ffn_w_gate_up": [2, 1024, 5632],
    "ffn_w_down": [2, 2816, 1024], "ffn_ln_g": [2, 1024], "ffn_ln_b": [2, 1024],
}


def host_consts():
    ident = np.eye(128, dtype=np.float32)
    triu = np.triu(np.ones((128, 128), dtype=np.float32))
    A = np.zeros((12, 128, 128), dtype=np.float32)
    s = np.arange(128)[:, None]
    t = np.arange(128)[None, :]
    for g, w in enumerate((2, 4, 8, 16)):
        A[g] = ((s <= t) & (s > t - w)) / np.float32(w) - (s == t)
        A[4 + g] = (s >= 128 + t - w + 1) / np.float32(w)
        cnt = np.minimum(t + 1, w).astype(np.float32)
        A[8 + g] = ((s <= t) & (s > t - w)) / cnt - (s == t)
    col = np.zeros((128, 4), dtype=np.float32)
    freqs = (10000.0 ** (-np.arange(0, 32, 2, dtype=np.float32) / 32)).astype(np.float32)
    col[64:80, 0] = freqs
    col[80:96, 0] = freqs
    col[:, 1] = 1.0
    col[64:80, 1] = -1.0
    col[:, 2] = LN_EPS
    col[:, 3] = RMS_EPS
    sel = np.zeros((128, 64), dtype=np.float32)
    sel[64, :] = 1.0
    return {"c_ident": ident, "c_triu": triu, "c_poolA": A.astype(np.float32), "c_col": col, "c_sel": sel}


class Ctx:
    pass


def _dbg_heads():
    import os
    return int(os.environ.get("MK_DBG_HEADS", NH))


def build(phases, standalone):
    nc = bass.Bass("TRN2", target_bir_lowering=False)
    dr = {}
    for name, shp in W_SPECS.items():
        dr[name] = nc.dram_tensor(name, shp, F32, kind="ExternalInput").ap()
    for name, (shp, dt) in CONST_SPECS.items():
        dr[name] = nc.dram_tensor(name, shp, dt, kind="ExternalInput").ap()
    dr["positions"] = nc.dram_tensor("positions", [1, S], I32, kind="ExternalInput").ap()
    if standalone:
        src = nc.dram_tensor("src", [S, D], F32, kind="ExternalInput").ap()
        dst = nc.dram_tensor("dst", [S, D], F32, kind="ExternalOutput").ap()
        chain = {phases[0]: (src, dst)}
    else:
        x = nc.dram_tensor("x", [S, D], F32, kind="ExternalInput").ap()
        out = nc.dram_tensor("out", [S, D], F32, kind="ExternalOutput").ap()
        s1 = nc.dram_tensor("scr1", [S, D], F32).ap()
        s2 = nc.dram_tensor("scr2", [S, D], F32).ap()
        s3 = nc.dram_tensor("scr3", [S, D], F32).ap()
        chain = {1: (x, s1), 2: (s1, s2), 3: (s2, s3), 4: (s3, out)}
    dr["oTd"] = nc.dram_tensor("scr_oT", [D, S], BF16).ap()

    with ExitStack() as es:
        P = Prog(nc, es)
        C = Ctx()
        C.nc, C.P, C.dr = nc, P, dr
        C.ps = es.enter_context(nc.psum_tensor("ps", [128, 4096], F32))
        C.bank = [C.ps[:, 512 * i:512 * (i + 1)] for i in range(8)]
        C.bR = [Reg("bank%d" % i, excl=True) for i in range(8)]
        C.ident = es.enter_context(nc.sbuf_tensor("ident", [128, 128], F32))
        C.identR = Reg("ident")
        C.col = es.enter_context(nc.sbuf_tensor("colc", [128, 4], F32))
        C.colR = Reg("col")
        C.mk = es.enter_context(nc.sbuf_tensor("marks", [128, 8], F32))
        P.dma("sp", lambda: nc.sync.dma_start(out=C.ident[:], in_=dr["c_ident"][:, :]), writes=[C.identR])
        P.dma("sp", lambda: nc.sync.dma_start(out=C.col[:], in_=dr["c_col"][:, :]), writes=[C.colR])
        C.marks = {
            "act": lambda: nc.scalar.activation(out=C.mk[:, 0:1], in_=C.mk[:, 1:2], func=AF.Copy),
            "dve": lambda: nc.vector.memset(C.mk[:, 2:3], 0.0),
            "pool": lambda: nc.gpsimd.memset(C.mk[:, 4:5], 0.0),
        }
        P.op("dve", lambda: nc.vector.memset(C.mk[:], 0.0))
        for ph in phases:
            srcap, dstap = chain[ph]
            with ExitStack() as pes:
                if ph == 1:
                    phase_mixer0(C, pes, srcap, dstap)
                elif ph == 2:
                    phase_ffn(C, pes, 0, srcap, dstap)
                elif ph == 3:
                    phase_mla(C, pes, srcap, dstap)
                elif ph == 4:
                    phase_ffn(C, pes, 1, srcap, dstap)
                P.barrier(C.marks)
        P.flush()
        P.final_wait()
    return nc


def bcast_row(C, pes, name, row_ap, n):
    nc, P = C.nc, C.P
    t = pes.enter_context(nc.sbuf_tensor(name, [128, n], F32))
    R = Reg(name)
    P.dma("sp", lambda: nc.sync.dma_start(out=t[:], in_=row_ap.partition_broadcast(128)), writes=[R])
    return t, R


class LNState:
    pass


def ln_setup(C, pes, g_row, b_row, tag, lean=False):
    nc = C.nc
    L = LNState()
    L.g, L.gR = bcast_row(C, pes, "lng_" + tag, g_row, D)
    L.b, L.bR = bcast_row(C, pes, "lnb_" + tag, b_row, D)
    if lean:
        s0 = pes.enter_context(nc.sbuf_tensor("lns0_%s" % tag, [128, D], F32))
        r0 = Reg("lns0")
        L.s, L.sR, L.y, L.yR = [s0, s0], [r0, r0], None, None
    else:
        L.s = [pes.enter_context(nc.sbuf_tensor("lns%d_%s" % (i, tag), [128, D], F32)) for i in range(2)]
        L.sR = [Reg("lns%d" % i) for i in range(2)]
        L.y = [pes.enter_context(nc.sbuf_tensor("lny%d_%s" % (i, tag), [128, D], F32)) for i in range(2)]
        L.yR = [Reg("lny%d" % i) for i in range(2)]
    L.st = [pes.enter_context(nc.sbuf_tensor("lnst%d_%s" % (i, tag), [128, 24], F32)) for i in range(2)]
    L.stR = [Reg("lnst%d" % i) for i in range(2)]
    L.n = 0
    return L


def ln_epilogue(C, L, bankA, bankB, x_ap, xR, dst_rows):
    nc, P = C.nc, C.P
    i = L.n % 2
    L.n += 1
    s, sR, st, stR = L.s[i], L.sR[i], L.st[i], L.stR[i]
    if L.y is None:
        y_ap, yR = x_ap, xR
    else:
        y_ap, yR = L.y[i][:], L.yR[i]
    bA, bB = C.bank[bankA], C.bank[bankB]
    if bankB == bankA + 1:
        P.op("dve", lambda: nc.vector.scalar_tensor_tensor(out=s[:], in0=x_ap[:, 0:1024], scalar=ALPHA,
                                                          in1=C.ps[:, bankA * 512:bankA * 512 + 1024], op0=ALU.mult, op1=ALU.add),
             reads=[xR, C.bR[bankA], C.bR[bankB]], writes=[sR], strict=(L.y is None))
    else:
        P.op("dve", lambda: nc.vector.scalar_tensor_tensor(out=s[:, 0:512], in0=x_ap[:, 0:512], scalar=ALPHA, in1=bA,
                                                          op0=ALU.mult, op1=ALU.add),
             reads=[xR, C.bR[bankA]], writes=[sR], part=True, strict=(L.y is None))
        P.op("dve", lambda: nc.vector.scalar_tensor_tensor(out=s[:, 512:1024], in0=x_ap[:, 512:1024], scalar=ALPHA, in1=bB,
                                                          op0=ALU.mult, op1=ALU.add),
             reads=[xR, C.bR[bankB]], writes=[sR], part=True, strict=(L.y is None))
    P.op("dve", lambda: nc.vector.bn_stats(out=st[:, 0:6], in_=s[:, 0:512]), reads=[sR], writes=[stR], part=True)
    P.op("dve", lambda: nc.vector.bn_stats(out=st[:, 6:12], in_=s[:, 512:1024]), reads=[sR], writes=[stR], part=True)
    R1, R2, R3 = Reg("mv"), Reg("sd"), Reg("rstd")
    P.op("dve", lambda: nc.vector.bn_aggr(out=st[:, 12:14], in_=st[:, 0:12]), reads=[stR], writes=[R1])
    P.op("act", lambda: nc.scalar.activation(out=st[:, 14:15], in_=st[:, 13:14], func=AF.Sqrt, bias=C.col[:, 2:3], scale=1.0),
         reads=[R1, C.colR], writes=[R2])
    P.op("dve", lambda: nc.vector.scalar_tensor_tensor(out=s[:], in0=s[:], scalar=st[:, 12:13], in1=L.g[:],
                                                      op0=ALU.subtract, op1=ALU.mult), reads=[sR, R1, L.gR], writes=[sR])
    P.op("dve", lambda: nc.vector.reciprocal(out=st[:, 15:16], in_=st[:, 14:15]), reads=[R2], writes=[R3])
    P.op("dve", lambda: nc.vector.scalar_tensor_tensor(out=y_ap, in0=s[:], scalar=st[:, 15:16], in1=L.b[:],
                                                      op0=ALU.mult, op1=ALU.add), reads=[sR, R3, L.bR], writes=[yR])
    P.dma("sp", lambda: nc.sync.dma_start(out=dst_rows, in_=y_ap), reads=[yR])


def load_transpose_tile(C, src, t0, nchunk, xin, xinR, xT, xTR, tbanks, evac_engs=("act", "dve")):
    nc, P = C.nc, C.P
    rows = src[t0 * 128:(t0 + nchunk) * 128, :].rearrange("(c p) d -> p c d", p=128)
    P.dma("sp", lambda: nc.sync.dma_start(out=xin[:, 0:nchunk, :], in_=rows), writes=xinR)
    W = nchunk * 128
    for k in range(8):
        b = tbanks[k % len(tbanks)]
        for c in range(nchunk):
            P.op("pe", lambda c=c, k=k, b=b: nc.tensor.transpose(C.bank[b][:, c * 128:(c + 1) * 128],
                                                                xin[:, c, k * 128:(k + 1) * 128], C.ident[:]),
                 reads=[xinR[c], C.identR], writes=[C.bR[b]])
        e = evac_engs[k % len(evac_engs)]
        if e == "act":
            P.op("act", lambda k=k, b=b: nc.scalar.copy(out=xT[:, k, 0:W], in_=C.bank[b][:, 0:W]),
                 reads=[C.bR[b]], writes=[xTR], part=True)
        else:
            P.op("dve", lambda k=k, b=b: nc.vector.tensor_copy(out=xT[:, k, 0:W], in_=C.bank[b][:, 0:W]),
                 reads=[C.bR[b]], writes=[xTR], part=True)


def load_w_bf16(C, tile_ap, dram_ap, R, eng="pool"):
    nc, P = C.nc, C.P
    P.dma("pool", lambda: nc.gpsimd.dma_start(out=tile_ap, in_=dram_ap), writes=[R], part=True)


def phase_ffn(C, pes, layer, src, dst):
    nc, P, dr = C.nc, C.P, C.dr
    sfx = "_L%d" % layer
    wgu = pes.enter_context(nc.sbuf_tensor("wgu" + sfx, [128, 8, 2 * DFF], BF16))
    wd = pes.enter_context(nc.sbuf_tensor("wd" + sfx, [128, NFF, D], BF16))
    HJ = NFF // 2
    wguR = [[Reg("wgu%d_%d" % (k, ch)) for ch in range(4)] for k in range(8)]
    wdR = [Reg("wd%d" % j) for j in range(NFF)]
    gu = dr["ffn_w_gate_up"][layer]
    chunks = [(0, 0, HJ * 128), (1, DFF, DFF + HJ * 128), (2, HJ * 128, DFF), (3, DFF + HJ * 128, 2 * DFF)]
    for (ch, c0, c1) in chunks:
        for k in range(8):
            load_w_bf16(C, wgu[:, k, c0:c1], gu[k * 128:(k + 1) * 128, c0:c1], wguR[k][ch])
    dn = dr["ffn_w_down"][layer]
    for j in range(NFF):
        load_w_bf16(C, wd[:, j, :], dn[j * 128:(j + 1) * 128, :], wdR[j])
    L = ln_setup(C, pes, dr["ffn_ln_g"][layer:layer + 1, :], dr["ffn_ln_b"][layer:layer + 1, :], "f%d" % layer, lean=True)
    NP = NCH // 4
    xin = [pes.enter_context(nc.sbuf_tensor("fxin%d" % i + sfx, [128, 2, D], F32)) for i in range(2)]
    xinR = [[Reg("fxin%d_%d" % (i, c)) for c in range(2)] for i in range(2)]
    xT = pes.enter_context(nc.sbuf_tensor("fxT" + sfx, [128, 8, 512], BF16))
    xTR = [Reg("fxT%d" % i) for i in range(2)]
    hT = pes.enter_context(nc.sbuf_tensor("hT" + sfx, [128, NFF, 512], BF16))
    hTR = [Reg("hT%d" % j) for j in range(NFF)]
    sg = [pes.enter_context(nc.sbuf_tensor("sg%d" % i + sfx, [128, 512], F32)) for i in range(2)]
    sgR = [Reg("sg%d" % i) for i in range(2)]
    xr = [pes.enter_context(nc.sbuf_tensor("fxr%d" % i + sfx, [128, D], F32)) for i in range(2)]
    xrR = [Reg("fxr%d" % i) for i in range(2)]

    def transposes(p):
        for h2 in range(2):
            t = 2 * p + h2
            load_transpose_tile(C, src, t * 2, 2, xin[h2], xinR[h2], xT[:, :, h2 * 256:(h2 + 1) * 256], xTR[h2], (0, 1))

    def load_xr(gc):
        i = gc % 2
        P.dma("sp", lambda gc=gc, i=i: nc.sync.dma_start(out=xr[i][:], in_=src[gc * 128:(gc + 1) * 128, :]), writes=[xrR[i]])

    transposes(0)
    nsg = 0
    for p in range(NP):
        for j in range(NFF):
            bkg, bku = (2, 3) if j % 2 == 0 else (4, 5)
            for which, bk in ((0, bkg), (1, bku)):
                col = which * DFF + j * 128
                for k in range(8):
                    P.op("pe", lambda k=k, col=col, bk=bk: nc.tensor.matmul(
                        C.bank[bk][:, 0:512], wgu[:, k, col:col + 128], xT[:, k, :], start=(k == 0), stop=(k == 7)),
                        reads=[wguR[k][which + (2 if j >= HJ else 0)], xTR[0], xTR[1]], writes=[C.bR[bk]])
            si = nsg % 2
            nsg += 1
            P.op("act", lambda bkg=bkg, si=si: nc.scalar.activation(out=sg[si][:], in_=C.bank[bkg][:, 0:512], func=AF.Silu),
                 reads=[C.bR[bkg]], writes=[sgR[si]])
            P.op("dve", lambda bku=bku, si=si, j=j: nc.vector.tensor_tensor(out=hT[:, j, :], in0=sg[si][:], in1=C.bank[bku][:, 0:512],
                                                                        op=ALU.mult),
                 reads=[sgR[si], C.bR[bku]], writes=[hTR[j]])
        if p + 1 < NP:
            transposes(p + 1)
        load_xr(4 * p)
        load_xr(4 * p + 1)
        for c in range(4):
            gc = 4 * p + c
            pair = (6, 7) if c % 2 == 0 else (0, 1)
            for half in range(2):
                bk = pair[half]
                for j in range(NFF):
                    P.op("pe", lambda j=j, c=c, half=half, bk=bk: nc.tensor.matmul(
                        C.bank[bk][:, 0:512], hT[:, j, c * 128:(c + 1) * 128], wd[:, j, half * 512:(half + 1) * 512],
                        start=(j == 0), stop=(j == NFF - 1)),
                        reads=[hTR[j], wdR[j]], writes=[C.bR[bk]])
            ln_epilogue(C, L, pair[0], pair[1], xr[gc % 2][:], xrR[gc % 2], dst[gc * 128:(gc + 1) * 128, :])
            if c + 2 < 4:
                load_xr(gc + 2)


def phase_mixer0(C, pes, src, dst):
    nc, P, dr = C.nc, C.P, C.dr
    T = 4
    W = 512
    NT = NCH // T
    sb = lambda name, shp, dt=F32: pes.enter_context(nc.sbuf_tensor(name, shp, dt))
    win = sb("m_win", [128, 8, 1536], BF16)
    winR = [[Reg("win%d_%d" % (k, g)) for g in range(3)] for k in range(8)]
    for g in range(3):
        for k in range(8):
            load_w_bf16(C, win[:, k, g * 512:(g + 1) * 512], dr["even_w_in"][k * 128:(k + 1) * 128, g * 512:(g + 1) * 512], winR[k][g])
    wout = sb("m_wout", [128, 8, D], BF16)
    woutR = [Reg("wout%d" % k) for k in range(8)]
    for k in range(8):
        load_w_bf16(C, wout[:, k, :], dr["even_w_out"][k * 128:(k + 1) * 128, :], woutR[k])
    poolw = sb("m_poolw", [128, 4, 128], BF16)
    poolwR = Reg("poolw")
    load_w_bf16(C, poolw[:], dr["even_pool_w"].rearrange("g c d -> c g d"), poolwR)
    A = sb("m_A", [128, 12, 128])
    AR = Reg("A")
    P.dma("sp", lambda: nc.sync.dma_start(out=A[:], in_=dr["c_poolA"].rearrange("n s t -> s n t")), writes=[AR])
    triu = sb("m_triu", [128, 128])
    triuR = Reg("triu")
    P.dma("sp", lambda: nc.sync.dma_start(out=triu[:], in_=dr["c_triu"][:, :]), writes=[triuR])
    wsn = sb("m_wsn", [128, 8, 128])
    wsnR = Reg("wsn")
    P.dma("sp", lambda: nc.sync.dma_start(out=wsn[:], in_=dr["even_spatial_w"].rearrange("h t s -> t h s")), writes=[wsnR])
    wsT = sb("m_wsT", [128, 8, 128], BF16)
    wsTR = Reg("wsT")
    for h in range(8):
        P.op("pe", lambda h=h: nc.tensor.transpose(C.bank[0][:, 0:128], wsn[:, h, :], C.ident[:]),
             reads=[wsnR, C.identR], writes=[C.bR[0]])
        P.op("dve", lambda h=h: nc.vector.tensor_tensor(out=wsT[:, h, :], in0=C.bank[0][:, 0:128], in1=triu[:], op=ALU.mult),
             reads=[C.bR[0], triuR], writes=[wsTR], part=True)
    bs = sb("m_bs", [128, 8])
    bsR = Reg("bs")
    pscale = sb("m_pscale", [128, 4])
    pscaleR = Reg("pscale")
    with nc.allow_non_contiguous_dma(reason="tiny per-head bias / scale columns"):
        P.dma("sp", lambda: nc.sync.dma_start(out=bs[:], in_=dr["even_spatial_b"].rearrange("h t -> t h")), writes=[bsR])
        P.dma("sp", lambda: nc.sync.dma_start(out=pscale[:], in_=dr["even_pool_scale"].rearrange("o (g d) -> d (o g)", g=4)),
              writes=[pscaleR])
        P.flush()
    vg, vgR = bcast_row(C, pes, "m_vg", dr["even_vnorm_g"], 512)
    vb, vbR = bcast_row(C, pes, "m_vb", dr["even_vnorm_b"], 512)
    L = ln_setup(C, pes, dr["mix_ln_g"][0:1, :], dr["mix_ln_b"][0:1, :], "m")
    xin = [sb("m_xin%d" % i, [128, T, D]) for i in range(3)]
    xinR = [[Reg("mxin%d_%d" % (i, c)) for c in range(T)] for i in range(3)]
    xT = [sb("m_xT%d" % i, [128, 8, W], BF16) for i in range(2)]
    xTR = [Reg("mxT%d" % i) for i in range(2)]
    mixT = [sb("m_mixT%d" % i, [128, 8, 128], BF16) for i in range(2)]
    mixTR = [Reg("mixT%d" % i) for i in range(2)]
    u_sb = [sb("m_u%d" % i, [128, 512]) for i in range(3)]
    uR = [Reg("u%d" % i) for i in range(3)]
    v_sb = [sb("m_v%d" % i, [128, 512]) for i in range(2)]
    vR = [Reg("v%d" % i) for i in range(2)]
    vbf = [sb("m_vbf%d" % i, [128, 512], BF16) for i in range(2)]
    vbfR = [Reg("vbf%d" % i) for i in range(2)]
    a_sb = [sb("m_a%d" % i, [128, 512]) for i in range(2)]
    aR = [Reg("a%d" % i) for i in range(2)]
    xp = [sb("m_xp%d" % i, [128, 512]) for i in range(4)]
    xpR = [Reg("xp%d" % i) for i in range(4)]
    pooledT = [sb("m_pooledT%d" % i, [128, 4, 128], BF16) for i in range(2)]
    pooledTR = [Reg("pooledT%d" % i) for i in range(2)]
    vst = [sb("m_vst%d" % i, [128, 16]) for i in range(2)]
    BXA, BPM, BU, BV, BSP, BPF, BOA, BOB = 0, 1, 2, 3, 4, 5, 6, 7

    def geo(gc):
        t, c = divmod(gc, T)
        return t, c, t % 2, slice(c * 128, (c + 1) * 128)

    def A_pe(gc):
        t, c, b, cs = geo(gc)
        for which, bk in ((0, BU), (1, BV)):
            for k in range(8):
                P.op("pe", lambda k=k, which=which, bk=bk, cs=cs, b=b: nc.tensor.matmul(
                    C.bank[bk][:, 0:512], xT[b][:, k, cs], win[:, k, which * 512:(which + 1) * 512],
                    start=(k == 0), stop=(k == 7)), reads=[xTR[b], winR[k][which]], writes=[C.bR[bk]])
        for k in range(8):
            P.op("pe", lambda k=k, cs=cs, b=b: nc.tensor.matmul(
                C.bank[BXA][:, 0:512], xT[b][:, k, cs], win[:, k, 1024:1536],
                start=(k == 0), stop=(k == 7)), reads=[xTR[b], winR[k][2]], writes=[C.bR[BXA]])
        i4 = gc % 4
        P.op("act", lambda i4=i4: nc.scalar.copy(out=xp[i4][:], in_=C.bank[BXA][:, 0:512]),
             reads=[C.bR[BXA]], writes=[xpR[i4]])

    def A_gelu(gc):
        i2, i3 = gc % 2, gc % 3
        P.op("act", lambda i3=i3: nc.scalar.activation(out=u_sb[i3][:], in_=C.bank[BU][:, 0:512], func=AF.Gelu),
             reads=[C.bR[BU]], writes=[uR[i3]])
        P.op("act", lambda i2=i2: nc.scalar.activation(out=v_sb[i2][:], in_=C.bank[BV][:, 0:512], func=AF.Gelu),
             reads=[C.bR[BV]], writes=[vR[i2]])

    def A_vln(gc):
        i2, i3 = gc % 2, gc % 3
        st = vst[i2]
        R0, R1, R2, R3 = Reg("vs0"), Reg("vs1"), Reg("vs2"), Reg("vs3")
        P.op("dve", lambda st=st, i2=i2: nc.vector.bn_stats(out=st[:, 0:6], in_=v_sb[i2][:]), reads=[vR[i2]], writes=[R0])
        P.op("dve", lambda st=st: nc.vector.bn_aggr(out=st[:, 6:8], in_=st[:, 0:6]), reads=[R0], writes=[R1])
        P.op("act", lambda st=st: nc.scalar.activation(out=st[:, 8:9], in_=st[:, 7:8], func=AF.Sqrt, bias=C.col[:, 2:3], scale=1.0),
             reads=[R1, C.colR], writes=[R2])
        P.op("dve", lambda st=st, i2=i2: nc.vector.scalar_tensor_tensor(out=v_sb[i2][:], in0=v_sb[i2][:], scalar=st[:, 6:7], in1=vg[:],
                                                                     op0=ALU.subtract, op1=ALU.mult),
             reads=[vR[i2], R1, vgR], writes=[vR[i2]])
        P.op("dve", lambda st=st: nc.vector.reciprocal(out=st[:, 9:10], in_=st[:, 8:9]), reads=[R2], writes=[R3])
        P.op("dve", lambda st=st, i2=i2: nc.vector.scalar_tensor_tensor(out=vbf[i2][:], in0=v_sb[i2][:], scalar=st[:, 9:10], in1=vb[:],
                                                                     op0=ALU.mult, op1=ALU.add),
             reads=[vR[i2], R3, vbR], writes=[vbfR[i2]])

    def B_pe(gc):
        i2, i3, ip = gc % 2, gc % 4, (gc - 1) % 4
        for h in range(8):
            P.op("pe", lambda h=h, i2=i2: nc.tensor.matmul(C.bank[BSP][:, h * 64:(h + 1) * 64], wsT[:, h, :],
                                                          vbf[i2][:, h * 64:(h + 1) * 64], start=True, stop=True),
                 reads=[wsTR, vbfR[i2]], writes=[C.bR[BSP]])
        for g in range(4):
            gs = slice(g * 128, (g + 1) * 128)
            if gc == 0:
                P.op("pe", lambda g=g, gs=gs, i3=i3: nc.tensor.matmul(C.bank[BPF][:, gs], xp[i3][:, gs], A[:, 8 + g, :],
                                                                     start=True, stop=True),
                     reads=[xpR[i3], AR], writes=[C.bR[BPF]])
            else:
                P.op("pe", lambda g=g, gs=gs, i3=i3: nc.tensor.matmul(C.bank[BPF][:, gs], xp[i3][:, gs], A[:, g, :],
                                                                     start=True, stop=False),
                     reads=[xpR[i3], AR], writes=[C.bR[BPF]])
                P.op("pe", lambda g=g, gs=gs, ip=ip: nc.tensor.matmul(C.bank[BPF][:, gs], xp[ip][:, gs], A[:, 4 + g, :],
                                                                     start=False, stop=True),
                     reads=[xpR[ip], AR], writes=[C.bR[BPF]])

    def B_add(gc):
        i2, i3 = gc % 2, gc % 3
        P.op("dve", lambda i2=i2: nc.vector.tensor_tensor(
            out=a_sb[i2][:].rearrange("p (h d) -> p h d", h=8), in0=C.bank[BSP][:, 0:512].rearrange("p (h d) -> p h d", h=8),
            in1=bs[:, 0:8].unsqueeze(2).to_broadcast([128, 8, 64]), op=ALU.add),
            reads=[C.bR[BSP], bsR], writes=[aR[i2]])
        P.op("pool", lambda i2=i2, i3=i3: nc.gpsimd.tensor_tensor(out=a_sb[i2][:], in0=a_sb[i2][:], in1=u_sb[i3][:], op=ALU.mult),
             reads=[aR[i2], uR[i3]], writes=[aR[i2]])

    def B_copy(gc):
        i2 = gc % 2
        P.op("act", lambda i2=i2: nc.scalar.copy(out=pooledT[i2][:].rearrange("p g t -> p (g t)"), in_=C.bank[BPF][:, 0:512]),
             reads=[C.bR[BPF]], writes=[pooledTR[i2]])

    def C_pe(gc):
        i2 = gc % 2
        for kb in range(4):
            P.op("pe", lambda kb=kb, i2=i2: nc.tensor.transpose(C.bank[BXA][:, kb * 128:(kb + 1) * 128],
                                                               a_sb[i2][:, kb * 128:(kb + 1) * 128], C.ident[:]),
                 reads=[aR[i2], C.identR], writes=[C.bR[BXA]])
        for g in range(4):
            P.op("pe", lambda g=g, i2=i2: nc.tensor.matmul(C.bank[BPM][:, g * 128:(g + 1) * 128], poolw[:, g, :], pooledT[i2][:, g, :],
                                                          start=True, stop=True),
                 reads=[poolwR, pooledTR[i2]], writes=[C.bR[BPM]])

    def C_copy(gc):
        i2 = gc % 2
        P.op("act", lambda i2=i2: nc.scalar.copy(out=mixT[i2][:, 0:4, :], in_=C.bank[BXA][:, 0:512].rearrange("p (k t) -> p k t", k=4)),
             reads=[C.bR[BXA]], writes=[mixTR[i2]], part=True)

    def C_scale(gc):
        i2 = gc % 2
        P.op("dve", lambda i2=i2: nc.vector.tensor_tensor(
            out=mixT[i2][:, 4:8, :], in0=C.bank[BPM][:, 0:512].rearrange("p (g t) -> p g t", g=4),
            in1=pscale[:, 0:4].unsqueeze(2).to_broadcast([128, 4, 128]), op=ALU.mult),
            reads=[C.bR[BPM], pscaleR], writes=[mixTR[i2]], part=True)

    def D_pe(gc):
        i2 = gc % 2
        for half, bk in ((0, BOA), (1, BOB)):
            for k in range(8):
                P.op("pe", lambda k=k, half=half, bk=bk, i2=i2: nc.tensor.matmul(
                    C.bank[bk][:, 0:512], mixT[i2][:, k, :], wout[:, k, half * 512:(half + 1) * 512],
                    start=(k == 0), stop=(k == 7)), reads=[mixTR[i2], woutR[k]], writes=[C.bR[bk]])

    def D_post(gc):
        t, c, b, cs = geo(gc)
        xi = t % 3
        ln_epilogue(C, L, BOA, BOB, xin[xi][:, c, :], xinR[xi][c], dst[gc * 128:(gc + 1) * 128, :])

    def TX(t):
        load_transpose_tile(C, src, t * T, T, xin[t % 3], xinR[t % 3], xT[t % 2], xTR[t % 2], (BU, BV))

    TX(0)
    ok = lambda g: 0 <= g < NCH
    for s_ in range(NCH + 8):
        if ok(s_ - 1):
            A_gelu(s_ - 1)
        if ok(s_ - 3):
            B_copy(s_ - 3)
        if ok(s_ - 5):
            C_copy(s_ - 5)
            C_scale(s_ - 5)
        if ok(s_ - 3):
            B_add(s_ - 3)
        if ok(s_ - 1):
            A_vln(s_ - 1)
        if ok(s_ - 7):
            D_post(s_ - 7)
        if s_ % T == 2 and (s_ // T) + 1 < NT:
            TX(s_ // T + 1)
        if ok(s_):
            A_pe(s_)
        if ok(s_ - 2):
            B_pe(s_ - 2)
        if ok(s_ - 6):
            D_pe(s_ - 6)
        if ok(s_ - 4):
            C_pe(s_ - 4)
        if s_ % 4 == 3:
            P.flush()


def phase_mla(C, pes, src, dst):
    nc, P, dr = C.nc, C.P, C.dr
    with ExitStack() as aes:
        mla_latents_and_attention(C, aes, src)
    mla_outproj(C, pes, src, dst)


def mla_latents_and_attention(C, pes, src):
    nc, P, dr = C.nc, C.P, C.dr
    sb = lambda name, shp, dt=F32: pes.enter_context(nc.sbuf_tensor(name, shp, dt))
    T, W, NT = 4, 512, 8
    w_in = dr["odd_w_in"]
    winR = Reg("a_win")
    w3 = w_in.rearrange("(k p) n -> p k n", p=128)
    wqu = sb("a_wqu", [128, 3, 1536], BF16)
    wqus = sb("a_wqus", [128, 3, 16, 96], BF16)
    wquR = Reg("a_wqu")
    q3 = dr["odd_w_q_up"].rearrange("(k p) n -> p k n", p=128)
    q4 = dr["odd_w_q_up"].rearrange("(k p) (h d) -> p k h d", p=128, h=16)
    wkn = sb("a_wkn", [128, 2, 16, 64], BF16)
    wv = sb("a_wv", [128, 2, 16, 64], BF16)
    wkvR = Reg("a_wkv")
    kv4 = dr["odd_w_kv_up"].rearrange("(k p) (h d) -> p k h d", p=128, h=16)

    def load_attn_weights():
        load_w_bf16(C, wqu[:], q3, wquR)
        for k in range(3):
            for hs in (slice(0, 8), slice(8, 16)):
                load_w_bf16(C, wqus[:, k, hs, 0:64], q4[:, k, hs, 0:64], wquR)
                load_w_bf16(C, wqus[:, k, hs, 64:80], q4[:, k, hs, 80:96], wquR)
                load_w_bf16(C, wqus[:, k, hs, 80:96], q4[:, k, hs, 64:80], wquR)
        for k in range(2):
            for hs in (slice(0, 8), slice(8, 16)):
                load_w_bf16(C, wkn[:, k, hs, :], kv4[:, k, hs, 0:64], wkvR)
                load_w_bf16(C, wv[:, k, hs, :], kv4[:, k, hs, 64:128], wkvR)
    gq = sb("a_gq", [128, 3])
    gkv = sb("a_gkv", [128, 2])
    gR = Reg("a_g")
    with nc.allow_non_contiguous_dma(reason="tiny norm-gain columns"):
        P.dma("sp", lambda: nc.sync.dma_start(out=gq[:], in_=dr["odd_q_norm_g"].rearrange("o (k p) -> p (o k)", p=128)), writes=[gR], part=True)
        P.dma("sp", lambda: nc.sync.dma_start(out=gkv[:], in_=dr["odd_kv_norm_g"].rearrange("o (k p) -> p (o k)", p=128)), writes=[gR], part=True)
        P.flush()
    ones = sb("a_ones", [128, 128], BF16)
    onesR = Reg("a_ones")
    P.op("pool", lambda: nc.gpsimd.memset(ones[:], 1.0), writes=[onesR])
    sel = sb("a_sel", [128, 64])
    selR = Reg("a_sel")
    P.dma("sp", lambda: nc.sync.dma_start(out=sel[:], in_=dr["c_sel"][:, :]), writes=[selR])
    tri = sb("a_tri", [128, 128])
    trib = sb("a_trib", [128, 128], BF16)
    triR = Reg("a_tri")
    tribR = Reg("a_trib")
    P.dma("sp", lambda: nc.sync.dma_start(out=tri[:], in_=dr["c_triu"][:, :]), writes=[triR])
    P.op("pool", lambda: nc.gpsimd.tensor_copy(out=trib[:], in_=tri[:]), reads=[triR], writes=[tribR])
    negm = sb("a_negm", [128, 128], BF16)
    identb = sb("a_identb", [128, 128], BF16)
    mskR = Reg("a_msk")
    P.op("dve", lambda: nc.vector.tensor_scalar(out=negm[:], in0=tri[:], scalar1=-1.0, scalar2=30000.0, op0=ALU.add, op1=ALU.mult),
         reads=[triR], writes=[mskR], part=True)
    P.op("dve", lambda: nc.vector.tensor_copy(out=identb[:], in_=C.ident[:]), reads=[C.identR], writes=[mskR], part=True)

    cosT = sb("a_cos", [128, S])
    sinT = sb("a_sin", [128, S])
    csR = [Reg("a_cs%d" % t) for t in range(NT)]
    cqT = sb("a_cqT", [128, 3, S], BF16)
    ckvT = sb("a_ckvT", [128, 2, S], BF16)
    latR = [Reg("a_lat%d" % t) for t in range(NT)]
    kT = [sb("a_kT%d" % i, [96, S], BF16) for i in range(2)]
    kTropeR = [Reg("a_kTr%d" % t) for t in range(NT)]
    kTR = [Reg("a_kT%d" % i) for i in range(2)]
    t1 = sb("a_t1", [128, W])
    t2 = sb("a_t2", [128, W])
    t1R, t2R = Reg("a_t1"), Reg("a_t2")
    rp = slice(64, 96)
    shared_pes = pes
    pes = ExitStack()
    pes.__enter__()

    wq_in = sb("a_wqin", [128, 8, 384], BF16)
    wkv_in = sb("a_wkvin", [128, 8, 256], BF16)
    wkr = sb("a_wkr", [128, 8, 96], BF16)
    wkrs = sb("a_wkrs", [128, 8, 96], BF16)
    load_w_bf16(C, wq_in[:], w3[:, :, 0:384], winR)
    load_w_bf16(C, wkv_in[:], w3[:, :, 384:640], winR)
    load_w_bf16(C, wkr[:], w3[:, :, 576:672], winR)
    load_w_bf16(C, wkrs[:, :, 0:64], w3[:, :, 576:640], winR)
    load_w_bf16(C, wkrs[:, :, 64:80], w3[:, :, 656:672], winR)
    load_w_bf16(C, wkrs[:, :, 80:96], w3[:, :, 640:656], winR)
    load_attn_weights()
    xin = [sb("a_xin%d" % i, [128, T, D]) for i in range(2)]
    xinR = [[Reg("axin%d_%d" % (i, c)) for c in range(T)] for i in range(2)]
    xT = [sb("a_xT%d" % i, [128, 8, W], BF16) for i in range(2)]
    xTR = [Reg("axT%d" % i) for i in range(2)]
    posi = sb("a_posi", [128, W], I32)
    ang = sb("a_ang", [128, W])
    tq = sb("a_tq", [128, W])
    ki = sb("a_ki", [128, W], I32)
    sq = [sb("a_sq%d" % i, [128, W], BF16) for i in range(2)]
    sqR = [Reg("a_sq%d" % i) for i in range(2)]
    rstd = sb("a_rstd", [128, W])
    rstdR = Reg("a_rstd")
    posR, angR, tqR, kiR = Reg("a_pos"), Reg("a_ang"), Reg("a_tq"), Reg("a_ki")

    def rope_tables(t):
        ts_ = slice(t * W, (t + 1) * W)
        P.dma("sp", lambda ts_=ts_: nc.sync.dma_start(out=posi[rp, :], in_=dr["positions"][0:1, ts_].partition_broadcast(32)), writes=[posR])
        P.op("dve", lambda: nc.vector.tensor_copy(out=ang[rp, :], in_=posi[rp, :]), reads=[posR], writes=[angR])
        P.op("dve", lambda: nc.vector.tensor_scalar(out=ang[rp, :], in0=ang[rp, :], scalar1=C.col[rp, 0:1], scalar2=None, op0=ALU.mult),
             reads=[angR, C.colR], writes=[angR])
        for which in (0, 1):
            off = 0.0 if which == 0 else PI / 2
            P.op("dve", lambda off=off: nc.vector.tensor_scalar(out=tq[rp, :], in0=ang[rp, :], scalar1=off, scalar2=1.0 / TWO_PI,
                                                               op0=ALU.add, op1=ALU.mult), reads=[angR], writes=[tqR])
            P.op("dve", lambda: nc.vector.tensor_copy(out=ki[rp, :], in_=tq[rp, :]), reads=[tqR], writes=[kiR])
            P.op("dve", lambda: nc.vector.tensor_copy(out=tq[rp, :], in_=ki[rp, :]), reads=[kiR], writes=[tqR])
            P.op("dve", lambda: nc.vector.scalar_tensor_tensor(out=tq[rp, :], in0=tq[rp, :], scalar=-TWO_PI, in1=ang[rp, :],
                                                              op0=ALU.mult, op1=ALU.add), reads=[tqR, angR], writes=[tqR])
            P.op("dve", lambda off=off: nc.vector.tensor_scalar(out=tq[rp, :], in0=tq[rp, :], scalar1=off, scalar2=PI,
                                                               op0=ALU.add, op1=ALU.min), reads=[tqR], writes=[tqR])
            P.op("dve", lambda: nc.vector.tensor_scalar(out=tq[rp, :], in0=tq[rp, :], scalar1=-PI, scalar2=None, op0=ALU.max),
                 reads=[tqR], writes=[tqR])
            if which == 0:
                P.op("act", lambda ts_=ts_: nc.scalar.activation(out=sinT[rp, ts_], in_=tq[rp, :], func=AF.Sin, scale=C.col[rp, 1:2]),
                     reads=[tqR, C.colR], writes=[csR[t]], part=True)
            else:
                P.op("act", lambda ts_=ts_: nc.scalar.activation(out=cosT[rp, ts_], in_=tq[rp, :], func=AF.Sin),
                     reads=[tqR], writes=[csR[t]], part=True)

    def proj_mms(t, wt, nblk, bk0):
        b = t % 2
        for kb in range(nblk):
            bk = bk0 + kb
            for k in range(8):
                P.op("pe", lambda k=k, kb=kb, bk=bk, wt=wt, b=b: nc.tensor.matmul(
                    C.bank[bk][:, 0:W], wt[:, k, kb * 128:(kb + 1) * 128], xT[b][:, k, :], start=(k == 0), stop=(k == 7)),
                    reads=[winR, xTR[b]], writes=[C.bR[bk]])

    def rms_post(t, nblk, dstL, gcol, inv_n, bk0):
        ts_ = slice(t * W, (t + 1) * W)
        SSB = 5
        for kb in range(nblk):
            bk = bk0 + kb
            si = kb % 2
            P.op("act", lambda bk=bk, si=si: nc.scalar.activation(out=sq[si][:], in_=C.bank[bk][:, 0:W], func=AF.Square),
                 reads=[C.bR[bk]], writes=[sqR[si]])
            P.op("pe", lambda si=si, kb=kb, nblk=nblk: nc.tensor.matmul(C.bank[SSB][:, 0:W], ones[:], sq[si][:],
                                                                       start=(kb == 0), stop=(kb == nblk - 1)),
                 reads=[onesR, sqR[si]], writes=[C.bR[SSB]])
        P.op("act", lambda inv_n=inv_n: nc.scalar.activation(out=rstd[:], in_=C.bank[SSB][:, 0:W], func=AF.Sqrt,
                                                            bias=C.col[:, 3:4], scale=inv_n),
             reads=[C.bR[SSB], C.colR], writes=[rstdR])
        P.op("dve", lambda: nc.vector.reciprocal(out=rstd[:], in_=rstd[:]), reads=[rstdR], writes=[rstdR])
        for kb in range(nblk):
            bk = bk0 + kb
            P.op("dve", lambda kb=kb, bk=bk, dstL=dstL, gcol=gcol, ts_=ts_: nc.vector.scalar_tensor_tensor(
                out=dstL[:, kb, ts_], in0=C.bank[bk][:, 0:W], scalar=gcol[:, kb:kb + 1], in1=rstd[:], op0=ALU.mult, op1=ALU.mult),
                reads=[C.bR[bk], gR, rstdR], writes=[latR[t]], part=True)

    rope_tables(0)
    load_transpose_tile(C, src, 0, T, xin[0], xinR[0], xT[0], xTR[0], (0, 1))
    for t in range(NT):
        b = t % 2
        ts_ = slice(t * W, (t + 1) * W)
        for (wt, bk) in ((wkr, 0), (wkrs, 1)):
            for k in range(8):
                P.op("pe", lambda k=k, wt=wt, bk=bk, b=b: nc.tensor.matmul(C.bank[bk][0:96, 0:W], wt[:, k, :], xT[b][:, k, :],
                                                                          start=(k == 0), stop=(k == 7)),
                     reads=[winR, xTR[b]], writes=[C.bR[bk]])
        proj_mms(t, wq_in, 3, 2)
        P.op("dve", lambda ts_=ts_: nc.vector.tensor_tensor(out=t1[rp, :], in0=C.bank[0][rp, 0:W], in1=cosT[rp, ts_], op=ALU.mult),
             reads=[C.bR[0], csR[t]], writes=[t1R])
        P.op("dve", lambda ts_=ts_: nc.vector.tensor_tensor(out=t2[rp, :], in0=C.bank[1][rp, 0:W], in1=sinT[rp, ts_], op=ALU.mult),
             reads=[C.bR[1], csR[t]], writes=[t2R])
        for i in range(2):
            P.op("pool", lambda i=i, ts_=ts_: nc.gpsimd.tensor_tensor(out=kT[i][rp, ts_], in0=t1[rp, :], in1=t2[rp, :], op=ALU.add),
                 reads=[t1R, t2R], writes=[kTropeR[t]], part=True)
        proj_mms(t, wkv_in, 2, 6)
        rms_post(t, 3, cqT, gq, 1.0 / 384, 2)
        if t + 1 < NT:
            load_transpose_tile(C, src, (t + 1) * T, T, xin[1 - b], xinR[1 - b], xT[1 - b], xTR[1 - b], (0, 1))
        rms_post(t, 2, ckvT, gkv, 1.0 / 256, 6)
        if t + 1 < NT:
            rope_tables(t + 1)

    P.barrier(C.marks)
    pes.__exit__(None, None, None)
    pes = ExitStack()
    pes.__enter__()
    qT = [sb("a_qT%d" % i, [96, S], BF16) for i in range(2)]
    qTR = [Reg("a_qT%d" % i) for i in range(2)]
    Vgs = [sb("a_V%d" % i, [128, NCH, 4, 65], BF16) for i in range(2)]
    VgRs = [Reg("a_V%d" % i) for i in range(2)]
    VoneR = Reg("a_Vone")
    for i in range(2):
        P.op("pool", lambda i=i: nc.gpsimd.memset(Vgs[i][:, :, :, 64:65], 1.0), writes=[VoneR], part=True)
    pT = [sb("a_pT%d" % i, [128, 1024], BF16) for i in range(3)]
    pTR = [Reg("a_pT%d" % i) for i in range(3)]
    oa = [sb("a_oa%d" % i, [128, W]) for i in range(2)]
    oaR = [Reg("a_oa%d" % i) for i in range(2)]
    oTh = [sb("a_oTh%d" % i, [64, S], BF16) for i in range(2)]
    oThR = [Reg("a_oTh%d" % i) for i in range(2)]
    allLat = latR + csR + kTropeR
    OB = (0, 1)
    SB3 = ((2, 3), (4, 5), (6, 7))
    npt = 0
    noa = 0
    def gen_v(hg):
        Vg, VgR = Vgs[hg % 2], VgRs[hg % 2]
        for c2 in range(NCH // 2):
            bk = 2 + (c2 % 2)
            for cc in range(2):
                c = 2 * c2 + cc
                for k in range(2):
                    P.op("pe", lambda k=k, c=c, cc=cc, bk=bk, hg=hg: nc.tensor.matmul(
                        C.bank[bk][:, cc * 256:(cc + 1) * 256], ckvT[:, k, c * 128:(c + 1) * 128],
                        wv[:, k, hg * 4:(hg + 1) * 4, :].rearrange("p h d -> p (h d)"), start=(k == 0), stop=(k == 1)),
                        reads=[wkvR] + latR, writes=[C.bR[bk]])
            P.op("dve", lambda c2=c2, bk=bk: nc.vector.tensor_copy(
                out=Vg[:, 2 * c2:2 * c2 + 2, :, 0:64], in_=C.bank[bk][:, 0:512].rearrange("p (c h d) -> p c h d", c=2, h=4)),
                reads=[C.bR[bk]], writes=[VgR], part=True)

    def gen_qk(h, t, pair=(4, 5)):
        b = h % 2
        ts_ = slice(t * W, (t + 1) * W)
        BQ, BS_, BK = pair[0], pair[1], pair[0]
        for (wt, bk) in ((None, BQ), (wqus, BS_)):
            for k in range(3):
                lhs = wqu[:, k, h * 96:(h + 1) * 96] if wt is None else wqus[:, k, h, :]
                P.op("pe", lambda k=k, lhs=lhs, bk=bk, ts_=ts_: nc.tensor.matmul(C.bank[bk][0:96, 0:W], lhs, cqT[:, k, ts_],
                                                                                 start=(k == 0), stop=(k == 2)),
                     reads=[wquR] + latR, writes=[C.bR[bk]])
        P.op("dve", lambda ts_=ts_, b=b: nc.vector.tensor_copy(out=qT[b][0:64, ts_], in_=C.bank[BQ][0:64, 0:W]),
             reads=[C.bR[BQ]], writes=[qTR[b]], part=True)
        P.op("dve", lambda ts_=ts_: nc.vector.tensor_tensor(out=t1[rp, :], in0=C.bank[BQ][rp, 0:W], in1=cosT[rp, ts_], op=ALU.mult),
             reads=[C.bR[BQ]] + csR, writes=[t1R])
        P.op("dve", lambda ts_=ts_: nc.vector.tensor_tensor(out=t2[rp, :], in0=C.bank[BS_][rp, 0:W], in1=sinT[rp, ts_], op=ALU.mult),
             reads=[C.bR[BS_]] + csR, writes=[t2R])
        P.op("pool", lambda ts_=ts_, b=b: nc.gpsimd.tensor_tensor(out=qT[b][rp, ts_], in0=t1[rp, :], in1=t2[rp, :], op=ALU.add),
             reads=[t1R, t2R], writes=[qTR[b]], part=True)
        for k in range(2):
            P.op("pe", lambda k=k, ts_=ts_: nc.tensor.matmul(C.bank[BK][0:64, 0:W], wkn[:, k, h, :], ckvT[:, k, ts_],
                                                             start=(k == 0), stop=(k == 1)),
                 reads=[wkvR] + latR, writes=[C.bR[BK]])
        P.op("dve", lambda ts_=ts_, b=b: nc.vector.tensor_copy(out=kT[b][0:64, ts_], in_=C.bank[BK][0:64, 0:W]),
             reads=[C.bR[BK]], writes=[kTR[b]], part=True)

    def emit_S(it):
        sb2 = it["sb2"]
        if it["kind"] == "gv":
            hg2, c2 = it["hg2"], it["c2"]
            for cc in range(2):
                c = 2 * c2 + cc
                for k in range(2):
                    P.op("pe", lambda k=k, c=c, cc=cc, sb2=sb2, hg2=hg2: nc.tensor.matmul(
                        C.bank[sb2[0]][:, cc * 256:(cc + 1) * 256], ckvT[:, k, c * 128:(c + 1) * 128],
                        wv[:, k, hg2 * 4:(hg2 + 1) * 4, :].rearrange("p h d -> p (h d)"), start=(k == 0), stop=(k == 1)),
                        reads=[wkvR] + latR, writes=[C.bR[sb2[0]]])
            return
        if it["kind"] in ("g1", "g2"):
            h2 = it["h2"]
            ts_ = slice(it["t"] * W, (it["t"] + 1) * W)
            if it["kind"] == "g1":
                for k in range(3):
                    P.op("pe", lambda k=k, ts_=ts_, h2=h2, sb2=sb2: nc.tensor.matmul(
                        C.bank[sb2[0]][0:96, 0:W], wqu[:, k, h2 * 96:(h2 + 1) * 96], cqT[:, k, ts_], start=(k == 0), stop=(k == 2)),
                        reads=[wquR] + latR, writes=[C.bR[sb2[0]]])
                for k in range(2):
                    P.op("pe", lambda k=k, ts_=ts_, h2=h2, sb2=sb2: nc.tensor.matmul(
                        C.bank[sb2[1]][0:64, 0:W], wkn[:, k, h2, :], ckvT[:, k, ts_], start=(k == 0), stop=(k == 1)),
                        reads=[wkvR] + latR, writes=[C.bR[sb2[1]]])
            else:
                for k in range(3):
                    P.op("pe", lambda k=k, ts_=ts_, h2=h2, sb2=sb2: nc.tensor.matmul(
                        C.bank[sb2[0]][0:96, 0:W], wqus[:, k, h2, :], cqT[:, k, ts_], start=(k == 0), stop=(k == 2)),
                        reads=[wquR] + latR, writes=[C.bR[sb2[0]]])
            return
        b, q0 = it["b"], it["q0"]
        if it["kind"] == "off":
            for u in range(2):
                kt = it["kt0"] + u
                P.op("pe", lambda kt=kt, u=u, sb2=sb2, b=b, q0=q0: nc.tensor.matmul(
                    C.bank[sb2[u]][:, 0:W], kT[b][:, kt * 128:(kt + 1) * 128], qT[b][:, q0:q0 + W], start=True, stop=True),
                    reads=[kTR[b], qTR[b]] + kTropeR, writes=[C.bR[sb2[u]]])
        else:
            kt, n0 = it["kt"], it["n0"]
            P.op("pe", lambda kt=kt, sb2=sb2, b=b, q0=q0, n0=n0: nc.tensor.matmul(
                C.bank[sb2[0]][:, n0:W], kT[b][:, kt * 128:(kt + 1) * 128], qT[b][:, q0 + n0:q0 + W], start=True, stop=False),
                reads=[kTR[b], qTR[b]] + kTropeR, writes=[C.bR[sb2[0]]])
            P.op("pe", lambda sb2=sb2, n0=n0: nc.tensor.matmul(
                C.bank[sb2[0]][:, n0:n0 + 128], identb[:], negm[:], start=False, stop=True),
                reads=[mskR], writes=[C.bR[sb2[0]]])

    def emit_EP(it):
        sb2 = it["sb2"]
        if it["kind"] == "gv":
            hg2, c2 = it["hg2"], it["c2"]
            P.op("dve", lambda c2=c2, sb2=sb2, hg2=hg2: nc.vector.tensor_copy(
                out=Vgs[hg2 % 2][:, 2 * c2:2 * c2 + 2, :, 0:64],
                in_=C.bank[sb2[0]][:, 0:512].rearrange("p (c h d) -> p c h d", c=2, h=4)),
                reads=[C.bR[sb2[0]]], writes=[VgRs[hg2 % 2]], part=True)
            return
        if it["kind"] in ("g1", "g2"):
            b2 = it["h2"] % 2
            ts_ = slice(it["t"] * W, (it["t"] + 1) * W)
            if it["kind"] == "g1":
                P.op("act", lambda ts_=ts_, b2=b2, sb2=sb2: nc.scalar.copy(out=qT[b2][0:64, ts_], in_=C.bank[sb2[0]][0:64, 0:W]),
                     reads=[C.bR[sb2[0]]], writes=[qTR[b2]], part=True)
                P.op("dve", lambda ts_=ts_, sb2=sb2: nc.vector.tensor_tensor(out=t1[rp, :], in0=C.bank[sb2[0]][rp, 0:W], in1=cosT[rp, ts_], op=ALU.mult),
                     reads=[C.bR[sb2[0]]] + csR, writes=[t1R])
                P.op("act", lambda ts_=ts_, b2=b2, sb2=sb2: nc.scalar.copy(out=kT[b2][0:64, ts_], in_=C.bank[sb2[1]][0:64, 0:W]),
                     reads=[C.bR[sb2[1]]], writes=[kTR[b2]], part=True)
            else:
                P.op("dve", lambda ts_=ts_, sb2=sb2: nc.vector.tensor_tensor(out=t2[rp, :], in0=C.bank[sb2[0]][rp, 0:W], in1=sinT[rp, ts_], op=ALU.mult),
                     reads=[C.bR[sb2[0]]] + csR, writes=[t2R])
                P.op("pool", lambda ts_=ts_, b2=b2: nc.gpsimd.tensor_tensor(out=qT[b2][rp, ts_], in0=t1[rp, :], in1=t2[rp, :], op=ALU.add),
                     reads=[t1R, t2R], writes=[qTR[b2]], part=True)
            return
        pi, ob, hh = it["pi"], it["ob"], it["hh"]
        Vg, VgR = Vgs[it["hg"] % 2], VgRs[it["hg"] % 2]
        if it["kind"] == "off":
            P.op("act", lambda sb2=sb2, pi=pi: nc.scalar.activation(out=pT[pi][:, 0:1024], in_=C.ps[:, sb2[0] * 512:sb2[0] * 512 + 1024],
                                                                   func=AF.Exp, scale=SCALE),
                 reads=[C.bR[sb2[0]], C.bR[sb2[1]]], writes=[pTR[pi]])
            for u in range(2):
                kt = it["kt0"] + u
                st = it["first"] and u == 0
                P.op("pe", lambda kt=kt, u=u, pi=pi, ob=ob, hh=hh, st=st: nc.tensor.matmul(
                    C.bank[ob][0:65, 0:W], Vg[:, kt, hh, :], pT[pi][:, u * W:(u + 1) * W], start=st, stop=False),
                    reads=[VgR, VoneR, pTR[pi]], writes=[C.bR[ob]])
        else:
            kt, n0 = it["kt"], it["n0"]
            P.op("act", lambda sb2=sb2, pi=pi, n0=n0: nc.scalar.activation(out=pT[pi][:, n0:W], in_=C.bank[sb2[0]][:, n0:W],
                                                                          func=AF.Exp, scale=SCALE),
                 reads=[C.bR[sb2[0]]], writes=[pTR[pi]])
            P.op("pe", lambda kt=kt, pi=pi, ob=ob, hh=hh, n0=n0, st=it["first"], sp_=it["last"]: nc.tensor.matmul(
                C.bank[ob][0:65, n0:W], Vg[:, kt, hh, :], pT[pi][:, n0:W], start=st, stop=sp_),
                reads=[VgR, VoneR, pTR[pi]], writes=[C.bR[ob]])

    def emit_norm_pre(h, j, ob):
        nonlocal noa
        oi = noa % 2
        noa += 1
        P.op("dve", lambda oi=oi, ob=ob: nc.vector.tensor_copy(out=oa[oi][0:65, :], in_=C.bank[ob][0:65, 0:W]),
             reads=[C.bR[ob]], writes=[oaR[oi]])
        return oi

    def emit_norm_recip(oi, q):
        qs = slice(q * 128, (q + 1) * 128)
        P.op("dve", lambda oi=oi, qs=qs: nc.vector.reciprocal(out=oa[oi][64:65, qs], in_=oa[oi][64:65, qs]),
             reads=[oaR[oi]], writes=[oaR[oi]])

    def emit_norm_post(h, j, ob, oi):
        ob_i = h % 2
        q0 = j * W
        P.op("pe", lambda oi=oi, ob=ob: nc.tensor.matmul(C.bank[ob][0:64, 0:W], sel[0:65, :], oa[oi][0:65, :], start=True, stop=True),
             reads=[selR, oaR[oi]], writes=[C.bR[ob]])
        P.op("dve", lambda oi=oi, ob=ob, ob_i=ob_i, q0=q0: nc.vector.tensor_tensor(
            out=oTh[ob_i][:, q0:q0 + W], in0=oa[oi][0:64, :], in1=C.bank[ob][0:64, 0:W], op=ALU.mult),
            reads=[oaR[oi], C.bR[ob]], writes=[oThR[ob_i]], part=True)
        if j == 7:
            P.dma("sp", lambda h=h, ob_i=ob_i: nc.sync.dma_start(out=C.dr["oTd"][h * 64:(h + 1) * 64, :], in_=oTh[ob_i][:]),
                  reads=[oThR[ob_i]])

    LA = 2
    pend = []
    gen_v(0)
    for t in range(NT):
        gen_qk(0, t)
    for h in range(_dbg_heads()):
        hg, hh = divmod(h, 4)
        b = h % 2
        items = []
        for j in range(8):
            ob = OB[j % 2]
            first = True
            for kt0 in range(0, 4 * j, 2):
                items.append(dict(kind="off", kt0=kt0, b=b, q0=j * W, ob=ob, hh=hh, hg=hg, first=first, last=False, j=j,
                                  sb2=SB3[npt % 3], pi=npt % 3))
                npt += 1
                first = False
            for r in range(4):
                items.append(dict(kind="diag", kt=4 * j + r, n0=128 * r, b=b, q0=j * W, ob=ob, hh=hh, hg=hg, first=first,
                                  last=(r == 3), j=j, sb2=SB3[npt % 3], pi=npt % 3))
                npt += 1
                first = False
            if h + 1 < NH and hh == 3:
                for c2 in (2 * j, 2 * j + 1):
                    items.append(dict(kind="gv", hg2=hg + 1, c2=c2, sb2=SB3[npt % 3], pi=npt % 3))
                    npt += 1
            if h + 1 < NH:
                for kind in ("g1", "g2"):
                    items.append(dict(kind=kind, h2=h + 1, t=j, sb2=SB3[npt % 3], pi=npt % 3))
                    npt += 1
        n = len(items)
        for i in range(n + LA):
            if i < n:
                emit_S(items[i])
            if i - LA >= 0:
                it = items[i - LA]
                emit_EP(it)
                npend = []
                for (stg, args) in pend:
                    if stg < 0:
                        npend.append((stg + 1, args))
                    elif stg == 0:
                        args[3] = emit_norm_pre(args[0], args[1], args[2])
                        npend.append((1, args))
                    elif stg <= 4:
                        emit_norm_recip(args[3], stg - 1)
                        npend.append((stg + 1, args))
                    else:
                        emit_norm_post(*args)
                pend = npend
                if it.get("last"):
                    has_gen = (h + 1 < NH)
                    if has_gen:
                        pend.append((-1, [h, it["j"], it["ob"], None]))
                    else:
                        oi = emit_norm_pre(h, it["j"], it["ob"])
                        pend.append((1, [h, it["j"], it["ob"], oi]))
        if h + 2 < NH:
            pend = [(max(stg, 0), args) for (stg, args) in pend]
        else:
            for (stg, args) in pend:
                if stg <= 0:
                    args[3] = emit_norm_pre(args[0], args[1], args[2])
                    stg = 1
                for q in range(stg - 1, 4):
                    emit_norm_recip(args[3], q)
                emit_norm_post(*args)
            pend = []
        P.flush()
    P.barrier(C.marks)
    pes.__exit__(None, None, None)


def mla_outproj(C, pes, src, dst):
    nc, P, dr = C.nc, C.P, C.dr
    sb = lambda name, shp, dt=F32: pes.enter_context(nc.sbuf_tensor(name, shp, dt))
    T, W, NT = 4, 512, 8
    wout = sb("o_wout", [128, 8, D], BF16)
    woutR = [Reg("o_wout%d" % k) for k in range(8)]
    for k in range(8):
        load_w_bf16(C, wout[:, k, :], dr["odd_w_out"][k * 128:(k + 1) * 128, :], woutR[k])
    L = ln_setup(C, pes, dr["mix_ln_g"][1:2, :], dr["mix_ln_b"][1:2, :], "o")
    xin = [sb("o_xin%d" % i, [128, T, D]) for i in range(2)]
    xinR = [[Reg("oxin%d_%d" % (i, c)) for c in range(T)] for i in range(2)]
    oTt = [sb("o_oT%d" % i, [128, 8, W], BF16) for i in range(2)]
    oTtR = [Reg("o_oT%d" % i) for i in range(2)]
    o3 = dr["oTd"].rearrange("(k p) t -> p k t", p=128)
    def loads(t):
        b = t % 2
        rows = src[t * W:(t + 1) * W, :].rearrange("(c p) d -> p c d", p=128)
        P.dma("sp", lambda rows=rows, b=b: nc.sync.dma_start(out=oTt[b][:], in_=o3[:, :, t * W:(t + 1) * W]), writes=[oTtR[b]])
        P.dma("sp", lambda rows=rows, b=b: nc.sync.dma_start(out=xin[b][:], in_=rows), writes=xinR[b])

    loads(0)
    for t in range(NT):
        b = t % 2
        if t + 1 < NT:
            loads(t + 1)
        for c in range(T):
            cs = slice(c * 128, (c + 1) * 128)
            pair = (0, 1) if c % 2 == 0 else (2, 3)
            for half in range(2):
                bk = pair[half]
                for k in range(8):
                    P.op("pe", lambda k=k, half=half, bk=bk, cs=cs, b=b: nc.tensor.matmul(
                        C.bank[bk][:, 0:512], oTt[b][:, k, cs], wout[:, k, half * 512:(half + 1) * 512],
                        start=(k == 0), stop=(k == 7)), reads=[oTtR[b], woutR[k]], writes=[C.bR[bk]])
            gc = t * T + c
            ln_epilogue(C, L, pair[0], pair[1], xin[b][:, c, :], xinR[b][c], dst[gc * 128:(gc + 1) * 128, :])


_NC_CACHE = {}


def _get_nc(phases, standalone):
    key = (tuple(phases), standalone)
    if key not in _NC_CACHE:
        _NC_CACHE[key] = build(list(phases), standalone)
    return _NC_CACHE[key]


def _weight_maps(inputs):
    m = {}
    for name, shp in W_SPECS.items():
        m[name] = np.ascontiguousarray(np.asarray(inputs[name], dtype=np.float32).reshape(shp))
    m.update(host_consts())
    return m


FUSED = True


def kernel(**inputs):
    x = np.asarray(inputs["x"], dtype=np.float32)
    pos = np.asarray(inputs["positions"], dtype=np.int32)
    wm = _weight_maps(inputs)
    n = 8
    if FUSED:
        nc = _get_nc((1, 2, 3, 4), False)
        in_maps = []
        for b in range(n):
            d = dict(wm)
            d["x"] = np.ascontiguousarray(x[b])
            d["positions"] = np.ascontiguousarray(pos[b:b + 1])
            in_maps.append(d)
        res = run_bass_kernel_spmd(nc, in_maps, core_ids=list(range(n)))
        return np.stack([res.results[b]["out"] for b in range(n)], axis=0)
    cur = [np.ascontiguousarray(x[b]) for b in range(n)]
    for ph in (1, 2, 3, 4):
        nc = _get_nc((ph,), True)
        in_maps = []
        for b in range(n):
            d = dict(wm)
            d["src"] = cur[b]
            d["positions"] = np.ascontiguousarray(pos[b:b + 1])
            in_maps.append(d)
        res = run_bass_kernel_spmd(nc, in_maps, core_ids=list(range(n)))
        cur = [np.ascontiguousarray(res.results[b]["dst"]) for b in range(n)]
    return np.stack(cur, axis=0)
```
